# Optimizing a Trainium2 kernel written in Bass

```python
import math
import jax, jax.numpy as jnp
from jax import lax
import numpy as np

D_MODEL = 1024
BATCH = 4
SEQ = 8192
DEPTH = 1

DN_HEADS = 8
DN_HEAD_DIM = 128
DN_WIDTH = DN_HEADS * DN_HEAD_DIM
DN_CONV = 3
DN_CHUNK = 64
HY_WIDTH = 1024
HY_CONV = 3
HY_POS_DIM = 33
HY_FILTER_HIDDEN = 64
HY_MIN_DECAY = 3.07
HY_MAX_DECAY = 15.35
NORM_EPS = 1e-6

IN_SIZES = (3 * DN_WIDTH, DN_WIDTH, 2 * DN_HEADS, 2 * DN_HEADS,
            3 * HY_WIDTH, HY_WIDTH, 2 * D_MODEL)
IN_WIDTH = 4 * DN_WIDTH + 4 * DN_HEADS + 4 * HY_WIDTH + 2 * D_MODEL

kernel_name = "hybrid_gdn_hyena_parallel_encoder"


def _split(t, sizes):
    idx, acc = [], 0
    for s in sizes[:-1]:
        acc += s
        idx.append(acc)
    return jnp.split(t, idx, axis=-1)


def rmsnorm(x, w):
    xf = x.astype(jnp.float32)
    y = xf * lax.rsqrt(jnp.mean(xf * xf, axis=-1, keepdims=True) + NORM_EPS)
    return (y * w.astype(jnp.float32)).astype(x.dtype)


def l2norm(x):
    xf = x.astype(jnp.float32)
    return xf * lax.rsqrt(jnp.sum(xf * xf, axis=-1, keepdims=True) + NORM_EPS)


def depthwise_conv_centred(x, w):
    k, c = w.shape
    return lax.conv_general_dilated(
        x, w[:, None, :].astype(x.dtype), window_strides=(1,),
        padding=[(k // 2, k // 2)], dimension_numbers=("NWC", "WIO", "NWC"),
        feature_group_count=c)


def chunk_gated_delta_rule(q, k, v, g, beta):
    f32 = jnp.float32
    n, L, h, dk = q.shape
    dv = v.shape[-1]
    c = DN_CHUNK
    nc = L // c

    def chunks(t):
        return t.astype(f32).reshape(n, nc, c, h, -1).transpose(0, 3, 1, 2, 4)

    q = chunks(q) * (dk ** -0.5)
    k = chunks(k)
    v = chunks(v)
    beta = beta.astype(f32).reshape(n, nc, c, h).transpose(0, 3, 1, 2)
    g = jnp.cumsum(g.astype(f32).reshape(n, nc, c, h).transpose(0, 3, 1, 2), axis=-1)

    tril = jnp.tril(jnp.ones((c, c), bool))
    strict = jnp.tril(jnp.ones((c, c), bool), -1)
    diff = g[..., :, None] - g[..., None, :]
    decay = jnp.where(tril, jnp.exp(jnp.where(tril, diff, 0.0)), 0.0)

    k_beta = k * beta[..., None]
    a = jnp.where(strict, jnp.einsum("nhcid,nhcjd->nhcij", k_beta, k) * decay, 0.0)
    a = a + jnp.eye(c, dtype=f32)
    rhs = jnp.concatenate([v * beta[..., None], k_beta * jnp.exp(g)[..., None]], axis=-1)
    sol = lax.linalg.triangular_solve(a, rhs, left_side=True, lower=True, unit_diagonal=True)
    u, w = sol[..., :dv], sol[..., dv:]

    attn = jnp.einsum("nhcid,nhcjd->nhcij", q, k) * decay
    q_dec = q * jnp.exp(g)[..., None]
    k_dec = k * jnp.exp(g[..., -1:] - g)[..., None]
    g_last = jnp.exp(g[..., -1])

    def step(S, xs):
        q_i, k_i, u_i, w_i, attn_i, gl_i = xs
        v_new = u_i - jnp.einsum("nhcd,nhde->nhce", w_i, S)
        o = jnp.einsum("nhcd,nhde->nhce", q_i, S) + jnp.einsum("nhij,nhje->nhie", attn_i, v_new)
        S = S * gl_i[..., None, None] + jnp.einsum("nhcd,nhce->nhde", k_i, v_new)
        return S, o

    xs = tuple(jnp.moveaxis(t, 2, 0) for t in (q_dec, k_dec, u, w, attn, g_last))
    S0 = jnp.zeros((n, h, dk, dv), f32)
    _, o = lax.scan(step, S0, xs)
    return o.transpose(1, 0, 3, 2, 4).reshape(n, L, h, dv)


def hyena_filter(L, w1, b1, w2, b2, w3, freq, log_decay):
    f32 = jnp.float32
    t = jnp.linspace(0.0, 1.0, L, dtype=f32)[:, None]
    bands = (HY_POS_DIM - 1) // 2
    f = jnp.linspace(1e-4, bands - 1, bands, dtype=f32)[None, :]
    ang = (2.0 * math.pi / L) * jnp.arange(L, dtype=f32)[:, None] * f
    z = jnp.concatenate([t, jnp.cos(ang), -jnp.sin(ang)], axis=-1)
    fr = freq.astype(f32)
    hdn = jnp.sin(fr * (z @ w1.astype(f32) + b1.astype(f32)))
    hdn = jnp.sin(fr * (hdn @ w2.astype(f32) + b2.astype(f32)))
    filt = hdn @ w3.astype(f32)
    filt = filt * jnp.exp(-t * jnp.exp(log_decay.astype(f32)))
    h_f, h_b = filt[:, :HY_WIDTH], filt[:, HY_WIDTH:]
    return jnp.concatenate([h_f, jnp.zeros((1, HY_WIDTH), f32), h_b[:0:-1]], axis=0)


def bidirectional_long_conv(u, kern, bias):
    L = u.shape[1]
    U = jnp.fft.rfft(u, n=2 * L, axis=1)
    K = jnp.fft.rfft(kern, n=2 * L, axis=0)
    y = jnp.fft.irfft(U * K[None], n=2 * L, axis=1)[:, :L]
    return y + u * bias.astype(jnp.float32)


def hybrid_mixer(h, w_in, dn_conv_w, dn_a_log, dn_dt_bias, dn_norm_w, hy_conv_w,
                 hy_w1, hy_b1, hy_w2, hy_b2, hy_w3, hy_freq, hy_log_decay, hy_bias,
                 w_dn_out, w_hy_out, w_out):
    b, L, _ = h.shape
    f32 = jnp.float32
    proj = h @ w_in
    qkv, dn_z, dn_beta, dn_a, hy_xv, hy_z, gates = _split(proj, IN_SIZES)

    qkv = jax.nn.silu(depthwise_conv_centred(qkv, dn_conv_w))
    qkv = qkv.reshape(b, L, 3, DN_HEADS, DN_HEAD_DIM)
    q, k, v = l2norm(qkv[:, :, 0]), l2norm(qkv[:, :, 1]), qkv[:, :, 2]
    beta = jax.nn.sigmoid(dn_beta.astype(f32)).reshape(b, L, 2, DN_HEADS)
    g = -jnp.exp(dn_a_log.astype(f32)) * jax.nn.softplus(
        dn_a.astype(f32).reshape(b, L, 2, DN_HEADS) + dn_dt_bias.astype(f32))

    def both(t_fwd, t_bwd):
        return jnp.concatenate([t_fwd, jnp.flip(t_bwd, axis=1)], axis=0)

    o = chunk_gated_delta_rule(both(q, q), both(k, k), both(v, v),
                               both(g[:, :, 0], g[:, :, 1]), both(beta[:, :, 0], beta[:, :, 1]))
    o = o[:b] + jnp.flip(o[b:], axis=1)
    o = rmsnorm(o, dn_norm_w).reshape(b, L, DN_WIDTH) * jax.nn.silu(dn_z.astype(f32))
    y_dn = o.astype(h.dtype) @ w_dn_out

    xv = depthwise_conv_centred(hy_xv, hy_conv_w)
    x0, x1, hv = jnp.split(xv, 3, axis=-1)
    u = (x1 * hv).astype(f32)
    kern = hyena_filter(L, hy_w1, hy_b1, hy_w2, hy_b2, hy_w3, hy_freq, hy_log_decay)
    y = x0.astype(f32) * bidirectional_long_conv(u, kern, hy_bias)
    y = y * jax.nn.silu(hy_z.astype(f32))
    y_hy = y.astype(h.dtype) @ w_hy_out

    g_dn, g_hy = jnp.split(jax.nn.sigmoid(gates), 2, axis=-1)
    return (g_dn * y_dn + g_hy * y_hy) @ w_out


def setup_inputs(seed: int = 0) -> dict:
    key = jax.random.key(seed)
    ks = jax.random.split(key, 24)
    f32 = jnp.float32
    nrm = lambda k, shape, scale: jax.random.normal(k, shape, f32) * scale
    Dp = DEPTH
    x = jax.random.normal(ks[0], (BATCH, SEQ, D_MODEL), f32)
    norm_in_w = 1.0 + nrm(ks[1], (Dp, D_MODEL), 0.02)
    w_in = nrm(ks[2], (Dp, D_MODEL, IN_WIDTH), D_MODEL ** -0.5)
    dn_conv_w = nrm(ks[3], (Dp, DN_CONV, 3 * DN_WIDTH), DN_CONV ** -0.5)
    dn_a_log = jnp.log(jax.random.uniform(ks[4], (Dp, 2, DN_HEADS), f32, 1.0, 16.0))
    dt = jnp.exp(jax.random.uniform(ks[5], (Dp, 2, DN_HEADS), f32, math.log(1e-3), math.log(1e-1)))
    dn_dt_bias = dt + jnp.log(-jnp.expm1(-dt))
    dn_norm_w = 1.0 + nrm(ks[6], (Dp, DN_HEAD_DIM), 0.02)
    hy_conv_w = nrm(ks[7], (Dp, HY_CONV, 3 * HY_WIDTH), HY_CONV ** -0.5)
    hy_w1 = nrm(ks[8], (Dp, HY_POS_DIM, HY_FILTER_HIDDEN), HY_POS_DIM ** -0.5)
    hy_b1 = nrm(ks[9], (Dp, HY_FILTER_HIDDEN), 0.02)
    hy_w2 = nrm(ks[10], (Dp, HY_FILTER_HIDDEN, HY_FILTER_HIDDEN), HY_FILTER_HIDDEN ** -0.5)
    hy_b2 = nrm(ks[11], (Dp, HY_FILTER_HIDDEN), 0.02)
    hy_w3 = nrm(ks[12], (Dp, HY_FILTER_HIDDEN, 2 * HY_WIDTH), 0.1 * HY_FILTER_HIDDEN ** -0.5)
    hy_freq = 1.0 + nrm(ks[13], (Dp, HY_FILTER_HIDDEN), 0.02)
    base_decay = jnp.log(jnp.linspace(HY_MIN_DECAY, HY_MAX_DECAY, HY_WIDTH, dtype=f32))
    hy_log_decay = jnp.tile(base_decay, 2)[None, :] + nrm(ks[14], (Dp, 2 * HY_WIDTH), 0.05)
    hy_bias = nrm(ks[15], (Dp, HY_WIDTH), 0.1)
    w_dn_out = nrm(ks[16], (Dp, DN_WIDTH, D_MODEL), DN_WIDTH ** -0.5)
    w_hy_out = nrm(ks[17], (Dp, HY_WIDTH, D_MODEL), HY_WIDTH ** -0.5)
    w_out = nrm(ks[18], (Dp, D_MODEL, D_MODEL), D_MODEL ** -0.5)
    norm_out_w = 1.0 + nrm(ks[19], (D_MODEL,), 0.02)
    return {"x": x, "norm_in_w": norm_in_w, "w_in": w_in, "dn_conv_w": dn_conv_w,
            "dn_a_log": dn_a_log, "dn_dt_bias": dn_dt_bias, "dn_norm_w": dn_norm_w,
            "hy_conv_w": hy_conv_w, "hy_w1": hy_w1, "hy_b1": hy_b1, "hy_w2": hy_w2,
            "hy_b2": hy_b2, "hy_w3": hy_w3, "hy_freq": hy_freq, "hy_log_decay": hy_log_decay,
            "hy_bias": hy_bias, "w_dn_out": w_dn_out, "w_hy_out": w_hy_out, "w_out": w_out,
            "norm_out_w": norm_out_w}


def reference(x, norm_in_w, w_in, dn_conv_w, dn_a_log, dn_dt_bias, dn_norm_w, hy_conv_w,
              hy_w1, hy_b1, hy_w2, hy_b2, hy_w3, hy_freq, hy_log_decay, hy_bias,
              w_dn_out, w_hy_out, w_out, norm_out_w):
    for l in range(DEPTH):
        h = rmsnorm(x, norm_in_w[l])
        x = x + hybrid_mixer(h, w_in[l], dn_conv_w[l], dn_a_log[l], dn_dt_bias[l], dn_norm_w[l],
                             hy_conv_w[l], hy_w1[l], hy_b1[l], hy_w2[l], hy_b2[l], hy_w3[l],
                             hy_freq[l], hy_log_decay[l], hy_bias[l],
                             w_dn_out[l], w_hy_out[l], w_out[l]).astype(x.dtype)
    return rmsnorm(x, norm_out_w)
```

```python
import numpy as np
import concourse.bass as bass
import concourse.mybir as mybir
from concourse.bass_utils import run_bass_kernel_spmd
from contextlib import ExitStack

F32 = mybir.dt.float32
BF16 = mybir.dt.bfloat16
AF = mybir.ActivationFunctionType
ALU = mybir.AluOpType
AX = mybir.AxisListType

D = 1024
L = 8192
HALF = 4096
NH = 8
INW = 10272
EPS = 1e-6
C_QKV, C_DNZ, C_BETA, C_A, C_HXV, C_HZ, C_GATE = 0, 3072, 4096, 4112, 4128, 7200, 8224


class Prog:
    SEM_EPOCH = 20000

    def __init__(self, nc, es, same_engine_sync=True):
        self.nc = nc
        self.es = es
        self.engs = {'pe': nc.tensor, 'act': nc.scalar, 'dve': nc.vector, 'pool': nc.gpsimd, 'sp': nc.sync}
        self.sem = {}
        self.cnt = {}
        self.nsem = 0
        for e in self.engs:
            self._new_eng_sem(e)
        self.waited = {e: {} for e in self.engs}
        self.last_w = {}
        self.readers = {}
        self.dma_sem = {}
        self.same = same_engine_sync
        self.ninstr = {e: 0 for e in self.engs}
        self.last_tok = {}
        self.dma_toks = []

    def _mksem(self, name):
        self.nsem += 1
        return self.es.enter_context(self.nc.semaphore(f"{name}_{self.nsem}"))

    def _new_eng_sem(self, e):
        self.sem[e] = self._mksem("s" + e)
        self.cnt[e] = 0

    def _wait(self, e, tok):
        if tok is None:
            return
        sem, val, src = tok
        if src == e and (not self.same or e == 'pe'):
            return
        w = self.waited[e]
        k = id(sem)
        if k in w and w[k] >= val:
            return
        w[k] = val
        self.engs[e].wait_ge(sem, val)
        self.ninstr[e] += 1

    def _deps(self, e, reads, writes):
        for k in reads:
            self._wait(e, self.last_w.get(k))
        for k in writes:
            t = self.last_w.get(k)
            if t is not None and (t[2] != e or k in reads):
                self._wait(e, t)
            for t in self.readers.get(k, ()):
                if t[2] != e:
                    self._wait(e, t)

    def _commit(self, tok, reads, writes):
        for k in reads:
            self.readers.setdefault(k, []).append(tok)
        for k in writes:
            self.last_w[k] = tok
            self.readers[k] = []

    PSUM_NAMES = ('PF', 'PB', 'PFx', 'pp', 'pn', 'ph', 'pa', 'pb', 'pf', 'pTo', 'pTx', 'pm', 'P1', 'P2', 'P3', 'P4')

    def _excl(self, reads, writes):
        r2, w2 = [], list(writes)
        for k in reads:
            nm = k[0] if isinstance(k, tuple) else k
            if nm in self.PSUM_NAMES:
                if k not in w2:
                    w2.append(k)
            else:
                r2.append(k)
        return r2, w2

    def op(self, e, fn, reads=(), writes=()):
        reads, writes = self._excl(reads, writes)
        self._deps(e, reads, writes)
        if self.cnt[e] >= self.SEM_EPOCH:
            self._new_eng_sem(e)
        ins = fn(self.engs[e])
        self.cnt[e] += 1
        ins.then_inc(self.sem[e], 1)
        self.ninstr[e] += 1
        tok = (self.sem[e], self.cnt[e], e)
        self.last_tok[e] = tok
        self._commit(tok, reads, writes)
        return tok

    def dma(self, e, out, in_, reads=(), writes=(), key=None, **kw):
        self._deps(e, reads, writes)
        if key is None:
            key = (writes[0] if writes else reads[0])
        ds = self.dma_sem.get(key)
        if ds is None or ds[1] + 16 > self.SEM_EPOCH:
            ds = [self._mksem("d"), 0]
            self.dma_sem[key] = ds
        ds[1] += 16
        self.engs[e].dma_start(out=out, in_=in_, **kw).then_inc(ds[0], 16)
        self.ninstr[e] += 1
        tok = (ds[0], ds[1], 'dma')
        self.dma_toks.append(tok)
        self._commit(tok, reads, writes)
        return tok

    def barrier(self):
        toks = list(self.last_tok.values())
        latest = {}
        for t in self.dma_toks:
            k = id(t[0])
            if k not in latest or latest[k][1] < t[1]:
                latest[k] = t
        toks += list(latest.values())
        self.dma_toks = list(latest.values())
        for e in self.engs:
            for t in toks:
                if t[2] == e:
                    continue
                self._wait(e, t)
        self.last_w = {}
        self.readers = {}


def _consts():
    ident = np.eye(128, dtype=np.float32)
    pi, fi = np.meshgrid(np.arange(128), np.arange(128), indexing="ij")
    NEG = -30000.0
    tri = np.stack([
        (pi <= fi).astype(np.float32),
        (pi >= fi).astype(np.float32),
        np.where(fi < pi, 0.0, NEG),
        np.where(fi >= pi, 0.0, NEG),
        np.where(fi > pi, 0.0, NEG),
        np.where(fi <= pi, 0.0, NEG),
    ]).astype(np.float32)
    sel = np.zeros((32, 2), np.float32)
    sel[:16, 0] = 1.0
    sel[16:, 1] = 1.0
    msk = np.stack([(pi // 32 == fi // 32), (pi // 64 == fi // 64) & (pi // 32 != fi // 32), (pi // 64 != fi // 64)]).astype(np.float32)
    out = {"c_ident": ident, "c_tri": tri, "c_sel": sel, "c_msk": msk}
    NF, NP = 97 * 128, 12800
    j = np.arange(NP)
    pos = np.where(j < HALF, j, NF - j).astype(np.float64)
    pos = np.clip(pos, 0, None)
    tl = pos / (L - 1)
    bands = 16
    fb = np.linspace(1e-4, bands - 1, bands)
    ang = (2.0 * np.pi / L) * pos[None, :] * fb[:, None]
    out["c_zpos"] = np.concatenate([tl[None, :], np.cos(ang), -np.sin(ang)], axis=0).astype(np.float32)
    out["c_dl"] = (np.arange(512) / (L - 1)).astype(np.float32)[None, :]
    q = np.arange(25)
    out["c_tp0"] = np.where(q < 8, 512 * q / (L - 1), (NF - 512 * q) / (L - 1)).astype(np.float32)
    a1 = 2 * np.pi * np.outer(np.arange(97), np.arange(97)) / 97
    c1, s1 = np.cos(a1), np.sin(a1)
    a2 = 2 * np.pi * np.outer(np.arange(128), np.arange(65)) / 128
    c2, s2 = np.cos(a2), np.sin(a2)
    out["c_e1"] = np.concatenate([c1, -s1], axis=1).astype(np.float32)
    out["c_s2a"] = np.concatenate([c2, -s2], axis=1).astype(np.float32)
    out["c_s2b"] = np.concatenate([s2, c2], axis=1).astype(np.float32)
    out["c_i1c"] = np.concatenate([c1, s1], axis=1).astype(np.float32)
    out["c_i1d"] = np.concatenate([-s1, c1], axis=1).astype(np.float32)
    wgt = np.full(65, 2.0)
    wgt[0] = wgt[64] = 1.0
    out["c_cw"] = (wgt[:, None] * c2.T / NF).astype(np.float32)
    out["c_sw"] = (-wgt[:, None] * s2.T / NF).astype(np.float32)
    return out


def build(cfg):
    nc = bass.Bass("TRN2", target_bir_lowering=False)
    dbg = cfg.get("dbg", ())

    def din(name, shape, dt=F32):
        return nc.dram_tensor(name, list(shape), dt, kind="ExternalInput").ap()

    def dscr(name, shape, dt, ext=False):
        kind = "ExternalInput" if ext else "Internal"
        return nc.dram_tensor(name, list(shape), dt, kind=kind).ap()

    x = din("x", [L, D])
    norm_in_w = din("norm_in_w", [D])
    w_in = din("w_in", [D, INW])
    w_dn_out = din("w_dn_out", [D, D])
    w_hy_out = din("w_hy_out", [D, D])
    w_out = din("w_out", [D, D])
    norm_out_w = din("norm_out_w", [D])
    c_ident = din("c_ident", [128, 128])
    y = nc.dram_tensor("y", [HALF, D], F32, kind="ExternalOutput").ap()

    ext = cfg.get("ext_scratch", ())
    OG = dscr("OG", [D, HALF], BF16, "OG" in ext)
    YH = dscr("YH", [D, HALF], BF16, "YH" in ext)
    GS = dscr("GS", [2 * D, HALF], BF16, "GS" in ext)
    QS = dscr("QS", [D, HALF], BF16, "QS" in ext)
    KS = dscr("KS", [D, L], BF16, "KS" in ext)
    VS = dscr("VS", [D, L], BF16, "VS" in ext)
    ZS = dscr("ZS", [D, HALF], BF16, "ZS" in ext)
    BGS = dscr("BGS", [32, L], F32, "BGS" in ext)
    OBS = dscr("OBS", [HALF, D], F32, "OBS" in ext)
    US = dscr("US", [D, L], BF16, "US" in ext)
    G0S = dscr("G0S", [D, HALF], BF16, "G0S" in ext)
    KLS = dscr("KLS", [D, 97 * 128], BF16, "KLS" in ext)
    hy_w1 = din("hy_w1", [33, 64])
    hy_b1 = din("hy_b1", [64])
    hy_w2 = din("hy_w2", [64, 64])
    hy_b2 = din("hy_b2", [64])
    hy_freq = din("hy_freq", [64])
    hy_w3 = din("hy_w3", [64, 3, D])
    hy_log_decay = din("hy_log_decay", [3, D])
    hy_bias = din("hy_bias", [D])
    c_zpos = din("c_zpos", [33, 12800])
    c_dl = din("c_dl", [1, 512])
    c_tp0 = din("c_tp0", [25])
    c_e1 = din("c_e1", [97, 194])
    c_s2a = din("c_s2a", [128, 130])
    c_s2b = din("c_s2b", [128, 130])
    c_i1c = din("c_i1c", [97, 194])
    c_i1d = din("c_i1d", [97, 194])
    c_cw = din("c_cw", [65, 128])
    c_sw = din("c_sw", [65, 128])
    dn_conv_w = din("dn_conv_w", [3, 3 * D])
    hy_conv_w = din("hy_conv_w", [3, 3 * D])
    dn_a_log = din("dn_a_log", [16])
    dn_dt_bias = din("dn_dt_bias", [16])
    dn_norm_w = din("dn_norm_w", [128])
    c_sel = din("c_sel", [32, 2])
    c_tri = din("c_tri", [6, 128, 128])
    c_msk = din("c_msk", [3, 128, 128])

    dbg_out = {}

    def ddbg(name, shape, dt=F32):
        dbg_out[name] = nc.dram_tensor("dbg_" + name, list(shape), dt, kind="ExternalOutput").ap()
        return dbg_out[name]

    with ExitStack() as es:
        p = Prog(nc, es)
        cs = ExitStack()
        es.enter_context(cs)

        uniq = [0]

        def sb(stack, name, shape, dt):
            uniq[0] += 1
            return stack.enter_context(nc.sbuf_tensor(f"{name}_{uniq[0]}", list(shape), dt))

        def ps(stack, name, shape, dt=F32):
            uniq[0] += 1
            return stack.enter_context(nc.psum_tensor(f"{name}_{uniq[0]}", list(shape), dt))

        ident_f = sb(cs, "ident_f", [128, 128], F32)
        ident_b = sb(cs, "ident_b", [128, 128], BF16)
        nw_t = sb(cs, "nw_t", [128, 8], F32)
        p.dma('sp', ident_f[:], c_ident[:, :], writes=['ident_f'])
        p.op('dve', lambda e: e.tensor_copy(ident_b[:], ident_f[:]), reads=['ident_f'], writes=['ident_b'])
        p.dma('sp', nw_t[:], norm_in_w.rearrange("(k p) -> p k", p=128), writes=['nw_t'],
              allow_slow_non_contiguous=True)
        cwd = sb(cs, "cwd", [128, 24, 3], F32)
        cwh = sb(cs, "cwh", [128, 24, 3], F32)
        nwd = sb(cs, "nwd", [128, 1], F32)
        dtb = sb(cs, "dtb", [32, 1], F32)
        negA = sb(cs, "negA", [32, 1], F32)
        selt = sb(cs, "selt", [32, 2], F32)
        selb, selg = selt[:, 0:1], selt[:, 1:2]
        for j in range(3):
            p.dma('sp', cwd[:, :, j], dn_conv_w[j, :].rearrange("(g p) -> p g", p=128), writes=[('cwd', j)], allow_slow_non_contiguous=True)
            p.dma('sp', cwh[:, :, j], hy_conv_w[j, :].rearrange("(g p) -> p g", p=128), writes=[('cwh', j)], allow_slow_non_contiguous=True)
        p.dma('sp', nwd[:, :], dn_norm_w.rearrange("(p o) -> p o", o=1), writes=['nwd'], allow_slow_non_contiguous=True)
        p.op('pool', lambda e: e.memset(dtb[:], 0.0), writes=['dtb'])
        p.op('pool', lambda e: e.memset(negA[:], 0.0), writes=['negA'])
        p.dma('sp', dtb[16:32, :], dn_dt_bias.rearrange("(p o) -> p o", o=1), reads=[], writes=['dtb'], allow_slow_non_contiguous=True)
        p.dma('sp', negA[16:32, :], dn_a_log.rearrange("(p o) -> p o", o=1), writes=['negA'], allow_slow_non_contiguous=True)
        p.dma('sp', selt[:, :], c_sel[:, :], writes=['selb'])
        p.op('act', lambda e: e.activation(out=negA[:], in_=negA[:], func=AF.Exp), reads=['negA'], writes=['negA'])
        p.op('dve', lambda e: e.tensor_scalar(out=negA[:], in0=negA[:], scalar1=-1.0, scalar2=None, op0=ALU.mult), reads=['negA'], writes=['negA'])
        p.barrier()

        def build_hT(stk, hT, tok_base, pre_tok, post_tok, tag):
            with ExitStack() as ls:
                xt = [sb(ls, f"xt{tag}{i}", [128, D], F32) for i in range(2)]
                junk = sb(ls, f"junk{tag}", [128, D], BF16)
                xn = [sb(ls, f"xn{tag}{i}", [128, D], BF16) for i in range(2)]
                ssq = [sb(ls, f"ssq{tag}{i}", [128, 1], F32) for i in range(2)]
                pT = [ps(ls, f"pT{tag}{i}", [128, 8, 128], BF16) for i in range(2)]
                jobs = [(tok_base + 128 * i, 128, 1 + 128 * i) for i in range(HALF // 128)]
                for hc, tk in ((0, pre_tok), (HALF + 1, post_tok)):
                    if tk is None:
                        p.op('pool', lambda e, hc=hc: e.memset(hT[:, :, hc:hc + 1], 0.0), writes=[('hT', 'halo', hc)])
                    else:
                        jobs.append((tk, 1, hc))
                for ji, (t0, n, c0) in enumerate(jobs):
                    b = ji % 2
                    kx, kn, ks, kp = (f'xt{tag}', b), (f'xn{tag}', b), (f'ssq{tag}', b), (f'pT{tag}', b)
                    p.dma('sp', xt[b][0:n, :], x[t0:t0 + n, :], writes=[kx])
                    p.op('act', lambda e: e.activation(out=junk[0:n, :], in_=xt[b][0:n, :], func=AF.Square,
                                                       accum_out=ssq[b][0:n, :]), reads=[kx], writes=['junk' + tag, ks])
                    p.op('act', lambda e: e.activation(out=ssq[b][0:n, :], in_=ssq[b][0:n, :], func=AF.Ln, scale=1.0 / D, bias=EPS),
                         reads=[ks], writes=[ks])
                    p.op('act', lambda e: e.activation(out=ssq[b][0:n, :], in_=ssq[b][0:n, :], func=AF.Exp, scale=-0.5),
                         reads=[ks], writes=[ks])
                    p.op('dve', lambda e: e.tensor_scalar(out=xn[b][0:n, :], in0=xt[b][0:n, :], scalar1=ssq[b][0:n, :],
                                                          scalar2=None, op0=ALU.mult), reads=[kx, ks], writes=[kn])
                    for k in range(8):
                        p.op('pe', lambda e, k=k: e.transpose(pT[b][:, k, 0:n], xn[b][0:n, k * 128:(k + 1) * 128],
                                                              ident_b[0:n, 0:n]),
                             reads=[kn, 'ident_b'], writes=[kp])
                    key = ('hT', (c0 - 1) // 512) if n == 128 else ('hT', 'halo', c0)
                    p.op('dve', lambda e: e.tensor_tensor(out=hT[:, :, c0:c0 + n], in0=pT[b][:, :, 0:n],
                                                          in1=nw_t[:, :].unsqueeze(2).broadcast_to([128, 8, n]), op=ALU.mult),
                         reads=[kp, 'nw_t'], writes=[key])
                p.barrier()

        def hT_keys(nblk=8):
            return [('hT', i) for i in range(nblk)]

        wctr = [0]

        def load_w_group(wst, wbf, src_ap):
            b = wctr[0] % len(wst)
            wctr[0] += 1
            ncol = src_ap.shape[1]
            p.dma('sp', wst[b][:, :, 0:ncol], src_ap.rearrange("(k p) c -> p k c", p=128), writes=[('wst', b)])
            p.op('pool', lambda e: e.tensor_copy(wbf[b][:, :, 0:ncol], wst[b][:, :, 0:ncol]), reads=[('wst', b)],
                 writes=[('wbf', b, k) for k in range(8)])
            return b

        W = 2048
        NB = W // 512

        def run_rr(job_specs, make_gen, K, stagger=1):
            active = {}
            nxt_job = 0
            free = list(range(K))
            since = stagger
            while nxt_job < len(job_specs) or active:
                since += 1
                while free and nxt_job < len(job_specs) and since > stagger:
                    sl = free.pop(0)
                    active[sl] = make_gen(sl, job_specs[nxt_job])
                    nxt_job += 1
                    since = 0 if stagger > 0 else since
                for sl in sorted(active.keys()):
                    try:
                        next(active[sl])
                    except StopIteration:
                        del active[sl]
                        free.append(sl)

        def phase_A(own):
            tok_base = 0 if own else HALF
            KJ = 3
            with ExitStack() as s2:
                hT = sb(s2, "hT", [128, 8, HALF + 2], BF16)
                if own:
                    build_hT(s2, hT, 0, None, HALF, "o")
                else:
                    build_hT(s2, hT, HALF, HALF - 1, None, "x")
                wst = [sb(s2, f"wst{i}", [128, 8, 128], F32) for i in range(3)]
                wbf = [sb(s2, f"wbf{i}", [128, 8, 128], BF16) for i in range(3)]
                pp = [ps(s2, f"pp{i}", [128, 512], F32) for i in range(5)]
                pn = [ps(s2, f"pn{i}", [128, 512], F32) for i in range(3)]
                R = [sb(s2, f"R{i}", [128, W + 2], F32) for i in range(KJ)]
                acc = [sb(s2, f"acc{i}", [128, W], F32) for i in range(KJ)]
                sil = acc
                sqb = [sb(s2, f"sqb{i}", [128, W], BF16) for i in range(KJ)]
                rst = [sb(s2, f"rst{i}", [128, W], F32) for i in range(KJ)]
                ob = [sb(s2, f"ob{i}", [128, W], BF16) for i in range(KJ)]
                keep2 = [sb(s2, f"keep{i}", [128, W], F32) for i in range(2)]
                ones_b = sb(s2, "ones_b", [128, 128], BF16)
                p.op('pool', lambda e: e.memset(ones_b[:], 1.0), writes=['ones_b'])
                ctr = {'pp': 0, 'pn': 0}

                def nxt(nm, n):
                    v = ctr[nm] % n
                    ctr[nm] += 1
                    return v

                def Rkeys(ri):
                    return [('R', ri, bk) for bk in range(5)]

                BW = (W + 2) // 5

                def project(sl, wb, ncol, ps_i):
                    c0 = ps_i * W
                    for bk in range(5):
                        pi = nxt('pp', 5)
                        cs = c0 + bk * BW
                        for k in range(8):
                            p.op('pe', lambda e, k=k: e.matmul(pp[pi][0:ncol, 0:BW], wbf[wb][:, k, 0:ncol], hT[:, k, cs:cs + BW],
                                                               start=(k == 0), stop=(k == 7)),
                                 reads=[('wbf', wb, k)], writes=[('pp', pi)])
                        dst = R[sl][0:ncol, bk * BW:(bk + 1) * BW]
                        if bk % 2 == 0:
                            p.op('act', lambda e: e.activation(out=dst, in_=pp[pi][0:ncol, 0:BW], func=AF.Copy), reads=[('pp', pi)], writes=[('R', sl, bk)])
                        else:
                            p.op('dve', lambda e: e.tensor_copy(dst, pp[pi][0:ncol, 0:BW]), reads=[('pp', pi)], writes=[('R', sl, bk)])

                def conv(sl, cw, g):
                    p.op('act', lambda e: e.activation(out=acc[sl][:, :], in_=R[sl][:, 0:W], func=AF.Copy, scale=cw[:, g, 0:1]),
                         reads=Rkeys(sl), writes=[('acc', sl)])
                    for j in (1, 2):
                        p.op('dve', lambda e, j=j: e.scalar_tensor_tensor(out=acc[sl][:, :], in0=R[sl][:, j:j + W], scalar=cw[:, g, j:j + 1],
                                                                           in1=acc[sl][:, :], op0=ALU.mult, op1=ALU.add),
                             reads=Rkeys(sl) + [('acc', sl)], writes=[('acc', sl)])

                def store(dst_rows, sl, ps_i, key):
                    c0 = tok_base + ps_i * W if dst_rows.shape[1] == L else ps_i * W
                    p.dma('sp', dst_rows[:, c0:c0 + W], ob[sl][:, :], reads=[('ob', sl)], writes=[key], key=('ob', sl))

                RST = lambda sl: [('rst', sl, bk) for bk in range(NB)]

                def job(sl, spec):
                    kind, wb, ps_i, idx = spec
                    if kind == 'gate':
                        for bk in range(NB):
                            pi = nxt('pp', 5)
                            cs = 1 + ps_i * W + bk * 512
                            for k in range(8):
                                p.op('pe', lambda e, k=k: e.matmul(pp[pi][:, :], wbf[wb][:, k, :], hT[:, k, cs:cs + 512], start=(k == 0), stop=(k == 7)),
                                     reads=[('wbf', wb, k)], writes=[('pp', pi)])
                            p.op('act', lambda e: e.activation(out=ob[sl][:, bk * 512:(bk + 1) * 512], in_=pp[pi][:, :], func=AF.Sigmoid),
                                 reads=[('pp', pi)], writes=[('ob', sl)])
                        store(GS[idx * 128:(idx + 1) * 128, :], sl, ps_i, ('GS', idx, ps_i))
                        return
                    ncol = 32 if kind == 'bg' else 128
                    project(sl, wb, ncol, ps_i)
                    yield
                    if kind == 'bg':
                        xin = R[sl][0:32, 1:W + 1]
                        rk = Rkeys(sl)
                        t0, t1, t2 = acc[sl][0:32, :], xin, rst[sl][0:32, :]
                        K0, K1, K2 = ('acc', sl), ('R', sl, 0), RST(sl)
                        p.op('act', lambda e: e.activation(out=t0, in_=xin, func=AF.Exp, scale=-1.0), reads=rk, writes=[K0])
                        p.op('dve', lambda e: e.tensor_scalar(out=t0, in0=t0, scalar1=1.0, scalar2=None, op0=ALU.add), reads=[K0], writes=[K0])
                        p.op('dve', lambda e: e.reciprocal(out=t0, in_=t0), reads=[K0], writes=[K0])
                        p.op('dve', lambda e: e.tensor_scalar(out=t1, in0=xin, scalar1=dtb[:, 0:1], scalar2=None, op0=ALU.add),
                             reads=rk + ['dtb'], writes=rk)
                        p.op('act', lambda e: e.activation(out=t2, in_=t1, func=AF.Abs), reads=[K1], writes=K2)
                        p.op('act', lambda e: e.activation(out=t2, in_=t2, func=AF.Exp, scale=-1.0), reads=K2, writes=K2)
                        p.op('act', lambda e: e.activation(out=t2, in_=t2, func=AF.Ln, bias=1.0), reads=K2, writes=K2)
                        p.op('dve', lambda e: e.scalar_tensor_tensor(out=t1, in0=t1, scalar=0.0, in1=t2, op0=ALU.max, op1=ALU.add),
                             reads=[K1] + K2, writes=[K1])
                        p.op('dve', lambda e: e.tensor_scalar(out=t1, in0=t1, scalar1=negA[:, 0:1], scalar2=selg[:, 0:1], op0=ALU.mult, op1=ALU.mult),
                             reads=[K1, 'negA', 'selb'], writes=[K1])
                        p.op('dve', lambda e: e.scalar_tensor_tensor(out=t2, in0=t0, scalar=selb[:, 0:1], in1=t1, op0=ALU.mult, op1=ALU.add),
                             reads=[K0, K1, 'selb'], writes=K2)
                        c0 = tok_base + ps_i * W
                        p.dma('sp', BGS[:, c0:c0 + W], t2, reads=K2, writes=[('BGS', own, ps_i)], key=('rst', sl))
                        return
                    if kind in ('z', 'hz'):
                        p.op('act', lambda e: e.activation(out=sil[sl][:, :], in_=R[sl][:, 1:W + 1], func=AF.Silu), reads=Rkeys(sl), writes=[('acc', sl)])
                        yield
                        if kind == 'z':
                            p.op('dve', lambda e: e.tensor_scalar(out=ob[sl][:, :], in0=sil[sl][:, :], scalar1=nwd[:, 0:1], scalar2=None, op0=ALU.mult),
                                 reads=[('acc', sl), 'nwd'], writes=[('ob', sl)])
                            store(ZS[idx * 128:(idx + 1) * 128, :], sl, ps_i, ('ZS', idx, ps_i))
                        else:
                            p.op('pool', lambda e: e.tensor_tensor(out=ob[sl][:, :], in0=sil[sl][:, :], in1=keep2[ps_i][:, :], op=ALU.mult),
                                 reads=[('acc', sl), ('keep', ps_i)], writes=[('ob', sl)])
                            store(G0S[idx * 128:(idx + 1) * 128, :], sl, ps_i, ('G0S', idx, ps_i))
                        return
                    if kind in ('q', 'k', 'v'):
                        gi = {'q': idx, 'k': NH + idx, 'v': 2 * NH + idx}[kind]
                        conv(sl, cwd, gi)
                        yield
                        if kind == 'v':
                            p.op('act', lambda e: e.activation(out=ob[sl][:, :], in_=acc[sl][:, :], func=AF.Silu), reads=[('acc', sl)], writes=[('ob', sl)])
                            store(VS[idx * 128:(idx + 1) * 128, :], sl, ps_i, ('VS', idx, own, ps_i))
                            return
                        p.op('act', lambda e: e.activation(out=sil[sl][:, :], in_=acc[sl][:, :], func=AF.Silu), reads=[('acc', sl)], writes=[('acc', sl)])
                        yield
                        p.op('act', lambda e: e.activation(out=sqb[sl][:, :], in_=sil[sl][:, :], func=AF.Square),
                             reads=[('acc', sl)], writes=[('sqb', sl)])
                        yield
                        for bk in range(NB):
                            ni = nxt('pn', 3)
                            p.op('pe', lambda e: e.matmul(pn[ni][:, :], ones_b[:, :], sqb[sl][:, bk * 512:(bk + 1) * 512], start=True, stop=True),
                                 reads=['ones_b', ('sqb', sl)], writes=[('pn', ni)])
                            p.op('act', lambda e: e.activation(out=rst[sl][:, bk * 512:(bk + 1) * 512], in_=pn[ni][:, :], func=AF.Ln, bias=EPS),
                                 reads=[('pn', ni)], writes=[('rst', sl, bk)])
                        scale = 128.0 ** -0.5 if kind == 'q' else 1.0
                        p.op('act', lambda e: e.activation(out=rst[sl][:, :], in_=rst[sl][:, :], func=AF.Exp, scale=-0.5, bias=float(np.log(scale))),
                             reads=RST(sl), writes=RST(sl))
                        yield
                        p.op('pool', lambda e: e.tensor_tensor(out=ob[sl][:, :], in0=sil[sl][:, :], in1=rst[sl][:, :], op=ALU.mult),
                             reads=[('acc', sl)] + RST(sl), writes=[('ob', sl)])
                        dstT = QS if kind == 'q' else KS
                        key = ('QS', idx, ps_i) if kind == 'q' else ('KS', idx, own, ps_i)
                        store(dstT[idx * 128:(idx + 1) * 128, :], sl, ps_i, key)
                        return
                    gi = {'x0': idx, 'x1': 8 + idx, 'hv': 16 + idx}[kind]
                    conv(sl, cwh, gi)
                    yield
                    if kind in ('x0', 'x1'):
                        p.op('pool', lambda e: e.tensor_copy(keep2[ps_i][:, :], acc[sl][:, :]), reads=[('acc', sl)], writes=[('keep', ps_i)])
                    else:
                        p.op('pool', lambda e: e.tensor_tensor(out=ob[sl][:, :], in0=acc[sl][:, :], in1=keep2[ps_i][:, :], op=ALU.mult),
                             reads=[('acc', sl), ('keep', ps_i)], writes=[('ob', sl)])
                        store(US[idx * 128:(idx + 1) * 128, :], sl, ps_i, ('US', idx, own, ps_i))

                groups = []
                for h in range(NH):
                    for kind in (('q', 'k', 'v', 'z') if own else ('k', 'v')):
                        gi = {'q': h, 'k': NH + h, 'v': 2 * NH + h}.get(kind)
                        groups.append((kind, C_QKV + gi * 128 if kind != 'z' else C_DNZ + h * 128, 128, h))
                groups.append(('bg', C_BETA, 32, 0))
                if "hyproj" in cfg["phases"]:
                    for cb in range(8):
                        for kind in (('x0', 'hz', 'x1', 'hv') if own else ('x1', 'hv')):
                            gi = {'x0': cb, 'x1': 8 + cb, 'hv': 16 + cb}.get(kind)
                            groups.append((kind, C_HXV + gi * 128 if kind != 'hz' else C_HZ + cb * 128, 128, cb))
                if own and "gates" in cfg["phases"]:
                    for gi in range(16):
                        groups.append(('gate', C_GATE + gi * 128, 128, gi))
                specs = []
                for (kind, col0, ncol, idx) in groups:
                    for ps_i in range(2):
                        specs.append((kind, col0, ncol, ps_i, idx))
                wb_of = {}

                gorder = [(g[0], g[3]) for g in groups]
                ginfo = {(g[0], g[3]): g for g in groups}

                def ensure_w(gk):
                    if gk not in wb_of:
                        kind, col0, ncol, idx = ginfo[gk]
                        wb_of[gk] = load_w_group(wst, wbf, w_in[:, col0:col0 + ncol])

                def make(sl, sp):
                    kind, col0, ncol, ps_i, idx = sp
                    ensure_w((kind, idx))
                    gi_ = gorder.index((kind, idx))
                    if ps_i == 0 and gi_ + 1 < len(gorder):
                        ensure_w(gorder[gi_ + 1])
                    return job(sl, (kind, wb_of[(kind, idx)], ps_i, idx))

                run_rr(specs, make, KJ, stagger=cfg.get('stagA', 0))
                p.barrier()

        if "projA" in cfg["phases"]:
            phase_A(False)
            phase_A(True)

        def phase_B():
            with ExitStack() as sB:
                NEG = -30000.0
                cst = sb(sB, "tricst", [128, 6, 128], F32)
                p.dma('sp', cst[:, :, :], c_tri.rearrange("c p f -> p c f"), writes=['tricst'])
                ones_f = sb(sB, "ones_f", [128, 128], F32)
                p.op('pool', lambda e: e.memset(ones_f[:], 1.0), writes=['ones_f'])
                mskf = sb(sB, "mskf", [128, 3, 128], F32)
                p.dma('sp', mskf[:, :, :], c_msk.rearrange("c p f -> p c f"), writes=['mskf'])
                m32h = sb(sB, "m32h", [128, 8, 128], BF16)
                mo64h = sb(sB, "mo64h", [128, 8, 128], BF16)
                mo128h = sb(sB, "mo128h", [128, 8, 128], BF16)
                identh = sb(sB, "identh", [128, 8, 128], BF16)
                for i_, t_ in enumerate((m32h, mo64h, mo128h)):
                    for h in range(NH):
                        p.op('dve', lambda e, i_=i_, t_=t_, h=h: e.tensor_copy(t_[:, h, :], mskf[:, i_, :]), reads=['mskf'], writes=['mskb'])
                for h in range(NH):
                    p.op('dve', lambda e, h=h: e.tensor_copy(identh[:, h, :], ident_f[:, :]), reads=['ident_f'], writes=['mskb'])
                tri = {0: cst[:, 0, :], 1: cst[:, 1, :]}
                nmd = {0: cst[:, 2, :], 1: cst[:, 4, :]}
                nme = {0: cst[:, 3, :], 1: cst[:, 5, :]}
                tt = sb(sB, "tt", [128, 32, 32], F32)
                Gp = [sb(sB, f"Gp{d}", [128, 32, 8], F32) for d in range(2)]
                Glb = sb(sB, "Glb", [128, 32, 16], F32)
                eG = [sb(sB, f"eG{d}", [128, 32, 8], F32) for d in range(2)]
                kd = [sb(sB, f"kd{d}", [128, 32, 8], F32) for d in range(2)]
                gam = sb(sB, "gam", [128, 32, 16], F32)
                bneg = [sb(sB, f"bneg{d}", [128, 32, 8], F32) for d in range(2)]
                beg = [sb(sB, f"beg{d}", [128, 32, 8], F32) for d in range(2)]
                PF = [ps(sB, f"PF{i}", [128, 8, 128], F32) for i in range(3)]
                PB = [ps(sB, f"PB{i}", [128, 8, 128], BF16) for i in range(2)]
                pfc = [0]
                pbc = [0]

                def npf():
                    v = pfc[0] % 3
                    pfc[0] += 1
                    return v

                def npb():
                    v = pbc[0] % 2
                    pbc[0] += 1
                    return v

                HB = [128, 8, 128]
                PW = []
                for i in range(2):
                    d_ = {}
                    for nm, dt in (("rhsG", F32), ("tmp", F32), ("E", F32), ("N0", BF16), ("N1", BF16),
                                   ("M0", BF16), ("M1", BF16), ("Q0", BF16), ("Q1", BF16), ("ktok", BF16),
                                   ("kc", BF16), ("vc", BF16), ("qc", BF16)):
                        d_[nm] = sb(sB, f"pw{i}_{nm}", HB, dt)
                    d_["eGr"] = d_["rhsG"]
                    d_["kbg"] = d_["N1"]
                    PW.append(d_)
                SW = []
                for i in range(3):
                    d_ = {}
                    for nm, dt in (("vb", F32), ("kbgT", BF16), ("kdec", BF16), ("attnT", BF16), ("qdT", BF16), ("Q", BF16)):
                        d_[nm] = sb(sB, f"sw{i}_{nm}", HB, dt)
                    SW.append(d_)
                S = sb(sB, "S", HB, F32)
                rb = sb(sB, "rb", HB, BF16)
                Sb = sb(sB, "Sb", HB, BF16)
                vnb = sb(sB, "vnb", HB, BF16)
                osum = sb(sB, "osum", HB, F32)
                oBt = sb(sB, "oBt", HB, F32)
                oss = sb(sB, "oss", [128, 8], F32)
                onb = sb(sB, "onb", HB, BF16)
                zsc = sb(sB, "zsc", HB, BF16)
                ogc = sb(sB, "ogc", HB, BF16)
                identb3 = ident_b[:, :].unsqueeze(1).broadcast_to(HB)

                def bc_j(ap2):
                    return ap2.unsqueeze(2).broadcast_to(HB)

                def bc_h(ap2):
                    return ap2.unsqueeze(1).broadcast_to(HB)

                def prep_half(own):
                    with ExitStack() as sh:
                        bgsb = sb(sh, "bgsb", [32, HALF], F32)
                        prep_half_(own, bgsb)

                def prep_half_(own, bgsb):
                    base = 0 if own else HALF
                    p.dma('sp', bgsb[:, :], BGS[:, base:base + HALF], reads=[('BGS', own, 0), ('BGS', own, 1)], writes=['bgsb'])
                    pf = PF[npf()]
                    pfv = pf[:, :, :].rearrange("p h j -> p (h j)")
                    for n in range(32):
                        p.op('pe', lambda e, n=n: e.transpose(pfv[:, n * 32:(n + 1) * 32], bgsb[0:32, n * 128:(n + 1) * 128], ident_f[0:32, 0:32]),
                             reads=['bgsb', 'ident_f'], writes=['PFx'])
                    p.op('dve', lambda e: e.tensor_copy(tt[:, :, :].rearrange("p c r -> p (c r)"), pfv), reads=['PFx'], writes=['tt'])
                    for d in range(2):
                        p.op('pe', lambda e, d=d: e.matmul(pfv[:, 0:256], tri[d], tt[:, :, 16 + 8 * d:24 + 8 * d], start=True, stop=True),
                             reads=['tt', 'tricst'], writes=['PFx'])
                        p.op('dve', lambda e, d=d: e.tensor_copy(Gp[d][:, :, :].rearrange("p c h -> p (c h)"), pfv[:, 0:256]),
                             reads=['PFx'], writes=[('Gp', d)])
                    p.op('pe', lambda e: e.matmul(pfv[:, 0:512], ones_f[:, :], tt[:, :, 16:32], start=True, stop=True),
                         reads=['tt', 'ones_f'], writes=['PFx'])
                    p.op('dve', lambda e: e.tensor_copy(Glb[:, :, :].rearrange("p c h -> p (c h)"), pfv[:, 0:512]), reads=['PFx'], writes=['Glb'])
                    p.op('act', lambda e: e.activation(out=gam[:, :, :], in_=Glb[:, :, :], func=AF.Exp), reads=['Glb'], writes=['gam'])
                    for d in range(2):
                        p.op('act', lambda e, d=d: e.activation(out=eG[d][:, :, :], in_=Gp[d][:, :, :], func=AF.Exp), reads=[('Gp', d)], writes=[('eG', d)])
                        p.op('dve', lambda e, d=d: e.tensor_tensor(out=kd[d][:, :, :], in0=Glb[:, :, 8 * d:8 * d + 8], in1=Gp[d][:, :, :], op=ALU.subtract),
                             reads=['Glb', ('Gp', d)], writes=[('kd', d)])
                        p.op('act', lambda e, d=d: e.activation(out=kd[d][:, :, :], in_=kd[d][:, :, :], func=AF.Exp), reads=[('kd', d)], writes=[('kd', d)])
                        p.op('dve', lambda e, d=d: e.tensor_scalar(out=bneg[d][:, :, :], in0=tt[:, :, 8 * d:8 * d + 8], scalar1=-1.0, scalar2=None, op0=ALU.mult),
                             reads=['tt'], writes=[('bneg', d)])
                        p.op('dve', lambda e, d=d: e.tensor_tensor(out=beg[d][:, :, :], in0=tt[:, :, 8 * d:8 * d + 8], in1=eG[d][:, :, :], op=ALU.mult),
                             reads=['tt', ('eG', d)], writes=[('beg', d)])
                    p.barrier()

                def prep_gen(u, n, d, own, need_out):
                    pw = PW[u % 2]
                    sw = SW[u % 3]
                    P_ = f"pw{u % 2}"
                    S_ = f"sw{u % 3}"
                    t0 = (0 if own else HALF) + n * 128
                    kq = [('KS', h, own, (n * 128) // W) for h in range(NH)]
                    kv = [('VS', h, own, (n * 128) // W) for h in range(NH)]
                    p.dma('sp', pw["kc"][:, :, :], KS[:, t0:t0 + 128].rearrange("(h d) t -> d h t", d=128), reads=kq, writes=[P_ + "kc"])
                    p.dma('sp', pw["vc"][:, :, :], VS[:, t0:t0 + 128].rearrange("(h d) t -> d h t", d=128), reads=kv, writes=[P_ + "vc"])
                    if need_out:
                        p.dma('sp', pw["qc"][:, :, :], QS[:, t0:t0 + 128].rearrange("(h d) t -> d h t", d=128),
                              reads=[('QS', h, (n * 128) // W) for h in range(NH)], writes=[P_ + "qc"])
                    p.op('dve', lambda e: e.tensor_tensor(out=pw["rhsG"][:, :, :], in0=bc_h(tri[d]), in1=bc_j(tt[:, n, 16 + 8 * d:24 + 8 * d]), op=ALU.mult),
                         reads=['tt', 'tricst'], writes=[P_ + "rhsG"])
                    yield
                    a = npf()
                    for hh in range(2):
                        p.op('pe', lambda e, hh=hh: e.matmul(PF[a][:, 4 * hh:4 * hh + 4, :], ones_f[:, :], pw["rhsG"][:, 4 * hh:4 * hh + 4, :], start=True, stop=True),
                             reads=[P_ + "rhsG", 'ones_f'], writes=[('PF', a)])
                    p.op('dve', lambda e: e.tensor_tensor(out=pw["tmp"][:, :, :], in0=PF[a][:, :, :], in1=bc_j(Gp[d][:, n, :]), op=ALU.subtract),
                         reads=[('PF', a), ('Gp', d)], writes=[P_ + "tmp"])
                    if need_out:
                        p.op('act', lambda e: e.activation(out=pw["eGr"][:, :, :], in_=PF[a][:, :, :], func=AF.Exp), reads=[('PF', a)], writes=[P_ + "rhsG"])
                    yield
                    if need_out:
                        p.op('pool', lambda e: e.tensor_tensor(out=pw["E"][:, :, :], in0=pw["tmp"][:, :, :], in1=bc_h(nme[d]), op=ALU.add),
                             reads=[P_ + "tmp", 'tricst'], writes=[P_ + "E"])
                        p.op('act', lambda e: e.activation(out=pw["E"][:, :, :], in_=pw["E"][:, :, :], func=AF.Exp), reads=[P_ + "E"], writes=[P_ + "E"])
                    p.op('pool', lambda e: e.tensor_tensor(out=pw["tmp"][:, :, :], in0=bc_h(nmd[d]), in1=pw["tmp"][:, :, :], op=ALU.subtract),
                         reads=[P_ + "tmp", 'tricst'], writes=[P_ + "tmp"])
                    p.op('act', lambda e: e.activation(out=pw["tmp"][:, :, :], in_=pw["tmp"][:, :, :], func=AF.Exp), reads=[P_ + "tmp"], writes=[P_ + "tmp"])
                    yield
                    a = npf()
                    for h in range(NH):
                        p.op('pe', lambda e, h=h: e.matmul(PF[a][:, h, :], pw["kc"][:, h, :], pw["kc"][:, h, :], start=True, stop=True),
                             reads=[P_ + "kc"], writes=[('PF', a)])
                    p.op('pool', lambda e: e.tensor_tensor(out=pw["tmp"][:, :, :], in0=pw["tmp"][:, :, :], in1=bc_j(bneg[d][:, n, :]), op=ALU.mult),
                         reads=[P_ + "tmp", ('bneg', d)], writes=[P_ + "tmp"])
                    p.op('dve', lambda e: e.tensor_tensor(out=pw["N0"][:, :, :], in0=PF[a][:, :, :], in1=pw["tmp"][:, :, :], op=ALU.mult),
                         reads=[('PF', a), P_ + "tmp"], writes=[P_ + "N0"])
                    yield
                    b = npb()
                    for h in range(NH):
                        p.op('pe', lambda e, h=h: e.transpose(PB[b][:, h, :], pw["N0"][:, h, :], ident_b[:, :]),
                             reads=[P_ + "N0", 'ident_b'], writes=[('PB', b)])
                    p.op('act', lambda e: e.activation(out=pw["M0"][:, :, :], in_=PB[b][:, :, :], func=AF.Copy), reads=[('PB', b)], writes=[P_ + "M0"])
                    yield
                    if need_out:
                        a = npf()
                        for h in range(NH):
                            p.op('pe', lambda e, h=h: e.matmul(PF[a][:, h, :], pw["kc"][:, h, :], pw["qc"][:, h, :], start=True, stop=True),
                                 reads=[P_ + "kc", P_ + "qc"], writes=[('PF', a)])
                        p.op('dve', lambda e: e.tensor_tensor(out=sw["attnT"][:, :, :], in0=PF[a][:, :, :], in1=pw["E"][:, :, :], op=ALU.mult),
                             reads=[('PF', a), P_ + "E"], writes=[S_ + "attnT"])
                        p.op('pool', lambda e: e.tensor_tensor(out=sw["qdT"][:, :, :], in0=pw["qc"][:, :, :], in1=pw["eGr"][:, :, :], op=ALU.mult),
                             reads=[P_ + "qc", P_ + "rhsG"], writes=[S_ + "qdT"])
                        yield
                    def mmg(dst_ps, lk, rk, lkey, rkey):
                        for h in range(NH):
                            p.op('pe', lambda e, h=h: e.matmul(PF[dst_ps][:, h, :], lk[:, h, :], rk[:, h, :], start=True, stop=True),
                                 reads=[lkey, rkey], writes=[('PF', dst_ps)])
                    N_, M_, T_, W_ = pw["N0"], pw["M0"], pw["Q0"], pw["Q1"]
                    kN, kM, kT, kW = P_ + "N0", P_ + "M0", P_ + "Q0", P_ + "Q1"
                    No1, Mo1, No2 = pw["N1"], pw["M1"], pw["ktok"]
                    kNo1, kMo1, kNo2 = P_ + "N1", P_ + "M1", P_ + "ktok"
                    p.op('dve', lambda e: e.tensor_tensor(out=No1[:, :, :], in0=N_[:, :, :], in1=mo64h[:, :, :], op=ALU.mult), reads=[kN, 'mskb'], writes=[kNo1])
                    p.op('dve', lambda e: e.tensor_tensor(out=Mo1[:, :, :], in0=M_[:, :, :], in1=mo64h[:, :, :], op=ALU.mult), reads=[kM, 'mskb'], writes=[kMo1])
                    p.op('dve', lambda e: e.tensor_tensor(out=No2[:, :, :], in0=N_[:, :, :], in1=mo128h[:, :, :], op=ALU.mult), reads=[kN, 'mskb'], writes=[kNo2])
                    p.op('dve', lambda e: e.tensor_tensor(out=N_[:, :, :], in0=N_[:, :, :], in1=m32h[:, :, :], op=ALU.mult), reads=[kN, 'mskb'], writes=[kN])
                    p.op('dve', lambda e: e.tensor_tensor(out=M_[:, :, :], in0=M_[:, :, :], in1=m32h[:, :, :], op=ALU.mult), reads=[kM, 'mskb'], writes=[kM])
                    p.op('dve', lambda e: e.tensor_tensor(out=T_[:, :, :], in0=N_[:, :, :], in1=identh[:, :, :], op=ALU.add), reads=[kN, 'mskb'], writes=[kT])
                    p.op('dve', lambda e: e.tensor_tensor(out=W_[:, :, :], in0=M_[:, :, :], in1=identh[:, :, :], op=ALU.add), reads=[kM, 'mskb'], writes=[kW])
                    yield
                    for lvl in range(1, 5):
                        a1, a2 = npf(), npf()
                        mmg(a1, M_, N_, kM, kN)
                        mmg(a2, N_, M_, kN, kM)
                        p.op('act', lambda e: e.activation(out=N_[:, :, :], in_=PF[a1][:, :, :], func=AF.Copy), reads=[('PF', a1)], writes=[kN])
                        p.op('act', lambda e: e.activation(out=M_[:, :, :], in_=PF[a2][:, :, :], func=AF.Copy), reads=[('PF', a2)], writes=[kM])
                        yield
                        a1, a2 = npf(), npf()
                        mmg(a1, M_, T_, kM, kT)
                        mmg(a2, N_, W_, kN, kW)
                        p.op('dve', lambda e: e.tensor_tensor(out=T_[:, :, :], in0=PF[a1][:, :, :], in1=T_[:, :, :], op=ALU.add), reads=[('PF', a1), kT], writes=[kT])
                        p.op('dve', lambda e: e.tensor_tensor(out=W_[:, :, :], in0=PF[a2][:, :, :], in1=W_[:, :, :], op=ALU.add), reads=[('PF', a2), kW], writes=[kW])
                        yield
                    a1, a2 = npf(), npf()
                    mmg(a1, Mo1, T_, kMo1, kT)
                    mmg(a2, No1, W_, kNo1, kW)
                    p.op('act', lambda e: e.activation(out=N_[:, :, :], in_=PF[a1][:, :, :], func=AF.Copy), reads=[('PF', a1)], writes=[kN])
                    p.op('dve', lambda e: e.tensor_copy(M_[:, :, :], PF[a2][:, :, :]), reads=[('PF', a2)], writes=[kM])
                    yield
                    a1, a2 = npf(), npf()
                    mmg(a1, W_, N_, kW, kN)
                    mmg(a2, T_, M_, kT, kM)
                    p.op('dve', lambda e: e.tensor_tensor(out=T_[:, :, :], in0=PF[a1][:, :, :], in1=T_[:, :, :], op=ALU.add), reads=[('PF', a1), kT], writes=[kT])
                    p.op('dve', lambda e: e.tensor_tensor(out=W_[:, :, :], in0=PF[a2][:, :, :], in1=W_[:, :, :], op=ALU.add), reads=[('PF', a2), kW], writes=[kW])
                    yield
                    a1 = npf()
                    mmg(a1, No2, W_, kNo2, kW)
                    p.op('act', lambda e: e.activation(out=M_[:, :, :], in_=PF[a1][:, :, :], func=AF.Copy), reads=[('PF', a1)], writes=[kM])
                    yield
                    a1 = npf()
                    mmg(a1, T_, M_, kT, kM)
                    p.op('dve', lambda e: e.tensor_tensor(out=sw["Q"][:, :, :], in0=PF[a1][:, :, :], in1=W_[:, :, :], op=ALU.add), reads=[('PF', a1), kW], writes=[S_ + "Q"])
                    yield

                    b = npb()
                    for h in range(NH):
                        p.op('pe', lambda e, h=h: e.transpose(PB[b][:, h, :], pw["kc"][:, h, :], ident_b[:, :]),
                             reads=[P_ + "kc", 'ident_b'], writes=[('PB', b)])
                    p.op('act', lambda e: e.activation(out=pw["ktok"][:, :, :], in_=PB[b][:, :, :], func=AF.Copy), reads=[('PB', b)], writes=[P_ + "ktok"])
                    p.op('pool', lambda e: e.tensor_tensor(out=pw["kbg"][:, :, :], in0=pw["ktok"][:, :, :], in1=bc_j(beg[d][:, n, :]), op=ALU.mult),
                         reads=[P_ + "ktok", ('beg', d)], writes=[P_ + "N1"])
                    p.op('pool', lambda e: e.tensor_tensor(out=sw["kdec"][:, :, :], in0=pw["ktok"][:, :, :], in1=bc_j(kd[d][:, n, :]), op=ALU.mult),
                         reads=[P_ + "ktok", ('kd', d)], writes=[S_ + "kdec"])
                    yield
                    b = npb()
                    for h in range(NH):
                        p.op('pe', lambda e, h=h: e.transpose(PB[b][:, h, :], pw["kbg"][:, h, :], ident_b[:, :]),
                             reads=[P_ + "N1", 'ident_b'], writes=[('PB', b)])
                    p.op('act', lambda e: e.activation(out=sw["kbgT"][:, :, :], in_=PB[b][:, :, :], func=AF.Copy), reads=[('PB', b)], writes=[S_ + "kbgT"])
                    yield
                    b = npb()
                    for h in range(NH):
                        p.op('pe', lambda e, h=h: e.transpose(PB[b][:, h, :], pw["vc"][:, h, :], ident_b[:, :]),
                             reads=[P_ + "vc", 'ident_b'], writes=[('PB', b)])
                    p.op('dve', lambda e: e.tensor_tensor(out=sw["vb"][:, :, :], in0=PB[b][:, :, :], in1=bc_j(tt[:, n, 8 * d:8 * d + 8]), op=ALU.mult),
                         reads=[('PB', b), 'tt'], writes=[S_ + "vb"])
                    yield

                def seq_gen(u, n, d, own, need_out, final_dir, first=True, nxt=None):
                    sw = SW[u % 3]
                    S_ = f"sw{u % 3}"
                    r0 = n * 128
                    if need_out and final_dir:
                        p.dma('sp', oBt[:, :, :].rearrange("p h e -> p (h e)"), OBS[r0:r0 + 128, :], reads=[('OBS', n)], writes=['oBt'])
                        p.dma('sp', zsc[:, :, :], ZS[:, r0:r0 + 128].rearrange("(h d) t -> d h t", d=128),
                              reads=[('ZS', h, r0 // W) for h in range(NH)], writes=['zsc'])
                    a = npf()
                    for h in range(NH):
                        p.op('pe', lambda e, h=h: e.matmul(PF[a][:, h, :], sw["kbgT"][:, h, :], Sb[:, h, :], start=True, stop=True),
                             reads=[S_ + "kbgT", 'Sb'], writes=[('PF', a)])
                    p.op('dve', lambda e: e.tensor_tensor(out=rb[:, :, :], in0=sw["vb"][:, :, :], in1=PF[a][:, :, :], op=ALU.subtract),
                         reads=[('PF', a), S_ + "vb"], writes=['rb'])
                    yield
                    a = npf()
                    for h in range(NH):
                        p.op('pe', lambda e, h=h: e.matmul(PF[a][:, h, :], sw["Q"][:, h, :], rb[:, h, :], start=True, stop=True),
                             reads=[S_ + "Q", 'rb'], writes=[('PF', a)])
                    p.op('act', lambda e: e.activation(out=vnb[:, :, :], in_=PF[a][:, :, :], func=AF.Copy), reads=[('PF', a)], writes=['vnb'])
                    yield
                    if need_out:
                        ao = npf()
                        for h in range(NH):
                            p.op('pe', lambda e, h=h: e.matmul(PF[ao][:, h, :], sw["qdT"][:, h, :], Sb[:, h, :], start=True, stop=False),
                                 reads=[S_ + "qdT", 'Sb'], writes=[('PF', ao)])
                            p.op('pe', lambda e, h=h: e.matmul(PF[ao][:, h, :], sw["attnT"][:, h, :], vnb[:, h, :], start=False, stop=True),
                                 reads=[S_ + "attnT", 'vnb'], writes=[('PF', ao)])
                    a = npf()
                    for h in range(NH):
                        p.op('pe', lambda e, h=h: e.matmul(PF[a][:, h, :], sw["kdec"][:, h, :], vnb[:, h, :], start=True, stop=True),
                             reads=[S_ + "kdec", 'vnb'], writes=[('PF', a)])
                    if first:
                        p.op('pool', lambda e: e.tensor_tensor(out=S[:, :, :], in0=S[:, :, :], in1=bc_j(gam[:, n, 8 * d:8 * d + 8]), op=ALU.mult),
                             reads=['S', 'gam'], writes=['S'])
                    p.op('dve', lambda e: e.tensor_tensor(out=S[:, :, :], in0=S[:, :, :], in1=PF[a][:, :, :], op=ALU.add),
                         reads=['S', ('PF', a)], writes=['S'])
                    p.op('act', lambda e: e.activation(out=Sb[:, :, :], in_=S[:, :, :], func=AF.Copy), reads=['S'], writes=['Sb'])
                    if nxt is not None:
                        n2, d2 = nxt
                        p.op('pool', lambda e: e.tensor_tensor(out=S[:, :, :], in0=S[:, :, :], in1=bc_j(gam[:, n2, 8 * d2:8 * d2 + 8]), op=ALU.mult),
                             reads=['S', 'gam'], writes=['S'])
                    if need_out:
                        if not final_dir:
                            p.op('act', lambda e: e.activation(out=osum[:, :, :], in_=PF[ao][:, :, :], func=AF.Copy), reads=[('PF', ao)], writes=['osum'])
                        else:
                            p.op('dve', lambda e: e.tensor_tensor(out=osum[:, :, :], in0=PF[ao][:, :, :], in1=oBt[:, :, :], op=ALU.add),
                                 reads=[('PF', ao), 'oBt'], writes=['osum'])
                    yield
                    if need_out:
                        if not final_dir:
                            p.dma('sp', OBS[r0:r0 + 128, :], osum[:, :, :].rearrange("p h e -> p (h e)"), reads=['osum'], writes=[('OBS', n)], key='osum')
                        else:
                            p.op('pool', lambda e: e.tensor_tensor(out=oBt[:, :, :], in0=osum[:, :, :], in1=osum[:, :, :], op=ALU.mult),
                                 reads=['osum'], writes=['oBt'])
                            p.op('dve', lambda e: e.tensor_reduce(out=oss[:, :], in_=oBt[:, :, :], axis=AX.X, op=ALU.add), reads=['oBt'], writes=['oss'])
                            p.op('act', lambda e: e.activation(out=oss[:, :], in_=oss[:, :], func=AF.Ln, scale=1.0 / 128, bias=EPS), reads=['oss'], writes=['oss'])
                            p.op('act', lambda e: e.activation(out=oss[:, :], in_=oss[:, :], func=AF.Exp, scale=-0.5), reads=['oss'], writes=['oss'])
                            p.op('pool', lambda e: e.tensor_tensor(out=onb[:, :, :], in0=osum[:, :, :], in1=bc_j(oss[:, :]), op=ALU.mult),
                                 reads=['osum', 'oss'], writes=['onb'])
                            b = npb()
                            for h in range(NH):
                                p.op('pe', lambda e, h=h: e.transpose(PB[b][:, h, :], onb[:, h, :], ident_b[:, :]),
                                     reads=['onb', 'ident_b'], writes=[('PB', b)])
                            p.op('dve', lambda e: e.tensor_tensor(out=ogc[:, :, :], in0=PB[b][:, :, :], in1=zsc[:, :, :], op=ALU.mult),
                                 reads=[('PB', b), 'zsc'], writes=['ogc'])
                            p.dma('sp', OG[:, r0:r0 + 128].rearrange("(h d) t -> d h t", d=128), ogc[:, :, :], reads=['ogc'], writes=['OGall'], key='ogc')
                        yield

                def run_units(units):
                    preps = {}
                    done_prep = set()
                    nxt_prep = 0
                    cur_seq = None
                    cur_u = 0
                    nun = len(units)
                    while cur_u < nun:
                        while nxt_prep < nun and nxt_prep <= cur_u + 2 and len(preps) < cfg.get('dn_par', 2) and (nxt_prep - 2) not in preps:
                            n, d, own, no, fd = units[nxt_prep]
                            preps[nxt_prep] = prep_gen(nxt_prep, n, d, own, no)
                            nxt_prep += 1
                        if cur_seq is None and cur_u in done_prep:
                            n, d, own, no, fd = units[cur_u]
                            nx = (units[cur_u + 1][0], units[cur_u + 1][1]) if cur_u + 1 < nun else None
                            cur_seq = seq_gen(cur_u, n, d, own, no, fd, first=(cur_u == 0), nxt=nx)
                        progressed = False
                        if cur_seq is not None:
                            try:
                                next(cur_seq)
                            except StopIteration:
                                cur_seq = None
                                cur_u += 1
                            progressed = True
                        for uu in sorted(list(preps.keys())):
                            try:
                                next(preps[uu])
                            except StopIteration:
                                del preps[uu]
                                done_prep.add(uu)
                            progressed = True
                        assert progressed or cur_u >= nun

                p.op('pool', lambda e: e.memset(S[:, :, :], 0.0), writes=['S'])
                p.op('pool', lambda e: e.memset(Sb[:, :, :], 0.0), writes=['Sb'])
                nck = cfg.get("nchunks", 32)
                if cfg.get("dn_test") == "A":
                    prep_half(True)
                    if "prep_steps" in cfg:
                        g = prep_gen(0, 0, 0, True, True)
                        for _ in range(cfg["prep_steps"]):
                            next(g)
                        p.barrier()
                        return
                    run_units([(n, 0, True, True, False) for n in range(nck)])
                    p.barrier()
                    return
                prep_half(False)
                run_units([(n, 1, False, False, False) for n in range(nck - 1, -1, -1)])
                p.barrier()
                prep_half(True)
                run_units([(n, 1, True, True, False) for n in range(nck - 1, -1, -1)])
                p.barrier()
                p.op('pool', lambda e: e.memset(S[:, :, :], 0.0), writes=['S'])
                p.op('pool', lambda e: e.memset(Sb[:, :, :], 0.0), writes=['Sb'])
                run_units([(n, 0, True, True, True) for n in range(nck)])
                p.barrier()

        if "dn" in cfg["phases"]:
            phase_B()

        N1, N2, NF = 97, 128, 97 * 128
        NEXT = 24608
        NPAD = 12800

        def phase_C1():
            with ExitStack() as sC:
                hd2 = sb(sC, "hd2", [64, NPAD], F32)
                w1t = sb(sC, "w1t", [33, 64], F32)
                w2t = sb(sC, "w2t", [64, 64], F32)
                w3t = sb(sC, "w3t", [64, 3, D], F32)
                frt = sb(sC, "frt", [64, 1], F32)
                fb1 = sb(sC, "fb1", [64, 1], F32)
                fb2 = sb(sC, "fb2", [64, 1], F32)
                ldt = sb(sC, "ldt", [128, 3, 8], F32)
                rate = sb(sC, "rate", [128, 3, 8], F32)
                nrate = sb(sC, "nrate", [128, 3, 8], F32)
                dl = sb(sC, "dl", [128, 512], F32)
                tp0 = sb(sC, "tp0", [128, 25], F32)
                bq = sb(sC, "bq", [128, 25], F32)
                zp = [sb(sC, f"zp{i}", [33, 512], F32) for i in range(2)]
                arg = [sb(sC, f"arg{i}", [64, 512], F32) for i in range(2)]
                kint = [sb(sC, f"kint{i}", [64, 512], mybir.dt.int32) for i in range(2)]
                kf = [sb(sC, f"kf{i}", [64, 512], F32) for i in range(2)]
                h1 = [sb(sC, f"h1_{i}", [64, 512], F32) for i in range(2)]
                win = [sb(sC, f"win{i}", [128, 512], F32) for i in range(2)]
                kl = [sb(sC, f"kl{i}", [128, NPAD], BF16) for i in range(2)]
                pm = [ps(sC, f"pm{i}", [128, 512], F32) for i in range(4)]
                PI = float(np.pi)
                p.dma('sp', w1t[:, :], hy_w1[:, :], writes=['w1t'])
                p.dma('sp', w2t[:, :], hy_w2[:, :], writes=['w2t'])
                p.dma('sp', w3t[:, :, :], hy_w3[:, :, :], writes=['w3t'])
                p.dma('sp', frt[:, :], hy_freq.rearrange("(p o) -> p o", o=1), writes=['frt'], allow_slow_non_contiguous=True)
                p.dma('sp', fb1[:, :], hy_b1.rearrange("(p o) -> p o", o=1), writes=['fb1'], allow_slow_non_contiguous=True)
                p.dma('sp', fb2[:, :], hy_b2.rearrange("(p o) -> p o", o=1), writes=['fb2'], allow_slow_non_contiguous=True)
                for s_ in range(3):
                    p.dma('sp', ldt[:, s_, :], hy_log_decay[s_, :].rearrange("(b p) -> p b", p=128), writes=[('ldt', s_)], allow_slow_non_contiguous=True)
                p.dma('sp', dl[:, :], c_dl[0, :].partition_broadcast(128), writes=['dl'])
                p.dma('sp', tp0[:, :], c_tp0.partition_broadcast(128), writes=['tp0'])
                p.op('dve', lambda e: e.tensor_tensor(out=fb1[:, :], in0=fb1[:, :], in1=frt[:, :], op=ALU.mult), reads=['fb1', 'frt'], writes=['fb1'])
                p.op('dve', lambda e: e.tensor_tensor(out=fb2[:, :], in0=fb2[:, :], in1=frt[:, :], op=ALU.mult), reads=['fb2', 'frt'], writes=['fb2'])
                p.op('act', lambda e: e.activation(out=rate[:, :, :], in_=ldt[:, :, :], func=AF.Exp), reads=[('ldt', i) for i in range(3)], writes=['rate'])
                p.op('dve', lambda e: e.tensor_scalar(out=nrate[:, :, :], in0=rate[:, :, :], scalar1=-1.0, scalar2=None, op0=ALU.mult), reads=['rate'], writes=['nrate'])
                pmc = [0]

                def npm():
                    v = pmc[0] % 4
                    pmc[0] += 1
                    return v

                def sin_layer(src_ps, src_key, fbias, fkey, dst, dst_keys, i):
                    p.op('dve', lambda e: e.tensor_scalar(out=arg[i][:, :], in0=src_ps, scalar1=frt[:, 0:1], scalar2=fbias[:, 0:1], op0=ALU.mult, op1=ALU.add),
                         reads=['frt', fkey, src_key], writes=[('arg', i)])
                    p.op('dve', lambda e: e.tensor_scalar(out=kint[i][:, :], in0=arg[i][:, :], scalar1=1.0 / (2 * PI), scalar2=64.0, op0=ALU.mult, op1=ALU.add),
                         reads=[('arg', i)], writes=[('kint', i)])
                    p.op('dve', lambda e: e.tensor_scalar(out=kf[i][:, :], in0=kint[i][:, :], scalar1=-64.0, scalar2=None, op0=ALU.add),
                         reads=[('kint', i)], writes=[('kf', i)])
                    p.op('dve', lambda e: e.scalar_tensor_tensor(out=arg[i][:, :], in0=kf[i][:, :], scalar=-2 * PI, in1=arg[i][:, :], op0=ALU.mult, op1=ALU.add),
                         reads=[('kf', i), ('arg', i)], writes=[('arg', i)])
                    p.op('act', lambda e: e.activation(out=dst, in_=arg[i][:, :], func=AF.Sin), reads=[('arg', i)], writes=dst_keys)

                for q in range(25):
                    i = q % 2
                    p.dma('sp', zp[i][:, :], c_zpos[:, q * 512:(q + 1) * 512], writes=[('zp', i)])
                    a = npm()
                    p.op('pe', lambda e: e.matmul(pm[a][0:64, :], w1t[:, :], zp[i][:, :], start=True, stop=True), reads=['w1t', ('zp', i)], writes=[('pm', a)])
                    sin_layer(pm[a][0:64, :], ('pm', a), fb1, 'fb1', h1[i][:, :], [('h1', i)], i)
                    a = npm()
                    p.op('pe', lambda e: e.matmul(pm[a][0:64, :], w2t[:, :], h1[i][:, :], start=True, stop=True), reads=['w2t', ('h1', i)], writes=[('pm', a)])
                    sin_layer(pm[a][0:64, :], ('pm', a), fb2, 'fb2', hd2[:, q * 512:(q + 1) * 512], [('hd2', q)], i)
                for cb in range(8):
                    kb = cb % 2
                    p.op('dve', lambda e: e.tensor_scalar(out=bq[:, 0:8], in0=tp0[:, 0:8], scalar1=nrate[:, 0, cb:cb + 1], scalar2=None, op0=ALU.mult),
                         reads=['tp0', 'nrate'], writes=['bq'])
                    p.op('dve', lambda e: e.tensor_scalar(out=bq[:, 8:25], in0=tp0[:, 8:25], scalar1=nrate[:, 1, cb:cb + 1], scalar2=None, op0=ALU.mult),
                         reads=['tp0', 'nrate'], writes=['bq'])
                    for q in range(25):
                        st = 0 if q < 8 else 1
                        a = npm()
                        wi = q % 2
                        p.op('pe', lambda e: e.matmul(pm[a][:, :], w3t[:, st, cb * 128:(cb + 1) * 128], hd2[:, q * 512:(q + 1) * 512], start=True, stop=True),
                             reads=['w3t', ('hd2', q)], writes=[('pm', a)])
                        sc = nrate[:, 0, cb:cb + 1] if q < 8 else rate[:, 1, cb:cb + 1]
                        p.op('act', lambda e: e.activation(out=win[wi][:, :], in_=dl[:, :], func=AF.Exp, scale=sc, bias=bq[:, q:q + 1]),
                             reads=['dl', 'rate', 'nrate', 'bq'], writes=[('win', wi)])
                        p.op('dve', lambda e: e.tensor_tensor(out=kl[kb][:, q * 512:(q + 1) * 512], in0=pm[a][:, :], in1=win[wi][:, :], op=ALU.mult),
                             reads=[('pm', a), ('win', wi)], writes=[('kl', kb, q)])
                    a = npm()
                    p.op('pe', lambda e: e.matmul(pm[a][:, 0:1], w3t[:, 2, cb * 128:(cb + 1) * 128], hd2[:, 0:1], start=True, stop=True),
                         reads=['w3t', ('hd2', 0)], writes=[('pm', a)])
                    p.op('dve', lambda e: e.tensor_copy(kl[kb][:, 0:1], pm[a][:, 0:1]), reads=[('pm', a)], writes=[('kl', kb, 0)])
                    p.op('pool', lambda e: e.memset(kl[kb][:, HALF:HALF + 129], 0.0), writes=[('kl', kb, 8)])
                    p.dma('sp', KLS[cb * 128:(cb + 1) * 128, :], kl[kb][:, 0:NF], reads=[('kl', kb, q) for q in range(25)], writes=[('KLS', cb)], key=('kl', kb))
                p.barrier()

        def phase_C2():
            KP = 4
            with ExitStack() as sC:
                EXT = sb(sC, "EXT", [128, NEXT], BF16)
                Xt = sb(sC, "Xt", [128, N2, 128], BF16)
                Kr = sb(sC, "Kr", [128, 128, 65], BF16)
                Ki = sb(sC, "Ki", [128, 128, 65], BF16)
                nKi = sb(sC, "nKi", [128, 128, 65], BF16)
                YE = sb(sC, "YE", [128, HALF], BF16)
                G0 = sb(sC, "G0", [128, HALF], BF16)
                yo = sb(sC, "yo", [128, HALF], F32)
                yhb = sb(sC, "yhb", [128, HALF], BF16)
                hbt = sb(sC, "hbt", [128, 8], F32)
                p.dma('sp', hbt[:, :], hy_bias.rearrange("(b p) -> p b", p=128), writes=['hbt'], allow_slow_non_contiguous=True)
                mats = {}
                stg = sb(sC, "mstg", [128, 194], F32)
                for nm, src, r, c in (("e1", c_e1, 97, 194), ("s2a", c_s2a, 128, 130), ("s2b", c_s2b, 128, 130),
                                      ("i1c", c_i1c, 97, 194), ("i1d", c_i1d, 97, 194), ("cw", c_cw, 65, 128), ("sw", c_sw, 65, 128)):
                    t_ = sb(sC, "m_" + nm, [128, c], BF16)
                    p.dma('sp', stg[0:r, 0:c], src[:, :], writes=['mstg'])
                    p.op('dve', lambda e: e.tensor_copy(t_[0:r, :], stg[0:r, 0:c]), reads=['mstg'], writes=['m_' + nm])
                    mats[nm] = t_
                Y1 = [sb(sC, f"Y1_{i}", [128, 2, 194], BF16) for i in range(KP)]
                Zs = [sb(sC, f"Zs{i}", [128, 2, 130], BF16) for i in range(KP)]
                Zt = [sb(sC, f"Zt{i}", [128, 2, 130], F32) for i in range(KP)]
                Zu = [sb(sC, f"Zu{i}", [128, 2, 130], F32) for i in range(KP)]
                Vs = [sb(sC, f"Vs{i}", [128, 2, 194], BF16) for i in range(KP)]
                Yc = [sb(sC, f"Yc{i}", [128, 8, N1], BF16) for i in range(2)]
                KP = 4
                PA_ = [ps(sC, f"PA_{i}", [128, 512], F32) for i in range(KP)]
                PB_ = [ps(sC, f"PB_{i}", [128, 512], F32) for i in range(KP)]
                P1 = PA_
                cnt = {'pr': 0, 'tp': 0}

                def to_Xt():
                    for g in range(N2 // 8):
                        b = cnt['tp'] % 4
                        cnt['tp'] += 1
                        pt = P1[b][:, :].bitcast(BF16).rearrange("p (a c) -> p a c", c=128)
                        for a in range(8):
                            t2 = g * 8 + a
                            p.op('pe', lambda e, a=a, t2=t2: e.transpose(pt[0:N1, a, :], EXT[:, 97 * t2:97 * t2 + 128 * (N1 - 1) + 1:128], ident_b[:, :]),
                                 reads=['EXT', 'ident_b'], writes=[('P1', b)])
                        eng = 'act' if g % 2 == 0 else 'dve'
                        if eng == 'act':
                            p.op('act', lambda e: e.activation(out=Xt[0:N1, g * 8:(g + 1) * 8, :], in_=pt[0:N1, 0:8, :], func=AF.Copy),
                                 reads=[('P1', b)], writes=[('Xt', g)])
                        else:
                            p.op('dve', lambda e: e.tensor_copy(Xt[0:N1, g * 8:(g + 1) * 8, :], pt[0:N1, 0:8, :]), reads=[('P1', b)], writes=[('Xt', g)])

                XtK = [('Xt', g) for g in range(N2 // 8)]
                XtC = [('Xtc', c0) for c0 in range(0, 128, 2)]

                def pair_gen(i, spec):
                    c0, is_filter = spec
                    kA, kB = ('P1', i), ('P2', i)
                    p1 = PA_[i][:, 0:388].rearrange("p (a f) -> p a f", f=194)
                    p2 = PB_[i][:, 0:260].rearrange("p (a f) -> p a f", f=130)
                    for a in range(2):
                        p.op('pe', lambda e, a=a: e.matmul(p1[:, a, :], Xt[0:N1, :, c0 + a], mats["e1"][0:N1, :], start=True, stop=True),
                             reads=XtK + [('Xtc', c0), 'm_e1'], writes=[kA])
                    p.op('act', lambda e: e.activation(out=Y1[i][:, :, :], in_=p1, func=AF.Copy), reads=[kA], writes=[('Y1', i)])
                    yield
                    for a in range(2):
                        p.op('pe', lambda e, a=a: e.matmul(p2[0:N1, a, :], Y1[i][:, a, 0:97], mats["s2a"][:, :], start=True, stop=False),
                             reads=[('Y1', i), 'm_s2a'], writes=[kB])
                        p.op('pe', lambda e, a=a: e.matmul(p2[0:N1, a, :], Y1[i][:, a, 97:194], mats["s2b"][:, :], start=False, stop=True),
                             reads=[('Y1', i), 'm_s2b'], writes=[kB])
                    if is_filter:
                        p.op('act', lambda e: e.activation(out=Kr[0:N1, c0:c0 + 2, :], in_=p2[0:N1, :, 0:65], func=AF.Copy), reads=[kB], writes=[('K', c0)])
                        p.op('dve', lambda e: e.tensor_copy(Ki[0:N1, c0:c0 + 2, :], p2[0:N1, :, 65:130]), reads=[kB], writes=[('K', c0)])
                        p.op('dve', lambda e: e.tensor_scalar(out=nKi[0:N1, c0:c0 + 2, :], in0=p2[0:N1, :, 65:130], scalar1=-1.0, scalar2=None, op0=ALU.mult),
                             reads=[kB], writes=[('K', c0)])
                        return
                    krb = Kr[0:N1, c0:c0 + 2, :].unsqueeze(2).broadcast_to([N1, 2, 2, 65])
                    p2v = p2[0:N1, :, :].rearrange("p a (r f) -> p a r f", r=2)
                    p.op('dve', lambda e: e.tensor_tensor(out=Zt[i][0:N1, :, :].rearrange("p a (r f) -> p a r f", r=2), in0=p2v, in1=krb, op=ALU.mult),
                         reads=[kB, ('K', c0)], writes=[('Zt', i)])
                    p.op('dve', lambda e: e.tensor_tensor(out=Zu[i][0:N1, :, 0:65], in0=p2[0:N1, :, 65:130], in1=nKi[0:N1, c0:c0 + 2, :], op=ALU.mult),
                         reads=[kB, ('K', c0)], writes=[('Zu', i)])
                    p.op('dve', lambda e: e.tensor_tensor(out=Zu[i][0:N1, :, 65:130], in0=p2[0:N1, :, 0:65], in1=Ki[0:N1, c0:c0 + 2, :], op=ALU.mult),
                         reads=[kB, ('K', c0)], writes=[('Zu', i)])
                    p.op('pool', lambda e: e.tensor_tensor(out=Zs[i][0:N1, :, :], in0=Zt[i][0:N1, :, :], in1=Zu[i][0:N1, :, :], op=ALU.add),
                         reads=[('Zt', i), ('Zu', i)], writes=[('Zs', i)])
                    yield
                    p3 = PA_[i][:, 0:388].rearrange("p (a f) -> p a f", f=194)
                    for a in range(2):
                        p.op('pe', lambda e, a=a: e.matmul(p3[0:65, a, :], Zs[i][0:N1, a, 0:65], mats["i1c"][0:N1, :], start=True, stop=False),
                             reads=[('Zs', i), 'm_i1c'], writes=[kA])
                        p.op('pe', lambda e, a=a: e.matmul(p3[0:65, a, :], Zs[i][0:N1, a, 65:130], mats["i1d"][0:N1, :], start=False, stop=True),
                             reads=[('Zs', i), 'm_i1d'], writes=[kA])
                    p.op('act', lambda e: e.activation(out=Vs[i][0:65, :, :], in_=p3[0:65, :, :], func=AF.Copy), reads=[kA], writes=[('Vs', i)])
                    yield
                    p4 = PB_[i][:, 0:256].rearrange("p (a f) -> p a f", f=128)
                    for a in range(2):
                        p.op('pe', lambda e, a=a: e.matmul(p4[0:N1, a, :], Vs[i][0:65, a, 0:97], mats["cw"][0:65, :], start=True, stop=False),
                             reads=[('Vs', i), 'm_cw'], writes=[kB])
                        p.op('pe', lambda e, a=a: e.matmul(p4[0:N1, a, :], Vs[i][0:65, a, 97:194], mats["sw"][0:65, :], start=False, stop=True),
                             reads=[('Vs', i), 'm_sw'], writes=[kB])
                    p.op('act', lambda e: e.activation(out=Xt[0:N1, :, c0:c0 + 2].rearrange("p t a -> p a t"), in_=p4[0:N1, :, :], func=AF.Copy),
                         reads=[kB], writes=[('Xtc', c0)])

                def from_Xt():
                    for g in range(N2 // 8):
                        b = cnt['tp'] % 4
                        cnt['tp'] += 1
                        pt = P1[b][:, :].bitcast(BF16)[:, 0:8 * 98].rearrange("p (a t) -> p a t", t=98)[:, :, 0:N1]
                        for a in range(8):
                            t2 = g * 8 + a
                            p.op('pe', lambda e, a=a, t2=t2: e.transpose(pt[:, a, :], Xt[0:N1, t2, :], ident_b[0:N1, 0:N1]),
                                 reads=XtK + XtC + ['ident_b'], writes=[('P1', b)])
                        yb = g % 2
                        p.op('act', lambda e: e.activation(out=Yc[yb][:, :, :], in_=pt, func=AF.Copy), reads=[('P1', b)], writes=[('Yc', yb)])
                        for a in range(8):
                            t2 = g * 8 + a
                            lo0 = 0
                            hi0 = min(N1, max(0, -(-(HALF - 97 * t2) // 128)))
                            if hi0 > lo0:
                                p.op('pool', lambda e, a=a, t2=t2, hi0=hi0: e.tensor_copy(YE[:, 97 * t2:97 * t2 + 128 * (hi0 - 1) + 1:128], Yc[yb][:, a, 0:hi0]),
                                     reads=[('Yc', yb)], writes=['YE'])
                            lo1 = max(0, -(-(NF - 97 * t2) // 128))
                            hi1 = min(N1, -(-(NF + HALF - 97 * t2) // 128))
                            if hi1 > lo1:
                                s0 = 97 * t2 + 128 * lo1 - NF
                                n_ = hi1 - lo1
                                p.op('pool', lambda e, a=a, s0=s0, n_=n_, lo1=lo1, hi1=hi1: e.tensor_copy(YE[:, s0:s0 + 128 * (n_ - 1) + 1:128], Yc[yb][:, a, lo1:hi1]),
                                     reads=[('Yc', yb)], writes=['YE'])

                nblk = cfg.get("hy_blocks", 8)
                for cb in range(nblk):
                    rows = slice(cb * 128, (cb + 1) * 128)
                    p.dma('sp', EXT[:, 0:NF], KLS[rows, :], reads=[('KLS', cb)], writes=['EXT'])
                    p.dma('sp', EXT[:, NF:NEXT], KLS[rows, 0:NEXT - NF], reads=[('KLS', cb)], writes=['EXT'], key='EXTb')
                    to_Xt()
                    run_rr([(c0, True) for c0 in range(0, 128, 2)], pair_gen, KP, stagger=cfg.get('stagC', 0))
                    p.dma('sp', EXT[:, 0:L], US[rows, :], reads=[('US', cb, o_, q_) for o_ in (True, False) for q_ in range(2)], writes=['EXT'])
                    p.dma('sp', EXT[:, NF:NF + L], US[rows, :], reads=[('US', cb, o_, q_) for o_ in (True, False) for q_ in range(2)], writes=['EXT'], key='EXTb')
                    p.op('pool', lambda e: e.memset(EXT[:, L:NF], 0.0), writes=['EXT'])
                    p.op('pool', lambda e: e.memset(EXT[:, NF + L:NEXT], 0.0), writes=['EXT'])
                    p.dma('sp', G0[:, :], G0S[rows, :], reads=[('G0S', cb, 0), ('G0S', cb, 1)], writes=['G0'])
                    to_Xt()
                    run_rr([(c0, False) for c0 in range(0, 128, 2)], pair_gen, KP, stagger=cfg.get('stagC', 0))
                    from_Xt()
                    p.op('dve', lambda e: e.scalar_tensor_tensor(out=yo[:, :], in0=EXT[:, 0:HALF], scalar=hbt[:, cb:cb + 1], in1=YE[:, :], op0=ALU.mult, op1=ALU.add),
                         reads=['EXT', 'hbt', 'YE'], writes=['yo'])
                    p.op('pool', lambda e: e.tensor_tensor(out=yhb[:, :], in0=yo[:, :], in1=G0[:, :], op=ALU.mult), reads=['yo', 'G0'], writes=['yhb'])
                    p.dma('sp', YH[rows, :], yhb[:, :], reads=['yhb'], writes=['YHall'], key='yhb')
                    if "YE" in cfg.get("dbg", ()):
                        p.dma('sp', dbg_out["YE"][rows, :], YE[:, :], reads=['YE'], writes=[('dbgYE', cb)], key='dbgYE')
                p.barrier()

        if "hyena" in cfg["phases"]:
            if "YE" in cfg.get("dbg", ()):
                ddbg("YE", [D, HALF], BF16)
            if "skipC1" not in cfg.get("dbg", ()):
                phase_C1()
            phase_C2()

        if "out" in cfg["phases"]:
            with ExitStack() as s4:
                wst4 = [sb(s4, f"w4st{i}", [128, 8, 512], F32) for i in range(2)]
                now_t = sb(s4, "now_t", [128, D], F32)
                p.dma('sp', now_t[:], norm_out_w.partition_broadcast(128), writes=['now_t'])
                wdn = sb(s4, "wdn", [128, 8, D], BF16)
                why = sb(s4, "why", [128, 8, D], BF16)
                wo = sb(s4, "wo", [128, 8, D], BF16)
                ci = 0
                for wsrc, wdst, nm in ((w_dn_out, wdn, 'wdn'), (w_hy_out, why, 'why'), (w_out, wo, 'wo')):
                    for hh in range(2):
                        b = ci % 2
                        ci += 1
                        p.dma('sp', wst4[b][:, :, :], wsrc[:, hh * 512:(hh + 1) * 512].rearrange("(k p) c -> p k c", p=128),
                              writes=[('w4st', b)])
                        p.op('pool', lambda e: e.tensor_copy(wdst[:, :, hh * 512:(hh + 1) * 512], wst4[b][:, :, :]),
                             reads=[('w4st', b)], writes=[(nm, hh)])
                ogb = [sb(s4, f"ogb{i}", [128, 8, 512], BF16) for i in range(2)]
                yhb = [sb(s4, f"yhb{i}", [128, 8, 512], BF16) for i in range(2)]
                gtb = [sb(s4, f"gtb{i}", [128, 16, 512], BF16) for i in range(2)]
                m1 = [sb(s4, f"m1_{i}", [128, 512], F32) for i in range(2)]
                m2 = [sb(s4, f"m2_{i}", [128, 512], F32) for i in range(2)]
                mb = [sb(s4, f"mb{i}", [128, 8, 512], BF16) for i in range(2)]
                xr = [sb(s4, f"xr{i}", [128, D], F32) for i in range(2)]
                res = [sb(s4, f"res{i}", [128, D], F32) for i in range(2)]
                junk4 = sb(s4, "junk4", [128, D], BF16)
                ss4 = [sb(s4, f"ss4_{i}", [128, 1], F32) for i in range(2)]
                ot = [sb(s4, f"ot{i}", [128, D], F32) for i in range(2)]
                pa = [ps(s4, f"pa{i}", [128, 512], F32) for i in range(2)]
                pb = [ps(s4, f"pb{i}", [128, 512], F32) for i in range(2)]
                pf = [ps(s4, f"pf{i}", [128, 512], F32) for i in range(4)]
                out_toks = []
                ti = 0
                for bk in range(8):
                    b = bk % 2
                    tsl = slice(bk * 512, (bk + 1) * 512)
                    p.dma('sp', ogb[b][:, :, :], OG[:, tsl].rearrange("(k p) t -> p k t", p=128),
                          reads=[('OG', k) for k in range(8)] + ['OGall'], writes=[('ogb', b)])
                    p.dma('sp', yhb[b][:, :, :], YH[:, tsl].rearrange("(k p) t -> p k t", p=128),
                          reads=[('YH', k) for k in range(8)] + ['YHall'], writes=[('yhb', b)])
                    p.dma('sp', gtb[b][:, :, :], GS[:, tsl].rearrange("(k p) t -> p k t", p=128),
                          reads=[('GS', k) for k in range(16)], writes=[('gtb', b)])
                    for dg in range(8):
                        q2 = dg % 2
                        for k in range(8):
                            p.op('pe', lambda e, k=k: e.matmul(pa[q2][:, :], wdn[:, k, dg * 128:(dg + 1) * 128], ogb[b][:, k, :],
                                                               start=(k == 0), stop=(k == 7)),
                                 reads=[('wdn', dg // 4), ('ogb', b)], writes=[('pa', q2)])
                        for k in range(8):
                            p.op('pe', lambda e, k=k: e.matmul(pb[q2][:, :], why[:, k, dg * 128:(dg + 1) * 128], yhb[b][:, k, :],
                                                               start=(k == 0), stop=(k == 7)),
                                 reads=[('why', dg // 4), ('yhb', b)], writes=[('pb', q2)])
                        p.op('dve', lambda e: e.tensor_tensor(out=m1[q2][:, :], in0=pa[q2][:, :], in1=gtb[b][:, dg, :], op=ALU.mult),
                             reads=[('pa', q2), ('gtb', b)], writes=[('m1', q2)])
                        p.op('dve', lambda e: e.tensor_tensor(out=m2[q2][:, :], in0=pb[q2][:, :], in1=gtb[b][:, 8 + dg, :], op=ALU.mult),
                             reads=[('pb', q2), ('gtb', b)], writes=[('m2', q2)])
                        p.op('pool', lambda e: e.tensor_tensor(out=mb[b][:, dg, :], in0=m1[q2][:, :], in1=m2[q2][:, :], op=ALU.add),
                             reads=[('m1', q2), ('m2', q2)], writes=[('mb', b, dg)])
                    for tt in range(4):
                        t0 = bk * 512 + tt * 128
                        r = ti % 2
                        ti += 1
                        p.dma('sp', xr[r][:, :], x[t0:t0 + 128, :], writes=[('xr', r)])
                        for nh in range(2):
                            fi = (2 * ti + nh) % 4
                            for k in range(8):
                                p.op('pe', lambda e, k=k: e.matmul(pf[fi][:, :], mb[b][:, k, tt * 128:(tt + 1) * 128],
                                                                   wo[:, k, nh * 512:(nh + 1) * 512], start=(k == 0), stop=(k == 7)),
                                     reads=[('mb', b, k), ('wo', nh)], writes=[('pf', fi)])
                            p.op('dve', lambda e: e.tensor_tensor(out=res[r][:, nh * 512:(nh + 1) * 512], in0=pf[fi][:, :],
                                                                  in1=xr[r][:, nh * 512:(nh + 1) * 512], op=ALU.add),
                                 reads=[('pf', fi), ('xr', r)], writes=[('res', r, nh)])
                        p.op('act', lambda e: e.activation(out=junk4[:, :], in_=res[r][:, :], func=AF.Square, accum_out=ss4[r][:, :]),
                             reads=[('res', r, 0), ('res', r, 1)], writes=['junk4', ('ss4', r)])
                        p.op('act', lambda e: e.activation(out=ss4[r][:, :], in_=ss4[r][:, :], func=AF.Ln, scale=1.0 / D, bias=EPS),
                             reads=[('ss4', r)], writes=[('ss4', r)])
                        p.op('act', lambda e: e.activation(out=ss4[r][:, :], in_=ss4[r][:, :], func=AF.Exp, scale=-0.5),
                             reads=[('ss4', r)], writes=[('ss4', r)])
                        p.op('dve', lambda e: e.scalar_tensor_tensor(out=ot[r][:, :], in0=res[r][:, :], scalar=ss4[r][:, :],
                                                                      in1=now_t[:, :], op0=ALU.mult, op1=ALU.mult),
                             reads=[('res', r, 0), ('res', r, 1), ('ss4', r), 'now_t'], writes=[('ot', r)])
                        out_toks.append(p.dma('sp', y[t0:t0 + 128, :], ot[r][:, :], reads=[('ot', r)], writes=[('y', t0)], key=('ot', r)))
                p.barrier()
        for nm in cfg.get("dump", ()):
            src = {"OG": OG, "YH": YH, "GS": GS, "QS": QS, "KS": KS, "VS": VS, "ZS": ZS, "BGS": BGS, "OBS": OBS, "US": US, "G0S": G0S, "KLS": KLS}[nm]
            dst = ddbg(nm, src.shape, src.dtype)
            nr = src.shape[0]
            step = 128 if nr >= 128 else nr
            for r0 in range(0, nr, step):
                p.dma('sp', dst[r0:r0 + step, :], src[r0:r0 + step, :], reads=[], writes=[('dump', nm, r0)], key=('dump', (r0 // step) % 4))
        p.barrier()
        print("instr counts", p.ninstr, "nsem", p.nsem)
    return nc


def _core_inputs(inputs, b, hf):
    xs = inputs["x"][b]
    w_in = inputs["w_in"][0]
    if hf == 1:
        xs = xs[::-1]
        perm = np.arange(INW)
        for base in (C_BETA, C_A):
            perm[base:base + 8] = np.arange(base + 8, base + 16)
            perm[base + 8:base + 16] = np.arange(base, base + 8)
        w_in = w_in[:, perm]
    dcw = inputs["dn_conv_w"][0]
    hcw = inputs["hy_conv_w"][0]
    alog = inputs["dn_a_log"][0]
    dtb = inputs["dn_dt_bias"][0]
    if hf == 1:
        dcw, hcw, alog, dtb = dcw[::-1], hcw[::-1], alog[::-1], dtb[::-1]
    w3 = inputs["hy_w3"][0]
    ld = inputs["hy_log_decay"][0]
    w3f, w3b, ldf, ldb = w3[:, :D], w3[:, D:], ld[:D], ld[D:]
    if hf == 0:
        w3s, lds = np.stack([w3f, w3b, w3f], axis=1), np.stack([ldf, ldb, ldf], axis=0)
    else:
        w3s, lds = np.stack([w3b, w3f, w3f], axis=1), np.stack([ldb, ldf, ldf], axis=0)
    m = {
        "hy_w1": np.ascontiguousarray(inputs["hy_w1"][0]), "hy_b1": np.ascontiguousarray(inputs["hy_b1"][0]),
        "hy_w2": np.ascontiguousarray(inputs["hy_w2"][0]), "hy_b2": np.ascontiguousarray(inputs["hy_b2"][0]),
        "hy_freq": np.ascontiguousarray(inputs["hy_freq"][0]), "hy_w3": np.ascontiguousarray(w3s),
        "hy_log_decay": np.ascontiguousarray(lds), "hy_bias": np.ascontiguousarray(inputs["hy_bias"][0]),
        "dn_conv_w": np.ascontiguousarray(dcw), "hy_conv_w": np.ascontiguousarray(hcw),
        "dn_a_log": np.ascontiguousarray(alog).reshape(16), "dn_dt_bias": np.ascontiguousarray(dtb).reshape(16),
        "dn_norm_w": np.ascontiguousarray(inputs["dn_norm_w"][0]),
        "x": np.ascontiguousarray(xs, dtype=np.float32),
        "norm_in_w": np.ascontiguousarray(inputs["norm_in_w"][0]),
        "w_in": np.ascontiguousarray(w_in),
        "w_dn_out": np.ascontiguousarray(inputs["w_dn_out"][0]),
        "w_hy_out": np.ascontiguousarray(inputs["w_hy_out"][0]),
        "w_out": np.ascontiguousarray(inputs["w_out"][0]),
        "norm_out_w": np.ascontiguousarray(inputs["norm_out_w"]),
    }
    m.update(_consts())
    return m


FULL_CFG = {"phases": ("projA", "hyproj", "gates", "dn", "hyena", "out")}


def kernel(**inputs):
    nc = build(FULL_CFG)
    in_maps = [_core_inputs(inputs, c // 2, c % 2) for c in range(8)]
    res = run_bass_kernel_spmd(nc, in_maps, core_ids=list(range(8)))
    out = np.empty((4, L, D), np.float32)
    for c in range(8):
        b, hf = c // 2, c % 2
        yc = res.results[c]["y"]
        if hf == 0:
            out[b, :HALF] = yc
        else:
            out[b, HALF:] = yc[::-1]
    return out
```

```python
import numpy as np
import concourse.bass as bass
import concourse.mybir as mybir
from concourse.bass_utils import run_bass_kernel_spmd
from contextlib import ExitStack

F32 = mybir.dt.float32
BF16 = mybir.dt.bfloat16
AF = mybir.ActivationFunctionType
ALU = mybir.AluOpType
AX = mybir.AxisListType

D = 1024
L = 8192
HALF = 4096
NH = 8
INW = 10272
EPS = 1e-6
C_QKV, C_DNZ, C_BETA, C_A, C_HXV, C_HZ, C_GATE = 0, 3072, 4096, 4112, 4128, 7200, 8224


class Prog:
    SEM_EPOCH = 20000

    def __init__(self, nc, es, same_engine_sync=True):
        self.nc = nc
        self.es = es
        self.engs = {'pe': nc.tensor, 'act': nc.scalar, 'dve': nc.vector, 'pool': nc.gpsimd, 'sp': nc.sync}
        self.sem = {}
        self.cnt = {}
        self.nsem = 0
        for e in self.engs:
            self._new_eng_sem(e)
        self.waited = {e: {} for e in self.engs}
        self.last_w = {}
        self.readers = {}
        self.dma_sem = {}
        self.same = same_engine_sync
        self.ninstr = {e: 0 for e in self.engs}
        self.last_tok = {}
        self.dma_toks = []

    def _mksem(self, name):
        self.nsem += 1
        return self.es.enter_context(self.nc.semaphore(f"{name}_{self.nsem}"))

    def _new_eng_sem(self, e):
        self.sem[e] = self._mksem("s" + e)
        self.cnt[e] = 0

    def _wait(self, e, tok):
        if tok is None:
            return
        sem, val, src = tok
        if src == e and (not self.same or e == 'pe'):
            return
        w = self.waited[e]
        k = id(sem)
        if k in w and w[k] >= val:
            return
        w[k] = val
        self.engs[e].wait_ge(sem, val)
        self.ninstr[e] += 1

    def _deps(self, e, reads, writes):
        for k in reads:
            self._wait(e, self.last_w.get(k))
        for k in writes:
            t = self.last_w.get(k)
            if t is not None and (t[2] != e or k in reads):
                self._wait(e, t)
            for t in self.readers.get(k, ()):
                if t[2] != e:
                    self._wait(e, t)

    def _commit(self, tok, reads, writes):
        for k in reads:
            self.readers.setdefault(k, []).append(tok)
        for k in writes:
            self.last_w[k] = tok
            self.readers[k] = []

    PSUM_NAMES = ('PF', 'PB', 'PFx', 'pp', 'pn', 'ph', 'pa', 'pb', 'pf', 'pTo', 'pTx', 'pm', 'P1', 'P2', 'P3', 'P4')

    def _excl(self, reads, writes):
        r2, w2 = [], list(writes)
        for k in reads:
            nm = k[0] if isinstance(k, tuple) else k
            if nm in self.PSUM_NAMES:
                if k not in w2:
                    w2.append(k)
            else:
                r2.append(k)
        return r2, w2

    def op(self, e, fn, reads=(), writes=()):
        reads, writes = self._excl(reads, writes)
        self._deps(e, reads, writes)
        if self.cnt[e] >= self.SEM_EPOCH:
            self._new_eng_sem(e)
        ins = fn(self.engs[e])
        self.cnt[e] += 1
        ins.then_inc(self.sem[e], 1)
        self.ninstr[e] += 1
        tok = (self.sem[e], self.cnt[e], e)
        self.last_tok[e] = tok
        self._commit(tok, reads, writes)
        return tok

    def dma(self, e, out, in_, reads=(), writes=(), key=None, **kw):
        self._deps(e, reads, writes)
        if key is None:
            key = (writes[0] if writes else reads[0])
        ds = self.dma_sem.get(key)
        if ds is None or ds[1] + 16 > self.SEM_EPOCH:
            ds = [self._mksem("d"), 0]
            self.dma_sem[key] = ds
        ds[1] += 16
        self.engs[e].dma_start(out=out, in_=in_, **kw).then_inc(ds[0], 16)
        self.ninstr[e] += 1
        tok = (ds[0], ds[1], 'dma')
        self.dma_toks.append(tok)
        self._commit(tok, reads, writes)
        return tok

    def barrier(self):
        toks = list(self.last_tok.values())
        latest = {}
        for t in self.dma_toks:
            k = id(t[0])
            if k not in latest or latest[k][1] < t[1]:
                latest[k] = t
        toks += list(latest.values())
        self.dma_toks = list(latest.values())
        for e in self.engs:
            for t in toks:
                if t[2] == e:
                    continue
                self._wait(e, t)
        self.last_w = {}
        self.readers = {}


def _consts():
    ident = np.eye(128, dtype=np.float32)
    pi, fi = np.meshgrid(np.arange(128), np.arange(128), indexing="ij")
    NEG = -30000.0
    tri = np.stack([
        (pi <= fi).astype(np.float32),
        (pi >= fi).astype(np.float32),
        np.where(fi < pi, 0.0, NEG),
        np.where(fi >= pi, 0.0, NEG),
        np.where(fi > pi, 0.0, NEG),
        np.where(fi <= pi, 0.0, NEG),
    ]).astype(np.float32)
    sel = np.zeros((32, 2), np.float32)
    sel[:16, 0] = 1.0
    sel[16:, 1] = 1.0
    msk = np.stack([(pi // 32 == fi // 32), (pi // 64 == fi // 64) & (pi // 32 != fi // 32), (pi // 64 != fi // 64)]).astype(np.float32)
    out = {"c_ident": ident, "c_tri": tri, "c_sel": sel, "c_msk": msk}
    NF, NP = 97 * 128, 12800
    j = np.arange(NP)
    pos = np.where(j < HALF, j, NF - j).astype(np.float64)
    pos = np.clip(pos, 0, None)
    tl = pos / (L - 1)
    bands = 16
    fb = np.linspace(1e-4, bands - 1, bands)
    ang = (2.0 * np.pi / L) * pos[None, :] * fb[:, None]
    out["c_zpos"] = np.concatenate([tl[None, :], np.cos(ang), -np.sin(ang)], axis=0).astype(np.float32)
    out["c_dl"] = (np.arange(512) / (L - 1)).astype(np.float32)[None, :]
    q = np.arange(25)
    out["c_tp0"] = np.where(q < 8, 512 * q / (L - 1), (NF - 512 * q) / (L - 1)).astype(np.float32)
    a1 = 2 * np.pi * np.outer(np.arange(97), np.arange(97)) / 97
    c1, s1 = np.cos(a1), np.sin(a1)
    a2 = 2 * np.pi * np.outer(np.arange(128), np.arange(65)) / 128
    c2, s2 = np.cos(a2), np.sin(a2)
    out["c_e1"] = np.concatenate([c1, -s1], axis=1).astype(np.float32)
    out["c_s2a"] = np.concatenate([c2, -s2], axis=1).astype(np.float32)
    out["c_s2b"] = np.concatenate([s2, c2], axis=1).astype(np.float32)
    out["c_i1c"] = np.concatenate([c1, s1], axis=1).astype(np.float32)
    out["c_i1d"] = np.concatenate([-s1, c1], axis=1).astype(np.float32)
    wgt = np.full(65, 2.0)
    wgt[0] = wgt[64] = 1.0
    out["c_cw"] = (wgt[:, None] * c2.T / NF).astype(np.float32)
    out["c_sw"] = (-wgt[:, None] * s2.T / NF).astype(np.float32)
    return out


def build(cfg):
    nc = bass.Bass("TRN2", target_bir_lowering=False)
    dbg = cfg.get("dbg", ())

    def din(name, shape, dt=F32):
        return nc.dram_tensor(name, list(shape), dt, kind="ExternalInput").ap()

    def dscr(name, shape, dt, ext=False):
        kind = "ExternalInput" if ext else "Internal"
        return nc.dram_tensor(name, list(shape), dt, kind=kind).ap()

    x = din("x", [L, D])
    norm_in_w = din("norm_in_w", [D])
    w_in = din("w_in", [D, INW])
    w_dn_out = din("w_dn_out", [D, D])
    w_hy_out = din("w_hy_out", [D, D])
    w_out = din("w_out", [D, D])
    norm_out_w = din("norm_out_w", [D])
    c_ident = din("c_ident", [128, 128])
    y = nc.dram_tensor("y", [HALF, D], F32, kind="ExternalOutput").ap()

    ext = cfg.get("ext_scratch", ())
    OG = dscr("OG", [D, HALF], BF16, "OG" in ext)
    YH = dscr("YH", [D, HALF], BF16, "YH" in ext)
    GS = dscr("GS", [2 * D, HALF], BF16, "GS" in ext)
    QS = dscr("QS", [D, HALF], BF16, "QS" in ext)
    KS = dscr("KS", [D, L], BF16, "KS" in ext)
    VS = dscr("VS", [D, L], BF16, "VS" in ext)
    ZS = dscr("ZS", [D, HALF], BF16, "ZS" in ext)
    BGS = dscr("BGS", [32, L], F32, "BGS" in ext)
    OBS = dscr("OBS", [HALF, D], F32, "OBS" in ext)
    US = dscr("US", [D, L], BF16, "US" in ext)
    G0S = dscr("G0S", [D, HALF], BF16, "G0S" in ext)
    KLS = dscr("KLS", [D, 97 * 128], BF16, "KLS" in ext)
    hy_w1 = din("hy_w1", [33, 64])
    hy_b1 = din("hy_b1", [64])
    hy_w2 = din("hy_w2", [64, 64])
    hy_b2 = din("hy_b2", [64])
    hy_freq = din("hy_freq", [64])
    hy_w3 = din("hy_w3", [64, 3, D])
    hy_log_decay = din("hy_log_decay", [3, D])
    hy_bias = din("hy_bias", [D])
    c_zpos = din("c_zpos", [33, 12800])
    c_dl = din("c_dl", [1, 512])
    c_tp0 = din("c_tp0", [25])
    c_e1 = din("c_e1", [97, 194])
    c_s2a = din("c_s2a", [128, 130])
    c_s2b = din("c_s2b", [128, 130])
    c_i1c = din("c_i1c", [97, 194])
    c_i1d = din("c_i1d", [97, 194])
    c_cw = din("c_cw", [65, 128])
    c_sw = din("c_sw", [65, 128])
    dn_conv_w = din("dn_conv_w", [3, 3 * D])
    hy_conv_w = din("hy_conv_w", [3, 3 * D])
    dn_a_log = din("dn_a_log", [16])
    dn_dt_bias = din("dn_dt_bias", [16])
    dn_norm_w = din("dn_norm_w", [128])
    c_sel = din("c_sel", [32, 2])
    c_tri = din("c_tri", [6, 128, 128])
    c_msk = din("c_msk", [3, 128, 128])

    dbg_out = {}

    def ddbg(name, shape, dt=F32):
        dbg_out[name] = nc.dram_tensor("dbg_" + name, list(shape), dt, kind="ExternalOutput").ap()
        return dbg_out[name]

    with ExitStack() as es:
        p = Prog(nc, es)
        cs = ExitStack()
        es.enter_context(cs)

        uniq = [0]

        def sb(stack, name, shape, dt):
            uniq[0] += 1
            return stack.enter_context(nc.sbuf_tensor(f"{name}_{uniq[0]}", list(shape), dt))

        def ps(stack, name, shape, dt=F32):
            uniq[0] += 1
            return stack.enter_context(nc.psum_tensor(f"{name}_{uniq[0]}", list(shape), dt))

        ident_f = sb(cs, "ident_f", [128, 128], F32)
        ident_b = sb(cs, "ident_b", [128, 128], BF16)
        nw_t = sb(cs, "nw_t", [128, 8], F32)
        p.dma('sp', ident_f[:], c_ident[:, :], writes=['ident_f'])
        p.op('dve', lambda e: e.tensor_copy(ident_b[:], ident_f[:]), reads=['ident_f'], writes=['ident_b'])
        p.dma('sp', nw_t[:], norm_in_w.rearrange("(k p) -> p k", p=128), writes=['nw_t'],
              allow_slow_non_contiguous=True)
        cwd = sb(cs, "cwd", [128, 24, 3], F32)
        cwh = sb(cs, "cwh", [128, 24, 3], F32)
        nwd = sb(cs, "nwd", [128, 1], F32)
        dtb = sb(cs, "dtb", [32, 1], F32)
        negA = sb(cs, "negA", [32, 1], F32)
        selt = sb(cs, "selt", [32, 2], F32)
        selb, selg = selt[:, 0:1], selt[:, 1:2]
        for j in range(3):
            p.dma('sp', cwd[:, :, j], dn_conv_w[j, :].rearrange("(g p) -> p g", p=128), writes=[('cwd', j)], allow_slow_non_contiguous=True)
            p.dma('sp', cwh[:, :, j], hy_conv_w[j, :].rearrange("(g p) -> p g", p=128), writes=[('cwh', j)], allow_slow_non_contiguous=True)
        p.dma('sp', nwd[:, :], dn_norm_w.rearrange("(p o) -> p o", o=1), writes=['nwd'], allow_slow_non_contiguous=True)
        p.op('pool', lambda e: e.memset(dtb[:], 0.0), writes=['dtb'])
        p.op('pool', lambda e: e.memset(negA[:], 0.0), writes=['negA'])
        p.dma('sp', dtb[16:32, :], dn_dt_bias.rearrange("(p o) -> p o", o=1), reads=[], writes=['dtb'], allow_slow_non_contiguous=True)
        p.dma('sp', negA[16:32, :], dn_a_log.rearrange("(p o) -> p o", o=1), writes=['negA'], allow_slow_non_contiguous=True)
        p.dma('sp', selt[:, :], c_sel[:, :], writes=['selb'])
        p.op('act', lambda e: e.activation(out=negA[:], in_=negA[:], func=AF.Exp), reads=['negA'], writes=['negA'])
        p.op('dve', lambda e: e.tensor_scalar(out=negA[:], in0=negA[:], scalar1=-1.0, scalar2=None, op0=ALU.mult), reads=['negA'], writes=['negA'])
        p.barrier()

        def build_hT(stk, hT, tok_base, pre_tok, post_tok, tag):
            with ExitStack() as ls:
                xt = [sb(ls, f"xt{tag}{i}", [128, D], F32) for i in range(2)]
                junk = sb(ls, f"junk{tag}", [128, D], BF16)
                xn = [sb(ls, f"xn{tag}{i}", [128, D], BF16) for i in range(2)]
                ssq = [sb(ls, f"ssq{tag}{i}", [128, 1], F32) for i in range(2)]
                pT = [ps(ls, f"pT{tag}{i}", [128, 8, 128], BF16) for i in range(2)]
                jobs = [(tok_base + 128 * i, 128, 1 + 128 * i) for i in range(HALF // 128)]
                for hc, tk in ((0, pre_tok), (HALF + 1, post_tok)):
                    if tk is None:
                        p.op('pool', lambda e, hc=hc: e.memset(hT[:, :, hc:hc + 1], 0.0), writes=[('hT', 'halo', hc)])
                    else:
                        jobs.append((tk, 1, hc))
                for ji, (t0, n, c0) in enumerate(jobs):
                    b = ji % 2
                    kx, kn, ks, kp = (f'xt{tag}', b), (f'xn{tag}', b), (f'ssq{tag}', b), (f'pT{tag}', b)
                    p.dma('sp', xt[b][0:n, :], x[t0:t0 + n, :], writes=[kx])
                    p.op('act', lambda e: e.activation(out=junk[0:n, :], in_=xt[b][0:n, :], func=AF.Square,
                                                       accum_out=ssq[b][0:n, :]), reads=[kx], writes=['junk' + tag, ks])
                    p.op('act', lambda e: e.activation(out=ssq[b][0:n, :], in_=ssq[b][0:n, :], func=AF.Ln, scale=1.0 / D, bias=EPS),
                         reads=[ks], writes=[ks])
                    p.op('act', lambda e: e.activation(out=ssq[b][0:n, :], in_=ssq[b][0:n, :], func=AF.Exp, scale=-0.5),
                         reads=[ks], writes=[ks])
                    p.op('dve', lambda e: e.tensor_scalar(out=xn[b][0:n, :], in0=xt[b][0:n, :], scalar1=ssq[b][0:n, :],
                                                          scalar2=None, op0=ALU.mult), reads=[kx, ks], writes=[kn])
                    for k in range(8):
                        p.op('pe', lambda e, k=k: e.transpose(pT[b][:, k, 0:n], xn[b][0:n, k * 128:(k + 1) * 128],
                                                              ident_b[0:n, 0:n]),
                             reads=[kn, 'ident_b'], writes=[kp])
                    key = ('hT', (c0 - 1) // 512) if n == 128 else ('hT', 'halo', c0)
                    p.op('dve', lambda e: e.tensor_tensor(out=hT[:, :, c0:c0 + n], in0=pT[b][:, :, 0:n],
                                                          in1=nw_t[:, :].unsqueeze(2).broadcast_to([128, 8, n]), op=ALU.mult),
                         reads=[kp, 'nw_t'], writes=[key])
                p.barrier()

        def hT_keys(nblk=8):
            return [('hT', i) for i in range(nblk)]

        wctr = [0]

        def load_w_group(wst, wbf, src_ap):
            b = wctr[0] % len(wst)
            wctr[0] += 1
            ncol = src_ap.shape[1]
            p.dma('sp', wst[b][:, :, 0:ncol], src_ap.rearrange("(k p) c -> p k c", p=128), writes=[('wst', b)])
            p.op('pool', lambda e: e.tensor_copy(wbf[b][:, :, 0:ncol], wst[b][:, :, 0:ncol]), reads=[('wst', b)],
                 writes=[('wbf', b, k) for k in range(8)])
            return b

        W = 2048
        NB = W // 512

        def run_rr(job_specs, make_gen, K, stagger=1):
            active = {}
            nxt_job = 0
            free = list(range(K))
            since = stagger
            while nxt_job < len(job_specs) or active:
                since += 1
                while free and nxt_job < len(job_specs) and since > stagger:
                    sl = free.pop(0)
                    active[sl] = make_gen(sl, job_specs[nxt_job])
                    nxt_job += 1
                    since = 0 if stagger > 0 else since
                for sl in sorted(active.keys()):
                    try:
                        next(active[sl])
                    except StopIteration:
                        del active[sl]
                        free.append(sl)

        def phase_A(own):
            tok_base = 0 if own else HALF
            KJ = 3
            with ExitStack() as s2:
                hT = sb(s2, "hT", [128, 8, HALF + 2], BF16)
                if own:
                    build_hT(s2, hT, 0, None, HALF, "o")
                else:
                    build_hT(s2, hT, HALF, HALF - 1, None, "x")
                wst = [sb(s2, f"wst{i}", [128, 8, 128], F32) for i in range(3)]
                wbf = [sb(s2, f"wbf{i}", [128, 8, 128], BF16) for i in range(3)]
                pp = [ps(s2, f"pp{i}", [128, 512], F32) for i in range(5)]
                pn = [ps(s2, f"pn{i}", [128, 512], F32) for i in range(3)]
                R = [sb(s2, f"R{i}", [128, W + 2], F32) for i in range(KJ)]
                acc = [sb(s2, f"acc{i}", [128, W], F32) for i in range(KJ)]
                sil = acc
                sqb = [sb(s2, f"sqb{i}", [128, W], BF16) for i in range(KJ)]
                rst = [sb(s2, f"rst{i}", [128, W], F32) for i in range(KJ)]
                ob = [sb(s2, f"ob{i}", [128, W], BF16) for i in range(KJ)]
                keep2 = [sb(s2, f"keep{i}", [128, W], F32) for i in range(2)]
                ones_b = sb(s2, "ones_b", [128, 128], BF16)
                p.op('pool', lambda e: e.memset(ones_b[:], 1.0), writes=['ones_b'])
                ctr = {'pp': 0, 'pn': 0}

                def nxt(nm, n):
                    v = ctr[nm] % n
                    ctr[nm] += 1
                    return v

                def Rkeys(ri):
                    return [('R', ri, bk) for bk in range(5)]

                BW = (W + 2) // 5

                def project(sl, wb, ncol, ps_i):
                    c0 = ps_i * W
                    for bk in range(5):
                        pi = nxt('pp', 5)
                        cs = c0 + bk * BW
                        for k in range(8):
                            p.op('pe', lambda e, k=k: e.matmul(pp[pi][0:ncol, 0:BW], wbf[wb][:, k, 0:ncol], hT[:, k, cs:cs + BW],
                                                               start=(k == 0), stop=(k == 7)),
                                 reads=[('wbf', wb, k)], writes=[('pp', pi)])
                        dst = R[sl][0:ncol, bk * BW:(bk + 1) * BW]
                        if bk % 2 == 0:
                            p.op('act', lambda e: e.activation(out=dst, in_=pp[pi][0:ncol, 0:BW], func=AF.Copy), reads=[('pp', pi)], writes=[('R', sl, bk)])
                        else:
                            p.op('dve', lambda e: e.tensor_copy(dst, pp[pi][0:ncol, 0:BW]), reads=[('pp', pi)], writes=[('R', sl, bk)])

                def conv(sl, cw, g):
                    p.op('act', lambda e: e.activation(out=acc[sl][:, :], in_=R[sl][:, 0:W], func=AF.Copy, scale=cw[:, g, 0:1]),
                         reads=Rkeys(sl), writes=[('acc', sl)])
                    for j in (1, 2):
                        p.op('dve', lambda e, j=j: e.scalar_tensor_tensor(out=acc[sl][:, :], in0=R[sl][:, j:j + W], scalar=cw[:, g, j:j + 1],
                                                                           in1=acc[sl][:, :], op0=ALU.mult, op1=ALU.add),
                             reads=Rkeys(sl) + [('acc', sl)], writes=[('acc', sl)])

                def store(dst_rows, sl, ps_i, key, eng):
                    c0 = tok_base + ps_i * W if dst_rows.shape[1] == L else ps_i * W
                    p.dma(eng, dst_rows[:, c0:c0 + W], ob[sl][:, :], reads=[('ob', sl)], writes=[key], key=('ob', sl))

                RST = lambda sl: [('rst', sl, bk) for bk in range(NB)]

                def job(sl, spec):
                    kind, wb, ps_i, idx = spec
                    if kind == 'gate':
                        for bk in range(NB):
                            pi = nxt('pp', 5)
                            cs = 1 + ps_i * W + bk * 512
                            for k in range(8):
                                p.op('pe', lambda e, k=k: e.matmul(pp[pi][:, :], wbf[wb][:, k, :], hT[:, k, cs:cs + 512], start=(k == 0), stop=(k == 7)),
                                     reads=[('wbf', wb, k)], writes=[('pp', pi)])
                            p.op('act', lambda e: e.activation(out=ob[sl][:, bk * 512:(bk + 1) * 512], in_=pp[pi][:, :], func=AF.Sigmoid),
                                 reads=[('pp', pi)], writes=[('ob', sl)])
                        store(GS[idx * 128:(idx + 1) * 128, :], sl, ps_i, ('GS', idx, ps_i), 'act')
                        return
                    ncol = 32 if kind == 'bg' else 128
                    project(sl, wb, ncol, ps_i)
                    yield
                    if kind == 'bg':
                        xin = R[sl][0:32, 1:W + 1]
                        rk = Rkeys(sl)
                        t0, t1, t2 = acc[sl][0:32, :], xin, rst[sl][0:32, :]
                        K0, K1, K2 = ('acc', sl), ('R', sl, 0), RST(sl)
                        p.op('act', lambda e: e.activation(out=t0, in_=xin, func=AF.Exp, scale=-1.0), reads=rk, writes=[K0])
                        p.op('dve', lambda e: e.tensor_scalar(out=t0, in0=t0, scalar1=1.0, scalar2=None, op0=ALU.add), reads=[K0], writes=[K0])
                        p.op('dve', lambda e: e.reciprocal(out=t0, in_=t0), reads=[K0], writes=[K0])
                        p.op('dve', lambda e: e.tensor_scalar(out=t1, in0=xin, scalar1=dtb[:, 0:1], scalar2=None, op0=ALU.add),
                             reads=rk + ['dtb'], writes=rk)
                        p.op('act', lambda e: e.activation(out=t2, in_=t1, func=AF.Abs), reads=[K1], writes=K2)
                        p.op('act', lambda e: e.activation(out=t2, in_=t2, func=AF.Exp, scale=-1.0), reads=K2, writes=K2)
                        p.op('act', lambda e: e.activation(out=t2, in_=t2, func=AF.Ln, bias=1.0), reads=K2, writes=K2)
                        p.op('dve', lambda e: e.scalar_tensor_tensor(out=t1, in0=t1, scalar=0.0, in1=t2, op0=ALU.max, op1=ALU.add),
                             reads=[K1] + K2, writes=[K1])
                        p.op('dve', lambda e: e.tensor_scalar(out=t1, in0=t1, scalar1=negA[:, 0:1], scalar2=selg[:, 0:1], op0=ALU.mult, op1=ALU.mult),
                             reads=[K1, 'negA', 'selb'], writes=[K1])
                        p.op('dve', lambda e: e.scalar_tensor_tensor(out=t2, in0=t0, scalar=selb[:, 0:1], in1=t1, op0=ALU.mult, op1=ALU.add),
                             reads=[K0, K1, 'selb'], writes=K2)
                        c0 = tok_base + ps_i * W
                        p.dma('pool', BGS[:, c0:c0 + W], t2, reads=K2, writes=[('BGS', own, ps_i)], key=('rst', sl))
                        return
                    if kind in ('z', 'hz'):
                        p.op('act', lambda e: e.activation(out=sil[sl][:, :], in_=R[sl][:, 1:W + 1], func=AF.Silu), reads=Rkeys(sl), writes=[('acc', sl)])
                        yield
                        if kind == 'z':
                            p.op('dve', lambda e: e.tensor_scalar(out=ob[sl][:, :], in0=sil[sl][:, :], scalar1=nwd[:, 0:1], scalar2=None, op0=ALU.mult),
                                 reads=[('acc', sl), 'nwd'], writes=[('ob', sl)])
                            store(ZS[idx * 128:(idx + 1) * 128, :], sl, ps_i, ('ZS', idx, ps_i), 'pool')
                        else:
                            p.op('pool', lambda e: e.tensor_tensor(out=ob[sl][:, :], in0=sil[sl][:, :], in1=keep2[ps_i][:, :], op=ALU.mult),
                                 reads=[('acc', sl), ('keep', ps_i)], writes=[('ob', sl)])
                            store(G0S[idx * 128:(idx + 1) * 128, :], sl, ps_i, ('G0S', idx, ps_i), 'pool')
                        return
                    if kind in ('q', 'k', 'v'):
                        gi = {'q': idx, 'k': NH + idx, 'v': 2 * NH + idx}[kind]
                        conv(sl, cwd, gi)
                        yield
                        if kind == 'v':
                            p.op('act', lambda e: e.activation(out=ob[sl][:, :], in_=acc[sl][:, :], func=AF.Silu), reads=[('acc', sl)], writes=[('ob', sl)])
                            store(VS[idx * 128:(idx + 1) * 128, :], sl, ps_i, ('VS', idx, own, ps_i), 'act')
                            return
                        p.op('act', lambda e: e.activation(out=sil[sl][:, :], in_=acc[sl][:, :], func=AF.Silu), reads=[('acc', sl)], writes=[('acc', sl)])
                        yield
                        p.op('act', lambda e: e.activation(out=sqb[sl][:, :], in_=sil[sl][:, :], func=AF.Square),
                             reads=[('acc', sl)], writes=[('sqb', sl)])
                        yield
                        for bk in range(NB):
                            ni = nxt('pn', 3)
                            p.op('pe', lambda e: e.matmul(pn[ni][:, :], ones_b[:, :], sqb[sl][:, bk * 512:(bk + 1) * 512], start=True, stop=True),
                                 reads=['ones_b', ('sqb', sl)], writes=[('pn', ni)])
                            p.op('act', lambda e: e.activation(out=rst[sl][:, bk * 512:(bk + 1) * 512], in_=pn[ni][:, :], func=AF.Ln, bias=EPS),
                                 reads=[('pn', ni)], writes=[('rst', sl, bk)])
                        scale = 128.0 ** -0.5 if kind == 'q' else 1.0
                        p.op('act', lambda e: e.activation(out=rst[sl][:, :], in_=rst[sl][:, :], func=AF.Exp, scale=-0.5, bias=float(np.log(scale))),
                             reads=RST(sl), writes=RST(sl))
                        yield
                        p.op('pool', lambda e: e.tensor_tensor(out=ob[sl][:, :], in0=sil[sl][:, :], in1=rst[sl][:, :], op=ALU.mult),
                             reads=[('acc', sl)] + RST(sl), writes=[('ob', sl)])
                        dstT = QS if kind == 'q' else KS
                        key = ('QS', idx, ps_i) if kind == 'q' else ('KS', idx, own, ps_i)
                        store(dstT[idx * 128:(idx + 1) * 128, :], sl, ps_i, key, 'pool')
                        return
                    gi = {'x0': idx, 'x1': 8 + idx, 'hv': 16 + idx}[kind]
                    conv(sl, cwh, gi)
                    yield
                    if kind in ('x0', 'x1'):
                        p.op('pool', lambda e: e.tensor_copy(keep2[ps_i][:, :], acc[sl][:, :]), reads=[('acc', sl)], writes=[('keep', ps_i)])
                    else:
                        p.op('pool', lambda e: e.tensor_tensor(out=ob[sl][:, :], in0=acc[sl][:, :], in1=keep2[ps_i][:, :], op=ALU.mult),
                             reads=[('acc', sl), ('keep', ps_i)], writes=[('ob', sl)])
                        store(US[idx * 128:(idx + 1) * 128, :], sl, ps_i, ('US', idx, own, ps_i), 'pool')

                groups = []
                for h in range(NH):
                    for kind in (('q', 'k', 'v', 'z') if own else ('k', 'v')):
                        gi = {'q': h, 'k': NH + h, 'v': 2 * NH + h}.get(kind)
                        groups.append((kind, C_QKV + gi * 128 if kind != 'z' else C_DNZ + h * 128, 128, h))
                groups.append(('bg', C_BETA, 32, 0))
                if "hyproj" in cfg["phases"]:
                    for cb in range(8):
                        for kind in (('x0', 'hz', 'x1', 'hv') if own else ('x1', 'hv')):
                            gi = {'x0': cb, 'x1': 8 + cb, 'hv': 16 + cb}.get(kind)
                            groups.append((kind, C_HXV + gi * 128 if kind != 'hz' else C_HZ + cb * 128, 128, cb))
                if own and "gates" in cfg["phases"]:
                    for gi in range(16):
                        groups.append(('gate', C_GATE + gi * 128, 128, gi))
                specs = []
                for (kind, col0, ncol, idx) in groups:
                    for ps_i in range(2):
                        specs.append((kind, col0, ncol, ps_i, idx))
                wb_of = {}

                gorder = [(g[0], g[3]) for g in groups]
                ginfo = {(g[0], g[3]): g for g in groups}

                def ensure_w(gk):
                    if gk not in wb_of:
                        kind, col0, ncol, idx = ginfo[gk]
                        wb_of[gk] = load_w_group(wst, wbf, w_in[:, col0:col0 + ncol])

                def make(sl, sp):
                    kind, col0, ncol, ps_i, idx = sp
                    ensure_w((kind, idx))
                    gi_ = gorder.index((kind, idx))
                    if ps_i == 0 and gi_ + 1 < len(gorder):
                        ensure_w(gorder[gi_ + 1])
                    return job(sl, (kind, wb_of[(kind, idx)], ps_i, idx))

                run_rr(specs, make, KJ, stagger=cfg.get('stagA', 0))
                p.barrier()

        if "projA" in cfg["phases"]:
            phase_A(False)
            phase_A(True)

        def phase_B():
            with ExitStack() as sB:
                NEG = -30000.0
                cst = sb(sB, "tricst", [128, 6, 128], F32)
                p.dma('sp', cst[:, :, :], c_tri.rearrange("c p f -> p c f"), writes=['tricst'])
                ones_f = sb(sB, "ones_f", [128, 128], F32)
                p.op('pool', lambda e: e.memset(ones_f[:], 1.0), writes=['ones_f'])
                mskf = sb(sB, "mskf", [128, 3, 128], F32)
                p.dma('sp', mskf[:, :, :], c_msk.rearrange("c p f -> p c f"), writes=['mskf'])
                m32h = sb(sB, "m32h", [128, 8, 128], BF16)
                mo64h = sb(sB, "mo64h", [128, 8, 128], BF16)
                mo128h = sb(sB, "mo128h", [128, 8, 128], BF16)
                identh = sb(sB, "identh", [128, 8, 128], BF16)
                for i_, t_ in enumerate((m32h, mo64h, mo128h)):
                    for h in range(NH):
                        p.op('dve', lambda e, i_=i_, t_=t_, h=h: e.tensor_copy(t_[:, h, :], mskf[:, i_, :]), reads=['mskf'], writes=['mskb'])
                for h in range(NH):
                    p.op('dve', lambda e, h=h: e.tensor_copy(identh[:, h, :], ident_f[:, :]), reads=['ident_f'], writes=['mskb'])
                tri = {0: cst[:, 0, :], 1: cst[:, 1, :]}
                nmd = {0: cst[:, 2, :], 1: cst[:, 4, :]}
                nme = {0: cst[:, 3, :], 1: cst[:, 5, :]}
                tt = sb(sB, "tt", [128, 32, 32], F32)
                Gp = [sb(sB, f"Gp{d}", [128, 32, 8], F32) for d in range(2)]
                Glb = sb(sB, "Glb", [128, 32, 16], F32)
                eG = [sb(sB, f"eG{d}", [128, 32, 8], F32) for d in range(2)]
                kd = [sb(sB, f"kd{d}", [128, 32, 8], F32) for d in range(2)]
                gam = sb(sB, "gam", [128, 32, 16], F32)
                bneg = [sb(sB, f"bneg{d}", [128, 32, 8], F32) for d in range(2)]
                beg = [sb(sB, f"beg{d}", [128, 32, 8], F32) for d in range(2)]
                PF = [ps(sB, f"PF{i}", [128, 8, 128], F32) for i in range(3)]
                PB = [ps(sB, f"PB{i}", [128, 8, 128], BF16) for i in range(2)]
                pfc = [0]
                pbc = [0]

                def npf():
                    v = pfc[0] % 3
                    pfc[0] += 1
                    return v

                def npb():
                    v = pbc[0] % 2
                    pbc[0] += 1
                    return v

                HB = [128, 8, 128]
                PW = []
                for i in range(2):
                    d_ = {}
                    for nm, dt in (("rhsG", F32), ("tmp", F32), ("E", F32), ("N0", BF16), ("N1", BF16),
                                   ("M0", BF16), ("M1", BF16), ("Q0", BF16), ("Q1", BF16), ("ktok", BF16),
                                   ("kc", BF16), ("vc", BF16), ("qc", BF16)):
                        d_[nm] = sb(sB, f"pw{i}_{nm}", HB, dt)
                    d_["eGr"] = d_["rhsG"]
                    d_["kbg"] = d_["N1"]
                    PW.append(d_)
                SW = []
                for i in range(3):
                    d_ = {}
                    for nm, dt in (("vb", F32), ("kbgT", BF16), ("kdec", BF16), ("attnT", BF16), ("qdT", BF16), ("Q", BF16)):
                        d_[nm] = sb(sB, f"sw{i}_{nm}", HB, dt)
                    SW.append(d_)
                S = sb(sB, "S", HB, F32)
                rb = sb(sB, "rb", HB, BF16)
                Sb = sb(sB, "Sb", HB, BF16)
                vnb = sb(sB, "vnb", HB, BF16)
                osum = sb(sB, "osum", HB, F32)
                oBt = sb(sB, "oBt", HB, F32)
                oss = sb(sB, "oss", [128, 8], F32)
                onb = sb(sB, "onb", HB, BF16)
                zsc = sb(sB, "zsc", HB, BF16)
                ogc = sb(sB, "ogc", HB, BF16)
                identb3 = ident_b[:, :].unsqueeze(1).broadcast_to(HB)

                def bc_j(ap2):
                    return ap2.unsqueeze(2).broadcast_to(HB)

                def bc_h(ap2):
                    return ap2.unsqueeze(1).broadcast_to(HB)

                def prep_half(own):
                    with ExitStack() as sh:
                        bgsb = sb(sh, "bgsb", [32, HALF], F32)
                        prep_half_(own, bgsb)

                def prep_half_(own, bgsb):
                    base = 0 if own else HALF
                    p.dma('sp', bgsb[:, :], BGS[:, base:base + HALF], reads=[('BGS', own, 0), ('BGS', own, 1)], writes=['bgsb'])
                    pf = PF[npf()]
                    pfv = pf[:, :, :].rearrange("p h j -> p (h j)")
                    for n in range(32):
                        p.op('pe', lambda e, n=n: e.transpose(pfv[:, n * 32:(n + 1) * 32], bgsb[0:32, n * 128:(n + 1) * 128], ident_f[0:32, 0:32]),
                             reads=['bgsb', 'ident_f'], writes=['PFx'])
                    p.op('dve', lambda e: e.tensor_copy(tt[:, :, :].rearrange("p c r -> p (c r)"), pfv), reads=['PFx'], writes=['tt'])
                    for d in range(2):
                        p.op('pe', lambda e, d=d: e.matmul(pfv[:, 0:256], tri[d], tt[:, :, 16 + 8 * d:24 + 8 * d], start=True, stop=True),
                             reads=['tt', 'tricst'], writes=['PFx'])
                        p.op('dve', lambda e, d=d: e.tensor_copy(Gp[d][:, :, :].rearrange("p c h -> p (c h)"), pfv[:, 0:256]),
                             reads=['PFx'], writes=[('Gp', d)])
                    p.op('pe', lambda e: e.matmul(pfv[:, 0:512], ones_f[:, :], tt[:, :, 16:32], start=True, stop=True),
                         reads=['tt', 'ones_f'], writes=['PFx'])
                    p.op('dve', lambda e: e.tensor_copy(Glb[:, :, :].rearrange("p c h -> p (c h)"), pfv[:, 0:512]), reads=['PFx'], writes=['Glb'])
                    p.op('act', lambda e: e.activation(out=gam[:, :, :], in_=Glb[:, :, :], func=AF.Exp), reads=['Glb'], writes=['gam'])
                    for d in range(2):
                        p.op('act', lambda e, d=d: e.activation(out=eG[d][:, :, :], in_=Gp[d][:, :, :], func=AF.Exp), reads=[('Gp', d)], writes=[('eG', d)])
                        p.op('dve', lambda e, d=d: e.tensor_tensor(out=kd[d][:, :, :], in0=Glb[:, :, 8 * d:8 * d + 8], in1=Gp[d][:, :, :], op=ALU.subtract),
                             reads=['Glb', ('Gp', d)], writes=[('kd', d)])
                        p.op('act', lambda e, d=d: e.activation(out=kd[d][:, :, :], in_=kd[d][:, :, :], func=AF.Exp), reads=[('kd', d)], writes=[('kd', d)])
                        p.op('dve', lambda e, d=d: e.tensor_scalar(out=bneg[d][:, :, :], in0=tt[:, :, 8 * d:8 * d + 8], scalar1=-1.0, scalar2=None, op0=ALU.mult),
                             reads=['tt'], writes=[('bneg', d)])
                        p.op('dve', lambda e, d=d: e.tensor_tensor(out=beg[d][:, :, :], in0=tt[:, :, 8 * d:8 * d + 8], in1=eG[d][:, :, :], op=ALU.mult),
                             reads=['tt', ('eG', d)], writes=[('beg', d)])
                    p.barrier()

                def prep_gen(u, n, d, own, need_out):
                    pw = PW[u % 2]
                    sw = SW[u % 3]
                    P_ = f"pw{u % 2}"
                    S_ = f"sw{u % 3}"
                    t0 = (0 if own else HALF) + n * 128
                    kq = [('KS', h, own, (n * 128) // W) for h in range(NH)]
                    kv = [('VS', h, own, (n * 128) // W) for h in range(NH)]
                    p.dma('sp', pw["kc"][:, :, :], KS[:, t0:t0 + 128].rearrange("(h d) t -> d h t", d=128), reads=kq, writes=[P_ + "kc"])
                    p.dma('sp', pw["vc"][:, :, :], VS[:, t0:t0 + 128].rearrange("(h d) t -> d h t", d=128), reads=kv, writes=[P_ + "vc"])
                    if need_out:
                        p.dma('sp', pw["qc"][:, :, :], QS[:, t0:t0 + 128].rearrange("(h d) t -> d h t", d=128),
                              reads=[('QS', h, (n * 128) // W) for h in range(NH)], writes=[P_ + "qc"])
                    p.op('dve', lambda e: e.tensor_tensor(out=pw["rhsG"][:, :, :], in0=bc_h(tri[d]), in1=bc_j(tt[:, n, 16 + 8 * d:24 + 8 * d]), op=ALU.mult),
                         reads=['tt', 'tricst'], writes=[P_ + "rhsG"])
                    yield
                    a = npf()
                    for hh in range(2):
                        p.op('pe', lambda e, hh=hh: e.matmul(PF[a][:, 4 * hh:4 * hh + 4, :], ones_f[:, :], pw["rhsG"][:, 4 * hh:4 * hh + 4, :], start=True, stop=True),
                             reads=[P_ + "rhsG", 'ones_f'], writes=[('PF', a)])
                    p.op('dve', lambda e: e.tensor_tensor(out=pw["tmp"][:, :, :], in0=PF[a][:, :, :], in1=bc_j(Gp[d][:, n, :]), op=ALU.subtract),
                         reads=[('PF', a), ('Gp', d)], writes=[P_ + "tmp"])
                    if need_out:
                        p.op('act', lambda e: e.activation(out=pw["eGr"][:, :, :], in_=PF[a][:, :, :], func=AF.Exp), reads=[('PF', a)], writes=[P_ + "rhsG"])
                    yield
                    if need_out:
                        p.op('pool', lambda e: e.tensor_tensor(out=pw["E"][:, :, :], in0=pw["tmp"][:, :, :], in1=bc_h(nme[d]), op=ALU.add),
                             reads=[P_ + "tmp", 'tricst'], writes=[P_ + "E"])
                        p.op('act', lambda e: e.activation(out=pw["E"][:, :, :], in_=pw["E"][:, :, :], func=AF.Exp), reads=[P_ + "E"], writes=[P_ + "E"])
                    p.op('pool', lambda e: e.tensor_tensor(out=pw["tmp"][:, :, :], in0=bc_h(nmd[d]), in1=pw["tmp"][:, :, :], op=ALU.subtract),
                         reads=[P_ + "tmp", 'tricst'], writes=[P_ + "tmp"])
                    p.op('act', lambda e: e.activation(out=pw["tmp"][:, :, :], in_=pw["tmp"][:, :, :], func=AF.Exp), reads=[P_ + "tmp"], writes=[P_ + "tmp"])
                    yield
                    a = npf()
                    for h in range(NH):
                        p.op('pe', lambda e, h=h: e.matmul(PF[a][:, h, :], pw["kc"][:, h, :], pw["kc"][:, h, :], start=True, stop=True),
                             reads=[P_ + "kc"], writes=[('PF', a)])
                    p.op('pool', lambda e: e.tensor_tensor(out=pw["tmp"][:, :, :], in0=pw["tmp"][:, :, :], in1=bc_j(bneg[d][:, n, :]), op=ALU.mult),
                         reads=[P_ + "tmp", ('bneg', d)], writes=[P_ + "tmp"])
                    p.op('dve', lambda e: e.tensor_tensor(out=pw["N0"][:, :, :], in0=PF[a][:, :, :], in1=pw["tmp"][:, :, :], op=ALU.mult),
                         reads=[('PF', a), P_ + "tmp"], writes=[P_ + "N0"])
                    yield
                    b = npb()
                    for h in range(NH):
                        p.op('pe', lambda e, h=h: e.transpose(PB[b][:, h, :], pw["N0"][:, h, :], ident_b[:, :]),
                             reads=[P_ + "N0", 'ident_b'], writes=[('PB', b)])
                    p.op('act', lambda e: e.activation(out=pw["M0"][:, :, :], in_=PB[b][:, :, :], func=AF.Copy), reads=[('PB', b)], writes=[P_ + "M0"])
                    yield
                    if need_out:
                        a = npf()
                        for h in range(NH):
                            p.op('pe', lambda e, h=h: e.matmul(PF[a][:, h, :], pw["kc"][:, h, :], pw["qc"][:, h, :], start=True, stop=True),
                                 reads=[P_ + "kc", P_ + "qc"], writes=[('PF', a)])
                        p.op('dve', lambda e: e.tensor_tensor(out=sw["attnT"][:, :, :], in0=PF[a][:, :, :], in1=pw["E"][:, :, :], op=ALU.mult),
                             reads=[('PF', a), P_ + "E"], writes=[S_ + "attnT"])
                        p.op('pool', lambda e: e.tensor_tensor(out=sw["qdT"][:, :, :], in0=pw["qc"][:, :, :], in1=pw["eGr"][:, :, :], op=ALU.mult),
                             reads=[P_ + "qc", P_ + "rhsG"], writes=[S_ + "qdT"])
                        yield
                    def mmg(dst_ps, lk, rk, lkey, rkey):
                        for h in range(NH):
                            p.op('pe', lambda e, h=h: e.matmul(PF[dst_ps][:, h, :], lk[:, h, :], rk[:, h, :], start=True, stop=True),
                                 reads=[lkey, rkey], writes=[('PF', dst_ps)])
                    N_, M_, T_, W_ = pw["N0"], pw["M0"], pw["Q0"], pw["Q1"]
                    kN, kM, kT, kW = P_ + "N0", P_ + "M0", P_ + "Q0", P_ + "Q1"
                    No1, Mo1, No2 = pw["N1"], pw["M1"], pw["ktok"]
                    kNo1, kMo1, kNo2 = P_ + "N1", P_ + "M1", P_ + "ktok"
                    p.op('dve', lambda e: e.tensor_tensor(out=No1[:, :, :], in0=N_[:, :, :], in1=mo64h[:, :, :], op=ALU.mult), reads=[kN, 'mskb'], writes=[kNo1])
                    p.op('dve', lambda e: e.tensor_tensor(out=Mo1[:, :, :], in0=M_[:, :, :], in1=mo64h[:, :, :], op=ALU.mult), reads=[kM, 'mskb'], writes=[kMo1])
                    p.op('dve', lambda e: e.tensor_tensor(out=No2[:, :, :], in0=N_[:, :, :], in1=mo128h[:, :, :], op=ALU.mult), reads=[kN, 'mskb'], writes=[kNo2])
                    p.op('dve', lambda e: e.tensor_tensor(out=N_[:, :, :], in0=N_[:, :, :], in1=m32h[:, :, :], op=ALU.mult), reads=[kN, 'mskb'], writes=[kN])
                    p.op('dve', lambda e: e.tensor_tensor(out=M_[:, :, :], in0=M_[:, :, :], in1=m32h[:, :, :], op=ALU.mult), reads=[kM, 'mskb'], writes=[kM])
                    p.op('dve', lambda e: e.tensor_tensor(out=T_[:, :, :], in0=N_[:, :, :], in1=identh[:, :, :], op=ALU.add), reads=[kN, 'mskb'], writes=[kT])
                    p.op('dve', lambda e: e.tensor_tensor(out=W_[:, :, :], in0=M_[:, :, :], in1=identh[:, :, :], op=ALU.add), reads=[kM, 'mskb'], writes=[kW])
                    yield
                    for lvl in range(1, 5):
                        a1, a2 = npf(), npf()
                        mmg(a1, M_, N_, kM, kN)
                        mmg(a2, N_, M_, kN, kM)
                        p.op('act', lambda e: e.activation(out=N_[:, :, :], in_=PF[a1][:, :, :], func=AF.Copy), reads=[('PF', a1)], writes=[kN])
                        p.op('act', lambda e: e.activation(out=M_[:, :, :], in_=PF[a2][:, :, :], func=AF.Copy), reads=[('PF', a2)], writes=[kM])
                        yield
                        a1, a2 = npf(), npf()
                        mmg(a1, M_, T_, kM, kT)
                        mmg(a2, N_, W_, kN, kW)
                        p.op('dve', lambda e: e.tensor_tensor(out=T_[:, :, :], in0=PF[a1][:, :, :], in1=T_[:, :, :], op=ALU.add), reads=[('PF', a1), kT], writes=[kT])
                        p.op('dve', lambda e: e.tensor_tensor(out=W_[:, :, :], in0=PF[a2][:, :, :], in1=W_[:, :, :], op=ALU.add), reads=[('PF', a2), kW], writes=[kW])
                        yield
                    a1, a2 = npf(), npf()
                    mmg(a1, Mo1, T_, kMo1, kT)
                    mmg(a2, No1, W_, kNo1, kW)
                    p.op('act', lambda e: e.activation(out=N_[:, :, :], in_=PF[a1][:, :, :], func=AF.Copy), reads=[('PF', a1)], writes=[kN])
                    p.op('dve', lambda e: e.tensor_copy(M_[:, :, :], PF[a2][:, :, :]), reads=[('PF', a2)], writes=[kM])
                    yield
                    a1, a2 = npf(), npf()
                    mmg(a1, W_, N_, kW, kN)
                    mmg(a2, T_, M_, kT, kM)
                    p.op('dve', lambda e: e.tensor_tensor(out=T_[:, :, :], in0=PF[a1][:, :, :], in1=T_[:, :, :], op=ALU.add), reads=[('PF', a1), kT], writes=[kT])
                    p.op('dve', lambda e: e.tensor_tensor(out=W_[:, :, :], in0=PF[a2][:, :, :], in1=W_[:, :, :], op=ALU.add), reads=[('PF', a2), kW], writes=[kW])
                    yield
                    a1 = npf()
                    mmg(a1, No2, W_, kNo2, kW)
                    p.op('act', lambda e: e.activation(out=M_[:, :, :], in_=PF[a1][:, :, :], func=AF.Copy), reads=[('PF', a1)], writes=[kM])
                    yield
                    a1 = npf()
                    mmg(a1, T_, M_, kT, kM)
                    p.op('dve', lambda e: e.tensor_tensor(out=sw["Q"][:, :, :], in0=PF[a1][:, :, :], in1=W_[:, :, :], op=ALU.add), reads=[('PF', a1), kW], writes=[S_ + "Q"])
                    yield

                    b = npb()
                    for h in range(NH):
                        p.op('pe', lambda e, h=h: e.transpose(PB[b][:, h, :], pw["kc"][:, h, :], ident_b[:, :]),
                             reads=[P_ + "kc", 'ident_b'], writes=[('PB', b)])
                    p.op('act', lambda e: e.activation(out=pw["ktok"][:, :, :], in_=PB[b][:, :, :], func=AF.Copy), reads=[('PB', b)], writes=[P_ + "ktok"])
                    p.op('pool', lambda e: e.tensor_tensor(out=pw["kbg"][:, :, :], in0=pw["ktok"][:, :, :], in1=bc_j(beg[d][:, n, :]), op=ALU.mult),
                         reads=[P_ + "ktok", ('beg', d)], writes=[P_ + "N1"])
                    p.op('pool', lambda e: e.tensor_tensor(out=sw["kdec"][:, :, :], in0=pw["ktok"][:, :, :], in1=bc_j(kd[d][:, n, :]), op=ALU.mult),
                         reads=[P_ + "ktok", ('kd', d)], writes=[S_ + "kdec"])
                    yield
                    b = npb()
                    for h in range(NH):
                        p.op('pe', lambda e, h=h: e.transpose(PB[b][:, h, :], pw["kbg"][:, h, :], ident_b[:, :]),
                             reads=[P_ + "N1", 'ident_b'], writes=[('PB', b)])
                    p.op('act', lambda e: e.activation(out=sw["kbgT"][:, :, :], in_=PB[b][:, :, :], func=AF.Copy), reads=[('PB', b)], writes=[S_ + "kbgT"])
                    yield
                    b = npb()
                    for h in range(NH):
                        p.op('pe', lambda e, h=h: e.transpose(PB[b][:, h, :], pw["vc"][:, h, :], ident_b[:, :]),
                             reads=[P_ + "vc", 'ident_b'], writes=[('PB', b)])
                    p.op('dve', lambda e: e.tensor_tensor(out=sw["vb"][:, :, :], in0=PB[b][:, :, :], in1=bc_j(tt[:, n, 8 * d:8 * d + 8]), op=ALU.mult),
                         reads=[('PB', b), 'tt'], writes=[S_ + "vb"])
                    yield

                def seq_gen(u, n, d, own, need_out, final_dir):
                    sw = SW[u % 3]
                    S_ = f"sw{u % 3}"
                    r0 = n * 128
                    if need_out and final_dir:
                        p.dma('sp', oBt[:, :, :].rearrange("p h e -> p (h e)"), OBS[r0:r0 + 128, :], reads=[('OBS', n)], writes=['oBt'])
                        p.dma('sp', zsc[:, :, :], ZS[:, r0:r0 + 128].rearrange("(h d) t -> d h t", d=128),
                              reads=[('ZS', h, r0 // W) for h in range(NH)], writes=['zsc'])
                    a = npf()
                    for h in range(NH):
                        p.op('pe', lambda e, h=h: e.matmul(PF[a][:, h, :], sw["kbgT"][:, h, :], Sb[:, h, :], start=True, stop=True),
                             reads=[S_ + "kbgT", 'Sb'], writes=[('PF', a)])
                    p.op('dve', lambda e: e.tensor_tensor(out=rb[:, :, :], in0=sw["vb"][:, :, :], in1=PF[a][:, :, :], op=ALU.subtract),
                         reads=[('PF', a), S_ + "vb"], writes=['rb'])
                    yield
                    a = npf()
                    for h in range(NH):
                        p.op('pe', lambda e, h=h: e.matmul(PF[a][:, h, :], sw["Q"][:, h, :], rb[:, h, :], start=True, stop=True),
                             reads=[S_ + "Q", 'rb'], writes=[('PF', a)])
                    p.op('act', lambda e: e.activation(out=vnb[:, :, :], in_=PF[a][:, :, :], func=AF.Copy), reads=[('PF', a)], writes=['vnb'])
                    yield
                    if need_out:
                        ao = npf()
                        for h in range(NH):
                            p.op('pe', lambda e, h=h: e.matmul(PF[ao][:, h, :], sw["qdT"][:, h, :], Sb[:, h, :], start=True, stop=False),
                                 reads=[S_ + "qdT", 'Sb'], writes=[('PF', ao)])
                            p.op('pe', lambda e, h=h: e.matmul(PF[ao][:, h, :], sw["attnT"][:, h, :], vnb[:, h, :], start=False, stop=True),
                                 reads=[S_ + "attnT", 'vnb'], writes=[('PF', ao)])
                    a = npf()
                    for h in range(NH):
                        p.op('pe', lambda e, h=h: e.matmul(PF[a][:, h, :], sw["kdec"][:, h, :], vnb[:, h, :], start=True, stop=True),
                             reads=[S_ + "kdec", 'vnb'], writes=[('PF', a)])
                    p.op('pool', lambda e: e.tensor_tensor(out=S[:, :, :], in0=S[:, :, :], in1=bc_j(gam[:, n, 8 * d:8 * d + 8]), op=ALU.mult),
                         reads=['S', 'gam'], writes=['S'])
                    p.op('dve', lambda e: e.tensor_tensor(out=S[:, :, :], in0=S[:, :, :], in1=PF[a][:, :, :], op=ALU.add),
                         reads=['S', ('PF', a)], writes=['S'])
                    p.op('act', lambda e: e.activation(out=Sb[:, :, :], in_=S[:, :, :], func=AF.Copy), reads=['S'], writes=['Sb'])
                    if need_out:
                        if not final_dir:
                            p.op('act', lambda e: e.activation(out=osum[:, :, :], in_=PF[ao][:, :, :], func=AF.Copy), reads=[('PF', ao)], writes=['osum'])
                        else:
                            p.op('dve', lambda e: e.tensor_tensor(out=osum[:, :, :], in0=PF[ao][:, :, :], in1=oBt[:, :, :], op=ALU.add),
                                 reads=[('PF', ao), 'oBt'], writes=['osum'])
                    yield
                    if need_out:
                        if not final_dir:
                            p.dma('act', OBS[r0:r0 + 128, :], osum[:, :, :].rearrange("p h e -> p (h e)"), reads=['osum'], writes=[('OBS', n)], key='osum')
                        else:
                            p.op('pool', lambda e: e.tensor_tensor(out=oBt[:, :, :], in0=osum[:, :, :], in1=osum[:, :, :], op=ALU.mult),
                                 reads=['osum'], writes=['oBt'])
                            p.op('dve', lambda e: e.tensor_reduce(out=oss[:, :], in_=oBt[:, :, :], axis=AX.X, op=ALU.add), reads=['oBt'], writes=['oss'])
                            p.op('act', lambda e: e.activation(out=oss[:, :], in_=oss[:, :], func=AF.Ln, scale=1.0 / 128, bias=EPS), reads=['oss'], writes=['oss'])
                            p.op('act', lambda e: e.activation(out=oss[:, :], in_=oss[:, :], func=AF.Exp, scale=-0.5), reads=['oss'], writes=['oss'])
                            p.op('pool', lambda e: e.tensor_tensor(out=onb[:, :, :], in0=osum[:, :, :], in1=bc_j(oss[:, :]), op=ALU.mult),
                                 reads=['osum', 'oss'], writes=['onb'])
                            b = npb()
                            for h in range(NH):
                                p.op('pe', lambda e, h=h: e.transpose(PB[b][:, h, :], onb[:, h, :], ident_b[:, :]),
                                     reads=['onb', 'ident_b'], writes=[('PB', b)])
                            p.op('dve', lambda e: e.tensor_tensor(out=ogc[:, :, :], in0=PB[b][:, :, :], in1=zsc[:, :, :], op=ALU.mult),
                                 reads=[('PB', b), 'zsc'], writes=['ogc'])
                            p.dma('sp', OG[:, r0:r0 + 128].rearrange("(h d) t -> d h t", d=128), ogc[:, :, :], reads=['ogc'], writes=['OGall'], key='ogc')
                        yield

                def run_units(units):
                    preps = {}
                    done_prep = set()
                    nxt_prep = 0
                    cur_seq = None
                    cur_u = 0
                    nun = len(units)
                    while cur_u < nun:
                        while nxt_prep < nun and nxt_prep <= cur_u + 2 and len(preps) < cfg.get('dn_par', 2) and (nxt_prep - 2) not in preps:
                            n, d, own, no, fd = units[nxt_prep]
                            preps[nxt_prep] = prep_gen(nxt_prep, n, d, own, no)
                            nxt_prep += 1
                        if cur_seq is None and cur_u in done_prep:
                            n, d, own, no, fd = units[cur_u]
                            cur_seq = seq_gen(cur_u, n, d, own, no, fd)
                        progressed = False
                        if cur_seq is not None:
                            try:
                                next(cur_seq)
                            except StopIteration:
                                cur_seq = None
                                cur_u += 1
                            progressed = True
                        for uu in sorted(list(preps.keys())):
                            try:
                                next(preps[uu])
                            except StopIteration:
                                del preps[uu]
                                done_prep.add(uu)
                            progressed = True
                        assert progressed or cur_u >= nun

                p.op('pool', lambda e: e.memset(S[:, :, :], 0.0), writes=['S'])
                p.op('pool', lambda e: e.memset(Sb[:, :, :], 0.0), writes=['Sb'])
                nck = cfg.get("nchunks", 32)
                if cfg.get("dn_test") == "A":
                    prep_half(True)
                    if "prep_steps" in cfg:
                        g = prep_gen(0, 0, 0, True, True)
                        for _ in range(cfg["prep_steps"]):
                            next(g)
                        p.barrier()
                        return
                    run_units([(n, 0, True, True, False) for n in range(nck)])
                    p.barrier()
                    return
                prep_half(False)
                run_units([(n, 1, False, False, False) for n in range(nck - 1, -1, -1)])
                p.barrier()
                prep_half(True)
                run_units([(n, 1, True, True, False) for n in range(nck - 1, -1, -1)])
                p.barrier()
                p.op('pool', lambda e: e.memset(S[:, :, :], 0.0), writes=['S'])
                p.op('pool', lambda e: e.memset(Sb[:, :, :], 0.0), writes=['Sb'])
                run_units([(n, 0, True, True, True) for n in range(nck)])
                p.barrier()

        if "dn" in cfg["phases"]:
            phase_B()

        N1, N2, NF = 97, 128, 97 * 128
        NEXT = 24608
        NPAD = 12800

        def phase_C1():
            with ExitStack() as sC:
                hd2 = sb(sC, "hd2", [64, NPAD], F32)
                w1t = sb(sC, "w1t", [33, 64], F32)
                w2t = sb(sC, "w2t", [64, 64], F32)
                w3t = sb(sC, "w3t", [64, 3, D], F32)
                frt = sb(sC, "frt", [64, 1], F32)
                fb1 = sb(sC, "fb1", [64, 1], F32)
                fb2 = sb(sC, "fb2", [64, 1], F32)
                ldt = sb(sC, "ldt", [128, 3, 8], F32)
                rate = sb(sC, "rate", [128, 3, 8], F32)
                nrate = sb(sC, "nrate", [128, 3, 8], F32)
                dl = sb(sC, "dl", [128, 512], F32)
                tp0 = sb(sC, "tp0", [128, 25], F32)
                bq = sb(sC, "bq", [128, 25], F32)
                zp = [sb(sC, f"zp{i}", [33, 512], F32) for i in range(2)]
                arg = [sb(sC, f"arg{i}", [64, 512], F32) for i in range(2)]
                kint = [sb(sC, f"kint{i}", [64, 512], mybir.dt.int32) for i in range(2)]
                kf = [sb(sC, f"kf{i}", [64, 512], F32) for i in range(2)]
                h1 = [sb(sC, f"h1_{i}", [64, 512], F32) for i in range(2)]
                win = [sb(sC, f"win{i}", [128, 512], F32) for i in range(2)]
                kl = [sb(sC, f"kl{i}", [128, NPAD], BF16) for i in range(2)]
                pm = [ps(sC, f"pm{i}", [128, 512], F32) for i in range(4)]
                PI = float(np.pi)
                p.dma('sp', w1t[:, :], hy_w1[:, :], writes=['w1t'])
                p.dma('sp', w2t[:, :], hy_w2[:, :], writes=['w2t'])
                p.dma('sp', w3t[:, :, :], hy_w3[:, :, :], writes=['w3t'])
                p.dma('sp', frt[:, :], hy_freq.rearrange("(p o) -> p o", o=1), writes=['frt'], allow_slow_non_contiguous=True)
                p.dma('sp', fb1[:, :], hy_b1.rearrange("(p o) -> p o", o=1), writes=['fb1'], allow_slow_non_contiguous=True)
                p.dma('sp', fb2[:, :], hy_b2.rearrange("(p o) -> p o", o=1), writes=['fb2'], allow_slow_non_contiguous=True)
                for s_ in range(3):
                    p.dma('sp', ldt[:, s_, :], hy_log_decay[s_, :].rearrange("(b p) -> p b", p=128), writes=[('ldt', s_)], allow_slow_non_contiguous=True)
                p.dma('sp', dl[:, :], c_dl[0, :].partition_broadcast(128), writes=['dl'])
                p.dma('sp', tp0[:, :], c_tp0.partition_broadcast(128), writes=['tp0'])
                p.op('dve', lambda e: e.tensor_tensor(out=fb1[:, :], in0=fb1[:, :], in1=frt[:, :], op=ALU.mult), reads=['fb1', 'frt'], writes=['fb1'])
                p.op('dve', lambda e: e.tensor_tensor(out=fb2[:, :], in0=fb2[:, :], in1=frt[:, :], op=ALU.mult), reads=['fb2', 'frt'], writes=['fb2'])
                p.op('act', lambda e: e.activation(out=rate[:, :, :], in_=ldt[:, :, :], func=AF.Exp), reads=[('ldt', i) for i in range(3)], writes=['rate'])
                p.op('dve', lambda e: e.tensor_scalar(out=nrate[:, :, :], in0=rate[:, :, :], scalar1=-1.0, scalar2=None, op0=ALU.mult), reads=['rate'], writes=['nrate'])
                pmc = [0]

                def npm():
                    v = pmc[0] % 4
                    pmc[0] += 1
                    return v

                def sin_layer(src_ps, src_key, fbias, fkey, dst, dst_keys, i):
                    p.op('dve', lambda e: e.tensor_scalar(out=arg[i][:, :], in0=src_ps, scalar1=frt[:, 0:1], scalar2=fbias[:, 0:1], op0=ALU.mult, op1=ALU.add),
                         reads=['frt', fkey, src_key], writes=[('arg', i)])
                    p.op('dve', lambda e: e.tensor_scalar(out=kint[i][:, :], in0=arg[i][:, :], scalar1=1.0 / (2 * PI), scalar2=64.0, op0=ALU.mult, op1=ALU.add),
                         reads=[('arg', i)], writes=[('kint', i)])
                    p.op('dve', lambda e: e.tensor_scalar(out=kf[i][:, :], in0=kint[i][:, :], scalar1=-64.0, scalar2=None, op0=ALU.add),
                         reads=[('kint', i)], writes=[('kf', i)])
                    p.op('dve', lambda e: e.scalar_tensor_tensor(out=arg[i][:, :], in0=kf[i][:, :], scalar=-2 * PI, in1=arg[i][:, :], op0=ALU.mult, op1=ALU.add),
                         reads=[('kf', i), ('arg', i)], writes=[('arg', i)])
                    p.op('act', lambda e: e.activation(out=dst, in_=arg[i][:, :], func=AF.Sin), reads=[('arg', i)], writes=dst_keys)

                for q in range(25):
                    i = q % 2
                    p.dma('sp', zp[i][:, :], c_zpos[:, q * 512:(q + 1) * 512], writes=[('zp', i)])
                    a = npm()
                    p.op('pe', lambda e: e.matmul(pm[a][0:64, :], w1t[:, :], zp[i][:, :], start=True, stop=True), reads=['w1t', ('zp', i)], writes=[('pm', a)])
                    sin_layer(pm[a][0:64, :], ('pm', a), fb1, 'fb1', h1[i][:, :], [('h1', i)], i)
                    a = npm()
                    p.op('pe', lambda e: e.matmul(pm[a][0:64, :], w2t[:, :], h1[i][:, :], start=True, stop=True), reads=['w2t', ('h1', i)], writes=[('pm', a)])
                    sin_layer(pm[a][0:64, :], ('pm', a), fb2, 'fb2', hd2[:, q * 512:(q + 1) * 512], [('hd2', q)], i)
                for cb in range(8):
                    kb = cb % 2
                    p.op('dve', lambda e: e.tensor_scalar(out=bq[:, 0:8], in0=tp0[:, 0:8], scalar1=nrate[:, 0, cb:cb + 1], scalar2=None, op0=ALU.mult),
                         reads=['tp0', 'nrate'], writes=['bq'])
                    p.op('dve', lambda e: e.tensor_scalar(out=bq[:, 8:25], in0=tp0[:, 8:25], scalar1=nrate[:, 1, cb:cb + 1], scalar2=None, op0=ALU.mult),
                         reads=['tp0', 'nrate'], writes=['bq'])
                    for q in range(25):
                        st = 0 if q < 8 else 1
                        a = npm()
                        wi = q % 2
                        p.op('pe', lambda e: e.matmul(pm[a][:, :], w3t[:, st, cb * 128:(cb + 1) * 128], hd2[:, q * 512:(q + 1) * 512], start=True, stop=True),
                             reads=['w3t', ('hd2', q)], writes=[('pm', a)])
                        sc = nrate[:, 0, cb:cb + 1] if q < 8 else rate[:, 1, cb:cb + 1]
                        p.op('act', lambda e: e.activation(out=win[wi][:, :], in_=dl[:, :], func=AF.Exp, scale=sc, bias=bq[:, q:q + 1]),
                             reads=['dl', 'rate', 'nrate', 'bq'], writes=[('win', wi)])
                        p.op('dve', lambda e: e.tensor_tensor(out=kl[kb][:, q * 512:(q + 1) * 512], in0=pm[a][:, :], in1=win[wi][:, :], op=ALU.mult),
                             reads=[('pm', a), ('win', wi)], writes=[('kl', kb, q)])
                    a = npm()
                    p.op('pe', lambda e: e.matmul(pm[a][:, 0:1], w3t[:, 2, cb * 128:(cb + 1) * 128], hd2[:, 0:1], start=True, stop=True),
                         reads=['w3t', ('hd2', 0)], writes=[('pm', a)])
                    p.op('dve', lambda e: e.tensor_copy(kl[kb][:, 0:1], pm[a][:, 0:1]), reads=[('pm', a)], writes=[('kl', kb, 0)])
                    p.op('pool', lambda e: e.memset(kl[kb][:, HALF:HALF + 129], 0.0), writes=[('kl', kb, 8)])
                    p.dma('sp', KLS[cb * 128:(cb + 1) * 128, :], kl[kb][:, 0:NF], reads=[('kl', kb, q) for q in range(25)], writes=[('KLS', cb)], key=('kl', kb))
                p.barrier()

        def phase_C2():
            KP = 4
            with ExitStack() as sC:
                EXT = sb(sC, "EXT", [128, NEXT], BF16)
                Xt = sb(sC, "Xt", [128, N2, 128], BF16)
                Kr = sb(sC, "Kr", [128, 128, 65], BF16)
                Ki = sb(sC, "Ki", [128, 128, 65], BF16)
                nKi = sb(sC, "nKi", [128, 128, 65], BF16)
                YE = sb(sC, "YE", [128, HALF], BF16)
                G0 = sb(sC, "G0", [128, HALF], BF16)
                yo = sb(sC, "yo", [128, HALF], F32)
                yhb = sb(sC, "yhb", [128, HALF], BF16)
                hbt = sb(sC, "hbt", [128, 8], F32)
                p.dma('sp', hbt[:, :], hy_bias.rearrange("(b p) -> p b", p=128), writes=['hbt'], allow_slow_non_contiguous=True)
                mats = {}
                stg = sb(sC, "mstg", [128, 194], F32)
                for nm, src, r, c in (("e1", c_e1, 97, 194), ("s2a", c_s2a, 128, 130), ("s2b", c_s2b, 128, 130),
                                      ("i1c", c_i1c, 97, 194), ("i1d", c_i1d, 97, 194), ("cw", c_cw, 65, 128), ("sw", c_sw, 65, 128)):
                    t_ = sb(sC, "m_" + nm, [128, c], BF16)
                    p.dma('sp', stg[0:r, 0:c], src[:, :], writes=['mstg'])
                    p.op('dve', lambda e: e.tensor_copy(t_[0:r, :], stg[0:r, 0:c]), reads=['mstg'], writes=['m_' + nm])
                    mats[nm] = t_
                Y1 = [sb(sC, f"Y1_{i}", [128, 2, 194], BF16) for i in range(KP)]
                Zs = [sb(sC, f"Zs{i}", [128, 2, 130], BF16) for i in range(KP)]
                Zt = [sb(sC, f"Zt{i}", [128, 2, 130], F32) for i in range(KP)]
                Zu = [sb(sC, f"Zu{i}", [128, 2, 130], F32) for i in range(KP)]
                Vs = [sb(sC, f"Vs{i}", [128, 2, 194], BF16) for i in range(KP)]
                Yc = [sb(sC, f"Yc{i}", [128, 8, N1], BF16) for i in range(2)]
                KP = 4
                PA_ = [ps(sC, f"PA_{i}", [128, 512], F32) for i in range(KP)]
                PB_ = [ps(sC, f"PB_{i}", [128, 512], F32) for i in range(KP)]
                P1 = PA_
                cnt = {'pr': 0, 'tp': 0}

                def to_Xt():
                    for g in range(N2 // 8):
                        b = cnt['tp'] % 4
                        cnt['tp'] += 1
                        pt = P1[b][:, :].bitcast(BF16).rearrange("p (a c) -> p a c", c=128)
                        for a in range(8):
                            t2 = g * 8 + a
                            p.op('pe', lambda e, a=a, t2=t2: e.transpose(pt[0:N1, a, :], EXT[:, 97 * t2:97 * t2 + 128 * (N1 - 1) + 1:128], ident_b[:, :]),
                                 reads=['EXT', 'ident_b'], writes=[('P1', b)])
                        eng = 'act' if g % 2 == 0 else 'dve'
                        if eng == 'act':
                            p.op('act', lambda e: e.activation(out=Xt[0:N1, g * 8:(g + 1) * 8, :], in_=pt[0:N1, 0:8, :], func=AF.Copy),
                                 reads=[('P1', b)], writes=[('Xt', g)])
                        else:
                            p.op('dve', lambda e: e.tensor_copy(Xt[0:N1, g * 8:(g + 1) * 8, :], pt[0:N1, 0:8, :]), reads=[('P1', b)], writes=[('Xt', g)])

                XtK = [('Xt', g) for g in range(N2 // 8)]
                XtC = [('Xtc', c0) for c0 in range(0, 128, 2)]

                def pair_gen(i, spec):
                    c0, is_filter = spec
                    kA, kB = ('P1', i), ('P2', i)
                    p1 = PA_[i][:, 0:388].rearrange("p (a f) -> p a f", f=194)
                    p2 = PB_[i][:, 0:260].rearrange("p (a f) -> p a f", f=130)
                    for a in range(2):
                        p.op('pe', lambda e, a=a: e.matmul(p1[:, a, :], Xt[0:N1, :, c0 + a], mats["e1"][0:N1, :], start=True, stop=True),
                             reads=XtK + [('Xtc', c0), 'm_e1'], writes=[kA])
                    p.op('act', lambda e: e.activation(out=Y1[i][:, :, :], in_=p1, func=AF.Copy), reads=[kA], writes=[('Y1', i)])
                    yield
                    for a in range(2):
                        p.op('pe', lambda e, a=a: e.matmul(p2[0:N1, a, :], Y1[i][:, a, 0:97], mats["s2a"][:, :], start=True, stop=False),
                             reads=[('Y1', i), 'm_s2a'], writes=[kB])
                        p.op('pe', lambda e, a=a: e.matmul(p2[0:N1, a, :], Y1[i][:, a, 97:194], mats["s2b"][:, :], start=False, stop=True),
                             reads=[('Y1', i), 'm_s2b'], writes=[kB])
                    if is_filter:
                        p.op('act', lambda e: e.activation(out=Kr[0:N1, c0:c0 + 2, :], in_=p2[0:N1, :, 0:65], func=AF.Copy), reads=[kB], writes=[('K', c0)])
                        p.op('dve', lambda e: e.tensor_copy(Ki[0:N1, c0:c0 + 2, :], p2[0:N1, :, 65:130]), reads=[kB], writes=[('K', c0)])
                        p.op('dve', lambda e: e.tensor_scalar(out=nKi[0:N1, c0:c0 + 2, :], in0=p2[0:N1, :, 65:130], scalar1=-1.0, scalar2=None, op0=ALU.mult),
                             reads=[kB], writes=[('K', c0)])
                        return
                    krb = Kr[0:N1, c0:c0 + 2, :].unsqueeze(2).broadcast_to([N1, 2, 2, 65])
                    p2v = p2[0:N1, :, :].rearrange("p a (r f) -> p a r f", r=2)
                    p.op('dve', lambda e: e.tensor_tensor(out=Zt[i][0:N1, :, :].rearrange("p a (r f) -> p a r f", r=2), in0=p2v, in1=krb, op=ALU.mult),
                         reads=[kB, ('K', c0)], writes=[('Zt', i)])
                    p.op('dve', lambda e: e.tensor_tensor(out=Zu[i][0:N1, :, 0:65], in0=p2[0:N1, :, 65:130], in1=nKi[0:N1, c0:c0 + 2, :], op=ALU.mult),
                         reads=[kB, ('K', c0)], writes=[('Zu', i)])
                    p.op('dve', lambda e: e.tensor_tensor(out=Zu[i][0:N1, :, 65:130], in0=p2[0:N1, :, 0:65], in1=Ki[0:N1, c0:c0 + 2, :], op=ALU.mult),
                         reads=[kB, ('K', c0)], writes=[('Zu', i)])
                    p.op('pool', lambda e: e.tensor_tensor(out=Zs[i][0:N1, :, :], in0=Zt[i][0:N1, :, :], in1=Zu[i][0:N1, :, :], op=ALU.add),
                         reads=[('Zt', i), ('Zu', i)], writes=[('Zs', i)])
                    yield
                    p3 = PA_[i][:, 0:388].rearrange("p (a f) -> p a f", f=194)
                    for a in range(2):
                        p.op('pe', lambda e, a=a: e.matmul(p3[0:65, a, :], Zs[i][0:N1, a, 0:65], mats["i1c"][0:N1, :], start=True, stop=False),
                             reads=[('Zs', i), 'm_i1c'], writes=[kA])
                        p.op('pe', lambda e, a=a: e.matmul(p3[0:65, a, :], Zs[i][0:N1, a, 65:130], mats["i1d"][0:N1, :], start=False, stop=True),
                             reads=[('Zs', i), 'm_i1d'], writes=[kA])
                    p.op('act', lambda e: e.activation(out=Vs[i][0:65, :, :], in_=p3[0:65, :, :], func=AF.Copy), reads=[kA], writes=[('Vs', i)])
                    yield
                    p4 = PB_[i][:, 0:256].rearrange("p (a f) -> p a f", f=128)
                    for a in range(2):
                        p.op('pe', lambda e, a=a: e.matmul(p4[0:N1, a, :], Vs[i][0:65, a, 0:97], mats["cw"][0:65, :], start=True, stop=False),
                             reads=[('Vs', i), 'm_cw'], writes=[kB])
                        p.op('pe', lambda e, a=a: e.matmul(p4[0:N1, a, :], Vs[i][0:65, a, 97:194], mats["sw"][0:65, :], start=False, stop=True),
                             reads=[('Vs', i), 'm_sw'], writes=[kB])
                    p.op('act', lambda e: e.activation(out=Xt[0:N1, :, c0:c0 + 2].rearrange("p t a -> p a t"), in_=p4[0:N1, :, :], func=AF.Copy),
                         reads=[kB], writes=[('Xtc', c0)])

                def from_Xt():
                    for g in range(N2 // 8):
                        b = cnt['tp'] % 4
                        cnt['tp'] += 1
                        pt = P1[b][:, :].bitcast(BF16)[:, 0:8 * 98].rearrange("p (a t) -> p a t", t=98)[:, :, 0:N1]
                        for a in range(8):
                            t2 = g * 8 + a
                            p.op('pe', lambda e, a=a, t2=t2: e.transpose(pt[:, a, :], Xt[0:N1, t2, :], ident_b[0:N1, 0:N1]),
                                 reads=XtK + XtC + ['ident_b'], writes=[('P1', b)])
                        yb = g % 2
                        p.op('act', lambda e: e.activation(out=Yc[yb][:, :, :], in_=pt, func=AF.Copy), reads=[('P1', b)], writes=[('Yc', yb)])
                        for a in range(8):
                            t2 = g * 8 + a
                            lo0 = 0
                            hi0 = min(N1, max(0, -(-(HALF - 97 * t2) // 128)))
                            if hi0 > lo0:
                                p.op('pool', lambda e, a=a, t2=t2, hi0=hi0: e.tensor_copy(YE[:, 97 * t2:97 * t2 + 128 * (hi0 - 1) + 1:128], Yc[yb][:, a, 0:hi0]),
                                     reads=[('Yc', yb)], writes=['YE'])
                            lo1 = max(0, -(-(NF - 97 * t2) // 128))
                            hi1 = min(N1, -(-(NF + HALF - 97 * t2) // 128))
                            if hi1 > lo1:
                                s0 = 97 * t2 + 128 * lo1 - NF
                                n_ = hi1 - lo1
                                p.op('pool', lambda e, a=a, s0=s0, n_=n_, lo1=lo1, hi1=hi1: e.tensor_copy(YE[:, s0:s0 + 128 * (n_ - 1) + 1:128], Yc[yb][:, a, lo1:hi1]),
                                     reads=[('Yc', yb)], writes=['YE'])

                nblk = cfg.get("hy_blocks", 8)
                for cb in range(nblk):
                    rows = slice(cb * 128, (cb + 1) * 128)
                    p.dma('sp', EXT[:, 0:NF], KLS[rows, :], reads=[('KLS', cb)], writes=['EXT'])
                    p.dma('sp', EXT[:, NF:NEXT], KLS[rows, 0:NEXT - NF], reads=[('KLS', cb)], writes=['EXT'], key='EXTb')
                    to_Xt()
                    run_rr([(c0, True) for c0 in range(0, 128, 2)], pair_gen, KP, stagger=cfg.get('stagC', 0))
                    p.dma('sp', EXT[:, 0:L], US[rows, :], reads=[('US', cb, o_, q_) for o_ in (True, False) for q_ in range(2)], writes=['EXT'])
                    p.dma('sp', EXT[:, NF:NF + L], US[rows, :], reads=[('US', cb, o_, q_) for o_ in (True, False) for q_ in range(2)], writes=['EXT'], key='EXTb')
                    p.op('pool', lambda e: e.memset(EXT[:, L:NF], 0.0), writes=['EXT'])
                    p.op('pool', lambda e: e.memset(EXT[:, NF + L:NEXT], 0.0), writes=['EXT'])
                    p.dma('sp', G0[:, :], G0S[rows, :], reads=[('G0S', cb, 0), ('G0S', cb, 1)], writes=['G0'])
                    to_Xt()
                    run_rr([(c0, False) for c0 in range(0, 128, 2)], pair_gen, KP, stagger=cfg.get('stagC', 0))
                    from_Xt()
                    p.op('dve', lambda e: e.scalar_tensor_tensor(out=yo[:, :], in0=EXT[:, 0:HALF], scalar=hbt[:, cb:cb + 1], in1=YE[:, :], op0=ALU.mult, op1=ALU.add),
                         reads=['EXT', 'hbt', 'YE'], writes=['yo'])
                    p.op('pool', lambda e: e.tensor_tensor(out=yhb[:, :], in0=yo[:, :], in1=G0[:, :], op=ALU.mult), reads=['yo', 'G0'], writes=['yhb'])
                    p.dma('pool', YH[rows, :], yhb[:, :], reads=['yhb'], writes=['YHall'], key='yhb')
                    if "YE" in cfg.get("dbg", ()):
                        p.dma('sp', dbg_out["YE"][rows, :], YE[:, :], reads=['YE'], writes=[('dbgYE', cb)], key='dbgYE')
                p.barrier()

        if "hyena" in cfg["phases"]:
            if "YE" in cfg.get("dbg", ()):
                ddbg("YE", [D, HALF], BF16)
            if "skipC1" not in cfg.get("dbg", ()):
                phase_C1()
            phase_C2()

        if "out" in cfg["phases"]:
            with ExitStack() as s4:
                wst4 = [sb(s4, f"w4st{i}", [128, 8, 512], F32) for i in range(2)]
                now_t = sb(s4, "now_t", [128, D], F32)
                p.dma('sp', now_t[:], norm_out_w.partition_broadcast(128), writes=['now_t'])
                wdn = sb(s4, "wdn", [128, 8, D], BF16)
                why = sb(s4, "why", [128, 8, D], BF16)
                wo = sb(s4, "wo", [128, 8, D], BF16)
                ci = 0
                for wsrc, wdst, nm in ((w_dn_out, wdn, 'wdn'), (w_hy_out, why, 'why'), (w_out, wo, 'wo')):
                    for hh in range(2):
                        b = ci % 2
                        ci += 1
                        p.dma('sp', wst4[b][:, :, :], wsrc[:, hh * 512:(hh + 1) * 512].rearrange("(k p) c -> p k c", p=128),
                              writes=[('w4st', b)])
                        p.op('pool', lambda e: e.tensor_copy(wdst[:, :, hh * 512:(hh + 1) * 512], wst4[b][:, :, :]),
                             reads=[('w4st', b)], writes=[(nm, hh)])
                ogb = [sb(s4, f"ogb{i}", [128, 8, 512], BF16) for i in range(2)]
                yhb = [sb(s4, f"yhb{i}", [128, 8, 512], BF16) for i in range(2)]
                gtb = [sb(s4, f"gtb{i}", [128, 16, 512], BF16) for i in range(2)]
                m1 = [sb(s4, f"m1_{i}", [128, 512], F32) for i in range(2)]
                m2 = [sb(s4, f"m2_{i}", [128, 512], F32) for i in range(2)]
                mb = [sb(s4, f"mb{i}", [128, 8, 512], BF16) for i in range(2)]
                xr = [sb(s4, f"xr{i}", [128, D], F32) for i in range(2)]
                res = [sb(s4, f"res{i}", [128, D], F32) for i in range(2)]
                junk4 = sb(s4, "junk4", [128, D], BF16)
                ss4 = [sb(s4, f"ss4_{i}", [128, 1], F32) for i in range(2)]
                ot = [sb(s4, f"ot{i}", [128, D], F32) for i in range(2)]
                pa = [ps(s4, f"pa{i}", [128, 512], F32) for i in range(2)]
                pb = [ps(s4, f"pb{i}", [128, 512], F32) for i in range(2)]
                pf = [ps(s4, f"pf{i}", [128, 512], F32) for i in range(4)]
                out_toks = []
                ti = 0
                for bk in range(8):
                    b = bk % 2
                    tsl = slice(bk * 512, (bk + 1) * 512)
                    p.dma('sp', ogb[b][:, :, :], OG[:, tsl].rearrange("(k p) t -> p k t", p=128),
                          reads=[('OG', k) for k in range(8)] + ['OGall'], writes=[('ogb', b)])
                    p.dma('sp', yhb[b][:, :, :], YH[:, tsl].rearrange("(k p) t -> p k t", p=128),
                          reads=[('YH', k) for k in range(8)] + ['YHall'], writes=[('yhb', b)])
                    p.dma('sp', gtb[b][:, :, :], GS[:, tsl].rearrange("(k p) t -> p k t", p=128),
                          reads=[('GS', k) for k in range(16)], writes=[('gtb', b)])
                    for dg in range(8):
                        q2 = dg % 2
                        for k in range(8):
                            p.op('pe', lambda e, k=k: e.matmul(pa[q2][:, :], wdn[:, k, dg * 128:(dg + 1) * 128], ogb[b][:, k, :],
                                                               start=(k == 0), stop=(k == 7)),
                                 reads=[('wdn', dg // 4), ('ogb', b)], writes=[('pa', q2)])
                        for k in range(8):
                            p.op('pe', lambda e, k=k: e.matmul(pb[q2][:, :], why[:, k, dg * 128:(dg + 1) * 128], yhb[b][:, k, :],
                                                               start=(k == 0), stop=(k == 7)),
                                 reads=[('why', dg // 4), ('yhb', b)], writes=[('pb', q2)])
                        p.op('dve', lambda e: e.tensor_tensor(out=m1[q2][:, :], in0=pa[q2][:, :], in1=gtb[b][:, dg, :], op=ALU.mult),
                             reads=[('pa', q2), ('gtb', b)], writes=[('m1', q2)])
                        p.op('dve', lambda e: e.tensor_tensor(out=m2[q2][:, :], in0=pb[q2][:, :], in1=gtb[b][:, 8 + dg, :], op=ALU.mult),
                             reads=[('pb', q2), ('gtb', b)], writes=[('m2', q2)])
                        p.op('pool', lambda e: e.tensor_tensor(out=mb[b][:, dg, :], in0=m1[q2][:, :], in1=m2[q2][:, :], op=ALU.add),
                             reads=[('m1', q2), ('m2', q2)], writes=[('mb', b, dg)])
                    for tt in range(4):
                        t0 = bk * 512 + tt * 128
                        r = ti % 2
                        ti += 1
                        p.dma('sp', xr[r][:, :], x[t0:t0 + 128, :], writes=[('xr', r)])
                        for nh in range(2):
                            fi = (2 * ti + nh) % 4
                            for k in range(8):
                                p.op('pe', lambda e, k=k: e.matmul(pf[fi][:, :], mb[b][:, k, tt * 128:(tt + 1) * 128],
                                                                   wo[:, k, nh * 512:(nh + 1) * 512], start=(k == 0), stop=(k == 7)),
                                     reads=[('mb', b, k), ('wo', nh)], writes=[('pf', fi)])
                            p.op('dve', lambda e: e.tensor_tensor(out=res[r][:, nh * 512:(nh + 1) * 512], in0=pf[fi][:, :],
                                                                  in1=xr[r][:, nh * 512:(nh + 1) * 512], op=ALU.add),
                                 reads=[('pf', fi), ('xr', r)], writes=[('res', r, nh)])
                        p.op('act', lambda e: e.activation(out=junk4[:, :], in_=res[r][:, :], func=AF.Square, accum_out=ss4[r][:, :]),
                             reads=[('res', r, 0), ('res', r, 1)], writes=['junk4', ('ss4', r)])
                        p.op('act', lambda e: e.activation(out=ss4[r][:, :], in_=ss4[r][:, :], func=AF.Ln, scale=1.0 / D, bias=EPS),
                             reads=[('ss4', r)], writes=[('ss4', r)])
                        p.op('act', lambda e: e.activation(out=ss4[r][:, :], in_=ss4[r][:, :], func=AF.Exp, scale=-0.5),
                             reads=[('ss4', r)], writes=[('ss4', r)])
                        p.op('dve', lambda e: e.scalar_tensor_tensor(out=ot[r][:, :], in0=res[r][:, :], scalar=ss4[r][:, :],
                                                                      in1=now_t[:, :], op0=ALU.mult, op1=ALU.mult),
                             reads=[('res', r, 0), ('res', r, 1), ('ss4', r), 'now_t'], writes=[('ot', r)])
                        out_toks.append(p.dma('sp', y[t0:t0 + 128, :], ot[r][:, :], reads=[('ot', r)], writes=[('y', t0)], key=('ot', r)))
                p.barrier()
        for nm in cfg.get("dump", ()):
            src = {"OG": OG, "YH": YH, "GS": GS, "QS": QS, "KS": KS, "VS": VS, "ZS": ZS, "BGS": BGS, "OBS": OBS, "US": US, "G0S": G0S, "KLS": KLS}[nm]
            dst = ddbg(nm, src.shape, src.dtype)
            nr = src.shape[0]
            step = 128 if nr >= 128 else nr
            for r0 in range(0, nr, step):
                p.dma('sp', dst[r0:r0 + step, :], src[r0:r0 + step, :], reads=[], writes=[('dump', nm, r0)], key=('dump', (r0 // step) % 4))
        p.barrier()
        print("instr counts", p.ninstr, "nsem", p.nsem)
    return nc


def _core_inputs(inputs, b, hf):
    xs = inputs["x"][b]
    w_in = inputs["w_in"][0]
    if hf == 1:
        xs = xs[::-1]
        perm = np.arange(INW)
        for base in (C_BETA, C_A):
            perm[base:base + 8] = np.arange(base + 8, base + 16)
            perm[base + 8:base + 16] = np.arange(base, base + 8)
        w_in = w_in[:, perm]
    dcw = inputs["dn_conv_w"][0]
    hcw = inputs["hy_conv_w"][0]
    alog = inputs["dn_a_log"][0]
    dtb = inputs["dn_dt_bias"][0]
    if hf == 1:
        dcw, hcw, alog, dtb = dcw[::-1], hcw[::-1], alog[::-1], dtb[::-1]
    w3 = inputs["hy_w3"][0]
    ld = inputs["hy_log_decay"][0]
    w3f, w3b, ldf, ldb = w3[:, :D], w3[:, D:], ld[:D], ld[D:]
    if hf == 0:
        w3s, lds = np.stack([w3f, w3b, w3f], axis=1), np.stack([ldf, ldb, ldf], axis=0)
    else:
        w3s, lds = np.stack([w3b, w3f, w3f], axis=1), np.stack([ldb, ldf, ldf], axis=0)
    m = {
        "hy_w1": np.ascontiguousarray(inputs["hy_w1"][0]), "hy_b1": np.ascontiguousarray(inputs["hy_b1"][0]),
        "hy_w2": np.ascontiguousarray(inputs["hy_w2"][0]), "hy_b2": np.ascontiguousarray(inputs["hy_b2"][0]),
        "hy_freq": np.ascontiguousarray(inputs["hy_freq"][0]), "hy_w3": np.ascontiguousarray(w3s),
        "hy_log_decay": np.ascontiguousarray(lds), "hy_bias": np.ascontiguousarray(inputs["hy_bias"][0]),
        "dn_conv_w": np.ascontiguousarray(dcw), "hy_conv_w": np.ascontiguousarray(hcw),
        "dn_a_log": np.ascontiguousarray(alog).reshape(16), "dn_dt_bias": np.ascontiguousarray(dtb).reshape(16),
        "dn_norm_w": np.ascontiguousarray(inputs["dn_norm_w"][0]),
        "x": np.ascontiguousarray(xs, dtype=np.float32),
        "norm_in_w": np.ascontiguousarray(inputs["norm_in_w"][0]),
        "w_in": np.ascontiguousarray(w_in),
        "w_dn_out": np.ascontiguousarray(inputs["w_dn_out"][0]),
        "w_hy_out": np.ascontiguousarray(inputs["w_hy_out"][0]),
        "w_out": np.ascontiguousarray(inputs["w_out"][0]),
        "norm_out_w": np.ascontiguousarray(inputs["norm_out_w"]),
    }
    m.update(_consts())
    return m


FULL_CFG = {"phases": ("projA", "hyproj", "gates", "dn", "hyena", "out")}


def kernel(**inputs):
    nc = build(FULL_CFG)
    in_maps = [_core_inputs(inputs, c // 2, c % 2) for c in range(8)]
    res = run_bass_kernel_spmd(nc, in_maps, core_ids=list(range(8)))
    out = np.empty((4, L, D), np.float32)
    for c in range(8):
        b, hf = c // 2, c % 2
        yc = res.results[c]["y"]
        if hf == 0:
            out[b, :HALF] = yc
        else:
            out[b, HALF:] = yc[::-1]
    return out
```

```python
import numpy as np
import concourse.bass as bass
import concourse.mybir as mybir
from concourse.bass_utils import run_bass_kernel_spmd
from contextlib import ExitStack

F32 = mybir.dt.float32
BF16 = mybir.dt.bfloat16
AF = mybir.ActivationFunctionType
ALU = mybir.AluOpType
AX = mybir.AxisListType

D = 1024
L = 8192
HALF = 4096
NH = 8
INW = 10272
EPS = 1e-6
C_QKV, C_DNZ, C_BETA, C_A, C_HXV, C_HZ, C_GATE = 0, 3072, 4096, 4112, 4128, 7200, 8224


class Prog:
    SEM_EPOCH = 20000

    def __init__(self, nc, es, same_engine_sync=True):
        self.nc = nc
        self.es = es
        self.engs = {'pe': nc.tensor, 'act': nc.scalar, 'dve': nc.vector, 'pool': nc.gpsimd, 'sp': nc.sync}
        self.sem = {}
        self.cnt = {}
        self.nsem = 0
        for e in self.engs:
            self._new_eng_sem(e)
        self.waited = {e: {} for e in self.engs}
        self.last_w = {}
        self.readers = {}
        self.dma_sem = {}
        self.same = same_engine_sync
        self.ninstr = {e: 0 for e in self.engs}
        self.last_tok = {}
        self.dma_toks = []

    def _mksem(self, name):
        self.nsem += 1
        return self.es.enter_context(self.nc.semaphore(f"{name}_{self.nsem}"))

    def _new_eng_sem(self, e):
        self.sem[e] = self._mksem("s" + e)
        self.cnt[e] = 0

    def _wait(self, e, tok):
        if tok is None:
            return
        sem, val, src = tok
        if src == e and (not self.same or e == 'pe'):
            return
        w = self.waited[e]
        k = id(sem)
        if k in w and w[k] >= val:
            return
        w[k] = val
        self.engs[e].wait_ge(sem, val)
        self.ninstr[e] += 1

    def _deps(self, e, reads, writes):
        for k in reads:
            self._wait(e, self.last_w.get(k))
        for k in writes:
            t = self.last_w.get(k)
            if t is not None and (t[2] != e or k in reads):
                self._wait(e, t)
            for t in self.readers.get(k, ()):
                if t[2] != e:
                    self._wait(e, t)

    def _commit(self, tok, reads, writes):
        for k in reads:
            self.readers.setdefault(k, []).append(tok)
        for k in writes:
            self.last_w[k] = tok
            self.readers[k] = []

    PSUM_NAMES = ('PF', 'PB', 'PFx', 'pp', 'pn', 'ph', 'pa', 'pb', 'pf', 'pTo', 'pTx', 'pm', 'P1', 'P2', 'P3', 'P4')

    def _excl(self, reads, writes):
        r2, w2 = [], list(writes)
        for k in reads:
            nm = k[0] if isinstance(k, tuple) else k
            if nm in self.PSUM_NAMES:
                if k not in w2:
                    w2.append(k)
            else:
                r2.append(k)
        return r2, w2

    def op(self, e, fn, reads=(), writes=()):
        reads, writes = self._excl(reads, writes)
        self._deps(e, reads, writes)
        if self.cnt[e] >= self.SEM_EPOCH:
            self._new_eng_sem(e)
        ins = fn(self.engs[e])
        self.cnt[e] += 1
        ins.then_inc(self.sem[e], 1)
        self.ninstr[e] += 1
        tok = (self.sem[e], self.cnt[e], e)
        self.last_tok[e] = tok
        self._commit(tok, reads, writes)
        return tok

    def dma(self, e, out, in_, reads=(), writes=(), key=None, **kw):
        self._deps(e, reads, writes)
        if key is None:
            key = (writes[0] if writes else reads[0])
        ds = self.dma_sem.get(key)
        if ds is None or ds[1] + 16 > self.SEM_EPOCH:
            ds = [self._mksem("d"), 0]
            self.dma_sem[key] = ds
        ds[1] += 16
        self.engs[e].dma_start(out=out, in_=in_, **kw).then_inc(ds[0], 16)
        self.ninstr[e] += 1
        tok = (ds[0], ds[1], 'dma')
        self.dma_toks.append(tok)
        self._commit(tok, reads, writes)
        return tok

    def barrier(self):
        toks = list(self.last_tok.values())
        latest = {}
        for t in self.dma_toks:
            k = id(t[0])
            if k not in latest or latest[k][1] < t[1]:
                latest[k] = t
        toks += list(latest.values())
        self.dma_toks = list(latest.values())
        for e in self.engs:
            for t in toks:
                if t[2] == e:
                    continue
                self._wait(e, t)
        self.last_w = {}
        self.readers = {}


def _consts():
    ident = np.eye(128, dtype=np.float32)
    pi, fi = np.meshgrid(np.arange(128), np.arange(128), indexing="ij")
    NEG = -30000.0
    tri = np.stack([
        (pi <= fi).astype(np.float32),
        (pi >= fi).astype(np.float32),
        np.where(fi < pi, 0.0, NEG),
        np.where(fi >= pi, 0.0, NEG),
        np.where(fi > pi, 0.0, NEG),
        np.where(fi <= pi, 0.0, NEG),
    ]).astype(np.float32)
    sel = np.zeros((32, 2), np.float32)
    sel[:16, 0] = 1.0
    sel[16:, 1] = 1.0
    msk = np.stack([(pi // 32 == fi // 32), (pi // 64 == fi // 64) & (pi // 32 != fi // 32), (pi // 64 != fi // 64)]).astype(np.float32)
    out = {"c_ident": ident, "c_tri": tri, "c_sel": sel, "c_msk": msk}
    NF, NP = 97 * 128, 12800
    j = np.arange(NP)
    pos = np.where(j < HALF, j, NF - j).astype(np.float64)
    pos = np.clip(pos, 0, None)
    tl = pos / (L - 1)
    bands = 16
    fb = np.linspace(1e-4, bands - 1, bands)
    ang = (2.0 * np.pi / L) * pos[None, :] * fb[:, None]
    out["c_zpos"] = np.concatenate([tl[None, :], np.cos(ang), -np.sin(ang)], axis=0).astype(np.float32)
    out["c_dl"] = (np.arange(512) / (L - 1)).astype(np.float32)[None, :]
    q = np.arange(25)
    out["c_tp0"] = np.where(q < 8, 512 * q / (L - 1), (NF - 512 * q) / (L - 1)).astype(np.float32)
    a1 = 2 * np.pi * np.outer(np.arange(97), np.arange(97)) / 97
    c1, s1 = np.cos(a1), np.sin(a1)
    a2 = 2 * np.pi * np.outer(np.arange(128), np.arange(65)) / 128
    c2, s2 = np.cos(a2), np.sin(a2)
    out["c_e1"] = np.concatenate([c1, -s1], axis=1).astype(np.float32)
    out["c_s2a"] = np.concatenate([c2, -s2], axis=1).astype(np.float32)
    out["c_s2b"] = np.concatenate([s2, c2], axis=1).astype(np.float32)
    out["c_i1c"] = np.concatenate([c1, s1], axis=1).astype(np.float32)
    out["c_i1d"] = np.concatenate([-s1, c1], axis=1).astype(np.float32)
    wgt = np.full(65, 2.0)
    wgt[0] = wgt[64] = 1.0
    out["c_cw"] = (wgt[:, None] * c2.T / NF).astype(np.float32)
    out["c_sw"] = (-wgt[:, None] * s2.T / NF).astype(np.float32)
    return out


def build(cfg):
    nc = bass.Bass("TRN2", target_bir_lowering=False)
    dbg = cfg.get("dbg", ())

    def din(name, shape, dt=F32):
        return nc.dram_tensor(name, list(shape), dt, kind="ExternalInput").ap()

    def dscr(name, shape, dt, ext=False):
        kind = "ExternalInput" if ext else "Internal"
        return nc.dram_tensor(name, list(shape), dt, kind=kind).ap()

    x = din("x", [L, D])
    norm_in_w = din("norm_in_w", [D])
    w_in = din("w_in", [D, INW])
    w_dn_out = din("w_dn_out", [D, D])
    w_hy_out = din("w_hy_out", [D, D])
    w_out = din("w_out", [D, D])
    norm_out_w = din("norm_out_w", [D])
    c_ident = din("c_ident", [128, 128])
    y = nc.dram_tensor("y", [HALF, D], F32, kind="ExternalOutput").ap()

    ext = cfg.get("ext_scratch", ())
    OG = dscr("OG", [D, HALF], BF16, "OG" in ext)
    YH = dscr("YH", [D, HALF], BF16, "YH" in ext)
    GS = dscr("GS", [2 * D, HALF], BF16, "GS" in ext)
    QS = dscr("QS", [D, HALF], BF16, "QS" in ext)
    KS = dscr("KS", [D, L], BF16, "KS" in ext)
    VS = dscr("VS", [D, L], BF16, "VS" in ext)
    ZS = dscr("ZS", [D, HALF], BF16, "ZS" in ext)
    BGS = dscr("BGS", [32, L], F32, "BGS" in ext)
    OBS = dscr("OBS", [HALF, D], F32, "OBS" in ext)
    US = dscr("US", [D, L], BF16, "US" in ext)
    G0S = dscr("G0S", [D, HALF], BF16, "G0S" in ext)
    KLS = dscr("KLS", [D, 97 * 128], BF16, "KLS" in ext)
    hy_w1 = din("hy_w1", [33, 64])
    hy_b1 = din("hy_b1", [64])
    hy_w2 = din("hy_w2", [64, 64])
    hy_b2 = din("hy_b2", [64])
    hy_freq = din("hy_freq", [64])
    hy_w3 = din("hy_w3", [64, 3, D])
    hy_log_decay = din("hy_log_decay", [3, D])
    hy_bias = din("hy_bias", [D])
    c_zpos = din("c_zpos", [33, 12800])
    c_dl = din("c_dl", [1, 512])
    c_tp0 = din("c_tp0", [25])
    c_e1 = din("c_e1", [97, 194])
    c_s2a = din("c_s2a", [128, 130])
    c_s2b = din("c_s2b", [128, 130])
    c_i1c = din("c_i1c", [97, 194])
    c_i1d = din("c_i1d", [97, 194])
    c_cw = din("c_cw", [65, 128])
    c_sw = din("c_sw", [65, 128])
    dn_conv_w = din("dn_conv_w", [3, 3 * D])
    hy_conv_w = din("hy_conv_w", [3, 3 * D])
    dn_a_log = din("dn_a_log", [16])
    dn_dt_bias = din("dn_dt_bias", [16])
    dn_norm_w = din("dn_norm_w", [128])
    c_sel = din("c_sel", [32, 2])
    c_tri = din("c_tri", [6, 128, 128])
    c_msk = din("c_msk", [3, 128, 128])

    dbg_out = {}

    def ddbg(name, shape, dt=F32):
        dbg_out[name] = nc.dram_tensor("dbg_" + name, list(shape), dt, kind="ExternalOutput").ap()
        return dbg_out[name]

    with ExitStack() as es:
        p = Prog(nc, es)
        cs = ExitStack()
        es.enter_context(cs)

        uniq = [0]

        def sb(stack, name, shape, dt):
            uniq[0] += 1
            return stack.enter_context(nc.sbuf_tensor(f"{name}_{uniq[0]}", list(shape), dt))

        def ps(stack, name, shape, dt=F32):
            uniq[0] += 1
            return stack.enter_context(nc.psum_tensor(f"{name}_{uniq[0]}", list(shape), dt))

        ident_f = sb(cs, "ident_f", [128, 128], F32)
        ident_b = sb(cs, "ident_b", [128, 128], BF16)
        nw_t = sb(cs, "nw_t", [128, 8], F32)
        p.dma('sp', ident_f[:], c_ident[:, :], writes=['ident_f'])
        p.op('dve', lambda e: e.tensor_copy(ident_b[:], ident_f[:]), reads=['ident_f'], writes=['ident_b'])
        p.dma('sp', nw_t[:], norm_in_w.rearrange("(k p) -> p k", p=128), writes=['nw_t'],
              allow_slow_non_contiguous=True)
        cwd = sb(cs, "cwd", [128, 24, 3], F32)
        cwh = sb(cs, "cwh", [128, 24, 3], F32)
        nwd = sb(cs, "nwd", [128, 1], F32)
        dtb = sb(cs, "dtb", [32, 1], F32)
        negA = sb(cs, "negA", [32, 1], F32)
        selt = sb(cs, "selt", [32, 2], F32)
        selb, selg = selt[:, 0:1], selt[:, 1:2]
        for j in range(3):
            p.dma('sp', cwd[:, :, j], dn_conv_w[j, :].rearrange("(g p) -> p g", p=128), writes=[('cwd', j)], allow_slow_non_contiguous=True)
            p.dma('sp', cwh[:, :, j], hy_conv_w[j, :].rearrange("(g p) -> p g", p=128), writes=[('cwh', j)], allow_slow_non_contiguous=True)
        p.dma('sp', nwd[:, :], dn_norm_w.rearrange("(p o) -> p o", o=1), writes=['nwd'], allow_slow_non_contiguous=True)
        p.op('pool', lambda e: e.memset(dtb[:], 0.0), writes=['dtb'])
        p.op('pool', lambda e: e.memset(negA[:], 0.0), writes=['negA'])
        p.dma('sp', dtb[16:32, :], dn_dt_bias.rearrange("(p o) -> p o", o=1), reads=[], writes=['dtb'], allow_slow_non_contiguous=True)
        p.dma('sp', negA[16:32, :], dn_a_log.rearrange("(p o) -> p o", o=1), writes=['negA'], allow_slow_non_contiguous=True)
        p.dma('sp', selt[:, :], c_sel[:, :], writes=['selb'])
        p.op('act', lambda e: e.activation(out=negA[:], in_=negA[:], func=AF.Exp), reads=['negA'], writes=['negA'])
        p.op('dve', lambda e: e.tensor_scalar(out=negA[:], in0=negA[:], scalar1=-1.0, scalar2=None, op0=ALU.mult), reads=['negA'], writes=['negA'])
        p.barrier()

        def build_hT(stk, hT, tok_base, pre_tok, post_tok, tag):
            with ExitStack() as ls:
                xt = [sb(ls, f"xt{tag}{i}", [128, D], F32) for i in range(2)]
                junk = sb(ls, f"junk{tag}", [128, D], BF16)
                xn = [sb(ls, f"xn{tag}{i}", [128, D], BF16) for i in range(2)]
                ssq = [sb(ls, f"ssq{tag}{i}", [128, 1], F32) for i in range(2)]
                pT = [ps(ls, f"pT{tag}{i}", [128, 8, 128], BF16) for i in range(2)]
                jobs = [(tok_base + 128 * i, 128, 1 + 128 * i) for i in range(HALF // 128)]
                for hc, tk in ((0, pre_tok), (HALF + 1, post_tok)):
                    if tk is None:
                        p.op('pool', lambda e, hc=hc: e.memset(hT[:, :, hc:hc + 1], 0.0), writes=[('hT', 'halo', hc)])
                    else:
                        jobs.append((tk, 1, hc))
                for ji, (t0, n, c0) in enumerate(jobs):
                    b = ji % 2
                    kx, kn, ks, kp = (f'xt{tag}', b), (f'xn{tag}', b), (f'ssq{tag}', b), (f'pT{tag}', b)
                    p.dma('sp', xt[b][0:n, :], x[t0:t0 + n, :], writes=[kx])
                    p.op('act', lambda e: e.activation(out=junk[0:n, :], in_=xt[b][0:n, :], func=AF.Square,
                                                       accum_out=ssq[b][0:n, :]), reads=[kx], writes=['junk' + tag, ks])
                    p.op('act', lambda e: e.activation(out=ssq[b][0:n, :], in_=ssq[b][0:n, :], func=AF.Ln, scale=1.0 / D, bias=EPS),
                         reads=[ks], writes=[ks])
                    p.op('act', lambda e: e.activation(out=ssq[b][0:n, :], in_=ssq[b][0:n, :], func=AF.Exp, scale=-0.5),
                         reads=[ks], writes=[ks])
                    p.op('dve', lambda e: e.tensor_scalar(out=xn[b][0:n, :], in0=xt[b][0:n, :], scalar1=ssq[b][0:n, :],
                                                          scalar2=None, op0=ALU.mult), reads=[kx, ks], writes=[kn])
                    for k in range(8):
                        p.op('pe', lambda e, k=k: e.transpose(pT[b][:, k, 0:n], xn[b][0:n, k * 128:(k + 1) * 128],
                                                              ident_b[0:n, 0:n]),
                             reads=[kn, 'ident_b'], writes=[kp])
                    key = ('hT', (c0 - 1) // 512) if n == 128 else ('hT', 'halo', c0)
                    p.op('dve', lambda e: e.tensor_tensor(out=hT[:, :, c0:c0 + n], in0=pT[b][:, :, 0:n],
                                                          in1=nw_t[:, :].unsqueeze(2).broadcast_to([128, 8, n]), op=ALU.mult),
                         reads=[kp, 'nw_t'], writes=[key])
                p.barrier()

        def hT_keys(nblk=8):
            return [('hT', i) for i in range(nblk)]

        wctr = [0]

        def load_w_group(wst, wbf, src_ap):
            b = wctr[0] % len(wst)
            wctr[0] += 1
            ncol = src_ap.shape[1]
            p.dma('sp', wst[b][:, :, 0:ncol], src_ap.rearrange("(k p) c -> p k c", p=128), writes=[('wst', b)])
            p.op('pool', lambda e: e.tensor_copy(wbf[b][:, :, 0:ncol], wst[b][:, :, 0:ncol]), reads=[('wst', b)],
                 writes=[('wbf', b, k) for k in range(8)])
            return b

        W = 2048
        NB = W // 512

        def run_rr(job_specs, make_gen, K, stagger=1):
            active = {}
            nxt_job = 0
            free = list(range(K))
            since = stagger
            while nxt_job < len(job_specs) or active:
                since += 1
                while free and nxt_job < len(job_specs) and since > stagger:
                    sl = free.pop(0)
                    active[sl] = make_gen(sl, job_specs[nxt_job])
                    nxt_job += 1
                    since = 0 if stagger > 0 else since
                for sl in sorted(active.keys()):
                    try:
                        next(active[sl])
                    except StopIteration:
                        del active[sl]
                        free.append(sl)

        def phase_A(own):
            tok_base = 0 if own else HALF
            KJ = 3
            with ExitStack() as s2:
                hT = sb(s2, "hT", [128, 8, HALF + 2], BF16)
                if own:
                    build_hT(s2, hT, 0, None, HALF, "o")
                else:
                    build_hT(s2, hT, HALF, HALF - 1, None, "x")
                wst = [sb(s2, f"wst{i}", [128, 8, 128], F32) for i in range(3)]
                wbf = [sb(s2, f"wbf{i}", [128, 8, 128], BF16) for i in range(3)]
                pp = [ps(s2, f"pp{i}", [128, 512], F32) for i in range(5)]
                pn = [ps(s2, f"pn{i}", [128, 512], F32) for i in range(3)]
                R = [sb(s2, f"R{i}", [128, W + 2], F32) for i in range(KJ)]
                acc = [sb(s2, f"acc{i}", [128, W], F32) for i in range(KJ)]
                sil = acc
                sqb = [sb(s2, f"sqb{i}", [128, W], BF16) for i in range(KJ)]
                rst = [sb(s2, f"rst{i}", [128, W], F32) for i in range(KJ)]
                ob = [sb(s2, f"ob{i}", [128, W], BF16) for i in range(KJ)]
                keep2 = [sb(s2, f"keep{i}", [128, W], F32) for i in range(2)]
                ones_b = sb(s2, "ones_b", [128, 128], BF16)
                p.op('pool', lambda e: e.memset(ones_b[:], 1.0), writes=['ones_b'])
                ctr = {'pp': 0, 'pn': 0}

                def nxt(nm, n):
                    v = ctr[nm] % n
                    ctr[nm] += 1
                    return v

                def Rkeys(ri):
                    return [('R', ri, bk) for bk in range(5)]

                BW = (W + 2) // 5

                def project(sl, wb, ncol, ps_i):
                    c0 = ps_i * W
                    for bk in range(5):
                        pi = nxt('pp', 5)
                        cs = c0 + bk * BW
                        for k in range(8):
                            p.op('pe', lambda e, k=k: e.matmul(pp[pi][0:ncol, 0:BW], wbf[wb][:, k, 0:ncol], hT[:, k, cs:cs + BW],
                                                               start=(k == 0), stop=(k == 7)),
                                 reads=[('wbf', wb, k)], writes=[('pp', pi)])
                        dst = R[sl][0:ncol, bk * BW:(bk + 1) * BW]
                        if bk % 2 == 0:
                            p.op('act', lambda e: e.activation(out=dst, in_=pp[pi][0:ncol, 0:BW], func=AF.Copy), reads=[('pp', pi)], writes=[('R', sl, bk)])
                        else:
                            p.op('dve', lambda e: e.tensor_copy(dst, pp[pi][0:ncol, 0:BW]), reads=[('pp', pi)], writes=[('R', sl, bk)])

                def conv(sl, cw, g):
                    p.op('act', lambda e: e.activation(out=acc[sl][:, :], in_=R[sl][:, 0:W], func=AF.Copy, scale=cw[:, g, 0:1]),
                         reads=Rkeys(sl), writes=[('acc', sl)])
                    for j in (1, 2):
                        p.op('dve', lambda e, j=j: e.scalar_tensor_tensor(out=acc[sl][:, :], in0=R[sl][:, j:j + W], scalar=cw[:, g, j:j + 1],
                                                                           in1=acc[sl][:, :], op0=ALU.mult, op1=ALU.add),
                             reads=Rkeys(sl) + [('acc', sl)], writes=[('acc', sl)])

                def store(dst_rows, sl, ps_i, key, eng):
                    c0 = tok_base + ps_i * W if dst_rows.shape[1] == L else ps_i * W
                    p.dma(eng, dst_rows[:, c0:c0 + W], ob[sl][:, :], reads=[('ob', sl)], writes=[key], key=('ob', sl))

                RST = lambda sl: [('rst', sl, bk) for bk in range(NB)]

                def job(sl, spec):
                    kind, wb, ps_i, idx = spec
                    if kind == 'gate':
                        for bk in range(NB):
                            pi = nxt('pp', 5)
                            cs = 1 + ps_i * W + bk * 512
                            for k in range(8):
                                p.op('pe', lambda e, k=k: e.matmul(pp[pi][:, :], wbf[wb][:, k, :], hT[:, k, cs:cs + 512], start=(k == 0), stop=(k == 7)),
                                     reads=[('wbf', wb, k)], writes=[('pp', pi)])
                            p.op('act', lambda e: e.activation(out=ob[sl][:, bk * 512:(bk + 1) * 512], in_=pp[pi][:, :], func=AF.Sigmoid),
                                 reads=[('pp', pi)], writes=[('ob', sl)])
                        store(GS[idx * 128:(idx + 1) * 128, :], sl, ps_i, ('GS', idx, ps_i), 'act')
                        return
                    ncol = 32 if kind == 'bg' else 128
                    project(sl, wb, ncol, ps_i)
                    yield
                    if kind == 'bg':
                        xin = R[sl][0:32, 1:W + 1]
                        rk = Rkeys(sl)
                        t0, t1, t2 = acc[sl][0:32, :], xin, rst[sl][0:32, :]
                        K0, K1, K2 = ('acc', sl), ('R', sl, 0), RST(sl)
                        p.op('act', lambda e: e.activation(out=t0, in_=xin, func=AF.Exp, scale=-1.0), reads=rk, writes=[K0])
                        p.op('dve', lambda e: e.tensor_scalar(out=t0, in0=t0, scalar1=1.0, scalar2=None, op0=ALU.add), reads=[K0], writes=[K0])
                        p.op('dve', lambda e: e.reciprocal(out=t0, in_=t0), reads=[K0], writes=[K0])
                        p.op('dve', lambda e: e.tensor_scalar(out=t1, in0=xin, scalar1=dtb[:, 0:1], scalar2=None, op0=ALU.add),
                             reads=rk + ['dtb'], writes=rk)
                        p.op('act', lambda e: e.activation(out=t2, in_=t1, func=AF.Abs), reads=[K1], writes=K2)
                        p.op('act', lambda e: e.activation(out=t2, in_=t2, func=AF.Exp, scale=-1.0), reads=K2, writes=K2)
                        p.op('act', lambda e: e.activation(out=t2, in_=t2, func=AF.Ln, bias=1.0), reads=K2, writes=K2)
                        p.op('dve', lambda e: e.scalar_tensor_tensor(out=t1, in0=t1, scalar=0.0, in1=t2, op0=ALU.max, op1=ALU.add),
                             reads=[K1] + K2, writes=[K1])
                        p.op('dve', lambda e: e.tensor_scalar(out=t1, in0=t1, scalar1=negA[:, 0:1], scalar2=selg[:, 0:1], op0=ALU.mult, op1=ALU.mult),
                             reads=[K1, 'negA', 'selb'], writes=[K1])
                        p.op('dve', lambda e: e.scalar_tensor_tensor(out=t2, in0=t0, scalar=selb[:, 0:1], in1=t1, op0=ALU.mult, op1=ALU.add),
                             reads=[K0, K1, 'selb'], writes=K2)
                        c0 = tok_base + ps_i * W
                        p.dma('pool', BGS[:, c0:c0 + W], t2, reads=K2, writes=[('BGS', own, ps_i)], key=('rst', sl))
                        return
                    if kind in ('z', 'hz'):
                        p.op('act', lambda e: e.activation(out=sil[sl][:, :], in_=R[sl][:, 1:W + 1], func=AF.Silu), reads=Rkeys(sl), writes=[('acc', sl)])
                        yield
                        if kind == 'z':
                            p.op('dve', lambda e: e.tensor_scalar(out=ob[sl][:, :], in0=sil[sl][:, :], scalar1=nwd[:, 0:1], scalar2=None, op0=ALU.mult),
                                 reads=[('acc', sl), 'nwd'], writes=[('ob', sl)])
                            store(ZS[idx * 128:(idx + 1) * 128, :], sl, ps_i, ('ZS', idx, ps_i), 'pool')
                        else:
                            p.op('pool', lambda e: e.tensor_tensor(out=ob[sl][:, :], in0=sil[sl][:, :], in1=keep2[ps_i][:, :], op=ALU.mult),
                                 reads=[('acc', sl), ('keep', ps_i)], writes=[('ob', sl)])
                            store(G0S[idx * 128:(idx + 1) * 128, :], sl, ps_i, ('G0S', idx, ps_i), 'pool')
                        return
                    if kind in ('q', 'k', 'v'):
                        gi = {'q': idx, 'k': NH + idx, 'v': 2 * NH + idx}[kind]
                        conv(sl, cwd, gi)
                        yield
                        if kind == 'v':
                            p.op('act', lambda e: e.activation(out=ob[sl][:, :], in_=acc[sl][:, :], func=AF.Silu), reads=[('acc', sl)], writes=[('ob', sl)])
                            store(VS[idx * 128:(idx + 1) * 128, :], sl, ps_i, ('VS', idx, own, ps_i), 'act')
                            return
                        p.op('act', lambda e: e.activation(out=sil[sl][:, :], in_=acc[sl][:, :], func=AF.Silu), reads=[('acc', sl)], writes=[('acc', sl)])
                        yield
                        p.op('act', lambda e: e.activation(out=sqb[sl][:, :], in_=sil[sl][:, :], func=AF.Square),
                             reads=[('acc', sl)], writes=[('sqb', sl)])
                        yield
                        for bk in range(NB):
                            ni = nxt('pn', 3)
                            p.op('pe', lambda e: e.matmul(pn[ni][:, :], ones_b[:, :], sqb[sl][:, bk * 512:(bk + 1) * 512], start=True, stop=True),
                                 reads=['ones_b', ('sqb', sl)], writes=[('pn', ni)])
                            p.op('act', lambda e: e.activation(out=rst[sl][:, bk * 512:(bk + 1) * 512], in_=pn[ni][:, :], func=AF.Ln, bias=EPS),
                                 reads=[('pn', ni)], writes=[('rst', sl, bk)])
                        scale = 128.0 ** -0.5 if kind == 'q' else 1.0
                        p.op('act', lambda e: e.activation(out=rst[sl][:, :], in_=rst[sl][:, :], func=AF.Exp, scale=-0.5, bias=float(np.log(scale))),
                             reads=RST(sl), writes=RST(sl))
                        yield
                        p.op('pool', lambda e: e.tensor_tensor(out=ob[sl][:, :], in0=sil[sl][:, :], in1=rst[sl][:, :], op=ALU.mult),
                             reads=[('acc', sl)] + RST(sl), writes=[('ob', sl)])
                        dstT = QS if kind == 'q' else KS
                        key = ('QS', idx, ps_i) if kind == 'q' else ('KS', idx, own, ps_i)
                        store(dstT[idx * 128:(idx + 1) * 128, :], sl, ps_i, key, 'pool')
                        return
                    gi = {'x0': idx, 'x1': 8 + idx, 'hv': 16 + idx}[kind]
                    conv(sl, cwh, gi)
                    yield
                    if kind in ('x0', 'x1'):
                        p.op('pool', lambda e: e.tensor_copy(keep2[ps_i][:, :], acc[sl][:, :]), reads=[('acc', sl)], writes=[('keep', ps_i)])
                    else:
                        p.op('pool', lambda e: e.tensor_tensor(out=ob[sl][:, :], in0=acc[sl][:, :], in1=keep2[ps_i][:, :], op=ALU.mult),
                             reads=[('acc', sl), ('keep', ps_i)], writes=[('ob', sl)])
                        store(US[idx * 128:(idx + 1) * 128, :], sl, ps_i, ('US', idx, own, ps_i), 'pool')

                groups = []
                for h in range(NH):
                    for kind in (('q', 'k', 'v', 'z') if own else ('k', 'v')):
                        gi = {'q': h, 'k': NH + h, 'v': 2 * NH + h}.get(kind)
                        groups.append((kind, C_QKV + gi * 128 if kind != 'z' else C_DNZ + h * 128, 128, h))
                groups.append(('bg', C_BETA, 32, 0))
                if "hyproj" in cfg["phases"]:
                    for cb in range(8):
                        for kind in (('x0', 'hz', 'x1', 'hv') if own else ('x1', 'hv')):
                            gi = {'x0': cb, 'x1': 8 + cb, 'hv': 16 + cb}.get(kind)
                            groups.append((kind, C_HXV + gi * 128 if kind != 'hz' else C_HZ + cb * 128, 128, cb))
                if own and "gates" in cfg["phases"]:
                    for gi in range(16):
                        groups.append(('gate', C_GATE + gi * 128, 128, gi))
                specs = []
                for (kind, col0, ncol, idx) in groups:
                    for ps_i in range(2):
                        specs.append((kind, col0, ncol, ps_i, idx))
                wb_of = {}

                gorder = [(g[0], g[3]) for g in groups]
                ginfo = {(g[0], g[3]): g for g in groups}

                def ensure_w(gk):
                    if gk not in wb_of:
                        kind, col0, ncol, idx = ginfo[gk]
                        wb_of[gk] = load_w_group(wst, wbf, w_in[:, col0:col0 + ncol])

                def make(sl, sp):
                    kind, col0, ncol, ps_i, idx = sp
                    ensure_w((kind, idx))
                    gi_ = gorder.index((kind, idx))
                    if ps_i == 0 and gi_ + 1 < len(gorder):
                        ensure_w(gorder[gi_ + 1])
                    return job(sl, (kind, wb_of[(kind, idx)], ps_i, idx))

                run_rr(specs, make, KJ, stagger=cfg.get('stagA', 0))
                p.barrier()

        if "projA" in cfg["phases"]:
            phase_A(False)
            phase_A(True)

        def phase_B():
            with ExitStack() as sB:
                NEG = -30000.0
                cst = sb(sB, "tricst", [128, 6, 128], F32)
                p.dma('sp', cst[:, :, :], c_tri.rearrange("c p f -> p c f"), writes=['tricst'])
                ones_f = sb(sB, "ones_f", [128, 128], F32)
                p.op('pool', lambda e: e.memset(ones_f[:], 1.0), writes=['ones_f'])
                mskf = sb(sB, "mskf", [128, 3, 128], F32)
                p.dma('sp', mskf[:, :, :], c_msk.rearrange("c p f -> p c f"), writes=['mskf'])
                m32h = sb(sB, "m32h", [128, 8, 128], BF16)
                mo64h = sb(sB, "mo64h", [128, 8, 128], BF16)
                mo128h = sb(sB, "mo128h", [128, 8, 128], BF16)
                identh = sb(sB, "identh", [128, 8, 128], BF16)
                for i_, t_ in enumerate((m32h, mo64h, mo128h)):
                    for h in range(NH):
                        p.op('dve', lambda e, i_=i_, t_=t_, h=h: e.tensor_copy(t_[:, h, :], mskf[:, i_, :]), reads=['mskf'], writes=['mskb'])
                for h in range(NH):
                    p.op('dve', lambda e, h=h: e.tensor_copy(identh[:, h, :], ident_f[:, :]), reads=['ident_f'], writes=['mskb'])
                tri = {0: cst[:, 0, :], 1: cst[:, 1, :]}
                nmd = {0: cst[:, 2, :], 1: cst[:, 4, :]}
                nme = {0: cst[:, 3, :], 1: cst[:, 5, :]}
                tt = sb(sB, "tt", [128, 32, 32], F32)
                Gp = [sb(sB, f"Gp{d}", [128, 32, 8], F32) for d in range(2)]
                Glb = sb(sB, "Glb", [128, 32, 16], F32)
                eG = [sb(sB, f"eG{d}", [128, 32, 8], F32) for d in range(2)]
                kd = [sb(sB, f"kd{d}", [128, 32, 8], F32) for d in range(2)]
                gam = sb(sB, "gam", [128, 32, 16], F32)
                bneg = [sb(sB, f"bneg{d}", [128, 32, 8], F32) for d in range(2)]
                beg = [sb(sB, f"beg{d}", [128, 32, 8], F32) for d in range(2)]
                PF = [ps(sB, f"PF{i}", [128, 8, 128], F32) for i in range(3)]
                PB = [ps(sB, f"PB{i}", [128, 8, 128], BF16) for i in range(2)]
                pfc = [0]
                pbc = [0]

                def npf():
                    v = pfc[0] % 3
                    pfc[0] += 1
                    return v

                def npb():
                    v = pbc[0] % 2
                    pbc[0] += 1
                    return v

                HB = [128, 8, 128]
                PW = []
                for i in range(2):
                    d_ = {}
                    for nm, dt in (("rhsG", F32), ("tmp", F32), ("E", F32), ("N0", BF16), ("N1", BF16),
                                   ("M0", BF16), ("M1", BF16), ("Q0", BF16), ("Q1", BF16), ("ktok", BF16),
                                   ("kc", BF16), ("vc", BF16), ("qc", BF16)):
                        d_[nm] = sb(sB, f"pw{i}_{nm}", HB, dt)
                    d_["eGr"] = d_["rhsG"]
                    d_["kbg"] = d_["N1"]
                    PW.append(d_)
                SW = []
                for i in range(3):
                    d_ = {}
                    for nm, dt in (("vb", F32), ("kbgT", BF16), ("kdec", BF16), ("attnT", BF16), ("qdT", BF16), ("Q", BF16)):
                        d_[nm] = sb(sB, f"sw{i}_{nm}", HB, dt)
                    SW.append(d_)
                S = sb(sB, "S", HB, F32)
                rb = sb(sB, "rb", HB, BF16)
                Sb = sb(sB, "Sb", HB, BF16)
                vnb = sb(sB, "vnb", HB, BF16)
                osum = sb(sB, "osum", HB, F32)
                oBt = sb(sB, "oBt", HB, F32)
                oss = sb(sB, "oss", [128, 8], F32)
                onb = sb(sB, "onb", HB, BF16)
                zsc = sb(sB, "zsc", HB, BF16)
                ogc = sb(sB, "ogc", HB, BF16)
                identb3 = ident_b[:, :].unsqueeze(1).broadcast_to(HB)

                def bc_j(ap2):
                    return ap2.unsqueeze(2).broadcast_to(HB)

                def bc_h(ap2):
                    return ap2.unsqueeze(1).broadcast_to(HB)

                def prep_half(own):
                    with ExitStack() as sh:
                        bgsb = sb(sh, "bgsb", [32, HALF], F32)
                        prep_half_(own, bgsb)

                def prep_half_(own, bgsb):
                    base = 0 if own else HALF
                    p.dma('sp', bgsb[:, :], BGS[:, base:base + HALF], reads=[('BGS', own, 0), ('BGS', own, 1)], writes=['bgsb'])
                    pf = PF[npf()]
                    pfv = pf[:, :, :].rearrange("p h j -> p (h j)")
                    for n in range(32):
                        p.op('pe', lambda e, n=n: e.transpose(pfv[:, n * 32:(n + 1) * 32], bgsb[0:32, n * 128:(n + 1) * 128], ident_f[0:32, 0:32]),
                             reads=['bgsb', 'ident_f'], writes=['PFx'])
                    p.op('dve', lambda e: e.tensor_copy(tt[:, :, :].rearrange("p c r -> p (c r)"), pfv), reads=['PFx'], writes=['tt'])
                    for d in range(2):
                        p.op('pe', lambda e, d=d: e.matmul(pfv[:, 0:256], tri[d], tt[:, :, 16 + 8 * d:24 + 8 * d], start=True, stop=True),
                             reads=['tt', 'tricst'], writes=['PFx'])
                        p.op('dve', lambda e, d=d: e.tensor_copy(Gp[d][:, :, :].rearrange("p c h -> p (c h)"), pfv[:, 0:256]),
                             reads=['PFx'], writes=[('Gp', d)])
                    p.op('pe', lambda e: e.matmul(pfv[:, 0:512], ones_f[:, :], tt[:, :, 16:32], start=True, stop=True),
                         reads=['tt', 'ones_f'], writes=['PFx'])
                    p.op('dve', lambda e: e.tensor_copy(Glb[:, :, :].rearrange("p c h -> p (c h)"), pfv[:, 0:512]), reads=['PFx'], writes=['Glb'])
                    p.op('act', lambda e: e.activation(out=gam[:, :, :], in_=Glb[:, :, :], func=AF.Exp), reads=['Glb'], writes=['gam'])
                    for d in range(2):
                        p.op('act', lambda e, d=d: e.activation(out=eG[d][:, :, :], in_=Gp[d][:, :, :], func=AF.Exp), reads=[('Gp', d)], writes=[('eG', d)])
                        p.op('dve', lambda e, d=d: e.tensor_tensor(out=kd[d][:, :, :], in0=Glb[:, :, 8 * d:8 * d + 8], in1=Gp[d][:, :, :], op=ALU.subtract),
                             reads=['Glb', ('Gp', d)], writes=[('kd', d)])
                        p.op('act', lambda e, d=d: e.activation(out=kd[d][:, :, :], in_=kd[d][:, :, :], func=AF.Exp), reads=[('kd', d)], writes=[('kd', d)])
                        p.op('dve', lambda e, d=d: e.tensor_scalar(out=bneg[d][:, :, :], in0=tt[:, :, 8 * d:8 * d + 8], scalar1=-1.0, scalar2=None, op0=ALU.mult),
                             reads=['tt'], writes=[('bneg', d)])
                        p.op('dve', lambda e, d=d: e.tensor_tensor(out=beg[d][:, :, :], in0=tt[:, :, 8 * d:8 * d + 8], in1=eG[d][:, :, :], op=ALU.mult),
                             reads=['tt', ('eG', d)], writes=[('beg', d)])
                    p.barrier()

                def prep_gen(u, n, d, own, need_out):
                    pw = PW[u % 2]
                    sw = SW[u % 3]
                    P_ = f"pw{u % 2}"
                    S_ = f"sw{u % 3}"
                    t0 = (0 if own else HALF) + n * 128
                    kq = [('KS', h, own, (n * 128) // W) for h in range(NH)]
                    kv = [('VS', h, own, (n * 128) // W) for h in range(NH)]
                    p.dma('sp', pw["kc"][:, :, :], KS[:, t0:t0 + 128].rearrange("(h d) t -> d h t", d=128), reads=kq, writes=[P_ + "kc"])
                    p.dma('sp', pw["vc"][:, :, :], VS[:, t0:t0 + 128].rearrange("(h d) t -> d h t", d=128), reads=kv, writes=[P_ + "vc"])
                    if need_out:
                        p.dma('sp', pw["qc"][:, :, :], QS[:, t0:t0 + 128].rearrange("(h d) t -> d h t", d=128),
                              reads=[('QS', h, (n * 128) // W) for h in range(NH)], writes=[P_ + "qc"])
                    p.op('dve', lambda e: e.tensor_tensor(out=pw["rhsG"][:, :, :], in0=bc_h(tri[d]), in1=bc_j(tt[:, n, 16 + 8 * d:24 + 8 * d]), op=ALU.mult),
                         reads=['tt', 'tricst'], writes=[P_ + "rhsG"])
                    yield
                    a = npf()
                    for hh in range(2):
                        p.op('pe', lambda e, hh=hh: e.matmul(PF[a][:, 4 * hh:4 * hh + 4, :], ones_f[:, :], pw["rhsG"][:, 4 * hh:4 * hh + 4, :], start=True, stop=True),
                             reads=[P_ + "rhsG", 'ones_f'], writes=[('PF', a)])
                    p.op('dve', lambda e: e.tensor_tensor(out=pw["tmp"][:, :, :], in0=PF[a][:, :, :], in1=bc_j(Gp[d][:, n, :]), op=ALU.subtract),
                         reads=[('PF', a), ('Gp', d)], writes=[P_ + "tmp"])
                    if need_out:
                        p.op('act', lambda e: e.activation(out=pw["eGr"][:, :, :], in_=PF[a][:, :, :], func=AF.Exp), reads=[('PF', a)], writes=[P_ + "rhsG"])
                    yield
                    if need_out:
                        p.op('pool', lambda e: e.tensor_tensor(out=pw["E"][:, :, :], in0=pw["tmp"][:, :, :], in1=bc_h(nme[d]), op=ALU.add),
                             reads=[P_ + "tmp", 'tricst'], writes=[P_ + "E"])
                        p.op('act', lambda e: e.activation(out=pw["E"][:, :, :], in_=pw["E"][:, :, :], func=AF.Exp), reads=[P_ + "E"], writes=[P_ + "E"])
                    p.op('pool', lambda e: e.tensor_tensor(out=pw["tmp"][:, :, :], in0=bc_h(nmd[d]), in1=pw["tmp"][:, :, :], op=ALU.subtract),
                         reads=[P_ + "tmp", 'tricst'], writes=[P_ + "tmp"])
                    p.op('act', lambda e: e.activation(out=pw["tmp"][:, :, :], in_=pw["tmp"][:, :, :], func=AF.Exp), reads=[P_ + "tmp"], writes=[P_ + "tmp"])
                    yield
                    a = npf()
                    for h in range(NH):
                        p.op('pe', lambda e, h=h: e.matmul(PF[a][:, h, :], pw["kc"][:, h, :], pw["kc"][:, h, :], start=True, stop=True),
                             reads=[P_ + "kc"], writes=[('PF', a)])
                    p.op('pool', lambda e: e.tensor_tensor(out=pw["tmp"][:, :, :], in0=pw["tmp"][:, :, :], in1=bc_j(bneg[d][:, n, :]), op=ALU.mult),
                         reads=[P_ + "tmp", ('bneg', d)], writes=[P_ + "tmp"])
                    p.op('dve', lambda e: e.tensor_tensor(out=pw["N0"][:, :, :], in0=PF[a][:, :, :], in1=pw["tmp"][:, :, :], op=ALU.mult),
                         reads=[('PF', a), P_ + "tmp"], writes=[P_ + "N0"])
                    yield
                    b = npb()
                    for h in range(NH):
                        p.op('pe', lambda e, h=h: e.transpose(PB[b][:, h, :], pw["N0"][:, h, :], ident_b[:, :]),
                             reads=[P_ + "N0", 'ident_b'], writes=[('PB', b)])
                    p.op('act', lambda e: e.activation(out=pw["M0"][:, :, :], in_=PB[b][:, :, :], func=AF.Copy), reads=[('PB', b)], writes=[P_ + "M0"])
                    yield
                    if need_out:
                        a = npf()
                        for h in range(NH):
                            p.op('pe', lambda e, h=h: e.matmul(PF[a][:, h, :], pw["kc"][:, h, :], pw["qc"][:, h, :], start=True, stop=True),
                                 reads=[P_ + "kc", P_ + "qc"], writes=[('PF', a)])
                        p.op('dve', lambda e: e.tensor_tensor(out=sw["attnT"][:, :, :], in0=PF[a][:, :, :], in1=pw["E"][:, :, :], op=ALU.mult),
                             reads=[('PF', a), P_ + "E"], writes=[S_ + "attnT"])
                        p.op('pool', lambda e: e.tensor_tensor(out=sw["qdT"][:, :, :], in0=pw["qc"][:, :, :], in1=pw["eGr"][:, :, :], op=ALU.mult),
                             reads=[P_ + "qc", P_ + "rhsG"], writes=[S_ + "qdT"])
                        yield
                    def mmg(dst_ps, lk, rk, lkey, rkey):
                        for h in range(NH):
                            p.op('pe', lambda e, h=h: e.matmul(PF[dst_ps][:, h, :], lk[:, h, :], rk[:, h, :], start=True, stop=True),
                                 reads=[lkey, rkey], writes=[('PF', dst_ps)])
                    N_, M_, T_, W_ = pw["N0"], pw["M0"], pw["Q0"], pw["Q1"]
                    kN, kM, kT, kW = P_ + "N0", P_ + "M0", P_ + "Q0", P_ + "Q1"
                    No1, Mo1, No2 = pw["N1"], pw["M1"], pw["ktok"]
                    kNo1, kMo1, kNo2 = P_ + "N1", P_ + "M1", P_ + "ktok"
                    p.op('dve', lambda e: e.tensor_tensor(out=No1[:, :, :], in0=N_[:, :, :], in1=mo64h[:, :, :], op=ALU.mult), reads=[kN, 'mskb'], writes=[kNo1])
                    p.op('dve', lambda e: e.tensor_tensor(out=Mo1[:, :, :], in0=M_[:, :, :], in1=mo64h[:, :, :], op=ALU.mult), reads=[kM, 'mskb'], writes=[kMo1])
                    p.op('dve', lambda e: e.tensor_tensor(out=No2[:, :, :], in0=N_[:, :, :], in1=mo128h[:, :, :], op=ALU.mult), reads=[kN, 'mskb'], writes=[kNo2])
                    p.op('dve', lambda e: e.tensor_tensor(out=N_[:, :, :], in0=N_[:, :, :], in1=m32h[:, :, :], op=ALU.mult), reads=[kN, 'mskb'], writes=[kN])
                    p.op('dve', lambda e: e.tensor_tensor(out=M_[:, :, :], in0=M_[:, :, :], in1=m32h[:, :, :], op=ALU.mult), reads=[kM, 'mskb'], writes=[kM])
                    p.op('dve', lambda e: e.tensor_tensor(out=T_[:, :, :], in0=N_[:, :, :], in1=identh[:, :, :], op=ALU.add), reads=[kN, 'mskb'], writes=[kT])
                    p.op('dve', lambda e: e.tensor_tensor(out=W_[:, :, :], in0=M_[:, :, :], in1=identh[:, :, :], op=ALU.add), reads=[kM, 'mskb'], writes=[kW])
                    yield
                    for lvl in range(1, 5):
                        a1, a2 = npf(), npf()
                        mmg(a1, M_, N_, kM, kN)
                        mmg(a2, N_, M_, kN, kM)
                        p.op('act', lambda e: e.activation(out=N_[:, :, :], in_=PF[a1][:, :, :], func=AF.Copy), reads=[('PF', a1)], writes=[kN])
                        p.op('act', lambda e: e.activation(out=M_[:, :, :], in_=PF[a2][:, :, :], func=AF.Copy), reads=[('PF', a2)], writes=[kM])
                        yield
                        a1, a2 = npf(), npf()
                        mmg(a1, M_, T_, kM, kT)
                        mmg(a2, N_, W_, kN, kW)
                        p.op('dve', lambda e: e.tensor_tensor(out=T_[:, :, :], in0=PF[a1][:, :, :], in1=T_[:, :, :], op=ALU.add), reads=[('PF', a1), kT], writes=[kT])
                        p.op('dve', lambda e: e.tensor_tensor(out=W_[:, :, :], in0=PF[a2][:, :, :], in1=W_[:, :, :], op=ALU.add), reads=[('PF', a2), kW], writes=[kW])
                        yield
                    a1, a2 = npf(), npf()
                    mmg(a1, Mo1, T_, kMo1, kT)
                    mmg(a2, No1, W_, kNo1, kW)
                    p.op('act', lambda e: e.activation(out=N_[:, :, :], in_=PF[a1][:, :, :], func=AF.Copy), reads=[('PF', a1)], writes=[kN])
                    p.op('dve', lambda e: e.tensor_copy(M_[:, :, :], PF[a2][:, :, :]), reads=[('PF', a2)], writes=[kM])
                    yield
                    a1, a2 = npf(), npf()
                    mmg(a1, W_, N_, kW, kN)
                    mmg(a2, T_, M_, kT, kM)
                    p.op('dve', lambda e: e.tensor_tensor(out=T_[:, :, :], in0=PF[a1][:, :, :], in1=T_[:, :, :], op=ALU.add), reads=[('PF', a1), kT], writes=[kT])
                    p.op('dve', lambda e: e.tensor_tensor(out=W_[:, :, :], in0=PF[a2][:, :, :], in1=W_[:, :, :], op=ALU.add), reads=[('PF', a2), kW], writes=[kW])
                    yield
                    a1 = npf()
                    mmg(a1, No2, W_, kNo2, kW)
                    p.op('act', lambda e: e.activation(out=M_[:, :, :], in_=PF[a1][:, :, :], func=AF.Copy), reads=[('PF', a1)], writes=[kM])
                    yield
                    a1 = npf()
                    mmg(a1, T_, M_, kT, kM)
                    p.op('dve', lambda e: e.tensor_tensor(out=sw["Q"][:, :, :], in0=PF[a1][:, :, :], in1=W_[:, :, :], op=ALU.add), reads=[('PF', a1), kW], writes=[S_ + "Q"])
                    yield

                    b = npb()
                    for h in range(NH):
                        p.op('pe', lambda e, h=h: e.transpose(PB[b][:, h, :], pw["kc"][:, h, :], ident_b[:, :]),
                             reads=[P_ + "kc", 'ident_b'], writes=[('PB', b)])
                    p.op('act', lambda e: e.activation(out=pw["ktok"][:, :, :], in_=PB[b][:, :, :], func=AF.Copy), reads=[('PB', b)], writes=[P_ + "ktok"])
                    p.op('pool', lambda e: e.tensor_tensor(out=pw["kbg"][:, :, :], in0=pw["ktok"][:, :, :], in1=bc_j(beg[d][:, n, :]), op=ALU.mult),
                         reads=[P_ + "ktok", ('beg', d)], writes=[P_ + "N1"])
                    p.op('pool', lambda e: e.tensor_tensor(out=sw["kdec"][:, :, :], in0=pw["ktok"][:, :, :], in1=bc_j(kd[d][:, n, :]), op=ALU.mult),
                         reads=[P_ + "ktok", ('kd', d)], writes=[S_ + "kdec"])
                    yield
                    b = npb()
                    for h in range(NH):
                        p.op('pe', lambda e, h=h: e.transpose(PB[b][:, h, :], pw["kbg"][:, h, :], ident_b[:, :]),
                             reads=[P_ + "N1", 'ident_b'], writes=[('PB', b)])
                    p.op('act', lambda e: e.activation(out=sw["kbgT"][:, :, :], in_=PB[b][:, :, :], func=AF.Copy), reads=[('PB', b)], writes=[S_ + "kbgT"])
                    yield
                    b = npb()
                    for h in range(NH):
                        p.op('pe', lambda e, h=h: e.transpose(PB[b][:, h, :], pw["vc"][:, h, :], ident_b[:, :]),
                             reads=[P_ + "vc", 'ident_b'], writes=[('PB', b)])
                    p.op('dve', lambda e: e.tensor_tensor(out=sw["vb"][:, :, :], in0=PB[b][:, :, :], in1=bc_j(tt[:, n, 8 * d:8 * d + 8]), op=ALU.mult),
                         reads=[('PB', b), 'tt'], writes=[S_ + "vb"])
                    yield

                def seq_gen(u, n, d, own, need_out, final_dir):
                    sw = SW[u % 3]
                    S_ = f"sw{u % 3}"
                    r0 = n * 128
                    if need_out and final_dir:
                        p.dma('sp', oBt[:, :, :].rearrange("p h e -> p (h e)"), OBS[r0:r0 + 128, :], reads=[('OBS', n)], writes=['oBt'])
                        p.dma('sp', zsc[:, :, :], ZS[:, r0:r0 + 128].rearrange("(h d) t -> d h t", d=128),
                              reads=[('ZS', h, r0 // W) for h in range(NH)], writes=['zsc'])
                    a = npf()
                    for h in range(NH):
                        p.op('pe', lambda e, h=h: e.matmul(PF[a][:, h, :], sw["kbgT"][:, h, :], Sb[:, h, :], start=True, stop=True),
                             reads=[S_ + "kbgT", 'Sb'], writes=[('PF', a)])
                    p.op('dve', lambda e: e.tensor_tensor(out=rb[:, :, :], in0=sw["vb"][:, :, :], in1=PF[a][:, :, :], op=ALU.subtract),
                         reads=[('PF', a), S_ + "vb"], writes=['rb'])
                    yield
                    a = npf()
                    for h in range(NH):
                        p.op('pe', lambda e, h=h: e.matmul(PF[a][:, h, :], sw["Q"][:, h, :], rb[:, h, :], start=True, stop=True),
                             reads=[S_ + "Q", 'rb'], writes=[('PF', a)])
                    p.op('act', lambda e: e.activation(out=vnb[:, :, :], in_=PF[a][:, :, :], func=AF.Copy), reads=[('PF', a)], writes=['vnb'])
                    yield
                    if need_out:
                        ao = npf()
                        for h in range(NH):
                            p.op('pe', lambda e, h=h: e.matmul(PF[ao][:, h, :], sw["qdT"][:, h, :], Sb[:, h, :], start=True, stop=False),
                                 reads=[S_ + "qdT", 'Sb'], writes=[('PF', ao)])
                            p.op('pe', lambda e, h=h: e.matmul(PF[ao][:, h, :], sw["attnT"][:, h, :], vnb[:, h, :], start=False, stop=True),
                                 reads=[S_ + "attnT", 'vnb'], writes=[('PF', ao)])
                    a = npf()
                    for h in range(NH):
                        p.op('pe', lambda e, h=h: e.matmul(PF[a][:, h, :], sw["kdec"][:, h, :], vnb[:, h, :], start=True, stop=True),
                             reads=[S_ + "kdec", 'vnb'], writes=[('PF', a)])
                    p.op('pool', lambda e: e.tensor_tensor(out=S[:, :, :], in0=S[:, :, :], in1=bc_j(gam[:, n, 8 * d:8 * d + 8]), op=ALU.mult),
                         reads=['S', 'gam'], writes=['S'])
                    p.op('dve', lambda e: e.tensor_tensor(out=S[:, :, :], in0=S[:, :, :], in1=PF[a][:, :, :], op=ALU.add),
                         reads=['S', ('PF', a)], writes=['S'])
                    p.op('act', lambda e: e.activation(out=Sb[:, :, :], in_=S[:, :, :], func=AF.Copy), reads=['S'], writes=['Sb'])
                    if need_out:
                        if not final_dir:
                            p.op('act', lambda e: e.activation(out=osum[:, :, :], in_=PF[ao][:, :, :], func=AF.Copy), reads=[('PF', ao)], writes=['osum'])
                        else:
                            p.op('dve', lambda e: e.tensor_tensor(out=osum[:, :, :], in0=PF[ao][:, :, :], in1=oBt[:, :, :], op=ALU.add),
                                 reads=[('PF', ao), 'oBt'], writes=['osum'])
                    yield
                    if need_out:
                        if not final_dir:
                            p.dma('act', OBS[r0:r0 + 128, :], osum[:, :, :].rearrange("p h e -> p (h e)"), reads=['osum'], writes=[('OBS', n)], key='osum')
                        else:
                            p.op('pool', lambda e: e.tensor_tensor(out=oBt[:, :, :], in0=osum[:, :, :], in1=osum[:, :, :], op=ALU.mult),
                                 reads=['osum'], writes=['oBt'])
                            p.op('dve', lambda e: e.tensor_reduce(out=oss[:, :], in_=oBt[:, :, :], axis=AX.X, op=ALU.add), reads=['oBt'], writes=['oss'])
                            p.op('act', lambda e: e.activation(out=oss[:, :], in_=oss[:, :], func=AF.Ln, scale=1.0 / 128, bias=EPS), reads=['oss'], writes=['oss'])
                            p.op('act', lambda e: e.activation(out=oss[:, :], in_=oss[:, :], func=AF.Exp, scale=-0.5), reads=['oss'], writes=['oss'])
                            p.op('pool', lambda e: e.tensor_tensor(out=onb[:, :, :], in0=osum[:, :, :], in1=bc_j(oss[:, :]), op=ALU.mult),
                                 reads=['osum', 'oss'], writes=['onb'])
                            b = npb()
                            for h in range(NH):
                                p.op('pe', lambda e, h=h: e.transpose(PB[b][:, h, :], onb[:, h, :], ident_b[:, :]),
                                     reads=['onb', 'ident_b'], writes=[('PB', b)])
                            p.op('dve', lambda e: e.tensor_tensor(out=ogc[:, :, :], in0=PB[b][:, :, :], in1=zsc[:, :, :], op=ALU.mult),
                                 reads=[('PB', b), 'zsc'], writes=['ogc'])
                            p.dma('sp', OG[:, r0:r0 + 128].rearrange("(h d) t -> d h t", d=128), ogc[:, :, :], reads=['ogc'], writes=['OGall'], key='ogc')
                        yield

                def run_units(units):
                    preps = {}
                    done_prep = set()
                    nxt_prep = 0
                    cur_seq = None
                    cur_u = 0
                    nun = len(units)
                    while cur_u < nun:
                        while nxt_prep < nun and nxt_prep <= cur_u + 2 and len(preps) < cfg.get('dn_par', 2) and (nxt_prep - 2) not in preps:
                            n, d, own, no, fd = units[nxt_prep]
                            preps[nxt_prep] = prep_gen(nxt_prep, n, d, own, no)
                            nxt_prep += 1
                        if cur_seq is None and cur_u in done_prep:
                            n, d, own, no, fd = units[cur_u]
                            cur_seq = seq_gen(cur_u, n, d, own, no, fd)
                        progressed = False
                        if cur_seq is not None:
                            try:
                                next(cur_seq)
                            except StopIteration:
                                cur_seq = None
                                cur_u += 1
                            progressed = True
                        for uu in sorted(list(preps.keys())):
                            try:
                                next(preps[uu])
                            except StopIteration:
                                del preps[uu]
                                done_prep.add(uu)
                            progressed = True
                        assert progressed or cur_u >= nun

                p.op('pool', lambda e: e.memset(S[:, :, :], 0.0), writes=['S'])
                p.op('pool', lambda e: e.memset(Sb[:, :, :], 0.0), writes=['Sb'])
                nck = cfg.get("nchunks", 32)
                if cfg.get("dn_test") == "A":
                    prep_half(True)
                    if "prep_steps" in cfg:
                        g = prep_gen(0, 0, 0, True, True)
                        for _ in range(cfg["prep_steps"]):
                            next(g)
                        p.barrier()
                        return
                    run_units([(n, 0, True, True, False) for n in range(nck)])
                    p.barrier()
                    return
                prep_half(False)
                run_units([(n, 1, False, False, False) for n in range(nck - 1, -1, -1)])
                p.barrier()
                prep_half(True)
                run_units([(n, 1, True, True, False) for n in range(nck - 1, -1, -1)])
                p.barrier()
                p.op('pool', lambda e: e.memset(S[:, :, :], 0.0), writes=['S'])
                p.op('pool', lambda e: e.memset(Sb[:, :, :], 0.0), writes=['Sb'])
                run_units([(n, 0, True, True, True) for n in range(nck)])
                p.barrier()

        if "dn" in cfg["phases"]:
            phase_B()

        N1, N2, NF = 97, 128, 97 * 128
        NEXT = 24608
        NPAD = 12800

        def phase_C1():
            with ExitStack() as sC:
                hd2 = sb(sC, "hd2", [64, NPAD], F32)
                w1t = sb(sC, "w1t", [33, 64], F32)
                w2t = sb(sC, "w2t", [64, 64], F32)
                w3t = sb(sC, "w3t", [64, 3, D], F32)
                frt = sb(sC, "frt", [64, 1], F32)
                fb1 = sb(sC, "fb1", [64, 1], F32)
                fb2 = sb(sC, "fb2", [64, 1], F32)
                ldt = sb(sC, "ldt", [128, 3, 8], F32)
                rate = sb(sC, "rate", [128, 3, 8], F32)
                nrate = sb(sC, "nrate", [128, 3, 8], F32)
                dl = sb(sC, "dl", [128, 512], F32)
                tp0 = sb(sC, "tp0", [128, 25], F32)
                bq = sb(sC, "bq", [128, 25], F32)
                zp = [sb(sC, f"zp{i}", [33, 512], F32) for i in range(2)]
                arg = [sb(sC, f"arg{i}", [64, 512], F32) for i in range(2)]
                kint = [sb(sC, f"kint{i}", [64, 512], mybir.dt.int32) for i in range(2)]
                kf = [sb(sC, f"kf{i}", [64, 512], F32) for i in range(2)]
                h1 = [sb(sC, f"h1_{i}", [64, 512], F32) for i in range(2)]
                win = [sb(sC, f"win{i}", [128, 512], F32) for i in range(2)]
                kl = [sb(sC, f"kl{i}", [128, NPAD], BF16) for i in range(2)]
                pm = [ps(sC, f"pm{i}", [128, 512], F32) for i in range(4)]
                PI = float(np.pi)
                p.dma('sp', w1t[:, :], hy_w1[:, :], writes=['w1t'])
                p.dma('sp', w2t[:, :], hy_w2[:, :], writes=['w2t'])
                p.dma('sp', w3t[:, :, :], hy_w3[:, :, :], writes=['w3t'])
                p.dma('sp', frt[:, :], hy_freq.rearrange("(p o) -> p o", o=1), writes=['frt'], allow_slow_non_contiguous=True)
                p.dma('sp', fb1[:, :], hy_b1.rearrange("(p o) -> p o", o=1), writes=['fb1'], allow_slow_non_contiguous=True)
                p.dma('sp', fb2[:, :], hy_b2.rearrange("(p o) -> p o", o=1), writes=['fb2'], allow_slow_non_contiguous=True)
                for s_ in range(3):
                    p.dma('sp', ldt[:, s_, :], hy_log_decay[s_, :].rearrange("(b p) -> p b", p=128), writes=[('ldt', s_)], allow_slow_non_contiguous=True)
                p.dma('sp', dl[:, :], c_dl[0, :].partition_broadcast(128), writes=['dl'])
                p.dma('sp', tp0[:, :], c_tp0.partition_broadcast(128), writes=['tp0'])
                p.op('dve', lambda e: e.tensor_tensor(out=fb1[:, :], in0=fb1[:, :], in1=frt[:, :], op=ALU.mult), reads=['fb1', 'frt'], writes=['fb1'])
                p.op('dve', lambda e: e.tensor_tensor(out=fb2[:, :], in0=fb2[:, :], in1=frt[:, :], op=ALU.mult), reads=['fb2', 'frt'], writes=['fb2'])
                p.op('act', lambda e: e.activation(out=rate[:, :, :], in_=ldt[:, :, :], func=AF.Exp), reads=[('ldt', i) for i in range(3)], writes=['rate'])
                p.op('dve', lambda e: e.tensor_scalar(out=nrate[:, :, :], in0=rate[:, :, :], scalar1=-1.0, scalar2=None, op0=ALU.mult), reads=['rate'], writes=['nrate'])
                pmc = [0]

                def npm():
                    v = pmc[0] % 4
                    pmc[0] += 1
                    return v

                def sin_layer(src_ps, src_key, fbias, fkey, dst, dst_keys, i):
                    p.op('dve', lambda e: e.tensor_scalar(out=arg[i][:, :], in0=src_ps, scalar1=frt[:, 0:1], scalar2=fbias[:, 0:1], op0=ALU.mult, op1=ALU.add),
                         reads=['frt', fkey, src_key], writes=[('arg', i)])
                    p.op('dve', lambda e: e.tensor_scalar(out=kint[i][:, :], in0=arg[i][:, :], scalar1=1.0 / (2 * PI), scalar2=64.0, op0=ALU.mult, op1=ALU.add),
                         reads=[('arg', i)], writes=[('kint', i)])
                    p.op('dve', lambda e: e.tensor_scalar(out=kf[i][:, :], in0=kint[i][:, :], scalar1=-64.0, scalar2=None, op0=ALU.add),
                         reads=[('kint', i)], writes=[('kf', i)])
                    p.op('dve', lambda e: e.scalar_tensor_tensor(out=arg[i][:, :], in0=kf[i][:, :], scalar=-2 * PI, in1=arg[i][:, :], op0=ALU.mult, op1=ALU.add),
                         reads=[('kf', i), ('arg', i)], writes=[('arg', i)])
                    p.op('act', lambda e: e.activation(out=dst, in_=arg[i][:, :], func=AF.Sin), reads=[('arg', i)], writes=dst_keys)

                for q in range(25):
                    i = q % 2
                    p.dma('sp', zp[i][:, :], c_zpos[:, q * 512:(q + 1) * 512], writes=[('zp', i)])
                    a = npm()
                    p.op('pe', lambda e: e.matmul(pm[a][0:64, :], w1t[:, :], zp[i][:, :], start=True, stop=True), reads=['w1t', ('zp', i)], writes=[('pm', a)])
                    sin_layer(pm[a][0:64, :], ('pm', a), fb1, 'fb1', h1[i][:, :], [('h1', i)], i)
                    a = npm()
                    p.op('pe', lambda e: e.matmul(pm[a][0:64, :], w2t[:, :], h1[i][:, :], start=True, stop=True), reads=['w2t', ('h1', i)], writes=[('pm', a)])
                    sin_layer(pm[a][0:64, :], ('pm', a), fb2, 'fb2', hd2[:, q * 512:(q + 1) * 512], [('hd2', q)], i)
                for cb in range(8):
                    kb = cb % 2
                    p.op('dve', lambda e: e.tensor_scalar(out=bq[:, 0:8], in0=tp0[:, 0:8], scalar1=nrate[:, 0, cb:cb + 1], scalar2=None, op0=ALU.mult),
                         reads=['tp0', 'nrate'], writes=['bq'])
                    p.op('dve', lambda e: e.tensor_scalar(out=bq[:, 8:25], in0=tp0[:, 8:25], scalar1=nrate[:, 1, cb:cb + 1], scalar2=None, op0=ALU.mult),
                         reads=['tp0', 'nrate'], writes=['bq'])
                    for q in range(25):
                        st = 0 if q < 8 else 1
                        a = npm()
                        wi = q % 2
                        p.op('pe', lambda e: e.matmul(pm[a][:, :], w3t[:, st, cb * 128:(cb + 1) * 128], hd2[:, q * 512:(q + 1) * 512], start=True, stop=True),
                             reads=['w3t', ('hd2', q)], writes=[('pm', a)])
                        sc = nrate[:, 0, cb:cb + 1] if q < 8 else rate[:, 1, cb:cb + 1]
                        p.op('act', lambda e: e.activation(out=win[wi][:, :], in_=dl[:, :], func=AF.Exp, scale=sc, bias=bq[:, q:q + 1]),
                             reads=['dl', 'rate', 'nrate', 'bq'], writes=[('win', wi)])
                        p.op('dve', lambda e: e.tensor_tensor(out=kl[kb][:, q * 512:(q + 1) * 512], in0=pm[a][:, :], in1=win[wi][:, :], op=ALU.mult),
                             reads=[('pm', a), ('win', wi)], writes=[('kl', kb, q)])
                    a = npm()
                    p.op('pe', lambda e: e.matmul(pm[a][:, 0:1], w3t[:, 2, cb * 128:(cb + 1) * 128], hd2[:, 0:1], start=True, stop=True),
                         reads=['w3t', ('hd2', 0)], writes=[('pm', a)])
                    p.op('dve', lambda e: e.tensor_copy(kl[kb][:, 0:1], pm[a][:, 0:1]), reads=[('pm', a)], writes=[('kl', kb, 0)])
                    p.op('pool', lambda e: e.memset(kl[kb][:, HALF:HALF + 129], 0.0), writes=[('kl', kb, 8)])
                    p.dma('sp', KLS[cb * 128:(cb + 1) * 128, :], kl[kb][:, 0:NF], reads=[('kl', kb, q) for q in range(25)], writes=[('KLS', cb)], key=('kl', kb))
                p.barrier()

        def phase_C2():
            KP = 4
            with ExitStack() as sC:
                EXT = sb(sC, "EXT", [128, NEXT], BF16)
                Xt = sb(sC, "Xt", [128, N2, 128], BF16)
                Kr = sb(sC, "Kr", [128, 128, 65], BF16)
                Ki = sb(sC, "Ki", [128, 128, 65], BF16)
                nKi = sb(sC, "nKi", [128, 128, 65], BF16)
                YE = sb(sC, "YE", [128, HALF], BF16)
                G0 = sb(sC, "G0", [128, HALF], BF16)
                ub = sb(sC, "ub", [128, HALF], BF16)
                yo = sb(sC, "yo", [128, HALF], F32)
                yhb = sb(sC, "yhb", [128, HALF], BF16)
                hbt = sb(sC, "hbt", [128, 8], F32)
                p.dma('sp', hbt[:, :], hy_bias.rearrange("(b p) -> p b", p=128), writes=['hbt'], allow_slow_non_contiguous=True)
                mats = {}
                stg = sb(sC, "mstg", [128, 194], F32)
                for nm, src, r, c in (("e1", c_e1, 97, 194), ("s2a", c_s2a, 128, 130), ("s2b", c_s2b, 128, 130),
                                      ("i1c", c_i1c, 97, 194), ("i1d", c_i1d, 97, 194), ("cw", c_cw, 65, 128), ("sw", c_sw, 65, 128)):
                    t_ = sb(sC, "m_" + nm, [128, c], BF16)
                    p.dma('sp', stg[0:r, 0:c], src[:, :], writes=['mstg'])
                    p.op('dve', lambda e: e.tensor_copy(t_[0:r, :], stg[0:r, 0:c]), reads=['mstg'], writes=['m_' + nm])
                    mats[nm] = t_
                Y1 = [sb(sC, f"Y1_{i}", [128, 2, 194], BF16) for i in range(KP)]
                Zs = [sb(sC, f"Zs{i}", [128, 2, 130], BF16) for i in range(KP)]
                Zt = [sb(sC, f"Zt{i}", [128, 2, 130], F32) for i in range(KP)]
                Zu = [sb(sC, f"Zu{i}", [128, 2, 130], F32) for i in range(KP)]
                Vs = [sb(sC, f"Vs{i}", [128, 2, 194], BF16) for i in range(KP)]
                Yc = [sb(sC, f"Yc{i}", [128, 8, N1], BF16) for i in range(2)]
                KP = 4
                PA_ = [ps(sC, f"PA_{i}", [128, 512], F32) for i in range(KP)]
                PB_ = [ps(sC, f"PB_{i}", [128, 512], F32) for i in range(KP)]
                P1 = PA_
                cnt = {'pr': 0, 'tp': 0}

                def to_Xt():
                    for g in range(N2 // 8):
                        b = cnt['tp'] % 4
                        cnt['tp'] += 1
                        pt = P1[b][:, :].bitcast(BF16).rearrange("p (a c) -> p a c", c=128)
                        for a in range(8):
                            t2 = g * 8 + a
                            p.op('pe', lambda e, a=a, t2=t2: e.transpose(pt[0:N1, a, :], EXT[:, 97 * t2:97 * t2 + 128 * (N1 - 1) + 1:128], ident_b[:, :]),
                                 reads=['EXT', 'ident_b'], writes=[('P1', b)])
                        eng = 'act' if g % 2 == 0 else 'dve'
                        if eng == 'act':
                            p.op('act', lambda e: e.activation(out=Xt[0:N1, g * 8:(g + 1) * 8, :], in_=pt[0:N1, 0:8, :], func=AF.Copy),
                                 reads=[('P1', b)], writes=[('Xt', g)])
                        else:
                            p.op('dve', lambda e: e.tensor_copy(Xt[0:N1, g * 8:(g + 1) * 8, :], pt[0:N1, 0:8, :]), reads=[('P1', b)], writes=[('Xt', g)])

                XtK = [('Xt', g) for g in range(N2 // 8)]
                XtC = [('Xtc', c0) for c0 in range(0, 128, 2)]

                def pair_gen(i, spec):
                    c0, is_filter = spec
                    kA, kB = ('P1', i), ('P2', i)
                    p1 = PA_[i][:, 0:388].rearrange("p (a f) -> p a f", f=194)
                    p2 = PB_[i][:, 0:260].rearrange("p (a f) -> p a f", f=130)
                    for a in range(2):
                        p.op('pe', lambda e, a=a: e.matmul(p1[:, a, :], Xt[0:N1, :, c0 + a], mats["e1"][0:N1, :], start=True, stop=True),
                             reads=XtK + [('Xtc', c0), 'm_e1'], writes=[kA])
                    p.op('act', lambda e: e.activation(out=Y1[i][:, :, :], in_=p1, func=AF.Copy), reads=[kA], writes=[('Y1', i)])
                    yield
                    for a in range(2):
                        p.op('pe', lambda e, a=a: e.matmul(p2[0:N1, a, :], Y1[i][:, a, 0:97], mats["s2a"][:, :], start=True, stop=False),
                             reads=[('Y1', i), 'm_s2a'], writes=[kB])
                        p.op('pe', lambda e, a=a: e.matmul(p2[0:N1, a, :], Y1[i][:, a, 97:194], mats["s2b"][:, :], start=False, stop=True),
                             reads=[('Y1', i), 'm_s2b'], writes=[kB])
                    if is_filter:
                        p.op('act', lambda e: e.activation(out=Kr[0:N1, c0:c0 + 2, :], in_=p2[0:N1, :, 0:65], func=AF.Copy), reads=[kB], writes=[('K', c0)])
                        p.op('dve', lambda e: e.tensor_copy(Ki[0:N1, c0:c0 + 2, :], p2[0:N1, :, 65:130]), reads=[kB], writes=[('K', c0)])
                        p.op('dve', lambda e: e.tensor_scalar(out=nKi[0:N1, c0:c0 + 2, :], in0=p2[0:N1, :, 65:130], scalar1=-1.0, scalar2=None, op0=ALU.mult),
                             reads=[kB], writes=[('K', c0)])
                        return
                    krb = Kr[0:N1, c0:c0 + 2, :].unsqueeze(2).broadcast_to([N1, 2, 2, 65])
                    p2v = p2[0:N1, :, :].rearrange("p a (r f) -> p a r f", r=2)
                    p.op('dve', lambda e: e.tensor_tensor(out=Zt[i][0:N1, :, :].rearrange("p a (r f) -> p a r f", r=2), in0=p2v, in1=krb, op=ALU.mult),
                         reads=[kB, ('K', c0)], writes=[('Zt', i)])
                    p.op('dve', lambda e: e.tensor_tensor(out=Zu[i][0:N1, :, 0:65], in0=p2[0:N1, :, 65:130], in1=nKi[0:N1, c0:c0 + 2, :], op=ALU.mult),
                         reads=[kB, ('K', c0)], writes=[('Zu', i)])
                    p.op('dve', lambda e: e.tensor_tensor(out=Zu[i][0:N1, :, 65:130], in0=p2[0:N1, :, 0:65], in1=Ki[0:N1, c0:c0 + 2, :], op=ALU.mult),
                         reads=[kB, ('K', c0)], writes=[('Zu', i)])
                    p.op('pool', lambda e: e.tensor_tensor(out=Zs[i][0:N1, :, :], in0=Zt[i][0:N1, :, :], in1=Zu[i][0:N1, :, :], op=ALU.add),
                         reads=[('Zt', i), ('Zu', i)], writes=[('Zs', i)])
                    yield
                    p3 = PA_[i][:, 0:388].rearrange("p (a f) -> p a f", f=194)
                    for a in range(2):
                        p.op('pe', lambda e, a=a: e.matmul(p3[0:65, a, :], Zs[i][0:N1, a, 0:65], mats["i1c"][0:N1, :], start=True, stop=False),
                             reads=[('Zs', i), 'm_i1c'], writes=[kA])
                        p.op('pe', lambda e, a=a: e.matmul(p3[0:65, a, :], Zs[i][0:N1, a, 65:130], mats["i1d"][0:N1, :], start=False, stop=True),
                             reads=[('Zs', i), 'm_i1d'], writes=[kA])
                    p.op('act', lambda e: e.activation(out=Vs[i][0:65, :, :], in_=p3[0:65, :, :], func=AF.Copy), reads=[kA], writes=[('Vs', i)])
                    yield
                    p4 = PB_[i][:, 0:256].rearrange("p (a f) -> p a f", f=128)
                    for a in range(2):
                        p.op('pe', lambda e, a=a: e.matmul(p4[0:N1, a, :], Vs[i][0:65, a, 0:97], mats["cw"][0:65, :], start=True, stop=False),
                             reads=[('Vs', i), 'm_cw'], writes=[kB])
                        p.op('pe', lambda e, a=a: e.matmul(p4[0:N1, a, :], Vs[i][0:65, a, 97:194], mats["sw"][0:65, :], start=False, stop=True),
                             reads=[('Vs', i), 'm_sw'], writes=[kB])
                    p.op('act', lambda e: e.activation(out=Xt[0:N1, :, c0:c0 + 2].rearrange("p t a -> p a t"), in_=p4[0:N1, :, :], func=AF.Copy),
                         reads=[kB], writes=[('Xtc', c0)])

                def from_Xt():
                    for g in range(N2 // 8):
                        b = cnt['tp'] % 4
                        cnt['tp'] += 1
                        pt = P1[b][:, :].bitcast(BF16)[:, 0:8 * 98].rearrange("p (a t) -> p a t", t=98)[:, :, 0:N1]
                        for a in range(8):
                            t2 = g * 8 + a
                            p.op('pe', lambda e, a=a, t2=t2: e.transpose(pt[:, a, :], Xt[0:N1, t2, :], ident_b[0:N1, 0:N1]),
                                 reads=XtK + XtC + ['ident_b'], writes=[('P1', b)])
                        yb = g % 2
                        p.op('act', lambda e: e.activation(out=Yc[yb][:, :, :], in_=pt, func=AF.Copy), reads=[('P1', b)], writes=[('Yc', yb)])
                        for a in range(8):
                            t2 = g * 8 + a
                            lo0 = 0
                            hi0 = min(N1, max(0, -(-(HALF - 97 * t2) // 128)))
                            if hi0 > lo0:
                                p.op('pool', lambda e, a=a, t2=t2, hi0=hi0: e.tensor_copy(YE[:, 97 * t2:97 * t2 + 128 * (hi0 - 1) + 1:128], Yc[yb][:, a, 0:hi0]),
                                     reads=[('Yc', yb)], writes=['YE'])
                            lo1 = max(0, -(-(NF - 97 * t2) // 128))
                            hi1 = min(N1, -(-(NF + HALF - 97 * t2) // 128))
                            if hi1 > lo1:
                                s0 = 97 * t2 + 128 * lo1 - NF
                                n_ = hi1 - lo1
                                p.op('pool', lambda e, a=a, s0=s0, n_=n_, lo1=lo1, hi1=hi1: e.tensor_copy(YE[:, s0:s0 + 128 * (n_ - 1) + 1:128], Yc[yb][:, a, lo1:hi1]),
                                     reads=[('Yc', yb)], writes=['YE'])

                nblk = cfg.get("hy_blocks", 8)
                for cb in range(nblk):
                    rows = slice(cb * 128, (cb + 1) * 128)
                    p.dma('sp', EXT[:, 0:NF], KLS[rows, :], reads=[('KLS', cb)], writes=['EXT'])
                    p.dma('sp', EXT[:, NF:NEXT], KLS[rows, 0:NEXT - NF], reads=[('KLS', cb)], writes=['EXT'], key='EXTb')
                    to_Xt()
                    run_rr([(c0, True) for c0 in range(0, 128, 2)], pair_gen, KP, stagger=cfg.get('stagC', 0))
                    p.dma('sp', EXT[:, 0:L], US[rows, :], reads=[('US', cb, o_, q_) for o_ in (True, False) for q_ in range(2)], writes=['EXT'])
                    p.dma('sp', EXT[:, NF:NF + L], US[rows, :], reads=[('US', cb, o_, q_) for o_ in (True, False) for q_ in range(2)], writes=['EXT'], key='EXTb')
                    p.op('pool', lambda e: e.memset(EXT[:, L:NF], 0.0), writes=['EXT'])
                    p.op('pool', lambda e: e.memset(EXT[:, NF + L:NEXT], 0.0), writes=['EXT'])
                    p.dma('sp', G0[:, :], G0S[rows, :], reads=[('G0S', cb, 0), ('G0S', cb, 1)], writes=['G0'])
                    p.dma('sp', ub[:, :], US[rows, 0:HALF], reads=[('US', cb, True, q_) for q_ in range(2)], writes=['ub'])
                    to_Xt()
                    run_rr([(c0, False) for c0 in range(0, 128, 2)], pair_gen, KP, stagger=cfg.get('stagC', 0))
                    from_Xt()
                    p.op('dve', lambda e: e.scalar_tensor_tensor(out=yo[:, :], in0=ub[:, :], scalar=hbt[:, cb:cb + 1], in1=YE[:, :], op0=ALU.mult, op1=ALU.add),
                         reads=['ub', 'hbt', 'YE'], writes=['yo'])
                    p.op('pool', lambda e: e.tensor_tensor(out=yhb[:, :], in0=yo[:, :], in1=G0[:, :], op=ALU.mult), reads=['yo', 'G0'], writes=['yhb'])
                    p.dma('pool', YH[rows, :], yhb[:, :], reads=['yhb'], writes=['YHall'], key='yhb')
                    if "YE" in cfg.get("dbg", ()):
                        p.dma('sp', dbg_out["YE"][rows, :], YE[:, :], reads=['YE'], writes=[('dbgYE', cb)], key='dbgYE')
                p.barrier()

        if "hyena" in cfg["phases"]:
            if "YE" in cfg.get("dbg", ()):
                ddbg("YE", [D, HALF], BF16)
            if "skipC1" not in cfg.get("dbg", ()):
                phase_C1()
            phase_C2()

        if "out" in cfg["phases"]:
            with ExitStack() as s4:
                wst4 = [sb(s4, f"w4st{i}", [128, 8, 512], F32) for i in range(2)]
                now_t = sb(s4, "now_t", [128, D], F32)
                p.dma('sp', now_t[:], norm_out_w.partition_broadcast(128), writes=['now_t'])
                wdn = sb(s4, "wdn", [128, 8, D], BF16)
                why = sb(s4, "why", [128, 8, D], BF16)
                wo = sb(s4, "wo", [128, 8, D], BF16)
                ci = 0
                for wsrc, wdst, nm in ((w_dn_out, wdn, 'wdn'), (w_hy_out, why, 'why'), (w_out, wo, 'wo')):
                    for hh in range(2):
                        b = ci % 2
                        ci += 1
                        p.dma('sp', wst4[b][:, :, :], wsrc[:, hh * 512:(hh + 1) * 512].rearrange("(k p) c -> p k c", p=128),
                              writes=[('w4st', b)])
                        p.op('pool', lambda e: e.tensor_copy(wdst[:, :, hh * 512:(hh + 1) * 512], wst4[b][:, :, :]),
                             reads=[('w4st', b)], writes=[(nm, hh)])
                ogb = [sb(s4, f"ogb{i}", [128, 8, 512], BF16) for i in range(2)]
                yhb = [sb(s4, f"yhb{i}", [128, 8, 512], BF16) for i in range(2)]
                gtb = [sb(s4, f"gtb{i}", [128, 16, 512], BF16) for i in range(2)]
                m1 = [sb(s4, f"m1_{i}", [128, 512], F32) for i in range(2)]
                m2 = [sb(s4, f"m2_{i}", [128, 512], F32) for i in range(2)]
                mb = [sb(s4, f"mb{i}", [128, 8, 512], BF16) for i in range(2)]
                xr = [sb(s4, f"xr{i}", [128, D], F32) for i in range(2)]
                res = [sb(s4, f"res{i}", [128, D], F32) for i in range(2)]
                junk4 = sb(s4, "junk4", [128, D], BF16)
                ss4 = [sb(s4, f"ss4_{i}", [128, 1], F32) for i in range(2)]
                ot = [sb(s4, f"ot{i}", [128, D], F32) for i in range(2)]
                pa = [ps(s4, f"pa{i}", [128, 512], F32) for i in range(2)]
                pb = [ps(s4, f"pb{i}", [128, 512], F32) for i in range(2)]
                pf = [ps(s4, f"pf{i}", [128, 512], F32) for i in range(4)]
                out_toks = []
                ti = 0
                for bk in range(8):
                    b = bk % 2
                    tsl = slice(bk * 512, (bk + 1) * 512)
                    p.dma('sp', ogb[b][:, :, :], OG[:, tsl].rearrange("(k p) t -> p k t", p=128),
                          reads=[('OG', k) for k in range(8)] + ['OGall'], writes=[('ogb', b)])
                    p.dma('sp', yhb[b][:, :, :], YH[:, tsl].rearrange("(k p) t -> p k t", p=128),
                          reads=[('YH', k) for k in range(8)] + ['YHall'], writes=[('yhb', b)])
                    p.dma('sp', gtb[b][:, :, :], GS[:, tsl].rearrange("(k p) t -> p k t", p=128),
                          reads=[('GS', k) for k in range(16)], writes=[('gtb', b)])
                    for dg in range(8):
                        q2 = dg % 2
                        for k in range(8):
                            p.op('pe', lambda e, k=k: e.matmul(pa[q2][:, :], wdn[:, k, dg * 128:(dg + 1) * 128], ogb[b][:, k, :],
                                                               start=(k == 0), stop=(k == 7)),
                                 reads=[('wdn', dg // 4), ('ogb', b)], writes=[('pa', q2)])
                        for k in range(8):
                            p.op('pe', lambda e, k=k: e.matmul(pb[q2][:, :], why[:, k, dg * 128:(dg + 1) * 128], yhb[b][:, k, :],
                                                               start=(k == 0), stop=(k == 7)),
                                 reads=[('why', dg // 4), ('yhb', b)], writes=[('pb', q2)])
                        p.op('dve', lambda e: e.tensor_tensor(out=m1[q2][:, :], in0=pa[q2][:, :], in1=gtb[b][:, dg, :], op=ALU.mult),
                             reads=[('pa', q2), ('gtb', b)], writes=[('m1', q2)])
                        p.op('dve', lambda e: e.tensor_tensor(out=m2[q2][:, :], in0=pb[q2][:, :], in1=gtb[b][:, 8 + dg, :], op=ALU.mult),
                             reads=[('pb', q2), ('gtb', b)], writes=[('m2', q2)])
                        p.op('pool', lambda e: e.tensor_tensor(out=mb[b][:, dg, :], in0=m1[q2][:, :], in1=m2[q2][:, :], op=ALU.add),
                             reads=[('m1', q2), ('m2', q2)], writes=[('mb', b, dg)])
                    for tt in range(4):
                        t0 = bk * 512 + tt * 128
                        r = ti % 2
                        ti += 1
                        p.dma('sp', xr[r][:, :], x[t0:t0 + 128, :], writes=[('xr', r)])
                        for nh in range(2):
                            fi = (2 * ti + nh) % 4
                            for k in range(8):
                                p.op('pe', lambda e, k=k: e.matmul(pf[fi][:, :], mb[b][:, k, tt * 128:(tt + 1) * 128],
                                                                   wo[:, k, nh * 512:(nh + 1) * 512], start=(k == 0), stop=(k == 7)),
                                     reads=[('mb', b, k), ('wo', nh)], writes=[('pf', fi)])
                            p.op('dve', lambda e: e.tensor_tensor(out=res[r][:, nh * 512:(nh + 1) * 512], in0=pf[fi][:, :],
                                                                  in1=xr[r][:, nh * 512:(nh + 1) * 512], op=ALU.add),
                                 reads=[('pf', fi), ('xr', r)], writes=[('res', r, nh)])
                        p.op('act', lambda e: e.activation(out=junk4[:, :], in_=res[r][:, :], func=AF.Square, accum_out=ss4[r][:, :]),
                             reads=[('res', r, 0), ('res', r, 1)], writes=['junk4', ('ss4', r)])
                        p.op('act', lambda e: e.activation(out=ss4[r][:, :], in_=ss4[r][:, :], func=AF.Ln, scale=1.0 / D, bias=EPS),
                             reads=[('ss4', r)], writes=[('ss4', r)])
                        p.op('act', lambda e: e.activation(out=ss4[r][:, :], in_=ss4[r][:, :], func=AF.Exp, scale=-0.5),
                             reads=[('ss4', r)], writes=[('ss4', r)])
                        p.op('dve', lambda e: e.scalar_tensor_tensor(out=ot[r][:, :], in0=res[r][:, :], scalar=ss4[r][:, :],
                                                                      in1=now_t[:, :], op0=ALU.mult, op1=ALU.mult),
                             reads=[('res', r, 0), ('res', r, 1), ('ss4', r), 'now_t'], writes=[('ot', r)])
                        out_toks.append(p.dma('sp', y[t0:t0 + 128, :], ot[r][:, :], reads=[('ot', r)], writes=[('y', t0)], key=('ot', r)))
                p.barrier()
        for nm in cfg.get("dump", ()):
            src = {"OG": OG, "YH": YH, "GS": GS, "QS": QS, "KS": KS, "VS": VS, "ZS": ZS, "BGS": BGS, "OBS": OBS, "US": US, "G0S": G0S, "KLS": KLS}[nm]
            dst = ddbg(nm, src.shape, src.dtype)
            nr = src.shape[0]
            step = 128 if nr >= 128 else nr
            for r0 in range(0, nr, step):
                p.dma('sp', dst[r0:r0 + step, :], src[r0:r0 + step, :], reads=[], writes=[('dump', nm, r0)], key=('dump', (r0 // step) % 4))
        p.barrier()
        print("instr counts", p.ninstr, "nsem", p.nsem)
    return nc


def _core_inputs(inputs, b, hf):
    xs = inputs["x"][b]
    w_in = inputs["w_in"][0]
    if hf == 1:
        xs = xs[::-1]
        perm = np.arange(INW)
        for base in (C_BETA, C_A):
            perm[base:base + 8] = np.arange(base + 8, base + 16)
            perm[base + 8:base + 16] = np.arange(base, base + 8)
        w_in = w_in[:, perm]
    dcw = inputs["dn_conv_w"][0]
    hcw = inputs["hy_conv_w"][0]
    alog = inputs["dn_a_log"][0]
    dtb = inputs["dn_dt_bias"][0]
    if hf == 1:
        dcw, hcw, alog, dtb = dcw[::-1], hcw[::-1], alog[::-1], dtb[::-1]
    w3 = inputs["hy_w3"][0]
    ld = inputs["hy_log_decay"][0]
    w3f, w3b, ldf, ldb = w3[:, :D], w3[:, D:], ld[:D], ld[D:]
    if hf == 0:
        w3s, lds = np.stack([w3f, w3b, w3f], axis=1), np.stack([ldf, ldb, ldf], axis=0)
    else:
        w3s, lds = np.stack([w3b, w3f, w3f], axis=1), np.stack([ldb, ldf, ldf], axis=0)
    m = {
        "hy_w1": np.ascontiguousarray(inputs["hy_w1"][0]), "hy_b1": np.ascontiguousarray(inputs["hy_b1"][0]),
        "hy_w2": np.ascontiguousarray(inputs["hy_w2"][0]), "hy_b2": np.ascontiguousarray(inputs["hy_b2"][0]),
        "hy_freq": np.ascontiguousarray(inputs["hy_freq"][0]), "hy_w3": np.ascontiguousarray(w3s),
        "hy_log_decay": np.ascontiguousarray(lds), "hy_bias": np.ascontiguousarray(inputs["hy_bias"][0]),
        "dn_conv_w": np.ascontiguousarray(dcw), "hy_conv_w": np.ascontiguousarray(hcw),
        "dn_a_log": np.ascontiguousarray(alog).reshape(16), "dn_dt_bias": np.ascontiguousarray(dtb).reshape(16),
        "dn_norm_w": np.ascontiguousarray(inputs["dn_norm_w"][0]),
        "x": np.ascontiguousarray(xs, dtype=np.float32),
        "norm_in_w": np.ascontiguousarray(inputs["norm_in_w"][0]),
        "w_in": np.ascontiguousarray(w_in),
        "w_dn_out": np.ascontiguousarray(inputs["w_dn_out"][0]),
        "w_hy_out": np.ascontiguousarray(inputs["w_hy_out"][0]),
        "w_out": np.ascontiguousarray(inputs["w_out"][0]),
        "norm_out_w": np.ascontiguousarray(inputs["norm_out_w"]),
    }
    m.update(_consts())
    return m


FULL_CFG = {"phases": ("projA", "hyproj", "gates", "dn", "hyena", "out")}


def kernel(**inputs):
    nc = build(FULL_CFG)
    in_maps = [_core_inputs(inputs, c // 2, c % 2) for c in range(8)]
    res = run_bass_kernel_spmd(nc, in_maps, core_ids=list(range(8)))
    out = np.empty((4, L, D), np.float32)
    for c in range(8):
        b, hf = c // 2, c % 2
        yc = res.results[c]["y"]
        if hf == 0:
            out[b, :HALF] = yc
        else:
            out[b, HALF:] = yc[::-1]
    return out
```

```python
import numpy as np
import concourse.bass as bass
import concourse.mybir as mybir
from concourse.bass_utils import run_bass_kernel_spmd
from contextlib import ExitStack

F32 = mybir.dt.float32
BF16 = mybir.dt.bfloat16
AF = mybir.ActivationFunctionType
ALU = mybir.AluOpType
AX = mybir.AxisListType

D = 1024
L = 8192
HALF = 4096
NH = 8
INW = 10272
EPS = 1e-6
C_QKV, C_DNZ, C_BETA, C_A, C_HXV, C_HZ, C_GATE = 0, 3072, 4096, 4112, 4128, 7200, 8224


class Prog:
    SEM_EPOCH = 20000

    def __init__(self, nc, es, same_engine_sync=True):
        self.nc = nc
        self.es = es
        self.engs = {'pe': nc.tensor, 'act': nc.scalar, 'dve': nc.vector, 'pool': nc.gpsimd, 'sp': nc.sync}
        self.sem = {}
        self.cnt = {}
        self.nsem = 0
        for e in self.engs:
            self._new_eng_sem(e)
        self.waited = {e: {} for e in self.engs}
        self.last_w = {}
        self.readers = {}
        self.dma_sem = {}
        self.same = same_engine_sync
        self.ninstr = {e: 0 for e in self.engs}
        self.last_tok = {}
        self.dma_toks = []

    def _mksem(self, name):
        self.nsem += 1
        return self.es.enter_context(self.nc.semaphore(f"{name}_{self.nsem}"))

    def _new_eng_sem(self, e):
        self.sem[e] = self._mksem("s" + e)
        self.cnt[e] = 0

    def _wait(self, e, tok):
        if tok is None:
            return
        sem, val, src = tok
        if src == e and (not self.same or e == 'pe'):
            return
        w = self.waited[e]
        k = id(sem)
        if k in w and w[k] >= val:
            return
        w[k] = val
        self.engs[e].wait_ge(sem, val)
        self.ninstr[e] += 1

    def _deps(self, e, reads, writes):
        for k in reads:
            self._wait(e, self.last_w.get(k))
        for k in writes:
            t = self.last_w.get(k)
            if t is not None and (t[2] != e or k in reads):
                self._wait(e, t)
            for t in self.readers.get(k, ()):
                if t[2] != e:
                    self._wait(e, t)

    def _commit(self, tok, reads, writes):
        for k in reads:
            self.readers.setdefault(k, []).append(tok)
        for k in writes:
            self.last_w[k] = tok
            self.readers[k] = []

    PSUM_NAMES = ('PF', 'PB', 'PFx', 'pp', 'pn', 'ph', 'pa', 'pb', 'pf', 'pTo', 'pTx', 'pm', 'P1', 'P2', 'P3', 'P4')

    def _excl(self, reads, writes):
        r2, w2 = [], list(writes)
        for k in reads:
            nm = k[0] if isinstance(k, tuple) else k
            if nm in self.PSUM_NAMES:
                if k not in w2:
                    w2.append(k)
            else:
                r2.append(k)
        return r2, w2

    def op(self, e, fn, reads=(), writes=()):
        reads, writes = self._excl(reads, writes)
        self._deps(e, reads, writes)
        if self.cnt[e] >= self.SEM_EPOCH:
            self._new_eng_sem(e)
        ins = fn(self.engs[e])
        self.cnt[e] += 1
        ins.then_inc(self.sem[e], 1)
        self.ninstr[e] += 1
        tok = (self.sem[e], self.cnt[e], e)
        self.last_tok[e] = tok
        self._commit(tok, reads, writes)
        return tok

    def dma(self, e, out, in_, reads=(), writes=(), key=None, **kw):
        self._deps(e, reads, writes)
        if key is None:
            key = (writes[0] if writes else reads[0])
        ds = self.dma_sem.get(key)
        if ds is None or ds[1] + 16 > self.SEM_EPOCH:
            ds = [self._mksem("d"), 0]
            self.dma_sem[key] = ds
        ds[1] += 16
        self.engs[e].dma_start(out=out, in_=in_, **kw).then_inc(ds[0], 16)
        self.ninstr[e] += 1
        tok = (ds[0], ds[1], 'dma')
        self.dma_toks.append(tok)
        self._commit(tok, reads, writes)
        return tok

    def barrier(self):
        toks = list(self.last_tok.values())
        latest = {}
        for t in self.dma_toks:
            k = id(t[0])
            if k not in latest or latest[k][1] < t[1]:
                latest[k] = t
        toks += list(latest.values())
        self.dma_toks = list(latest.values())
        for e in self.engs:
            for t in toks:
                if t[2] == e:
                    continue
                self._wait(e, t)
        self.last_w = {}
        self.readers = {}


def _consts():
    ident = np.eye(128, dtype=np.float32)
    pi, fi = np.meshgrid(np.arange(128), np.arange(128), indexing="ij")
    NEG = -30000.0
    tri = np.stack([
        (pi <= fi).astype(np.float32),
        (pi >= fi).astype(np.float32),
        np.where(fi < pi, 0.0, NEG),
        np.where(fi >= pi, 0.0, NEG),
        np.where(fi > pi, 0.0, NEG),
        np.where(fi <= pi, 0.0, NEG),
    ]).astype(np.float32)
    sel = np.zeros((32, 2), np.float32)
    sel[:16, 0] = 1.0
    sel[16:, 1] = 1.0
    msk = np.stack([(pi // 32 == fi // 32), (pi // 64 == fi // 64) & (pi // 32 != fi // 32), (pi // 64 != fi // 64)]).astype(np.float32)
    out = {"c_ident": ident, "c_tri": tri, "c_sel": sel, "c_msk": msk}
    NF, NP = 97 * 128, 12800
    j = np.arange(NP)
    pos = np.where(j < HALF, j, NF - j).astype(np.float64)
    pos = np.clip(pos, 0, None)
    tl = pos / (L - 1)
    bands = 16
    fb = np.linspace(1e-4, bands - 1, bands)
    ang = (2.0 * np.pi / L) * pos[None, :] * fb[:, None]
    out["c_zpos"] = np.concatenate([tl[None, :], np.cos(ang), -np.sin(ang)], axis=0).astype(np.float32)
    out["c_dl"] = (np.arange(512) / (L - 1)).astype(np.float32)[None, :]
    q = np.arange(25)
    out["c_tp0"] = np.where(q < 8, 512 * q / (L - 1), (NF - 512 * q) / (L - 1)).astype(np.float32)
    a1 = 2 * np.pi * np.outer(np.arange(97), np.arange(97)) / 97
    c1, s1 = np.cos(a1), np.sin(a1)
    a2 = 2 * np.pi * np.outer(np.arange(128), np.arange(65)) / 128
    c2, s2 = np.cos(a2), np.sin(a2)
    out["c_e1"] = np.concatenate([c1, -s1], axis=1).astype(np.float32)
    out["c_s2a"] = np.concatenate([c2, -s2], axis=1).astype(np.float32)
    out["c_s2b"] = np.concatenate([s2, c2], axis=1).astype(np.float32)
    out["c_i1c"] = np.concatenate([c1, s1], axis=1).astype(np.float32)
    out["c_i1d"] = np.concatenate([-s1, c1], axis=1).astype(np.float32)
    wgt = np.full(65, 2.0)
    wgt[0] = wgt[64] = 1.0
    out["c_cw"] = (wgt[:, None] * c2.T / NF).astype(np.float32)
    out["c_sw"] = (-wgt[:, None] * s2.T / NF).astype(np.float32)
    return out


def build(cfg):
    nc = bass.Bass("TRN2", target_bir_lowering=False)
    dbg = cfg.get("dbg", ())

    def din(name, shape, dt=F32):
        return nc.dram_tensor(name, list(shape), dt, kind="ExternalInput").ap()

    def dscr(name, shape, dt, ext=False):
        kind = "ExternalInput" if ext else "Internal"
        return nc.dram_tensor(name, list(shape), dt, kind=kind).ap()

    x = din("x", [L, D])
    norm_in_w = din("norm_in_w", [D])
    w_in = din("w_in", [D, INW])
    w_dn_out = din("w_dn_out", [D, D])
    w_hy_out = din("w_hy_out", [D, D])
    w_out = din("w_out", [D, D])
    norm_out_w = din("norm_out_w", [D])
    c_ident = din("c_ident", [128, 128])
    y = nc.dram_tensor("y", [HALF, D], F32, kind="ExternalOutput").ap()

    ext = cfg.get("ext_scratch", ())
    OG = dscr("OG", [D, HALF], BF16, "OG" in ext)
    YH = dscr("YH", [D, HALF], BF16, "YH" in ext)
    GS = dscr("GS", [2 * D, HALF], BF16, "GS" in ext)
    QS = dscr("QS", [D, HALF], BF16, "QS" in ext)
    KS = dscr("KS", [D, L], BF16, "KS" in ext)
    VS = dscr("VS", [D, L], BF16, "VS" in ext)
    ZS = dscr("ZS", [D, HALF], BF16, "ZS" in ext)
    BGS = dscr("BGS", [32, L], F32, "BGS" in ext)
    OBS = dscr("OBS", [HALF, D], F32, "OBS" in ext)
    US = dscr("US", [D, L], BF16, "US" in ext)
    G0S = dscr("G0S", [D, HALF], BF16, "G0S" in ext)
    KLS = dscr("KLS", [D, 97 * 128], BF16, "KLS" in ext)
    hy_w1 = din("hy_w1", [33, 64])
    hy_b1 = din("hy_b1", [64])
    hy_w2 = din("hy_w2", [64, 64])
    hy_b2 = din("hy_b2", [64])
    hy_freq = din("hy_freq", [64])
    hy_w3 = din("hy_w3", [64, 3, D])
    hy_log_decay = din("hy_log_decay", [3, D])
    hy_bias = din("hy_bias", [D])
    c_zpos = din("c_zpos", [33, 12800])
    c_dl = din("c_dl", [1, 512])
    c_tp0 = din("c_tp0", [25])
    c_e1 = din("c_e1", [97, 194])
    c_s2a = din("c_s2a", [128, 130])
    c_s2b = din("c_s2b", [128, 130])
    c_i1c = din("c_i1c", [97, 194])
    c_i1d = din("c_i1d", [97, 194])
    c_cw = din("c_cw", [65, 128])
    c_sw = din("c_sw", [65, 128])
    dn_conv_w = din("dn_conv_w", [3, 3 * D])
    hy_conv_w = din("hy_conv_w", [3, 3 * D])
    dn_a_log = din("dn_a_log", [16])
    dn_dt_bias = din("dn_dt_bias", [16])
    dn_norm_w = din("dn_norm_w", [128])
    c_sel = din("c_sel", [32, 2])
    c_tri = din("c_tri", [6, 128, 128])
    c_msk = din("c_msk", [3, 128, 128])

    dbg_out = {}

    def ddbg(name, shape, dt=F32):
        dbg_out[name] = nc.dram_tensor("dbg_" + name, list(shape), dt, kind="ExternalOutput").ap()
        return dbg_out[name]

    with ExitStack() as es:
        p = Prog(nc, es)
        cs = ExitStack()
        es.enter_context(cs)

        uniq = [0]

        def sb(stack, name, shape, dt):
            uniq[0] += 1
            return stack.enter_context(nc.sbuf_tensor(f"{name}_{uniq[0]}", list(shape), dt))

        def ps(stack, name, shape, dt=F32):
            uniq[0] += 1
            return stack.enter_context(nc.psum_tensor(f"{name}_{uniq[0]}", list(shape), dt))

        ident_f = sb(cs, "ident_f", [128, 128], F32)
        ident_b = sb(cs, "ident_b", [128, 128], BF16)
        nw_t = sb(cs, "nw_t", [128, 8], F32)
        p.dma('sp', ident_f[:], c_ident[:, :], writes=['ident_f'])
        p.op('dve', lambda e: e.tensor_copy(ident_b[:], ident_f[:]), reads=['ident_f'], writes=['ident_b'])
        p.dma('sp', nw_t[:], norm_in_w.rearrange("(k p) -> p k", p=128), writes=['nw_t'],
              allow_slow_non_contiguous=True)
        cwd = sb(cs, "cwd", [128, 24, 3], F32)
        cwh = sb(cs, "cwh", [128, 24, 3], F32)
        nwd = sb(cs, "nwd", [128, 1], F32)
        dtb = sb(cs, "dtb", [32, 1], F32)
        negA = sb(cs, "negA", [32, 1], F32)
        selt = sb(cs, "selt", [32, 2], F32)
        selb, selg = selt[:, 0:1], selt[:, 1:2]
        for j in range(3):
            p.dma('sp', cwd[:, :, j], dn_conv_w[j, :].rearrange("(g p) -> p g", p=128), writes=[('cwd', j)], allow_slow_non_contiguous=True)
            p.dma('sp', cwh[:, :, j], hy_conv_w[j, :].rearrange("(g p) -> p g", p=128), writes=[('cwh', j)], allow_slow_non_contiguous=True)
        p.dma('sp', nwd[:, :], dn_norm_w.rearrange("(p o) -> p o", o=1), writes=['nwd'], allow_slow_non_contiguous=True)
        p.op('pool', lambda e: e.memset(dtb[:], 0.0), writes=['dtb'])
        p.op('pool', lambda e: e.memset(negA[:], 0.0), writes=['negA'])
        p.dma('sp', dtb[16:32, :], dn_dt_bias.rearrange("(p o) -> p o", o=1), reads=[], writes=['dtb'], allow_slow_non_contiguous=True)
        p.dma('sp', negA[16:32, :], dn_a_log.rearrange("(p o) -> p o", o=1), writes=['negA'], allow_slow_non_contiguous=True)
        p.dma('sp', selt[:, :], c_sel[:, :], writes=['selb'])
        p.op('act', lambda e: e.activation(out=negA[:], in_=negA[:], func=AF.Exp), reads=['negA'], writes=['negA'])
        p.op('dve', lambda e: e.tensor_scalar(out=negA[:], in0=negA[:], scalar1=-1.0, scalar2=None, op0=ALU.mult), reads=['negA'], writes=['negA'])
        p.barrier()

        def build_hT(stk, hT, tok_base, pre_tok, post_tok, tag):
            with ExitStack() as ls:
                xt = [sb(ls, f"xt{tag}{i}", [128, D], F32) for i in range(2)]
                junk = sb(ls, f"junk{tag}", [128, D], BF16)
                xn = [sb(ls, f"xn{tag}{i}", [128, D], BF16) for i in range(2)]
                ssq = [sb(ls, f"ssq{tag}{i}", [128, 1], F32) for i in range(2)]
                pT = [ps(ls, f"pT{tag}{i}", [128, 8, 128], BF16) for i in range(2)]
                jobs = [(tok_base + 128 * i, 128, 1 + 128 * i) for i in range(HALF // 128)]
                for hc, tk in ((0, pre_tok), (HALF + 1, post_tok)):
                    if tk is None:
                        p.op('pool', lambda e, hc=hc: e.memset(hT[:, :, hc:hc + 1], 0.0), writes=[('hT', 'halo', hc)])
                    else:
                        jobs.append((tk, 1, hc))
                for ji, (t0, n, c0) in enumerate(jobs):
                    b = ji % 2
                    kx, kn, ks, kp = (f'xt{tag}', b), (f'xn{tag}', b), (f'ssq{tag}', b), (f'pT{tag}', b)
                    p.dma('sp', xt[b][0:n, :], x[t0:t0 + n, :], writes=[kx])
                    p.op('act', lambda e: e.activation(out=junk[0:n, :], in_=xt[b][0:n, :], func=AF.Square,
                                                       accum_out=ssq[b][0:n, :]), reads=[kx], writes=['junk' + tag, ks])
                    p.op('act', lambda e: e.activation(out=ssq[b][0:n, :], in_=ssq[b][0:n, :], func=AF.Ln, scale=1.0 / D, bias=EPS),
                         reads=[ks], writes=[ks])
                    p.op('act', lambda e: e.activation(out=ssq[b][0:n, :], in_=ssq[b][0:n, :], func=AF.Exp, scale=-0.5),
                         reads=[ks], writes=[ks])
                    p.op('dve', lambda e: e.tensor_scalar(out=xn[b][0:n, :], in0=xt[b][0:n, :], scalar1=ssq[b][0:n, :],
                                                          scalar2=None, op0=ALU.mult), reads=[kx, ks], writes=[kn])
                    for k in range(8):
                        p.op('pe', lambda e, k=k: e.transpose(pT[b][:, k, 0:n], xn[b][0:n, k * 128:(k + 1) * 128],
                                                              ident_b[0:n, 0:n]),
                             reads=[kn, 'ident_b'], writes=[kp])
                    key = ('hT', (c0 - 1) // 512) if n == 128 else ('hT', 'halo', c0)
                    p.op('dve', lambda e: e.tensor_tensor(out=hT[:, :, c0:c0 + n], in0=pT[b][:, :, 0:n],
                                                          in1=nw_t[:, :].unsqueeze(2).broadcast_to([128, 8, n]), op=ALU.mult),
                         reads=[kp, 'nw_t'], writes=[key])
                p.barrier()

        def hT_keys(nblk=8):
            return [('hT', i) for i in range(nblk)]

        wctr = [0]

        def load_w_group(wst, wbf, src_ap):
            b = wctr[0] % len(wst)
            wctr[0] += 1
            ncol = src_ap.shape[1]
            p.dma('sp', wst[b][:, :, 0:ncol], src_ap.rearrange("(k p) c -> p k c", p=128), writes=[('wst', b)])
            p.op('pool', lambda e: e.tensor_copy(wbf[b][:, :, 0:ncol], wst[b][:, :, 0:ncol]), reads=[('wst', b)],
                 writes=[('wbf', b, k) for k in range(8)])
            return b

        W = 2048
        NB = W // 512

        def run_rr(job_specs, make_gen, K, stagger=1):
            active = {}
            nxt_job = 0
            free = list(range(K))
            since = stagger
            while nxt_job < len(job_specs) or active:
                since += 1
                while free and nxt_job < len(job_specs) and since > stagger:
                    sl = free.pop(0)
                    active[sl] = make_gen(sl, job_specs[nxt_job])
                    nxt_job += 1
                    since = 0 if stagger > 0 else since
                for sl in sorted(active.keys()):
                    try:
                        next(active[sl])
                    except StopIteration:
                        del active[sl]
                        free.append(sl)

        def phase_A(own):
            tok_base = 0 if own else HALF
            KJ = 3
            with ExitStack() as s2:
                hT = sb(s2, "hT", [128, 8, HALF + 2], BF16)
                if own:
                    build_hT(s2, hT, 0, None, HALF, "o")
                else:
                    build_hT(s2, hT, HALF, HALF - 1, None, "x")
                wst = [sb(s2, f"wst{i}", [128, 8, 128], F32) for i in range(3)]
                wbf = [sb(s2, f"wbf{i}", [128, 8, 128], BF16) for i in range(3)]
                pp = [ps(s2, f"pp{i}", [128, 512], F32) for i in range(5)]
                pn = [ps(s2, f"pn{i}", [128, 512], F32) for i in range(3)]
                R = [sb(s2, f"R{i}", [128, W + 2], F32) for i in range(KJ)]
                acc = [sb(s2, f"acc{i}", [128, W], F32) for i in range(KJ)]
                sil = acc
                sqb = [sb(s2, f"sqb{i}", [128, W], BF16) for i in range(KJ)]
                rst = [sb(s2, f"rst{i}", [128, W], F32) for i in range(KJ)]
                ob = [sb(s2, f"ob{i}", [128, W], BF16) for i in range(KJ)]
                keep2 = [sb(s2, f"keep{i}", [128, W], F32) for i in range(2)]
                ones_b = sb(s2, "ones_b", [128, 128], BF16)
                p.op('pool', lambda e: e.memset(ones_b[:], 1.0), writes=['ones_b'])
                ctr = {'pp': 0, 'pn': 0}

                def nxt(nm, n):
                    v = ctr[nm] % n
                    ctr[nm] += 1
                    return v

                def Rkeys(ri):
                    return [('R', ri, bk) for bk in range(5)]

                BW = (W + 2) // 5

                def project(sl, wb, ncol, ps_i):
                    c0 = ps_i * W
                    for bk in range(5):
                        pi = nxt('pp', 5)
                        cs = c0 + bk * BW
                        for k in range(8):
                            p.op('pe', lambda e, k=k: e.matmul(pp[pi][0:ncol, 0:BW], wbf[wb][:, k, 0:ncol], hT[:, k, cs:cs + BW],
                                                               start=(k == 0), stop=(k == 7)),
                                 reads=[('wbf', wb, k)], writes=[('pp', pi)])
                        dst = R[sl][0:ncol, bk * BW:(bk + 1) * BW]
                        if bk % 2 == 0:
                            p.op('act', lambda e: e.activation(out=dst, in_=pp[pi][0:ncol, 0:BW], func=AF.Copy), reads=[('pp', pi)], writes=[('R', sl, bk)])
                        else:
                            p.op('dve', lambda e: e.tensor_copy(dst, pp[pi][0:ncol, 0:BW]), reads=[('pp', pi)], writes=[('R', sl, bk)])

                def conv(sl, cw, g):
                    p.op('act', lambda e: e.activation(out=acc[sl][:, :], in_=R[sl][:, 0:W], func=AF.Copy, scale=cw[:, g, 0:1]),
                         reads=Rkeys(sl), writes=[('acc', sl)])
                    for j in (1, 2):
                        p.op('dve', lambda e, j=j: e.scalar_tensor_tensor(out=acc[sl][:, :], in0=R[sl][:, j:j + W], scalar=cw[:, g, j:j + 1],
                                                                           in1=acc[sl][:, :], op0=ALU.mult, op1=ALU.add),
                             reads=Rkeys(sl) + [('acc', sl)], writes=[('acc', sl)])

                def store(dst_rows, sl, ps_i, key, eng):
                    c0 = tok_base + ps_i * W if dst_rows.shape[1] == L else ps_i * W
                    p.dma(eng, dst_rows[:, c0:c0 + W], ob[sl][:, :], reads=[('ob', sl)], writes=[key], key=('ob', sl))

                RST = lambda sl: [('rst', sl, bk) for bk in range(NB)]

                def job(sl, spec):
                    kind, wb, ps_i, idx = spec
                    if kind == 'gate':
                        for bk in range(NB):
                            pi = nxt('pp', 5)
                            cs = 1 + ps_i * W + bk * 512
                            for k in range(8):
                                p.op('pe', lambda e, k=k: e.matmul(pp[pi][:, :], wbf[wb][:, k, :], hT[:, k, cs:cs + 512], start=(k == 0), stop=(k == 7)),
                                     reads=[('wbf', wb, k)], writes=[('pp', pi)])
                            p.op('act', lambda e: e.activation(out=ob[sl][:, bk * 512:(bk + 1) * 512], in_=pp[pi][:, :], func=AF.Sigmoid),
                                 reads=[('pp', pi)], writes=[('ob', sl)])
                        store(GS[idx * 128:(idx + 1) * 128, :], sl, ps_i, ('GS', idx, ps_i), 'act')
                        return
                    ncol = 32 if kind == 'bg' else 128
                    project(sl, wb, ncol, ps_i)
                    yield
                    if kind == 'bg':
                        xin = R[sl][0:32, 1:W + 1]
                        rk = Rkeys(sl)
                        t0, t1, t2 = acc[sl][0:32, :], xin, rst[sl][0:32, :]
                        K0, K1, K2 = ('acc', sl), ('R', sl, 0), RST(sl)
                        p.op('act', lambda e: e.activation(out=t0, in_=xin, func=AF.Exp, scale=-1.0), reads=rk, writes=[K0])
                        p.op('dve', lambda e: e.tensor_scalar(out=t0, in0=t0, scalar1=1.0, scalar2=None, op0=ALU.add), reads=[K0], writes=[K0])
                        p.op('dve', lambda e: e.reciprocal(out=t0, in_=t0), reads=[K0], writes=[K0])
                        p.op('dve', lambda e: e.tensor_scalar(out=t1, in0=xin, scalar1=dtb[:, 0:1], scalar2=None, op0=ALU.add),
                             reads=rk + ['dtb'], writes=rk)
                        p.op('act', lambda e: e.activation(out=t2, in_=t1, func=AF.Abs), reads=[K1], writes=K2)
                        p.op('act', lambda e: e.activation(out=t2, in_=t2, func=AF.Exp, scale=-1.0), reads=K2, writes=K2)
                        p.op('act', lambda e: e.activation(out=t2, in_=t2, func=AF.Ln, bias=1.0), reads=K2, writes=K2)
                        p.op('dve', lambda e: e.scalar_tensor_tensor(out=t1, in0=t1, scalar=0.0, in1=t2, op0=ALU.max, op1=ALU.add),
                             reads=[K1] + K2, writes=[K1])
                        p.op('dve', lambda e: e.tensor_scalar(out=t1, in0=t1, scalar1=negA[:, 0:1], scalar2=selg[:, 0:1], op0=ALU.mult, op1=ALU.mult),
                             reads=[K1, 'negA', 'selb'], writes=[K1])
                        p.op('dve', lambda e: e.scalar_tensor_tensor(out=t2, in0=t0, scalar=selb[:, 0:1], in1=t1, op0=ALU.mult, op1=ALU.add),
                             reads=[K0, K1, 'selb'], writes=K2)
                        c0 = tok_base + ps_i * W
                        p.dma('pool', BGS[:, c0:c0 + W], t2, reads=K2, writes=[('BGS', own, ps_i)], key=('rst', sl))
                        return
                    if kind in ('z', 'hz'):
                        p.op('act', lambda e: e.activation(out=sil[sl][:, :], in_=R[sl][:, 1:W + 1], func=AF.Silu), reads=Rkeys(sl), writes=[('acc', sl)])
                        yield
                        if kind == 'z':
                            p.op('dve', lambda e: e.tensor_scalar(out=ob[sl][:, :], in0=sil[sl][:, :], scalar1=nwd[:, 0:1], scalar2=None, op0=ALU.mult),
                                 reads=[('acc', sl), 'nwd'], writes=[('ob', sl)])
                            store(ZS[idx * 128:(idx + 1) * 128, :], sl, ps_i, ('ZS', idx, ps_i), 'pool')
                        else:
                            p.op('pool', lambda e: e.tensor_tensor(out=ob[sl][:, :], in0=sil[sl][:, :], in1=keep2[ps_i][:, :], op=ALU.mult),
                                 reads=[('acc', sl), ('keep', ps_i)], writes=[('ob', sl)])
                            store(G0S[idx * 128:(idx + 1) * 128, :], sl, ps_i, ('G0S', idx, ps_i), 'pool')
                        return
                    if kind in ('q', 'k', 'v'):
                        gi = {'q': idx, 'k': NH + idx, 'v': 2 * NH + idx}[kind]
                        conv(sl, cwd, gi)
                        yield
                        if kind == 'v':
                            p.op('act', lambda e: e.activation(out=ob[sl][:, :], in_=acc[sl][:, :], func=AF.Silu), reads=[('acc', sl)], writes=[('ob', sl)])
                            store(VS[idx * 128:(idx + 1) * 128, :], sl, ps_i, ('VS', idx, own, ps_i), 'act')
                            return
                        p.op('act', lambda e: e.activation(out=sil[sl][:, :], in_=acc[sl][:, :], func=AF.Silu), reads=[('acc', sl)], writes=[('acc', sl)])
                        yield
                        p.op('act', lambda e: e.activation(out=sqb[sl][:, :], in_=sil[sl][:, :], func=AF.Square),
                             reads=[('acc', sl)], writes=[('sqb', sl)])
                        yield
                        for bk in range(NB):
                            ni = nxt('pn', 3)
                            p.op('pe', lambda e: e.matmul(pn[ni][:, :], ones_b[:, :], sqb[sl][:, bk * 512:(bk + 1) * 512], start=True, stop=True),
                                 reads=['ones_b', ('sqb', sl)], writes=[('pn', ni)])
                            p.op('act', lambda e: e.activation(out=rst[sl][:, bk * 512:(bk + 1) * 512], in_=pn[ni][:, :], func=AF.Ln, bias=EPS),
                                 reads=[('pn', ni)], writes=[('rst', sl, bk)])
                        scale = 128.0 ** -0.5 if kind == 'q' else 1.0
                        p.op('act', lambda e: e.activation(out=rst[sl][:, :], in_=rst[sl][:, :], func=AF.Exp, scale=-0.5, bias=float(np.log(scale))),
                             reads=RST(sl), writes=RST(sl))
                        yield
                        p.op('pool', lambda e: e.tensor_tensor(out=ob[sl][:, :], in0=sil[sl][:, :], in1=rst[sl][:, :], op=ALU.mult),
                             reads=[('acc', sl)] + RST(sl), writes=[('ob', sl)])
                        dstT = QS if kind == 'q' else KS
                        key = ('QS', idx, ps_i) if kind == 'q' else ('KS', idx, own, ps_i)
                        store(dstT[idx * 128:(idx + 1) * 128, :], sl, ps_i, key, 'pool')
                        return
                    gi = {'x0': idx, 'x1': 8 + idx, 'hv': 16 + idx}[kind]
                    conv(sl, cwh, gi)
                    yield
                    if kind in ('x0', 'x1'):
                        p.op('pool', lambda e: e.tensor_copy(keep2[ps_i][:, :], acc[sl][:, :]), reads=[('acc', sl)], writes=[('keep', ps_i)])
                    else:
                        p.op('pool', lambda e: e.tensor_tensor(out=ob[sl][:, :], in0=acc[sl][:, :], in1=keep2[ps_i][:, :], op=ALU.mult),
                             reads=[('acc', sl), ('keep', ps_i)], writes=[('ob', sl)])
                        store(US[idx * 128:(idx + 1) * 128, :], sl, ps_i, ('US', idx, own, ps_i), 'pool')

                groups = []
                for h in range(NH):
                    for kind in (('q', 'k', 'v', 'z') if own else ('k', 'v')):
                        gi = {'q': h, 'k': NH + h, 'v': 2 * NH + h}.get(kind)
                        groups.append((kind, C_QKV + gi * 128 if kind != 'z' else C_DNZ + h * 128, 128, h))
                groups.append(('bg', C_BETA, 32, 0))
                if "hyproj" in cfg["phases"]:
                    for cb in range(8):
                        for kind in (('x0', 'hz', 'x1', 'hv') if own else ('x1', 'hv')):
                            gi = {'x0': cb, 'x1': 8 + cb, 'hv': 16 + cb}.get(kind)
                            groups.append((kind, C_HXV + gi * 128 if kind != 'hz' else C_HZ + cb * 128, 128, cb))
                if own and "gates" in cfg["phases"]:
                    for gi in range(16):
                        groups.append(('gate', C_GATE + gi * 128, 128, gi))
                specs = []
                for (kind, col0, ncol, idx) in groups:
                    for ps_i in range(2):
                        specs.append((kind, col0, ncol, ps_i, idx))
                wb_of = {}

                gorder = [(g[0], g[3]) for g in groups]
                ginfo = {(g[0], g[3]): g for g in groups}

                def ensure_w(gk):
                    if gk not in wb_of:
                        kind, col0, ncol, idx = ginfo[gk]
                        wb_of[gk] = load_w_group(wst, wbf, w_in[:, col0:col0 + ncol])

                def make(sl, sp):
                    kind, col0, ncol, ps_i, idx = sp
                    ensure_w((kind, idx))
                    gi_ = gorder.index((kind, idx))
                    if ps_i == 0 and gi_ + 1 < len(gorder):
                        ensure_w(gorder[gi_ + 1])
                    return job(sl, (kind, wb_of[(kind, idx)], ps_i, idx))

                run_rr(specs, make, KJ, stagger=cfg.get('stagA', 0))
                p.barrier()

        if "projA" in cfg["phases"]:
            phase_A(False)
            phase_A(True)

        def phase_B():
            with ExitStack() as sB:
                NEG = -30000.0
                cst = sb(sB, "tricst", [128, 6, 128], F32)
                p.dma('sp', cst[:, :, :], c_tri.rearrange("c p f -> p c f"), writes=['tricst'])
                ones_f = sb(sB, "ones_f", [128, 128], F32)
                p.op('pool', lambda e: e.memset(ones_f[:], 1.0), writes=['ones_f'])
                mskf = sb(sB, "mskf", [128, 3, 128], F32)
                p.dma('sp', mskf[:, :, :], c_msk.rearrange("c p f -> p c f"), writes=['mskf'])
                m32h = sb(sB, "m32h", [128, 8, 128], BF16)
                mo64h = sb(sB, "mo64h", [128, 8, 128], BF16)
                mo128h = sb(sB, "mo128h", [128, 8, 128], BF16)
                identh = sb(sB, "identh", [128, 8, 128], BF16)
                for i_, t_ in enumerate((m32h, mo64h, mo128h)):
                    for h in range(NH):
                        p.op('dve', lambda e, i_=i_, t_=t_, h=h: e.tensor_copy(t_[:, h, :], mskf[:, i_, :]), reads=['mskf'], writes=['mskb'])
                for h in range(NH):
                    p.op('dve', lambda e, h=h: e.tensor_copy(identh[:, h, :], ident_f[:, :]), reads=['ident_f'], writes=['mskb'])
                tri = {0: cst[:, 0, :], 1: cst[:, 1, :]}
                nmd = {0: cst[:, 2, :], 1: cst[:, 4, :]}
                nme = {0: cst[:, 3, :], 1: cst[:, 5, :]}
                tt = sb(sB, "tt", [128, 32, 32], F32)
                Gp = [sb(sB, f"Gp{d}", [128, 32, 8], F32) for d in range(2)]
                Glb = sb(sB, "Glb", [128, 32, 16], F32)
                eG = [sb(sB, f"eG{d}", [128, 32, 8], F32) for d in range(2)]
                kd = [sb(sB, f"kd{d}", [128, 32, 8], F32) for d in range(2)]
                gam = sb(sB, "gam", [128, 32, 16], F32)
                bneg = [sb(sB, f"bneg{d}", [128, 32, 8], F32) for d in range(2)]
                beg = [sb(sB, f"beg{d}", [128, 32, 8], F32) for d in range(2)]
                PF = [ps(sB, f"PF{i}", [128, 8, 128], F32) for i in range(3)]
                PB = [ps(sB, f"PB{i}", [128, 8, 128], BF16) for i in range(2)]
                pfc = [0]
                pbc = [0]

                def npf():
                    v = pfc[0] % 3
                    pfc[0] += 1
                    return v

                def npb():
                    v = pbc[0] % 2
                    pbc[0] += 1
                    return v

                HB = [128, 8, 128]
                PW = []
                for i in range(2):
                    d_ = {}
                    for nm, dt in (("rhsG", F32), ("tmp", F32), ("E", F32), ("N0", BF16), ("N1", BF16),
                                   ("M0", BF16), ("M1", BF16), ("Q0", BF16), ("Q1", BF16), ("ktok", BF16),
                                   ("kc", BF16), ("vc", BF16), ("qc", BF16)):
                        d_[nm] = sb(sB, f"pw{i}_{nm}", HB, dt)
                    d_["eGr"] = d_["rhsG"]
                    d_["kbg"] = d_["N1"]
                    PW.append(d_)
                SW = []
                for i in range(3):
                    d_ = {}
                    for nm, dt in (("vb", F32), ("kbgT", BF16), ("kdec", BF16), ("attnT", BF16), ("qdT", BF16), ("Q", BF16)):
                        d_[nm] = sb(sB, f"sw{i}_{nm}", HB, dt)
                    SW.append(d_)
                S = sb(sB, "S", HB, F32)
                rb = sb(sB, "rb", HB, BF16)
                Sb = sb(sB, "Sb", HB, BF16)
                vnb = sb(sB, "vnb", HB, BF16)
                osum = sb(sB, "osum", HB, F32)
                oBt = sb(sB, "oBt", HB, F32)
                oss = sb(sB, "oss", [128, 8], F32)
                onb = sb(sB, "onb", HB, BF16)
                zsc = sb(sB, "zsc", HB, BF16)
                ogc = sb(sB, "ogc", HB, BF16)
                identb3 = ident_b[:, :].unsqueeze(1).broadcast_to(HB)

                def bc_j(ap2):
                    return ap2.unsqueeze(2).broadcast_to(HB)

                def bc_h(ap2):
                    return ap2.unsqueeze(1).broadcast_to(HB)

                def prep_half(own):
                    with ExitStack() as sh:
                        bgsb = sb(sh, "bgsb", [32, HALF], F32)
                        prep_half_(own, bgsb)

                def prep_half_(own, bgsb):
                    base = 0 if own else HALF
                    p.dma('sp', bgsb[:, :], BGS[:, base:base + HALF], reads=[('BGS', own, 0), ('BGS', own, 1)], writes=['bgsb'])
                    pf = PF[npf()]
                    pfv = pf[:, :, :].rearrange("p h j -> p (h j)")
                    for n in range(32):
                        p.op('pe', lambda e, n=n: e.transpose(pfv[:, n * 32:(n + 1) * 32], bgsb[0:32, n * 128:(n + 1) * 128], ident_f[0:32, 0:32]),
                             reads=['bgsb', 'ident_f'], writes=['PFx'])
                    p.op('dve', lambda e: e.tensor_copy(tt[:, :, :].rearrange("p c r -> p (c r)"), pfv), reads=['PFx'], writes=['tt'])
                    for d in range(2):
                        p.op('pe', lambda e, d=d: e.matmul(pfv[:, 0:256], tri[d], tt[:, :, 16 + 8 * d:24 + 8 * d], start=True, stop=True),
                             reads=['tt', 'tricst'], writes=['PFx'])
                        p.op('dve', lambda e, d=d: e.tensor_copy(Gp[d][:, :, :].rearrange("p c h -> p (c h)"), pfv[:, 0:256]),
                             reads=['PFx'], writes=[('Gp', d)])
                    p.op('pe', lambda e: e.matmul(pfv[:, 0:512], ones_f[:, :], tt[:, :, 16:32], start=True, stop=True),
                         reads=['tt', 'ones_f'], writes=['PFx'])
                    p.op('dve', lambda e: e.tensor_copy(Glb[:, :, :].rearrange("p c h -> p (c h)"), pfv[:, 0:512]), reads=['PFx'], writes=['Glb'])
                    p.op('act', lambda e: e.activation(out=gam[:, :, :], in_=Glb[:, :, :], func=AF.Exp), reads=['Glb'], writes=['gam'])
                    for d in range(2):
                        p.op('act', lambda e, d=d: e.activation(out=eG[d][:, :, :], in_=Gp[d][:, :, :], func=AF.Exp), reads=[('Gp', d)], writes=[('eG', d)])
                        p.op('dve', lambda e, d=d: e.tensor_tensor(out=kd[d][:, :, :], in0=Glb[:, :, 8 * d:8 * d + 8], in1=Gp[d][:, :, :], op=ALU.subtract),
                             reads=['Glb', ('Gp', d)], writes=[('kd', d)])
                        p.op('act', lambda e, d=d: e.activation(out=kd[d][:, :, :], in_=kd[d][:, :, :], func=AF.Exp), reads=[('kd', d)], writes=[('kd', d)])
                        p.op('dve', lambda e, d=d: e.tensor_scalar(out=bneg[d][:, :, :], in0=tt[:, :, 8 * d:8 * d + 8], scalar1=-1.0, scalar2=None, op0=ALU.mult),
                             reads=['tt'], writes=[('bneg', d)])
                        p.op('dve', lambda e, d=d: e.tensor_tensor(out=beg[d][:, :, :], in0=tt[:, :, 8 * d:8 * d + 8], in1=eG[d][:, :, :], op=ALU.mult),
                             reads=['tt', ('eG', d)], writes=[('beg', d)])
                    p.barrier()

                def prep_gen(u, n, d, own, need_out):
                    pw = PW[u % 2]
                    sw = SW[u % 3]
                    P_ = f"pw{u % 2}"
                    S_ = f"sw{u % 3}"
                    t0 = (0 if own else HALF) + n * 128
                    kq = [('KS', h, own, (n * 128) // W) for h in range(NH)]
                    kv = [('VS', h, own, (n * 128) // W) for h in range(NH)]
                    p.dma('sp', pw["kc"][:, :, :], KS[:, t0:t0 + 128].rearrange("(h d) t -> d h t", d=128), reads=kq, writes=[P_ + "kc"])
                    p.dma('sp', pw["vc"][:, :, :], VS[:, t0:t0 + 128].rearrange("(h d) t -> d h t", d=128), reads=kv, writes=[P_ + "vc"])
                    if need_out:
                        p.dma('sp', pw["qc"][:, :, :], QS[:, t0:t0 + 128].rearrange("(h d) t -> d h t", d=128),
                              reads=[('QS', h, (n * 128) // W) for h in range(NH)], writes=[P_ + "qc"])
                    p.op('dve', lambda e: e.tensor_tensor(out=pw["rhsG"][:, :, :], in0=bc_h(tri[d]), in1=bc_j(tt[:, n, 16 + 8 * d:24 + 8 * d]), op=ALU.mult),
                         reads=['tt', 'tricst'], writes=[P_ + "rhsG"])
                    yield
                    a = npf()
                    for hh in range(2):
                        p.op('pe', lambda e, hh=hh: e.matmul(PF[a][:, 4 * hh:4 * hh + 4, :], ones_f[:, :], pw["rhsG"][:, 4 * hh:4 * hh + 4, :], start=True, stop=True),
                             reads=[P_ + "rhsG", 'ones_f'], writes=[('PF', a)])
                    p.op('dve', lambda e: e.tensor_tensor(out=pw["tmp"][:, :, :], in0=PF[a][:, :, :], in1=bc_j(Gp[d][:, n, :]), op=ALU.subtract),
                         reads=[('PF', a), ('Gp', d)], writes=[P_ + "tmp"])
                    if need_out:
                        p.op('act', lambda e: e.activation(out=pw["eGr"][:, :, :], in_=PF[a][:, :, :], func=AF.Exp), reads=[('PF', a)], writes=[P_ + "rhsG"])
                    yield
                    if need_out:
                        p.op('pool', lambda e: e.tensor_tensor(out=pw["E"][:, :, :], in0=pw["tmp"][:, :, :], in1=bc_h(nme[d]), op=ALU.add),
                             reads=[P_ + "tmp", 'tricst'], writes=[P_ + "E"])
                        p.op('act', lambda e: e.activation(out=pw["E"][:, :, :], in_=pw["E"][:, :, :], func=AF.Exp), reads=[P_ + "E"], writes=[P_ + "E"])
                    p.op('pool', lambda e: e.tensor_tensor(out=pw["tmp"][:, :, :], in0=bc_h(nmd[d]), in1=pw["tmp"][:, :, :], op=ALU.subtract),
                         reads=[P_ + "tmp", 'tricst'], writes=[P_ + "tmp"])
                    p.op('act', lambda e: e.activation(out=pw["tmp"][:, :, :], in_=pw["tmp"][:, :, :], func=AF.Exp), reads=[P_ + "tmp"], writes=[P_ + "tmp"])
                    yield
                    a = npf()
                    for h in range(NH):
                        p.op('pe', lambda e, h=h: e.matmul(PF[a][:, h, :], pw["kc"][:, h, :], pw["kc"][:, h, :], start=True, stop=True),
                             reads=[P_ + "kc"], writes=[('PF', a)])
                    p.op('pool', lambda e: e.tensor_tensor(out=pw["tmp"][:, :, :], in0=pw["tmp"][:, :, :], in1=bc_j(bneg[d][:, n, :]), op=ALU.mult),
                         reads=[P_ + "tmp", ('bneg', d)], writes=[P_ + "tmp"])
                    p.op('dve', lambda e: e.tensor_tensor(out=pw["N0"][:, :, :], in0=PF[a][:, :, :], in1=pw["tmp"][:, :, :], op=ALU.mult),
                         reads=[('PF', a), P_ + "tmp"], writes=[P_ + "N0"])
                    yield
                    b = npb()
                    for h in range(NH):
                        p.op('pe', lambda e, h=h: e.transpose(PB[b][:, h, :], pw["N0"][:, h, :], ident_b[:, :]),
                             reads=[P_ + "N0", 'ident_b'], writes=[('PB', b)])
                    p.op('act', lambda e: e.activation(out=pw["M0"][:, :, :], in_=PB[b][:, :, :], func=AF.Copy), reads=[('PB', b)], writes=[P_ + "M0"])
                    yield
                    if need_out:
                        a = npf()
                        for h in range(NH):
                            p.op('pe', lambda e, h=h: e.matmul(PF[a][:, h, :], pw["kc"][:, h, :], pw["qc"][:, h, :], start=True, stop=True),
                                 reads=[P_ + "kc", P_ + "qc"], writes=[('PF', a)])
                        p.op('dve', lambda e: e.tensor_tensor(out=sw["attnT"][:, :, :], in0=PF[a][:, :, :], in1=pw["E"][:, :, :], op=ALU.mult),
                             reads=[('PF', a), P_ + "E"], writes=[S_ + "attnT"])
                        p.op('pool', lambda e: e.tensor_tensor(out=sw["qdT"][:, :, :], in0=pw["qc"][:, :, :], in1=pw["eGr"][:, :, :], op=ALU.mult),
                             reads=[P_ + "qc", P_ + "rhsG"], writes=[S_ + "qdT"])
                        yield
                    def mmg(dst_ps, lk, rk, lkey, rkey):
                        for h in range(NH):
                            p.op('pe', lambda e, h=h: e.matmul(PF[dst_ps][:, h, :], lk[:, h, :], rk[:, h, :], start=True, stop=True),
                                 reads=[lkey, rkey], writes=[('PF', dst_ps)])
                    N_, M_, T_, W_ = pw["N0"], pw["M0"], pw["Q0"], pw["Q1"]
                    kN, kM, kT, kW = P_ + "N0", P_ + "M0", P_ + "Q0", P_ + "Q1"
                    No1, Mo1, No2 = pw["N1"], pw["M1"], pw["ktok"]
                    kNo1, kMo1, kNo2 = P_ + "N1", P_ + "M1", P_ + "ktok"
                    p.op('dve', lambda e: e.tensor_tensor(out=No1[:, :, :], in0=N_[:, :, :], in1=mo64h[:, :, :], op=ALU.mult), reads=[kN, 'mskb'], writes=[kNo1])
                    p.op('dve', lambda e: e.tensor_tensor(out=Mo1[:, :, :], in0=M_[:, :, :], in1=mo64h[:, :, :], op=ALU.mult), reads=[kM, 'mskb'], writes=[kMo1])
                    p.op('dve', lambda e: e.tensor_tensor(out=No2[:, :, :], in0=N_[:, :, :], in1=mo128h[:, :, :], op=ALU.mult), reads=[kN, 'mskb'], writes=[kNo2])
                    p.op('dve', lambda e: e.tensor_tensor(out=N_[:, :, :], in0=N_[:, :, :], in1=m32h[:, :, :], op=ALU.mult), reads=[kN, 'mskb'], writes=[kN])
                    p.op('dve', lambda e: e.tensor_tensor(out=M_[:, :, :], in0=M_[:, :, :], in1=m32h[:, :, :], op=ALU.mult), reads=[kM, 'mskb'], writes=[kM])
                    p.op('dve', lambda e: e.tensor_tensor(out=T_[:, :, :], in0=N_[:, :, :], in1=identh[:, :, :], op=ALU.add), reads=[kN, 'mskb'], writes=[kT])
                    p.op('dve', lambda e: e.tensor_tensor(out=W_[:, :, :], in0=M_[:, :, :], in1=identh[:, :, :], op=ALU.add), reads=[kM, 'mskb'], writes=[kW])
                    yield
                    for lvl in range(1, 5):
                        a1, a2 = npf(), npf()
                        mmg(a1, M_, N_, kM, kN)
                        mmg(a2, N_, M_, kN, kM)
                        p.op('act', lambda e: e.activation(out=N_[:, :, :], in_=PF[a1][:, :, :], func=AF.Copy), reads=[('PF', a1)], writes=[kN])
                        p.op('act', lambda e: e.activation(out=M_[:, :, :], in_=PF[a2][:, :, :], func=AF.Copy), reads=[('PF', a2)], writes=[kM])
                        yield
                        a1, a2 = npf(), npf()
                        mmg(a1, M_, T_, kM, kT)
                        mmg(a2, N_, W_, kN, kW)
                        p.op('dve', lambda e: e.tensor_tensor(out=T_[:, :, :], in0=PF[a1][:, :, :], in1=T_[:, :, :], op=ALU.add), reads=[('PF', a1), kT], writes=[kT])
                        p.op('dve', lambda e: e.tensor_tensor(out=W_[:, :, :], in0=PF[a2][:, :, :], in1=W_[:, :, :], op=ALU.add), reads=[('PF', a2), kW], writes=[kW])
                        yield
                    a1, a2 = npf(), npf()
                    mmg(a1, Mo1, T_, kMo1, kT)
                    mmg(a2, No1, W_, kNo1, kW)
                    p.op('act', lambda e: e.activation(out=N_[:, :, :], in_=PF[a1][:, :, :], func=AF.Copy), reads=[('PF', a1)], writes=[kN])
                    p.op('dve', lambda e: e.tensor_copy(M_[:, :, :], PF[a2][:, :, :]), reads=[('PF', a2)], writes=[kM])
                    yield
                    a1, a2 = npf(), npf()
                    mmg(a1, W_, N_, kW, kN)
                    mmg(a2, T_, M_, kT, kM)
                    p.op('dve', lambda e: e.tensor_tensor(out=T_[:, :, :], in0=PF[a1][:, :, :], in1=T_[:, :, :], op=ALU.add), reads=[('PF', a1), kT], writes=[kT])
                    p.op('dve', lambda e: e.tensor_tensor(out=W_[:, :, :], in0=PF[a2][:, :, :], in1=W_[:, :, :], op=ALU.add), reads=[('PF', a2), kW], writes=[kW])
                    yield
                    a1 = npf()
                    mmg(a1, No2, W_, kNo2, kW)
                    p.op('act', lambda e: e.activation(out=M_[:, :, :], in_=PF[a1][:, :, :], func=AF.Copy), reads=[('PF', a1)], writes=[kM])
                    yield
                    a1 = npf()
                    mmg(a1, T_, M_, kT, kM)
                    p.op('dve', lambda e: e.tensor_tensor(out=sw["Q"][:, :, :], in0=PF[a1][:, :, :], in1=W_[:, :, :], op=ALU.add), reads=[('PF', a1), kW], writes=[S_ + "Q"])
                    yield

                    b = npb()
                    for h in range(NH):
                        p.op('pe', lambda e, h=h: e.transpose(PB[b][:, h, :], pw["kc"][:, h, :], ident_b[:, :]),
                             reads=[P_ + "kc", 'ident_b'], writes=[('PB', b)])
                    p.op('act', lambda e: e.activation(out=pw["ktok"][:, :, :], in_=PB[b][:, :, :], func=AF.Copy), reads=[('PB', b)], writes=[P_ + "ktok"])
                    p.op('pool', lambda e: e.tensor_tensor(out=pw["kbg"][:, :, :], in0=pw["ktok"][:, :, :], in1=bc_j(beg[d][:, n, :]), op=ALU.mult),
                         reads=[P_ + "ktok", ('beg', d)], writes=[P_ + "N1"])
                    p.op('pool', lambda e: e.tensor_tensor(out=sw["kdec"][:, :, :], in0=pw["ktok"][:, :, :], in1=bc_j(kd[d][:, n, :]), op=ALU.mult),
                         reads=[P_ + "ktok", ('kd', d)], writes=[S_ + "kdec"])
                    yield
                    b = npb()
                    for h in range(NH):
                        p.op('pe', lambda e, h=h: e.transpose(PB[b][:, h, :], pw["kbg"][:, h, :], ident_b[:, :]),
                             reads=[P_ + "N1", 'ident_b'], writes=[('PB', b)])
                    p.op('act', lambda e: e.activation(out=sw["kbgT"][:, :, :], in_=PB[b][:, :, :], func=AF.Copy), reads=[('PB', b)], writes=[S_ + "kbgT"])
                    yield
                    b = npb()
                    for h in range(NH):
                        p.op('pe', lambda e, h=h: e.transpose(PB[b][:, h, :], pw["vc"][:, h, :], ident_b[:, :]),
                             reads=[P_ + "vc", 'ident_b'], writes=[('PB', b)])
                    p.op('dve', lambda e: e.tensor_tensor(out=sw["vb"][:, :, :], in0=PB[b][:, :, :], in1=bc_j(tt[:, n, 8 * d:8 * d + 8]), op=ALU.mult),
                         reads=[('PB', b), 'tt'], writes=[S_ + "vb"])
                    yield

                def seq_gen(u, n, d, own, need_out, final_dir):
                    sw = SW[u % 3]
                    S_ = f"sw{u % 3}"
                    r0 = n * 128
                    if need_out and final_dir:
                        p.dma('sp', oBt[:, :, :].rearrange("p h e -> p (h e)"), OBS[r0:r0 + 128, :], reads=[('OBS', n)], writes=['oBt'])
                        p.dma('sp', zsc[:, :, :], ZS[:, r0:r0 + 128].rearrange("(h d) t -> d h t", d=128),
                              reads=[('ZS', h, r0 // W) for h in range(NH)], writes=['zsc'])
                    a = npf()
                    for h in range(NH):
                        p.op('pe', lambda e, h=h: e.matmul(PF[a][:, h, :], sw["kbgT"][:, h, :], Sb[:, h, :], start=True, stop=True),
                             reads=[S_ + "kbgT", 'Sb'], writes=[('PF', a)])
                    p.op('dve', lambda e: e.tensor_tensor(out=rb[:, :, :], in0=sw["vb"][:, :, :], in1=PF[a][:, :, :], op=ALU.subtract),
                         reads=[('PF', a), S_ + "vb"], writes=['rb'])
                    yield
                    a = npf()
                    for h in range(NH):
                        p.op('pe', lambda e, h=h: e.matmul(PF[a][:, h, :], sw["Q"][:, h, :], rb[:, h, :], start=True, stop=True),
                             reads=[S_ + "Q", 'rb'], writes=[('PF', a)])
                    p.op('act', lambda e: e.activation(out=vnb[:, :, :], in_=PF[a][:, :, :], func=AF.Copy), reads=[('PF', a)], writes=['vnb'])
                    yield
                    if need_out:
                        ao = npf()
                        for h in range(NH):
                            p.op('pe', lambda e, h=h: e.matmul(PF[ao][:, h, :], sw["qdT"][:, h, :], Sb[:, h, :], start=True, stop=False),
                                 reads=[S_ + "qdT", 'Sb'], writes=[('PF', ao)])
                            p.op('pe', lambda e, h=h: e.matmul(PF[ao][:, h, :], sw["attnT"][:, h, :], vnb[:, h, :], start=False, stop=True),
                                 reads=[S_ + "attnT", 'vnb'], writes=[('PF', ao)])
                    a = npf()
                    for h in range(NH):
                        p.op('pe', lambda e, h=h: e.matmul(PF[a][:, h, :], sw["kdec"][:, h, :], vnb[:, h, :], start=True, stop=True),
                             reads=[S_ + "kdec", 'vnb'], writes=[('PF', a)])
                    p.op('pool', lambda e: e.tensor_tensor(out=S[:, :, :], in0=S[:, :, :], in1=bc_j(gam[:, n, 8 * d:8 * d + 8]), op=ALU.mult),
                         reads=['S', 'gam'], writes=['S'])
                    p.op('dve', lambda e: e.tensor_tensor(out=S[:, :, :], in0=S[:, :, :], in1=PF[a][:, :, :], op=ALU.add),
                         reads=['S', ('PF', a)], writes=['S'])
                    p.op('act', lambda e: e.activation(out=Sb[:, :, :], in_=S[:, :, :], func=AF.Copy), reads=['S'], writes=['Sb'])
                    if need_out:
                        if not final_dir:
                            p.op('act', lambda e: e.activation(out=osum[:, :, :], in_=PF[ao][:, :, :], func=AF.Copy), reads=[('PF', ao)], writes=['osum'])
                        else:
                            p.op('dve', lambda e: e.tensor_tensor(out=osum[:, :, :], in0=PF[ao][:, :, :], in1=oBt[:, :, :], op=ALU.add),
                                 reads=[('PF', ao), 'oBt'], writes=['osum'])
                    yield
                    if need_out:
                        if not final_dir:
                            p.dma('act', OBS[r0:r0 + 128, :], osum[:, :, :].rearrange("p h e -> p (h e)"), reads=['osum'], writes=[('OBS', n)], key='osum')
                        else:
                            p.op('pool', lambda e: e.tensor_tensor(out=oBt[:, :, :], in0=osum[:, :, :], in1=osum[:, :, :], op=ALU.mult),
                                 reads=['osum'], writes=['oBt'])
                            p.op('dve', lambda e: e.tensor_reduce(out=oss[:, :], in_=oBt[:, :, :], axis=AX.X, op=ALU.add), reads=['oBt'], writes=['oss'])
                            p.op('act', lambda e: e.activation(out=oss[:, :], in_=oss[:, :], func=AF.Ln, scale=1.0 / 128, bias=EPS), reads=['oss'], writes=['oss'])
                            p.op('act', lambda e: e.activation(out=oss[:, :], in_=oss[:, :], func=AF.Exp, scale=-0.5), reads=['oss'], writes=['oss'])
                            p.op('pool', lambda e: e.tensor_tensor(out=onb[:, :, :], in0=osum[:, :, :], in1=bc_j(oss[:, :]), op=ALU.mult),
                                 reads=['osum', 'oss'], writes=['onb'])
                            b = npb()
                            for h in range(NH):
                                p.op('pe', lambda e, h=h: e.transpose(PB[b][:, h, :], onb[:, h, :], ident_b[:, :]),
                                     reads=['onb', 'ident_b'], writes=[('PB', b)])
                            p.op('dve', lambda e: e.tensor_tensor(out=ogc[:, :, :], in0=PB[b][:, :, :], in1=zsc[:, :, :], op=ALU.mult),
                                 reads=[('PB', b), 'zsc'], writes=['ogc'])
                            p.dma('sp', OG[:, r0:r0 + 128].rearrange("(h d) t -> d h t", d=128), ogc[:, :, :], reads=['ogc'], writes=['OGall'], key='ogc')
                        yield

                def run_units(units):
                    preps = {}
                    done_prep = set()
                    nxt_prep = 0
                    cur_seq = None
                    cur_u = 0
                    nun = len(units)
                    while cur_u < nun:
                        while nxt_prep < nun and nxt_prep <= cur_u + 2 and len(preps) < cfg.get('dn_par', 2) and (nxt_prep - 2) not in preps:
                            n, d, own, no, fd = units[nxt_prep]
                            preps[nxt_prep] = prep_gen(nxt_prep, n, d, own, no)
                            nxt_prep += 1
                        if cur_seq is None and cur_u in done_prep:
                            n, d, own, no, fd = units[cur_u]
                            cur_seq = seq_gen(cur_u, n, d, own, no, fd)
                        progressed = False
                        if cur_seq is not None:
                            try:
                                next(cur_seq)
                            except StopIteration:
                                cur_seq = None
                                cur_u += 1
                            progressed = True
                        for uu in sorted(list(preps.keys())):
                            try:
                                next(preps[uu])
                            except StopIteration:
                                del preps[uu]
                                done_prep.add(uu)
                            progressed = True
                        assert progressed or cur_u >= nun

                p.op('pool', lambda e: e.memset(S[:, :, :], 0.0), writes=['S'])
                p.op('pool', lambda e: e.memset(Sb[:, :, :], 0.0), writes=['Sb'])
                nck = cfg.get("nchunks", 32)
                if cfg.get("dn_test") == "A":
                    prep_half(True)
                    if "prep_steps" in cfg:
                        g = prep_gen(0, 0, 0, True, True)
                        for _ in range(cfg["prep_steps"]):
                            next(g)
                        p.barrier()
                        return
                    run_units([(n, 0, True, True, False) for n in range(nck)])
                    p.barrier()
                    return
                prep_half(False)
                run_units([(n, 1, False, False, False) for n in range(nck - 1, -1, -1)])
                p.barrier()
                prep_half(True)
                run_units([(n, 1, True, True, False) for n in range(nck - 1, -1, -1)])
                p.barrier()
                p.op('pool', lambda e: e.memset(S[:, :, :], 0.0), writes=['S'])
                p.op('pool', lambda e: e.memset(Sb[:, :, :], 0.0), writes=['Sb'])
                run_units([(n, 0, True, True, True) for n in range(nck)])
                p.barrier()

        if "dn" in cfg["phases"]:
            phase_B()

        N1, N2, NF = 97, 128, 97 * 128
        NEXT = 24608
        NPAD = 12800

        def phase_C1():
            with ExitStack() as sC:
                hd2 = sb(sC, "hd2", [64, NPAD], F32)
                w1t = sb(sC, "w1t", [33, 64], F32)
                w2t = sb(sC, "w2t", [64, 64], F32)
                w3t = sb(sC, "w3t", [64, 3, D], F32)
                frt = sb(sC, "frt", [64, 1], F32)
                fb1 = sb(sC, "fb1", [64, 1], F32)
                fb2 = sb(sC, "fb2", [64, 1], F32)
                ldt = sb(sC, "ldt", [128, 3, 8], F32)
                rate = sb(sC, "rate", [128, 3, 8], F32)
                nrate = sb(sC, "nrate", [128, 3, 8], F32)
                dl = sb(sC, "dl", [128, 512], F32)
                tp0 = sb(sC, "tp0", [128, 25], F32)
                bq = sb(sC, "bq", [128, 25], F32)
                zp = [sb(sC, f"zp{i}", [33, 512], F32) for i in range(4)]
                arg = [sb(sC, f"arg{i}", [64, 512], F32) for i in range(4)]
                kint = [sb(sC, f"kint{i}", [64, 512], mybir.dt.int32) for i in range(4)]
                kf = [sb(sC, f"kf{i}", [64, 512], F32) for i in range(4)]
                h1 = [sb(sC, f"h1_{i}", [64, 512], F32) for i in range(4)]
                win = [sb(sC, f"win{i}", [128, 512], F32) for i in range(4)]
                kl = [sb(sC, f"kl{i}", [128, NPAD], BF16) for i in range(2)]
                pm = [ps(sC, f"pm{i}", [128, 512], F32) for i in range(4)]
                PI = float(np.pi)
                p.dma('sp', w1t[:, :], hy_w1[:, :], writes=['w1t'])
                p.dma('sp', w2t[:, :], hy_w2[:, :], writes=['w2t'])
                p.dma('sp', w3t[:, :, :], hy_w3[:, :, :], writes=['w3t'])
                p.dma('sp', frt[:, :], hy_freq.rearrange("(p o) -> p o", o=1), writes=['frt'], allow_slow_non_contiguous=True)
                p.dma('sp', fb1[:, :], hy_b1.rearrange("(p o) -> p o", o=1), writes=['fb1'], allow_slow_non_contiguous=True)
                p.dma('sp', fb2[:, :], hy_b2.rearrange("(p o) -> p o", o=1), writes=['fb2'], allow_slow_non_contiguous=True)
                for s_ in range(3):
                    p.dma('sp', ldt[:, s_, :], hy_log_decay[s_, :].rearrange("(b p) -> p b", p=128), writes=[('ldt', s_)], allow_slow_non_contiguous=True)
                p.dma('sp', dl[:, :], c_dl[0, :].partition_broadcast(128), writes=['dl'])
                p.dma('sp', tp0[:, :], c_tp0.partition_broadcast(128), writes=['tp0'])
                p.op('dve', lambda e: e.tensor_tensor(out=fb1[:, :], in0=fb1[:, :], in1=frt[:, :], op=ALU.mult), reads=['fb1', 'frt'], writes=['fb1'])
                p.op('dve', lambda e: e.tensor_tensor(out=fb2[:, :], in0=fb2[:, :], in1=frt[:, :], op=ALU.mult), reads=['fb2', 'frt'], writes=['fb2'])
                p.op('act', lambda e: e.activation(out=rate[:, :, :], in_=ldt[:, :, :], func=AF.Exp), reads=[('ldt', i) for i in range(3)], writes=['rate'])
                p.op('dve', lambda e: e.tensor_scalar(out=nrate[:, :, :], in0=rate[:, :, :], scalar1=-1.0, scalar2=None, op0=ALU.mult), reads=['rate'], writes=['nrate'])
                pmc = [0]

                def npm():
                    v = pmc[0] % 4
                    pmc[0] += 1
                    return v

                def sin_layer(src_ps, src_key, fbias, fkey, dst, dst_keys, i):
                    p.op('dve', lambda e: e.tensor_scalar(out=arg[i][:, :], in0=src_ps, scalar1=frt[:, 0:1], scalar2=fbias[:, 0:1], op0=ALU.mult, op1=ALU.add),
                         reads=['frt', fkey, src_key], writes=[('arg', i)])
                    p.op('dve', lambda e: e.tensor_scalar(out=kint[i][:, :], in0=arg[i][:, :], scalar1=1.0 / (2 * PI), scalar2=64.0, op0=ALU.mult, op1=ALU.add),
                         reads=[('arg', i)], writes=[('kint', i)])
                    p.op('dve', lambda e: e.tensor_scalar(out=kf[i][:, :], in0=kint[i][:, :], scalar1=-64.0, scalar2=None, op0=ALU.add),
                         reads=[('kint', i)], writes=[('kf', i)])
                    p.op('dve', lambda e: e.scalar_tensor_tensor(out=arg[i][:, :], in0=kf[i][:, :], scalar=-2 * PI, in1=arg[i][:, :], op0=ALU.mult, op1=ALU.add),
                         reads=[('kf', i), ('arg', i)], writes=[('arg', i)])
                    p.op('act', lambda e: e.activation(out=dst, in_=arg[i][:, :], func=AF.Sin), reads=[('arg', i)], writes=dst_keys)

                for q in range(25):
                    i = q % 4
                    p.dma('sp', zp[i][:, :], c_zpos[:, q * 512:(q + 1) * 512], writes=[('zp', i)])
                    a = npm()
                    p.op('pe', lambda e: e.matmul(pm[a][0:64, :], w1t[:, :], zp[i][:, :], start=True, stop=True), reads=['w1t', ('zp', i)], writes=[('pm', a)])
                    sin_layer(pm[a][0:64, :], ('pm', a), fb1, 'fb1', h1[i][:, :], [('h1', i)], i)
                    a = npm()
                    p.op('pe', lambda e: e.matmul(pm[a][0:64, :], w2t[:, :], h1[i][:, :], start=True, stop=True), reads=['w2t', ('h1', i)], writes=[('pm', a)])
                    sin_layer(pm[a][0:64, :], ('pm', a), fb2, 'fb2', hd2[:, q * 512:(q + 1) * 512], [('hd2', q)], i)
                for cb in range(8):
                    kb = cb % 2
                    p.op('dve', lambda e: e.tensor_scalar(out=bq[:, 0:8], in0=tp0[:, 0:8], scalar1=nrate[:, 0, cb:cb + 1], scalar2=None, op0=ALU.mult),
                         reads=['tp0', 'nrate'], writes=['bq'])
                    p.op('dve', lambda e: e.tensor_scalar(out=bq[:, 8:25], in0=tp0[:, 8:25], scalar1=nrate[:, 1, cb:cb + 1], scalar2=None, op0=ALU.mult),
                         reads=['tp0', 'nrate'], writes=['bq'])
                    for q in range(25):
                        st = 0 if q < 8 else 1
                        a = npm()
                        wi = q % 4
                        p.op('pe', lambda e: e.matmul(pm[a][:, :], w3t[:, st, cb * 128:(cb + 1) * 128], hd2[:, q * 512:(q + 1) * 512], start=True, stop=True),
                             reads=['w3t', ('hd2', q)], writes=[('pm', a)])
                        sc = nrate[:, 0, cb:cb + 1] if q < 8 else rate[:, 1, cb:cb + 1]
                        p.op('act', lambda e: e.activation(out=win[wi][:, :], in_=dl[:, :], func=AF.Exp, scale=sc, bias=bq[:, q:q + 1]),
                             reads=['dl', 'rate', 'nrate', 'bq'], writes=[('win', wi)])
                        p.op('dve', lambda e: e.tensor_tensor(out=kl[kb][:, q * 512:(q + 1) * 512], in0=pm[a][:, :], in1=win[wi][:, :], op=ALU.mult),
                             reads=[('pm', a), ('win', wi)], writes=[('kl', kb, q)])
                    a = npm()
                    p.op('pe', lambda e: e.matmul(pm[a][:, 0:1], w3t[:, 2, cb * 128:(cb + 1) * 128], hd2[:, 0:1], start=True, stop=True),
                         reads=['w3t', ('hd2', 0)], writes=[('pm', a)])
                    p.op('dve', lambda e: e.tensor_copy(kl[kb][:, 0:1], pm[a][:, 0:1]), reads=[('pm', a)], writes=[('kl', kb, 0)])
                    p.op('pool', lambda e: e.memset(kl[kb][:, HALF:HALF + 129], 0.0), writes=[('kl', kb, 8)])
                    p.dma('sp', KLS[cb * 128:(cb + 1) * 128, :], kl[kb][:, 0:NF], reads=[('kl', kb, q) for q in range(25)], writes=[('KLS', cb)], key=('kl', kb))
                p.barrier()

        def phase_C2():
            KP = 4
            with ExitStack() as sC:
                EXT = sb(sC, "EXT", [128, NEXT], BF16)
                Xt = sb(sC, "Xt", [128, N2, 128], BF16)
                Kr = sb(sC, "Kr", [128, 128, 65], BF16)
                Ki = sb(sC, "Ki", [128, 128, 65], BF16)
                nKi = sb(sC, "nKi", [128, 128, 65], BF16)
                YE = sb(sC, "YE", [128, HALF], BF16)
                G0 = sb(sC, "G0", [128, HALF], BF16)
                ub = sb(sC, "ub", [128, HALF], BF16)
                yo = sb(sC, "yo", [128, HALF], F32)
                yhb = sb(sC, "yhb", [128, HALF], BF16)
                hbt = sb(sC, "hbt", [128, 8], F32)
                p.dma('sp', hbt[:, :], hy_bias.rearrange("(b p) -> p b", p=128), writes=['hbt'], allow_slow_non_contiguous=True)
                mats = {}
                stg = sb(sC, "mstg", [128, 194], F32)
                for nm, src, r, c in (("e1", c_e1, 97, 194), ("s2a", c_s2a, 128, 130), ("s2b", c_s2b, 128, 130),
                                      ("i1c", c_i1c, 97, 194), ("i1d", c_i1d, 97, 194), ("cw", c_cw, 65, 128), ("sw", c_sw, 65, 128)):
                    t_ = sb(sC, "m_" + nm, [128, c], BF16)
                    p.dma('sp', stg[0:r, 0:c], src[:, :], writes=['mstg'])
                    p.op('dve', lambda e: e.tensor_copy(t_[0:r, :], stg[0:r, 0:c]), reads=['mstg'], writes=['m_' + nm])
                    mats[nm] = t_
                Y1 = [sb(sC, f"Y1_{i}", [128, 2, 194], BF16) for i in range(KP)]
                Zs = [sb(sC, f"Zs{i}", [128, 2, 130], BF16) for i in range(KP)]
                Zt = [sb(sC, f"Zt{i}", [128, 2, 130], F32) for i in range(KP)]
                Zu = [sb(sC, f"Zu{i}", [128, 2, 130], F32) for i in range(KP)]
                Vs = [sb(sC, f"Vs{i}", [128, 2, 194], BF16) for i in range(KP)]
                Yc = [sb(sC, f"Yc{i}", [128, 8, N1], BF16) for i in range(2)]
                KP = 4
                PA_ = [ps(sC, f"PA_{i}", [128, 512], F32) for i in range(KP)]
                PB_ = [ps(sC, f"PB_{i}", [128, 512], F32) for i in range(KP)]
                P1 = PA_
                cnt = {'pr': 0, 'tp': 0}

                def to_Xt():
                    for g in range(N2 // 8):
                        b = cnt['tp'] % 4
                        cnt['tp'] += 1
                        pt = P1[b][:, :].bitcast(BF16).rearrange("p (a c) -> p a c", c=128)
                        for a in range(8):
                            t2 = g * 8 + a
                            p.op('pe', lambda e, a=a, t2=t2: e.transpose(pt[0:N1, a, :], EXT[:, 97 * t2:97 * t2 + 128 * (N1 - 1) + 1:128], ident_b[:, :]),
                                 reads=['EXT', 'ident_b'], writes=[('P1', b)])
                        eng = 'act' if g % 2 == 0 else 'dve'
                        if eng == 'act':
                            p.op('act', lambda e: e.activation(out=Xt[0:N1, g * 8:(g + 1) * 8, :], in_=pt[0:N1, 0:8, :], func=AF.Copy),
                                 reads=[('P1', b)], writes=[('Xt', g)])
                        else:
                            p.op('dve', lambda e: e.tensor_copy(Xt[0:N1, g * 8:(g + 1) * 8, :], pt[0:N1, 0:8, :]), reads=[('P1', b)], writes=[('Xt', g)])

                XtK = [('Xt', g) for g in range(N2 // 8)]
                XtC = [('Xtc', c0) for c0 in range(0, 128, 2)]

                def pair_gen(i, spec):
                    c0, is_filter = spec
                    kA, kB = ('P1', i), ('P2', i)
                    p1 = PA_[i][:, 0:388].rearrange("p (a f) -> p a f", f=194)
                    p2 = PB_[i][:, 0:260].rearrange("p (a f) -> p a f", f=130)
                    for a in range(2):
                        p.op('pe', lambda e, a=a: e.matmul(p1[:, a, :], Xt[0:N1, :, c0 + a], mats["e1"][0:N1, :], start=True, stop=True),
                             reads=XtK + [('Xtc', c0), 'm_e1'], writes=[kA])
                    p.op('act', lambda e: e.activation(out=Y1[i][:, :, :], in_=p1, func=AF.Copy), reads=[kA], writes=[('Y1', i)])
                    yield
                    for a in range(2):
                        p.op('pe', lambda e, a=a: e.matmul(p2[0:N1, a, :], Y1[i][:, a, 0:97], mats["s2a"][:, :], start=True, stop=False),
                             reads=[('Y1', i), 'm_s2a'], writes=[kB])
                        p.op('pe', lambda e, a=a: e.matmul(p2[0:N1, a, :], Y1[i][:, a, 97:194], mats["s2b"][:, :], start=False, stop=True),
                             reads=[('Y1', i), 'm_s2b'], writes=[kB])
                    if is_filter:
                        p.op('act', lambda e: e.activation(out=Kr[0:N1, c0:c0 + 2, :], in_=p2[0:N1, :, 0:65], func=AF.Copy), reads=[kB], writes=[('K', c0)])
                        p.op('dve', lambda e: e.tensor_copy(Ki[0:N1, c0:c0 + 2, :], p2[0:N1, :, 65:130]), reads=[kB], writes=[('K', c0)])
                        p.op('dve', lambda e: e.tensor_scalar(out=nKi[0:N1, c0:c0 + 2, :], in0=p2[0:N1, :, 65:130], scalar1=-1.0, scalar2=None, op0=ALU.mult),
                             reads=[kB], writes=[('K', c0)])
                        return
                    krb = Kr[0:N1, c0:c0 + 2, :].unsqueeze(2).broadcast_to([N1, 2, 2, 65])
                    p2v = p2[0:N1, :, :].rearrange("p a (r f) -> p a r f", r=2)
                    p.op('dve', lambda e: e.tensor_tensor(out=Zt[i][0:N1, :, :].rearrange("p a (r f) -> p a r f", r=2), in0=p2v, in1=krb, op=ALU.mult),
                         reads=[kB, ('K', c0)], writes=[('Zt', i)])
                    p.op('dve', lambda e: e.tensor_tensor(out=Zu[i][0:N1, :, 0:65], in0=p2[0:N1, :, 65:130], in1=nKi[0:N1, c0:c0 + 2, :], op=ALU.mult),
                         reads=[kB, ('K', c0)], writes=[('Zu', i)])
                    p.op('dve', lambda e: e.tensor_tensor(out=Zu[i][0:N1, :, 65:130], in0=p2[0:N1, :, 0:65], in1=Ki[0:N1, c0:c0 + 2, :], op=ALU.mult),
                         reads=[kB, ('K', c0)], writes=[('Zu', i)])
                    p.op('pool', lambda e: e.tensor_tensor(out=Zs[i][0:N1, :, :], in0=Zt[i][0:N1, :, :], in1=Zu[i][0:N1, :, :], op=ALU.add),
                         reads=[('Zt', i), ('Zu', i)], writes=[('Zs', i)])
                    yield
                    p3 = PA_[i][:, 0:388].rearrange("p (a f) -> p a f", f=194)
                    for a in range(2):
                        p.op('pe', lambda e, a=a: e.matmul(p3[0:65, a, :], Zs[i][0:N1, a, 0:65], mats["i1c"][0:N1, :], start=True, stop=False),
                             reads=[('Zs', i), 'm_i1c'], writes=[kA])
                        p.op('pe', lambda e, a=a: e.matmul(p3[0:65, a, :], Zs[i][0:N1, a, 65:130], mats["i1d"][0:N1, :], start=False, stop=True),
                             reads=[('Zs', i), 'm_i1d'], writes=[kA])
                    p.op('act', lambda e: e.activation(out=Vs[i][0:65, :, :], in_=p3[0:65, :, :], func=AF.Copy), reads=[kA], writes=[('Vs', i)])
                    yield
                    p4 = PB_[i][:, 0:256].rearrange("p (a f) -> p a f", f=128)
                    for a in range(2):
                        p.op('pe', lambda e, a=a: e.matmul(p4[0:N1, a, :], Vs[i][0:65, a, 0:97], mats["cw"][0:65, :], start=True, stop=False),
                             reads=[('Vs', i), 'm_cw'], writes=[kB])
                        p.op('pe', lambda e, a=a: e.matmul(p4[0:N1, a, :], Vs[i][0:65, a, 97:194], mats["sw"][0:65, :], start=False, stop=True),
                             reads=[('Vs', i), 'm_sw'], writes=[kB])
                    p.op('act', lambda e: e.activation(out=Xt[0:N1, :, c0:c0 + 2].rearrange("p t a -> p a t"), in_=p4[0:N1, :, :], func=AF.Copy),
                         reads=[kB], writes=[('Xtc', c0)])

                def from_Xt():
                    for g in range(N2 // 8):
                        b = cnt['tp'] % 4
                        cnt['tp'] += 1
                        pt = P1[b][:, :].bitcast(BF16)[:, 0:8 * 98].rearrange("p (a t) -> p a t", t=98)[:, :, 0:N1]
                        for a in range(8):
                            t2 = g * 8 + a
                            p.op('pe', lambda e, a=a, t2=t2: e.transpose(pt[:, a, :], Xt[0:N1, t2, :], ident_b[0:N1, 0:N1]),
                                 reads=XtK + XtC + ['ident_b'], writes=[('P1', b)])
                        yb = g % 2
                        p.op('act', lambda e: e.activation(out=Yc[yb][:, :, :], in_=pt, func=AF.Copy), reads=[('P1', b)], writes=[('Yc', yb)])
                        for a in range(8):
                            t2 = g * 8 + a
                            lo0 = 0
                            hi0 = min(N1, max(0, -(-(HALF - 97 * t2) // 128)))
                            if hi0 > lo0:
                                p.op('pool', lambda e, a=a, t2=t2, hi0=hi0: e.tensor_copy(YE[:, 97 * t2:97 * t2 + 128 * (hi0 - 1) + 1:128], Yc[yb][:, a, 0:hi0]),
                                     reads=[('Yc', yb)], writes=['YE'])
                            lo1 = max(0, -(-(NF - 97 * t2) // 128))
                            hi1 = min(N1, -(-(NF + HALF - 97 * t2) // 128))
                            if hi1 > lo1:
                                s0 = 97 * t2 + 128 * lo1 - NF
                                n_ = hi1 - lo1
                                p.op('pool', lambda e, a=a, s0=s0, n_=n_, lo1=lo1, hi1=hi1: e.tensor_copy(YE[:, s0:s0 + 128 * (n_ - 1) + 1:128], Yc[yb][:, a, lo1:hi1]),
                                     reads=[('Yc', yb)], writes=['YE'])

                nblk = cfg.get("hy_blocks", 8)
                for cb in range(nblk):
                    rows = slice(cb * 128, (cb + 1) * 128)
                    p.dma('sp', EXT[:, 0:NF], KLS[rows, :], reads=[('KLS', cb)], writes=['EXT'])
                    p.dma('sp', EXT[:, NF:NEXT], KLS[rows, 0:NEXT - NF], reads=[('KLS', cb)], writes=['EXT'], key='EXTb')
                    to_Xt()
                    run_rr([(c0, True) for c0 in range(0, 128, 2)], pair_gen, KP, stagger=cfg.get('stagC', 0))
                    p.dma('sp', EXT[:, 0:L], US[rows, :], reads=[('US', cb, o_, q_) for o_ in (True, False) for q_ in range(2)], writes=['EXT'])
                    p.dma('sp', EXT[:, NF:NF + L], US[rows, :], reads=[('US', cb, o_, q_) for o_ in (True, False) for q_ in range(2)], writes=['EXT'], key='EXTb')
                    p.op('pool', lambda e: e.memset(EXT[:, L:NF], 0.0), writes=['EXT'])
                    p.op('pool', lambda e: e.memset(EXT[:, NF + L:NEXT], 0.0), writes=['EXT'])
                    p.dma('sp', G0[:, :], G0S[rows, :], reads=[('G0S', cb, 0), ('G0S', cb, 1)], writes=['G0'])
                    p.dma('sp', ub[:, :], US[rows, 0:HALF], reads=[('US', cb, True, q_) for q_ in range(2)], writes=['ub'])
                    to_Xt()
                    run_rr([(c0, False) for c0 in range(0, 128, 2)], pair_gen, KP, stagger=cfg.get('stagC', 0))
                    from_Xt()
                    p.op('dve', lambda e: e.scalar_tensor_tensor(out=yo[:, :], in0=ub[:, :], scalar=hbt[:, cb:cb + 1], in1=YE[:, :], op0=ALU.mult, op1=ALU.add),
                         reads=['ub', 'hbt', 'YE'], writes=['yo'])
                    p.op('pool', lambda e: e.tensor_tensor(out=yhb[:, :], in0=yo[:, :], in1=G0[:, :], op=ALU.mult), reads=['yo', 'G0'], writes=['yhb'])
                    p.dma('pool', YH[rows, :], yhb[:, :], reads=['yhb'], writes=['YHall'], key='yhb')
                    if "YE" in cfg.get("dbg", ()):
                        p.dma('sp', dbg_out["YE"][rows, :], YE[:, :], reads=['YE'], writes=[('dbgYE', cb)], key='dbgYE')
                p.barrier()

        if "hyena" in cfg["phases"]:
            if "YE" in cfg.get("dbg", ()):
                ddbg("YE", [D, HALF], BF16)
            if "skipC1" not in cfg.get("dbg", ()):
                phase_C1()
            phase_C2()

        if "out" in cfg["phases"]:
            with ExitStack() as s4:
                wst4 = [sb(s4, f"w4st{i}", [128, 8, 512], F32) for i in range(2)]
                now_t = sb(s4, "now_t", [128, D], F32)
                p.dma('sp', now_t[:], norm_out_w.partition_broadcast(128), writes=['now_t'])
                wdn = sb(s4, "wdn", [128, 8, D], BF16)
                why = sb(s4, "why", [128, 8, D], BF16)
                wo = sb(s4, "wo", [128, 8, D], BF16)
                ci = 0
                for wsrc, wdst, nm in ((w_dn_out, wdn, 'wdn'), (w_hy_out, why, 'why'), (w_out, wo, 'wo')):
                    for hh in range(2):
                        b = ci % 2
                        ci += 1
                        p.dma('sp', wst4[b][:, :, :], wsrc[:, hh * 512:(hh + 1) * 512].rearrange("(k p) c -> p k c", p=128),
                              writes=[('w4st', b)])
                        p.op('pool', lambda e: e.tensor_copy(wdst[:, :, hh * 512:(hh + 1) * 512], wst4[b][:, :, :]),
                             reads=[('w4st', b)], writes=[(nm, hh)])
                ogb = [sb(s4, f"ogb{i}", [128, 8, 512], BF16) for i in range(2)]
                yhb = [sb(s4, f"yhb{i}", [128, 8, 512], BF16) for i in range(2)]
                gtb = [sb(s4, f"gtb{i}", [128, 16, 512], BF16) for i in range(2)]
                m1 = [sb(s4, f"m1_{i}", [128, 512], F32) for i in range(2)]
                m2 = [sb(s4, f"m2_{i}", [128, 512], F32) for i in range(2)]
                mb = [sb(s4, f"mb{i}", [128, 8, 512], BF16) for i in range(2)]
                xr = [sb(s4, f"xr{i}", [128, D], F32) for i in range(2)]
                res = [sb(s4, f"res{i}", [128, D], F32) for i in range(2)]
                junk4 = sb(s4, "junk4", [128, D], BF16)
                ss4 = [sb(s4, f"ss4_{i}", [128, 1], F32) for i in range(2)]
                ot = [sb(s4, f"ot{i}", [128, D], F32) for i in range(2)]
                pa = [ps(s4, f"pa{i}", [128, 512], F32) for i in range(2)]
                pb = [ps(s4, f"pb{i}", [128, 512], F32) for i in range(2)]
                pf = [ps(s4, f"pf{i}", [128, 512], F32) for i in range(4)]
                out_toks = []
                ti = 0
                for bk in range(8):
                    b = bk % 2
                    tsl = slice(bk * 512, (bk + 1) * 512)
                    p.dma('sp', ogb[b][:, :, :], OG[:, tsl].rearrange("(k p) t -> p k t", p=128),
                          reads=[('OG', k) for k in range(8)] + ['OGall'], writes=[('ogb', b)])
                    p.dma('sp', yhb[b][:, :, :], YH[:, tsl].rearrange("(k p) t -> p k t", p=128),
                          reads=[('YH', k) for k in range(8)] + ['YHall'], writes=[('yhb', b)])
                    p.dma('sp', gtb[b][:, :, :], GS[:, tsl].rearrange("(k p) t -> p k t", p=128),
                          reads=[('GS', k) for k in range(16)], writes=[('gtb', b)])
                    for dg in range(8):
                        q2 = dg % 2
                        for k in range(8):
                            p.op('pe', lambda e, k=k: e.matmul(pa[q2][:, :], wdn[:, k, dg * 128:(dg + 1) * 128], ogb[b][:, k, :],
                                                               start=(k == 0), stop=(k == 7)),
                                 reads=[('wdn', dg // 4), ('ogb', b)], writes=[('pa', q2)])
                        for k in range(8):
                            p.op('pe', lambda e, k=k: e.matmul(pb[q2][:, :], why[:, k, dg * 128:(dg + 1) * 128], yhb[b][:, k, :],
                                                               start=(k == 0), stop=(k == 7)),
                                 reads=[('why', dg // 4), ('yhb', b)], writes=[('pb', q2)])
                        p.op('dve', lambda e: e.tensor_tensor(out=m1[q2][:, :], in0=pa[q2][:, :], in1=gtb[b][:, dg, :], op=ALU.mult),
                             reads=[('pa', q2), ('gtb', b)], writes=[('m1', q2)])
                        p.op('dve', lambda e: e.tensor_tensor(out=m2[q2][:, :], in0=pb[q2][:, :], in1=gtb[b][:, 8 + dg, :], op=ALU.mult),
                             reads=[('pb', q2), ('gtb', b)], writes=[('m2', q2)])
                        p.op('pool', lambda e: e.tensor_tensor(out=mb[b][:, dg, :], in0=m1[q2][:, :], in1=m2[q2][:, :], op=ALU.add),
                             reads=[('m1', q2), ('m2', q2)], writes=[('mb', b, dg)])
                    for tt in range(4):
                        t0 = bk * 512 + tt * 128
                        r = ti % 2
                        ti += 1
                        p.dma('sp', xr[r][:, :], x[t0:t0 + 128, :], writes=[('xr', r)])
                        for nh in range(2):
                            fi = (2 * ti + nh) % 4
                            for k in range(8):
                                p.op('pe', lambda e, k=k: e.matmul(pf[fi][:, :], mb[b][:, k, tt * 128:(tt + 1) * 128],
                                                                   wo[:, k, nh * 512:(nh + 1) * 512], start=(k == 0), stop=(k == 7)),
                                     reads=[('mb', b, k), ('wo', nh)], writes=[('pf', fi)])
                            p.op('dve', lambda e: e.tensor_tensor(out=res[r][:, nh * 512:(nh + 1) * 512], in0=pf[fi][:, :],
                                                                  in1=xr[r][:, nh * 512:(nh + 1) * 512], op=ALU.add),
                                 reads=[('pf', fi), ('xr', r)], writes=[('res', r, nh)])
                        p.op('act', lambda e: e.activation(out=junk4[:, :], in_=res[r][:, :], func=AF.Square, accum_out=ss4[r][:, :]),
                             reads=[('res', r, 0), ('res', r, 1)], writes=['junk4', ('ss4', r)])
                        p.op('act', lambda e: e.activation(out=ss4[r][:, :], in_=ss4[r][:, :], func=AF.Ln, scale=1.0 / D, bias=EPS),
                             reads=[('ss4', r)], writes=[('ss4', r)])
                        p.op('act', lambda e: e.activation(out=ss4[r][:, :], in_=ss4[r][:, :], func=AF.Exp, scale=-0.5),
                             reads=[('ss4', r)], writes=[('ss4', r)])
                        p.op('dve', lambda e: e.scalar_tensor_tensor(out=ot[r][:, :], in0=res[r][:, :], scalar=ss4[r][:, :],
                                                                      in1=now_t[:, :], op0=ALU.mult, op1=ALU.mult),
                             reads=[('res', r, 0), ('res', r, 1), ('ss4', r), 'now_t'], writes=[('ot', r)])
                        out_toks.append(p.dma('sp', y[t0:t0 + 128, :], ot[r][:, :], reads=[('ot', r)], writes=[('y', t0)], key=('ot', r)))
                p.barrier()
        for nm in cfg.get("dump", ()):
            src = {"OG": OG, "YH": YH, "GS": GS, "QS": QS, "KS": KS, "VS": VS, "ZS": ZS, "BGS": BGS, "OBS": OBS, "US": US, "G0S": G0S, "KLS": KLS}[nm]
            dst = ddbg(nm, src.shape, src.dtype)
            nr = src.shape[0]
            step = 128 if nr >= 128 else nr
            for r0 in range(0, nr, step):
                p.dma('sp', dst[r0:r0 + step, :], src[r0:r0 + step, :], reads=[], writes=[('dump', nm, r0)], key=('dump', (r0 // step) % 4))
        p.barrier()
        print("instr counts", p.ninstr, "nsem", p.nsem)
    return nc


def _core_inputs(inputs, b, hf):
    xs = inputs["x"][b]
    w_in = inputs["w_in"][0]
    if hf == 1:
        xs = xs[::-1]
        perm = np.arange(INW)
        for base in (C_BETA, C_A):
            perm[base:base + 8] = np.arange(base + 8, base + 16)
            perm[base + 8:base + 16] = np.arange(base, base + 8)
        w_in = w_in[:, perm]
    dcw = inputs["dn_conv_w"][0]
    hcw = inputs["hy_conv_w"][0]
    alog = inputs["dn_a_log"][0]
    dtb = inputs["dn_dt_bias"][0]
    if hf == 1:
        dcw, hcw, alog, dtb = dcw[::-1], hcw[::-1], alog[::-1], dtb[::-1]
    w3 = inputs["hy_w3"][0]
    ld = inputs["hy_log_decay"][0]
    w3f, w3b, ldf, ldb = w3[:, :D], w3[:, D:], ld[:D], ld[D:]
    if hf == 0:
        w3s, lds = np.stack([w3f, w3b, w3f], axis=1), np.stack([ldf, ldb, ldf], axis=0)
    else:
        w3s, lds = np.stack([w3b, w3f, w3f], axis=1), np.stack([ldb, ldf, ldf], axis=0)
    m = {
        "hy_w1": np.ascontiguousarray(inputs["hy_w1"][0]), "hy_b1": np.ascontiguousarray(inputs["hy_b1"][0]),
        "hy_w2": np.ascontiguousarray(inputs["hy_w2"][0]), "hy_b2": np.ascontiguousarray(inputs["hy_b2"][0]),
        "hy_freq": np.ascontiguousarray(inputs["hy_freq"][0]), "hy_w3": np.ascontiguousarray(w3s),
        "hy_log_decay": np.ascontiguousarray(lds), "hy_bias": np.ascontiguousarray(inputs["hy_bias"][0]),
        "dn_conv_w": np.ascontiguousarray(dcw), "hy_conv_w": np.ascontiguousarray(hcw),
        "dn_a_log": np.ascontiguousarray(alog).reshape(16), "dn_dt_bias": np.ascontiguousarray(dtb).reshape(16),
        "dn_norm_w": np.ascontiguousarray(inputs["dn_norm_w"][0]),
        "x": np.ascontiguousarray(xs, dtype=np.float32),
        "norm_in_w": np.ascontiguousarray(inputs["norm_in_w"][0]),
        "w_in": np.ascontiguousarray(w_in),
        "w_dn_out": np.ascontiguousarray(inputs["w_dn_out"][0]),
        "w_hy_out": np.ascontiguousarray(inputs["w_hy_out"][0]),
        "w_out": np.ascontiguousarray(inputs["w_out"][0]),
        "norm_out_w": np.ascontiguousarray(inputs["norm_out_w"]),
    }
    m.update(_consts())
    return m


FULL_CFG = {"phases": ("projA", "hyproj", "gates", "dn", "hyena", "out")}


def kernel(**inputs):
    nc = build(FULL_CFG)
    in_maps = [_core_inputs(inputs, c // 2, c % 2) for c in range(8)]
    res = run_bass_kernel_spmd(nc, in_maps, core_ids=list(range(8)))
    out = np.empty((4, L, D), np.float32)
    for c in range(8):
        b, hf = c // 2, c % 2
        yc = res.results[c]["y"]
        if hf == 0:
            out[b, :HALF] = yc
        else:
            out[b, HALF:] = yc[::-1]
    return out
```

```python
import numpy as np
import concourse.bass as bass
import concourse.mybir as mybir
from concourse.bass_utils import run_bass_kernel_spmd
from contextlib import ExitStack

F32 = mybir.dt.float32
BF16 = mybir.dt.bfloat16
AF = mybir.ActivationFunctionType
ALU = mybir.AluOpType
AX = mybir.AxisListType

D = 1024
L = 8192
HALF = 4096
NH = 8
INW = 10272
EPS = 1e-6
C_QKV, C_DNZ, C_BETA, C_A, C_HXV, C_HZ, C_GATE = 0, 3072, 4096, 4112, 4128, 7200, 8224


class Prog:
    SEM_EPOCH = 20000

    def __init__(self, nc, es, same_engine_sync=True):
        self.nc = nc
        self.es = es
        self.engs = {'pe': nc.tensor, 'act': nc.scalar, 'dve': nc.vector, 'pool': nc.gpsimd, 'sp': nc.sync}
        self.sem = {}
        self.cnt = {}
        self.nsem = 0
        for e in self.engs:
            self._new_eng_sem(e)
        self.waited = {e: {} for e in self.engs}
        self.last_w = {}
        self.readers = {}
        self.dma_sem = {}
        self.same = same_engine_sync
        self.ninstr = {e: 0 for e in self.engs}
        self.last_tok = {}
        self.dma_toks = []

    def _mksem(self, name):
        self.nsem += 1
        return self.es.enter_context(self.nc.semaphore(f"{name}_{self.nsem}"))

    def _new_eng_sem(self, e):
        self.sem[e] = self._mksem("s" + e)
        self.cnt[e] = 0

    def _wait(self, e, tok):
        if tok is None:
            return
        sem, val, src = tok
        if src == e and (not self.same or e == 'pe'):
            return
        w = self.waited[e]
        k = id(sem)
        if k in w and w[k] >= val:
            return
        w[k] = val
        self.engs[e].wait_ge(sem, val)
        self.ninstr[e] += 1

    def _deps(self, e, reads, writes):
        for k in reads:
            self._wait(e, self.last_w.get(k))
        for k in writes:
            t = self.last_w.get(k)
            if t is not None and (t[2] != e or k in reads):
                self._wait(e, t)
            for t in self.readers.get(k, ()):
                if t[2] != e:
                    self._wait(e, t)

    def _commit(self, tok, reads, writes):
        for k in reads:
            self.readers.setdefault(k, []).append(tok)
        for k in writes:
            self.last_w[k] = tok
            self.readers[k] = []

    PSUM_NAMES = ('PF', 'PB', 'PFx', 'pp', 'pn', 'ph', 'pa', 'pb', 'pf', 'pTo', 'pTx', 'pm', 'P1', 'P2', 'P3', 'P4')

    def _excl(self, reads, writes):
        r2, w2 = [], list(writes)
        for k in reads:
            nm = k[0] if isinstance(k, tuple) else k
            if nm in self.PSUM_NAMES:
                if k not in w2:
                    w2.append(k)
            else:
                r2.append(k)
        return r2, w2

    def op(self, e, fn, reads=(), writes=()):
        reads, writes = self._excl(reads, writes)
        self._deps(e, reads, writes)
        if self.cnt[e] >= self.SEM_EPOCH:
            self._new_eng_sem(e)
        ins = fn(self.engs[e])
        self.cnt[e] += 1
        ins.then_inc(self.sem[e], 1)
        self.ninstr[e] += 1
        tok = (self.sem[e], self.cnt[e], e)
        self.last_tok[e] = tok
        self._commit(tok, reads, writes)
        return tok

    def dma(self, e, out, in_, reads=(), writes=(), key=None, **kw):
        self._deps(e, reads, writes)
        if key is None:
            key = (writes[0] if writes else reads[0])
        ds = self.dma_sem.get(key)
        if ds is None or ds[1] + 16 > self.SEM_EPOCH:
            ds = [self._mksem("d"), 0]
            self.dma_sem[key] = ds
        ds[1] += 16
        self.engs[e].dma_start(out=out, in_=in_, **kw).then_inc(ds[0], 16)
        self.ninstr[e] += 1
        tok = (ds[0], ds[1], 'dma')
        self.dma_toks.append(tok)
        self._commit(tok, reads, writes)
        return tok

    def barrier(self):
        toks = list(self.last_tok.values())
        latest = {}
        for t in self.dma_toks:
            k = id(t[0])
            if k not in latest or latest[k][1] < t[1]:
                latest[k] = t
        toks += list(latest.values())
        self.dma_toks = list(latest.values())
        for e in self.engs:
            for t in toks:
                if t[2] == e:
                    continue
                self._wait(e, t)
        self.last_w = {}
        self.readers = {}


def _consts():
    ident = np.eye(128, dtype=np.float32)
    pi, fi = np.meshgrid(np.arange(128), np.arange(128), indexing="ij")
    NEG = -30000.0
    tri = np.stack([
        (pi <= fi).astype(np.float32),
        (pi >= fi).astype(np.float32),
        np.where(fi < pi, 0.0, NEG),
        np.where(fi >= pi, 0.0, NEG),
        np.where(fi > pi, 0.0, NEG),
        np.where(fi <= pi, 0.0, NEG),
    ]).astype(np.float32)
    sel = np.zeros((32, 2), np.float32)
    sel[:16, 0] = 1.0
    sel[16:, 1] = 1.0
    msk = np.stack([(pi // 32 == fi // 32), (pi // 64 == fi // 64) & (pi // 32 != fi // 32), (pi // 64 != fi // 64)]).astype(np.float32)
    out = {"c_ident": ident, "c_tri": tri, "c_sel": sel, "c_msk": msk}
    NF, NP = 97 * 128, 12800
    j = np.arange(NP)
    pos = np.where(j < HALF, j, NF - j).astype(np.float64)
    pos = np.clip(pos, 0, None)
    tl = pos / (L - 1)
    bands = 16
    fb = np.linspace(1e-4, bands - 1, bands)
    ang = (2.0 * np.pi / L) * pos[None, :] * fb[:, None]
    out["c_zpos"] = np.concatenate([tl[None, :], np.cos(ang), -np.sin(ang)], axis=0).astype(np.float32)
    out["c_dl"] = (np.arange(512) / (L - 1)).astype(np.float32)[None, :]
    q = np.arange(25)
    out["c_tp0"] = np.where(q < 8, 512 * q / (L - 1), (NF - 512 * q) / (L - 1)).astype(np.float32)
    a1 = 2 * np.pi * np.outer(np.arange(97), np.arange(97)) / 97
    c1, s1 = np.cos(a1), np.sin(a1)
    a2 = 2 * np.pi * np.outer(np.arange(128), np.arange(65)) / 128
    c2, s2 = np.cos(a2), np.sin(a2)
    out["c_e1"] = np.concatenate([c1, -s1], axis=1).astype(np.float32)
    out["c_s2a"] = np.concatenate([c2, -s2], axis=1).astype(np.float32)
    out["c_s2b"] = np.concatenate([s2, c2], axis=1).astype(np.float32)
    out["c_i1c"] = np.concatenate([c1, s1], axis=1).astype(np.float32)
    out["c_i1d"] = np.concatenate([-s1, c1], axis=1).astype(np.float32)
    wgt = np.full(65, 2.0)
    wgt[0] = wgt[64] = 1.0
    out["c_cw"] = (wgt[:, None] * c2.T / NF).astype(np.float32)
    out["c_sw"] = (-wgt[:, None] * s2.T / NF).astype(np.float32)
    return out


def build(cfg):
    nc = bass.Bass("TRN2", target_bir_lowering=False)
    dbg = cfg.get("dbg", ())

    def din(name, shape, dt=F32):
        return nc.dram_tensor(name, list(shape), dt, kind="ExternalInput").ap()

    def dscr(name, shape, dt, ext=False):
        kind = "ExternalInput" if ext else "Internal"
        return nc.dram_tensor(name, list(shape), dt, kind=kind).ap()

    x = din("x", [L, D])
    norm_in_w = din("norm_in_w", [D])
    w_in = din("w_in", [D, INW])
    w_dn_out = din("w_dn_out", [D, D])
    w_hy_out = din("w_hy_out", [D, D])
    w_out = din("w_out", [D, D])
    norm_out_w = din("norm_out_w", [D])
    c_ident = din("c_ident", [128, 128])
    y = nc.dram_tensor("y", [HALF, D], F32, kind="ExternalOutput").ap()

    ext = cfg.get("ext_scratch", ())
    OG = dscr("OG", [D, HALF], BF16, "OG" in ext)
    YH = dscr("YH", [D, HALF], BF16, "YH" in ext)
    GS = dscr("GS", [2 * D, HALF], BF16, "GS" in ext)
    QS = dscr("QS", [D, HALF], BF16, "QS" in ext)
    KS = dscr("KS", [D, L], BF16, "KS" in ext)
    VS = dscr("VS", [D, L], BF16, "VS" in ext)
    ZS = dscr("ZS", [D, HALF], BF16, "ZS" in ext)
    BGS = dscr("BGS", [32, L], F32, "BGS" in ext)
    OBS = dscr("OBS", [HALF, D], F32, "OBS" in ext)
    US = dscr("US", [D, L], BF16, "US" in ext)
    G0S = dscr("G0S", [D, HALF], BF16, "G0S" in ext)
    KLS = dscr("KLS", [D, 97 * 128], BF16, "KLS" in ext)
    hy_w1 = din("hy_w1", [33, 64])
    hy_b1 = din("hy_b1", [64])
    hy_w2 = din("hy_w2", [64, 64])
    hy_b2 = din("hy_b2", [64])
    hy_freq = din("hy_freq", [64])
    hy_w3 = din("hy_w3", [64, 3, D])
    hy_log_decay = din("hy_log_decay", [3, D])
    hy_bias = din("hy_bias", [D])
    c_zpos = din("c_zpos", [33, 12800])
    c_dl = din("c_dl", [1, 512])
    c_tp0 = din("c_tp0", [25])
    c_e1 = din("c_e1", [97, 194])
    c_s2a = din("c_s2a", [128, 130])
    c_s2b = din("c_s2b", [128, 130])
    c_i1c = din("c_i1c", [97, 194])
    c_i1d = din("c_i1d", [97, 194])
    c_cw = din("c_cw", [65, 128])
    c_sw = din("c_sw", [65, 128])
    dn_conv_w = din("dn_conv_w", [3, 3 * D])
    hy_conv_w = din("hy_conv_w", [3, 3 * D])
    dn_a_log = din("dn_a_log", [16])
    dn_dt_bias = din("dn_dt_bias", [16])
    dn_norm_w = din("dn_norm_w", [128])
    c_sel = din("c_sel", [32, 2])
    c_tri = din("c_tri", [6, 128, 128])
    c_msk = din("c_msk", [3, 128, 128])

    dbg_out = {}

    def ddbg(name, shape, dt=F32):
        dbg_out[name] = nc.dram_tensor("dbg_" + name, list(shape), dt, kind="ExternalOutput").ap()
        return dbg_out[name]

    with ExitStack() as es:
        p = Prog(nc, es)
        cs = ExitStack()
        es.enter_context(cs)

        uniq = [0]

        def sb(stack, name, shape, dt):
            uniq[0] += 1
            return stack.enter_context(nc.sbuf_tensor(f"{name}_{uniq[0]}", list(shape), dt))

        def ps(stack, name, shape, dt=F32):
            uniq[0] += 1
            return stack.enter_context(nc.psum_tensor(f"{name}_{uniq[0]}", list(shape), dt))

        ident_f = sb(cs, "ident_f", [128, 128], F32)
        ident_b = sb(cs, "ident_b", [128, 128], BF16)
        nw_t = sb(cs, "nw_t", [128, 8], F32)
        p.dma('sp', ident_f[:], c_ident[:, :], writes=['ident_f'])
        p.op('dve', lambda e: e.tensor_copy(ident_b[:], ident_f[:]), reads=['ident_f'], writes=['ident_b'])
        p.dma('sp', nw_t[:], norm_in_w.rearrange("(k p) -> p k", p=128), writes=['nw_t'],
              allow_slow_non_contiguous=True)
        cwd = sb(cs, "cwd", [128, 24, 3], F32)
        cwh = sb(cs, "cwh", [128, 24, 3], F32)
        nwd = sb(cs, "nwd", [128, 1], F32)
        dtb = sb(cs, "dtb", [32, 1], F32)
        negA = sb(cs, "negA", [32, 1], F32)
        selt = sb(cs, "selt", [32, 2], F32)
        selb, selg = selt[:, 0:1], selt[:, 1:2]
        for j in range(3):
            p.dma('sp', cwd[:, :, j], dn_conv_w[j, :].rearrange("(g p) -> p g", p=128), writes=[('cwd', j)], allow_slow_non_contiguous=True)
            p.dma('sp', cwh[:, :, j], hy_conv_w[j, :].rearrange("(g p) -> p g", p=128), writes=[('cwh', j)], allow_slow_non_contiguous=True)
        p.dma('sp', nwd[:, :], dn_norm_w.rearrange("(p o) -> p o", o=1), writes=['nwd'], allow_slow_non_contiguous=True)
        p.op('pool', lambda e: e.memset(dtb[:], 0.0), writes=['dtb'])
        p.op('pool', lambda e: e.memset(negA[:], 0.0), writes=['negA'])
        p.dma('sp', dtb[16:32, :], dn_dt_bias.rearrange("(p o) -> p o", o=1), reads=[], writes=['dtb'], allow_slow_non_contiguous=True)
        p.dma('sp', negA[16:32, :], dn_a_log.rearrange("(p o) -> p o", o=1), writes=['negA'], allow_slow_non_contiguous=True)
        p.dma('sp', selt[:, :], c_sel[:, :], writes=['selb'])
        p.op('act', lambda e: e.activation(out=negA[:], in_=negA[:], func=AF.Exp), reads=['negA'], writes=['negA'])
        p.op('dve', lambda e: e.tensor_scalar(out=negA[:], in0=negA[:], scalar1=-1.0, scalar2=None, op0=ALU.mult), reads=['negA'], writes=['negA'])
        p.barrier()

        def build_hT(stk, hT, tok_base, pre_tok, post_tok, tag):
            with ExitStack() as ls:
                xt = [sb(ls, f"xt{tag}{i}", [128, D], F32) for i in range(2)]
                junk = sb(ls, f"junk{tag}", [128, D], BF16)
                xn = [sb(ls, f"xn{tag}{i}", [128, D], BF16) for i in range(2)]
                ssq = [sb(ls, f"ssq{tag}{i}", [128, 1], F32) for i in range(2)]
                pT = [ps(ls, f"pT{tag}{i}", [128, 8, 128], BF16) for i in range(2)]
                jobs = [(tok_base + 128 * i, 128, 1 + 128 * i) for i in range(HALF // 128)]
                for hc, tk in ((0, pre_tok), (HALF + 1, post_tok)):
                    if tk is None:
                        p.op('pool', lambda e, hc=hc: e.memset(hT[:, :, hc:hc + 1], 0.0), writes=[('hT', 'halo', hc)])
                    else:
                        jobs.append((tk, 1, hc))
                for ji, (t0, n, c0) in enumerate(jobs):
                    b = ji % 2
                    kx, kn, ks, kp = (f'xt{tag}', b), (f'xn{tag}', b), (f'ssq{tag}', b), (f'pT{tag}', b)
                    p.dma('sp', xt[b][0:n, :], x[t0:t0 + n, :], writes=[kx])
                    p.op('act', lambda e: e.activation(out=junk[0:n, :], in_=xt[b][0:n, :], func=AF.Square,
                                                       accum_out=ssq[b][0:n, :]), reads=[kx], writes=['junk' + tag, ks])
                    p.op('act', lambda e: e.activation(out=ssq[b][0:n, :], in_=ssq[b][0:n, :], func=AF.Ln, scale=1.0 / D, bias=EPS),
                         reads=[ks], writes=[ks])
                    p.op('act', lambda e: e.activation(out=ssq[b][0:n, :], in_=ssq[b][0:n, :], func=AF.Exp, scale=-0.5),
                         reads=[ks], writes=[ks])
                    p.op('dve', lambda e: e.tensor_scalar(out=xn[b][0:n, :], in0=xt[b][0:n, :], scalar1=ssq[b][0:n, :],
                                                          scalar2=None, op0=ALU.mult), reads=[kx, ks], writes=[kn])
                    for k in range(8):
                        p.op('pe', lambda e, k=k: e.transpose(pT[b][:, k, 0:n], xn[b][0:n, k * 128:(k + 1) * 128],
                                                              ident_b[0:n, 0:n]),
                             reads=[kn, 'ident_b'], writes=[kp])
                    key = ('hT', (c0 - 1) // 512) if n == 128 else ('hT', 'halo', c0)
                    p.op('dve', lambda e: e.tensor_tensor(out=hT[:, :, c0:c0 + n], in0=pT[b][:, :, 0:n],
                                                          in1=nw_t[:, :].unsqueeze(2).broadcast_to([128, 8, n]), op=ALU.mult),
                         reads=[kp, 'nw_t'], writes=[key])
                p.barrier()

        def hT_keys(nblk=8):
            return [('hT', i) for i in range(nblk)]

        wctr = [0]

        def load_w_group(wst, wbf, src_ap):
            b = wctr[0] % len(wst)
            wctr[0] += 1
            ncol = src_ap.shape[1]
            p.dma('sp', wst[b][:, :, 0:ncol], src_ap.rearrange("(k p) c -> p k c", p=128), writes=[('wst', b)])
            p.op('pool', lambda e: e.tensor_copy(wbf[b][:, :, 0:ncol], wst[b][:, :, 0:ncol]), reads=[('wst', b)],
                 writes=[('wbf', b, k) for k in range(8)])
            return b

        W = 2048
        NB = W // 512

        def run_rr(job_specs, make_gen, K, stagger=1):
            active = {}
            nxt_job = 0
            free = list(range(K))
            since = stagger
            while nxt_job < len(job_specs) or active:
                since += 1
                while free and nxt_job < len(job_specs) and since > stagger:
                    sl = free.pop(0)
                    active[sl] = make_gen(sl, job_specs[nxt_job])
                    nxt_job += 1
                    since = 0 if stagger > 0 else since
                for sl in sorted(active.keys()):
                    try:
                        next(active[sl])
                    except StopIteration:
                        del active[sl]
                        free.append(sl)

        def phase_A(own):
            tok_base = 0 if own else HALF
            KJ = 3
            with ExitStack() as s2:
                hT = sb(s2, "hT", [128, 8, HALF + 2], BF16)
                if own:
                    build_hT(s2, hT, 0, None, HALF, "o")
                else:
                    build_hT(s2, hT, HALF, HALF - 1, None, "x")
                wst = [sb(s2, f"wst{i}", [128, 8, 128], F32) for i in range(3)]
                wbf = [sb(s2, f"wbf{i}", [128, 8, 128], BF16) for i in range(3)]
                pp = [ps(s2, f"pp{i}", [128, 512], F32) for i in range(5)]
                pn = [ps(s2, f"pn{i}", [128, 512], F32) for i in range(3)]
                R = [sb(s2, f"R{i}", [128, W + 2], F32) for i in range(KJ)]
                acc = [sb(s2, f"acc{i}", [128, W], F32) for i in range(KJ)]
                sil = acc
                sqb = [sb(s2, f"sqb{i}", [128, W], BF16) for i in range(KJ)]
                rst = [sb(s2, f"rst{i}", [128, W], F32) for i in range(KJ)]
                ob = [sb(s2, f"ob{i}", [128, W], BF16) for i in range(KJ)]
                keep2 = [sb(s2, f"keep{i}", [128, W], F32) for i in range(2)]
                ones_b = sb(s2, "ones_b", [128, 128], BF16)
                p.op('pool', lambda e: e.memset(ones_b[:], 1.0), writes=['ones_b'])
                ctr = {'pp': 0, 'pn': 0}

                def nxt(nm, n):
                    v = ctr[nm] % n
                    ctr[nm] += 1
                    return v

                def Rkeys(ri):
                    return [('R', ri, bk) for bk in range(5)]

                BW = (W + 2) // 5

                def project(sl, wb, ncol, ps_i):
                    c0 = ps_i * W
                    for bk in range(5):
                        pi = nxt('pp', 5)
                        cs = c0 + bk * BW
                        for k in range(8):
                            p.op('pe', lambda e, k=k: e.matmul(pp[pi][0:ncol, 0:BW], wbf[wb][:, k, 0:ncol], hT[:, k, cs:cs + BW],
                                                               start=(k == 0), stop=(k == 7)),
                                 reads=[('wbf', wb, k)], writes=[('pp', pi)])
                        dst = R[sl][0:ncol, bk * BW:(bk + 1) * BW]
                        if bk % 2 == 0:
                            p.op('act', lambda e: e.activation(out=dst, in_=pp[pi][0:ncol, 0:BW], func=AF.Copy), reads=[('pp', pi)], writes=[('R', sl, bk)])
                        else:
                            p.op('dve', lambda e: e.tensor_copy(dst, pp[pi][0:ncol, 0:BW]), reads=[('pp', pi)], writes=[('R', sl, bk)])

                def conv(sl, cw, g):
                    p.op('act', lambda e: e.activation(out=acc[sl][:, :], in_=R[sl][:, 0:W], func=AF.Copy, scale=cw[:, g, 0:1]),
                         reads=Rkeys(sl), writes=[('acc', sl)])
                    for j in (1, 2):
                        p.op('dve', lambda e, j=j: e.scalar_tensor_tensor(out=acc[sl][:, :], in0=R[sl][:, j:j + W], scalar=cw[:, g, j:j + 1],
                                                                           in1=acc[sl][:, :], op0=ALU.mult, op1=ALU.add),
                             reads=Rkeys(sl) + [('acc', sl)], writes=[('acc', sl)])

                def store(dst_rows, sl, ps_i, key, eng):
                    c0 = tok_base + ps_i * W if dst_rows.shape[1] == L else ps_i * W
                    p.dma(eng, dst_rows[:, c0:c0 + W], ob[sl][:, :], reads=[('ob', sl)], writes=[key], key=('ob', sl))

                RST = lambda sl: [('rst', sl, bk) for bk in range(NB)]

                def job(sl, spec):
                    kind, wb, ps_i, idx = spec
                    if kind == 'gate':
                        for bk in range(NB):
                            pi = nxt('pp', 5)
                            cs = 1 + ps_i * W + bk * 512
                            for k in range(8):
                                p.op('pe', lambda e, k=k: e.matmul(pp[pi][:, :], wbf[wb][:, k, :], hT[:, k, cs:cs + 512], start=(k == 0), stop=(k == 7)),
                                     reads=[('wbf', wb, k)], writes=[('pp', pi)])
                            p.op('act', lambda e: e.activation(out=ob[sl][:, bk * 512:(bk + 1) * 512], in_=pp[pi][:, :], func=AF.Sigmoid),
                                 reads=[('pp', pi)], writes=[('ob', sl)])
                        store(GS[idx * 128:(idx + 1) * 128, :], sl, ps_i, ('GS', idx, ps_i), 'act')
                        return
                    ncol = 32 if kind == 'bg' else 128
                    project(sl, wb, ncol, ps_i)
                    yield
                    if kind == 'bg':
                        xin = R[sl][0:32, 1:W + 1]
                        rk = Rkeys(sl)
                        t0, t1, t2 = acc[sl][0:32, :], xin, rst[sl][0:32, :]
                        K0, K1, K2 = ('acc', sl), ('R', sl, 0), RST(sl)
                        p.op('act', lambda e: e.activation(out=t0, in_=xin, func=AF.Exp, scale=-1.0), reads=rk, writes=[K0])
                        p.op('dve', lambda e: e.tensor_scalar(out=t0, in0=t0, scalar1=1.0, scalar2=None, op0=ALU.add), reads=[K0], writes=[K0])
                        p.op('dve', lambda e: e.reciprocal(out=t0, in_=t0), reads=[K0], writes=[K0])
                        p.op('dve', lambda e: e.tensor_scalar(out=t1, in0=xin, scalar1=dtb[:, 0:1], scalar2=None, op0=ALU.add),
                             reads=rk + ['dtb'], writes=rk)
                        p.op('act', lambda e: e.activation(out=t2, in_=t1, func=AF.Abs), reads=[K1], writes=K2)
                        p.op('act', lambda e: e.activation(out=t2, in_=t2, func=AF.Exp, scale=-1.0), reads=K2, writes=K2)
                        p.op('act', lambda e: e.activation(out=t2, in_=t2, func=AF.Ln, bias=1.0), reads=K2, writes=K2)
                        p.op('dve', lambda e: e.scalar_tensor_tensor(out=t1, in0=t1, scalar=0.0, in1=t2, op0=ALU.max, op1=ALU.add),
                             reads=[K1] + K2, writes=[K1])
                        p.op('dve', lambda e: e.tensor_scalar(out=t1, in0=t1, scalar1=negA[:, 0:1], scalar2=selg[:, 0:1], op0=ALU.mult, op1=ALU.mult),
                             reads=[K1, 'negA', 'selb'], writes=[K1])
                        p.op('dve', lambda e: e.scalar_tensor_tensor(out=t2, in0=t0, scalar=selb[:, 0:1], in1=t1, op0=ALU.mult, op1=ALU.add),
                             reads=[K0, K1, 'selb'], writes=K2)
                        c0 = tok_base + ps_i * W
                        p.dma('pool', BGS[:, c0:c0 + W], t2, reads=K2, writes=[('BGS', own, ps_i)], key=('rst', sl))
                        return
                    if kind in ('z', 'hz'):
                        p.op('act', lambda e: e.activation(out=sil[sl][:, :], in_=R[sl][:, 1:W + 1], func=AF.Silu), reads=Rkeys(sl), writes=[('acc', sl)])
                        yield
                        if kind == 'z':
                            p.op('dve', lambda e: e.tensor_scalar(out=ob[sl][:, :], in0=sil[sl][:, :], scalar1=nwd[:, 0:1], scalar2=None, op0=ALU.mult),
                                 reads=[('acc', sl), 'nwd'], writes=[('ob', sl)])
                            store(ZS[idx * 128:(idx + 1) * 128, :], sl, ps_i, ('ZS', idx, ps_i), 'pool')
                        else:
                            p.op('pool', lambda e: e.tensor_tensor(out=ob[sl][:, :], in0=sil[sl][:, :], in1=keep2[ps_i][:, :], op=ALU.mult),
                                 reads=[('acc', sl), ('keep', ps_i)], writes=[('ob', sl)])
                            store(G0S[idx * 128:(idx + 1) * 128, :], sl, ps_i, ('G0S', idx, ps_i), 'pool')
                        return
                    if kind in ('q', 'k', 'v'):
                        gi = {'q': idx, 'k': NH + idx, 'v': 2 * NH + idx}[kind]
                        conv(sl, cwd, gi)
                        yield
                        if kind == 'v':
                            p.op('act', lambda e: e.activation(out=ob[sl][:, :], in_=acc[sl][:, :], func=AF.Silu), reads=[('acc', sl)], writes=[('ob', sl)])
                            store(VS[idx * 128:(idx + 1) * 128, :], sl, ps_i, ('VS', idx, own, ps_i), 'act')
                            return
                        p.op('act', lambda e: e.activation(out=sil[sl][:, :], in_=acc[sl][:, :], func=AF.Silu), reads=[('acc', sl)], writes=[('acc', sl)])
                        yield
                        p.op('act', lambda e: e.activation(out=sqb[sl][:, :], in_=sil[sl][:, :], func=AF.Square),
                             reads=[('acc', sl)], writes=[('sqb', sl)])
                        yield
                        for bk in range(NB):
                            ni = nxt('pn', 3)
                            p.op('pe', lambda e: e.matmul(pn[ni][:, :], ones_b[:, :], sqb[sl][:, bk * 512:(bk + 1) * 512], start=True, stop=True),
                                 reads=['ones_b', ('sqb', sl)], writes=[('pn', ni)])
                            p.op('act', lambda e: e.activation(out=rst[sl][:, bk * 512:(bk + 1) * 512], in_=pn[ni][:, :], func=AF.Ln, bias=EPS),
                                 reads=[('pn', ni)], writes=[('rst', sl, bk)])
                        scale = 128.0 ** -0.5 if kind == 'q' else 1.0
                        p.op('act', lambda e: e.activation(out=rst[sl][:, :], in_=rst[sl][:, :], func=AF.Exp, scale=-0.5, bias=float(np.log(scale))),
                             reads=RST(sl), writes=RST(sl))
                        yield
                        p.op('pool', lambda e: e.tensor_tensor(out=ob[sl][:, :], in0=sil[sl][:, :], in1=rst[sl][:, :], op=ALU.mult),
                             reads=[('acc', sl)] + RST(sl), writes=[('ob', sl)])
                        dstT = QS if kind == 'q' else KS
                        key = ('QS', idx, ps_i) if kind == 'q' else ('KS', idx, own, ps_i)
                        store(dstT[idx * 128:(idx + 1) * 128, :], sl, ps_i, key, 'pool')
                        return
                    gi = {'x0': idx, 'x1': 8 + idx, 'hv': 16 + idx}[kind]
                    conv(sl, cwh, gi)
                    yield
                    if kind in ('x0', 'x1'):
                        p.op('pool', lambda e: e.tensor_copy(keep2[ps_i][:, :], acc[sl][:, :]), reads=[('acc', sl)], writes=[('keep', ps_i)])
                    else:
                        p.op('pool', lambda e: e.tensor_tensor(out=ob[sl][:, :], in0=acc[sl][:, :], in1=keep2[ps_i][:, :], op=ALU.mult),
                             reads=[('acc', sl), ('keep', ps_i)], writes=[('ob', sl)])
                        store(US[idx * 128:(idx + 1) * 128, :], sl, ps_i, ('US', idx, own, ps_i), 'pool')

                groups = []
                for h in range(NH):
                    for kind in (('q', 'k', 'v', 'z') if own else ('k', 'v')):
                        gi = {'q': h, 'k': NH + h, 'v': 2 * NH + h}.get(kind)
                        groups.append((kind, C_QKV + gi * 128 if kind != 'z' else C_DNZ + h * 128, 128, h))
                groups.append(('bg', C_BETA, 32, 0))
                if "hyproj" in cfg["phases"]:
                    for cb in range(8):
                        for kind in (('x0', 'hz', 'x1', 'hv') if own else ('x1', 'hv')):
                            gi = {'x0': cb, 'x1': 8 + cb, 'hv': 16 + cb}.get(kind)
                            groups.append((kind, C_HXV + gi * 128 if kind != 'hz' else C_HZ + cb * 128, 128, cb))
                if own and "gates" in cfg["phases"]:
                    for gi in range(16):
                        groups.append(('gate', C_GATE + gi * 128, 128, gi))
                specs = []
                for (kind, col0, ncol, idx) in groups:
                    for ps_i in range(2):
                        specs.append((kind, col0, ncol, ps_i, idx))
                wb_of = {}

                gorder = [(g[0], g[3]) for g in groups]
                ginfo = {(g[0], g[3]): g for g in groups}

                def ensure_w(gk):
                    if gk not in wb_of:
                        kind, col0, ncol, idx = ginfo[gk]
                        wb_of[gk] = load_w_group(wst, wbf, w_in[:, col0:col0 + ncol])

                def make(sl, sp):
                    kind, col0, ncol, ps_i, idx = sp
                    ensure_w((kind, idx))
                    gi_ = gorder.index((kind, idx))
                    if ps_i == 0 and gi_ + 1 < len(gorder):
                        ensure_w(gorder[gi_ + 1])
                    return job(sl, (kind, wb_of[(kind, idx)], ps_i, idx))

                run_rr(specs, make, KJ, stagger=cfg.get('stagA', 0))
                p.barrier()

        if "projA" in cfg["phases"]:
            phase_A(False)
            phase_A(True)

        def phase_B():
            with ExitStack() as sB:
                NEG = -30000.0
                cst = sb(sB, "tricst", [128, 6, 128], F32)
                p.dma('sp', cst[:, :, :], c_tri.rearrange("c p f -> p c f"), writes=['tricst'])
                ones_f = sb(sB, "ones_f", [128, 128], F32)
                p.op('pool', lambda e: e.memset(ones_f[:], 1.0), writes=['ones_f'])
                mskf = sb(sB, "mskf", [128, 3, 128], F32)
                p.dma('sp', mskf[:, :, :], c_msk.rearrange("c p f -> p c f"), writes=['mskf'])
                m32h = sb(sB, "m32h", [128, 8, 128], BF16)
                mo64h = sb(sB, "mo64h", [128, 8, 128], BF16)
                mo128h = sb(sB, "mo128h", [128, 8, 128], BF16)
                identh = sb(sB, "identh", [128, 8, 128], BF16)
                for i_, t_ in enumerate((m32h, mo64h, mo128h)):
                    for h in range(NH):
                        p.op('dve', lambda e, i_=i_, t_=t_, h=h: e.tensor_copy(t_[:, h, :], mskf[:, i_, :]), reads=['mskf'], writes=['mskb'])
                for h in range(NH):
                    p.op('dve', lambda e, h=h: e.tensor_copy(identh[:, h, :], ident_f[:, :]), reads=['ident_f'], writes=['mskb'])
                tri = {0: cst[:, 0, :], 1: cst[:, 1, :]}
                nmd = {0: cst[:, 2, :], 1: cst[:, 4, :]}
                nme = {0: cst[:, 3, :], 1: cst[:, 5, :]}
                tt = sb(sB, "tt", [128, 32, 32], F32)
                Gp = [sb(sB, f"Gp{d}", [128, 32, 8], F32) for d in range(2)]
                Glb = sb(sB, "Glb", [128, 32, 16], F32)
                eG = [sb(sB, f"eG{d}", [128, 32, 8], F32) for d in range(2)]
                kd = [sb(sB, f"kd{d}", [128, 32, 8], F32) for d in range(2)]
                gam = sb(sB, "gam", [128, 32, 16], F32)
                bneg = [sb(sB, f"bneg{d}", [128, 32, 8], F32) for d in range(2)]
                beg = [sb(sB, f"beg{d}", [128, 32, 8], F32) for d in range(2)]
                PF = [ps(sB, f"PF{i}", [128, 8, 128], F32) for i in range(3)]
                PB = [ps(sB, f"PB{i}", [128, 8, 128], BF16) for i in range(2)]
                pfc = [0]
                pbc = [0]

                def npf():
                    v = pfc[0] % 3
                    pfc[0] += 1
                    return v

                def npb():
                    v = pbc[0] % 2
                    pbc[0] += 1
                    return v

                HB = [128, 8, 128]
                PW = []
                for i in range(2):
                    d_ = {}
                    for nm, dt in (("rhsG", F32), ("tmp", F32), ("E", F32), ("N0", BF16), ("N1", BF16),
                                   ("M0", BF16), ("M1", BF16), ("Q0", BF16), ("Q1", BF16), ("ktok", BF16),
                                   ("kc", BF16), ("vc", BF16), ("qc", BF16)):
                        d_[nm] = sb(sB, f"pw{i}_{nm}", HB, dt)
                    d_["eGr"] = d_["rhsG"]
                    d_["kbg"] = d_["N1"]
                    PW.append(d_)
                SW = []
                for i in range(3):
                    d_ = {}
                    for nm, dt in (("vb", F32), ("kbgT", BF16), ("kdec", BF16), ("attnT", BF16), ("qdT", BF16), ("Q", BF16)):
                        d_[nm] = sb(sB, f"sw{i}_{nm}", HB, dt)
                    SW.append(d_)
                S = sb(sB, "S", HB, F32)
                rb = sb(sB, "rb", HB, BF16)
                Sb = sb(sB, "Sb", HB, BF16)
                vnb = sb(sB, "vnb", HB, BF16)
                osum = sb(sB, "osum", HB, F32)
                oBt = sb(sB, "oBt", HB, F32)
                oss = sb(sB, "oss", [128, 8], F32)
                onb = sb(sB, "onb", HB, BF16)
                zsc = sb(sB, "zsc", HB, BF16)
                ogc = sb(sB, "ogc", HB, BF16)
                identb3 = ident_b[:, :].unsqueeze(1).broadcast_to(HB)

                def bc_j(ap2):
                    return ap2.unsqueeze(2).broadcast_to(HB)

                def bc_h(ap2):
                    return ap2.unsqueeze(1).broadcast_to(HB)

                def prep_half(own):
                    with ExitStack() as sh:
                        bgsb = sb(sh, "bgsb", [32, HALF], F32)
                        prep_half_(own, bgsb)

                def prep_half_(own, bgsb):
                    base = 0 if own else HALF
                    p.dma('sp', bgsb[:, :], BGS[:, base:base + HALF], reads=[('BGS', own, 0), ('BGS', own, 1)], writes=['bgsb'])
                    pf = PF[npf()]
                    pfv = pf[:, :, :].rearrange("p h j -> p (h j)")
                    for n in range(32):
                        p.op('pe', lambda e, n=n: e.transpose(pfv[:, n * 32:(n + 1) * 32], bgsb[0:32, n * 128:(n + 1) * 128], ident_f[0:32, 0:32]),
                             reads=['bgsb', 'ident_f'], writes=['PFx'])
                    p.op('dve', lambda e: e.tensor_copy(tt[:, :, :].rearrange("p c r -> p (c r)"), pfv), reads=['PFx'], writes=['tt'])
                    for d in range(2):
                        p.op('pe', lambda e, d=d: e.matmul(pfv[:, 0:256], tri[d], tt[:, :, 16 + 8 * d:24 + 8 * d], start=True, stop=True),
                             reads=['tt', 'tricst'], writes=['PFx'])
                        p.op('dve', lambda e, d=d: e.tensor_copy(Gp[d][:, :, :].rearrange("p c h -> p (c h)"), pfv[:, 0:256]),
                             reads=['PFx'], writes=[('Gp', d)])
                    p.op('pe', lambda e: e.matmul(pfv[:, 0:512], ones_f[:, :], tt[:, :, 16:32], start=True, stop=True),
                         reads=['tt', 'ones_f'], writes=['PFx'])
                    p.op('dve', lambda e: e.tensor_copy(Glb[:, :, :].rearrange("p c h -> p (c h)"), pfv[:, 0:512]), reads=['PFx'], writes=['Glb'])
                    p.op('act', lambda e: e.activation(out=gam[:, :, :], in_=Glb[:, :, :], func=AF.Exp), reads=['Glb'], writes=['gam'])
                    for d in range(2):
                        p.op('act', lambda e, d=d: e.activation(out=eG[d][:, :, :], in_=Gp[d][:, :, :], func=AF.Exp), reads=[('Gp', d)], writes=[('eG', d)])
                        p.op('dve', lambda e, d=d: e.tensor_tensor(out=kd[d][:, :, :], in0=Glb[:, :, 8 * d:8 * d + 8], in1=Gp[d][:, :, :], op=ALU.subtract),
                             reads=['Glb', ('Gp', d)], writes=[('kd', d)])
                        p.op('act', lambda e, d=d: e.activation(out=kd[d][:, :, :], in_=kd[d][:, :, :], func=AF.Exp), reads=[('kd', d)], writes=[('kd', d)])
                        p.op('dve', lambda e, d=d: e.tensor_scalar(out=bneg[d][:, :, :], in0=tt[:, :, 8 * d:8 * d + 8], scalar1=-1.0, scalar2=None, op0=ALU.mult),
                             reads=['tt'], writes=[('bneg', d)])
                        p.op('dve', lambda e, d=d: e.tensor_tensor(out=beg[d][:, :, :], in0=tt[:, :, 8 * d:8 * d + 8], in1=eG[d][:, :, :], op=ALU.mult),
                             reads=['tt', ('eG', d)], writes=[('beg', d)])
                    p.barrier()

                def prep_gen(u, n, d, own, need_out):
                    pw = PW[u % 2]
                    sw = SW[u % 3]
                    P_ = f"pw{u % 2}"
                    S_ = f"sw{u % 3}"
                    t0 = (0 if own else HALF) + n * 128
                    kq = [('KS', h, own, (n * 128) // W) for h in range(NH)]
                    kv = [('VS', h, own, (n * 128) // W) for h in range(NH)]
                    p.dma('sp', pw["kc"][:, :, :], KS[:, t0:t0 + 128].rearrange("(h d) t -> d h t", d=128), reads=kq, writes=[P_ + "kc"])
                    p.dma('sp', pw["vc"][:, :, :], VS[:, t0:t0 + 128].rearrange("(h d) t -> d h t", d=128), reads=kv, writes=[P_ + "vc"])
                    if need_out:
                        p.dma('sp', pw["qc"][:, :, :], QS[:, t0:t0 + 128].rearrange("(h d) t -> d h t", d=128),
                              reads=[('QS', h, (n * 128) // W) for h in range(NH)], writes=[P_ + "qc"])
                    p.op('dve', lambda e: e.tensor_tensor(out=pw["rhsG"][:, :, :], in0=bc_h(tri[d]), in1=bc_j(tt[:, n, 16 + 8 * d:24 + 8 * d]), op=ALU.mult),
                         reads=['tt', 'tricst'], writes=[P_ + "rhsG"])
                    yield
                    a = npf()
                    for hh in range(2):
                        p.op('pe', lambda e, hh=hh: e.matmul(PF[a][:, 4 * hh:4 * hh + 4, :], ones_f[:, :], pw["rhsG"][:, 4 * hh:4 * hh + 4, :], start=True, stop=True),
                             reads=[P_ + "rhsG", 'ones_f'], writes=[('PF', a)])
                    p.op('dve', lambda e: e.tensor_tensor(out=pw["tmp"][:, :, :], in0=PF[a][:, :, :], in1=bc_j(Gp[d][:, n, :]), op=ALU.subtract),
                         reads=[('PF', a), ('Gp', d)], writes=[P_ + "tmp"])
                    if need_out:
                        p.op('act', lambda e: e.activation(out=pw["eGr"][:, :, :], in_=PF[a][:, :, :], func=AF.Exp), reads=[('PF', a)], writes=[P_ + "rhsG"])
                    yield
                    if need_out:
                        p.op('pool', lambda e: e.tensor_tensor(out=pw["E"][:, :, :], in0=pw["tmp"][:, :, :], in1=bc_h(nme[d]), op=ALU.add),
                             reads=[P_ + "tmp", 'tricst'], writes=[P_ + "E"])
                        p.op('act', lambda e: e.activation(out=pw["E"][:, :, :], in_=pw["E"][:, :, :], func=AF.Exp), reads=[P_ + "E"], writes=[P_ + "E"])
                    p.op('pool', lambda e: e.tensor_tensor(out=pw["tmp"][:, :, :], in0=bc_h(nmd[d]), in1=pw["tmp"][:, :, :], op=ALU.subtract),
                         reads=[P_ + "tmp", 'tricst'], writes=[P_ + "tmp"])
                    p.op('act', lambda e: e.activation(out=pw["tmp"][:, :, :], in_=pw["tmp"][:, :, :], func=AF.Exp), reads=[P_ + "tmp"], writes=[P_ + "tmp"])
                    yield
                    a = npf()
                    for h in range(NH):
                        p.op('pe', lambda e, h=h: e.matmul(PF[a][:, h, :], pw["kc"][:, h, :], pw["kc"][:, h, :], start=True, stop=True),
                             reads=[P_ + "kc"], writes=[('PF', a)])
                    p.op('pool', lambda e: e.tensor_tensor(out=pw["tmp"][:, :, :], in0=pw["tmp"][:, :, :], in1=bc_j(bneg[d][:, n, :]), op=ALU.mult),
                         reads=[P_ + "tmp", ('bneg', d)], writes=[P_ + "tmp"])
                    p.op('dve', lambda e: e.tensor_tensor(out=pw["N0"][:, :, :], in0=PF[a][:, :, :], in1=pw["tmp"][:, :, :], op=ALU.mult),
                         reads=[('PF', a), P_ + "tmp"], writes=[P_ + "N0"])
                    yield
                    b = npb()
                    for h in range(NH):
                        p.op('pe', lambda e, h=h: e.transpose(PB[b][:, h, :], pw["N0"][:, h, :], ident_b[:, :]),
                             reads=[P_ + "N0", 'ident_b'], writes=[('PB', b)])
                    p.op('act', lambda e: e.activation(out=pw["M0"][:, :, :], in_=PB[b][:, :, :], func=AF.Copy), reads=[('PB', b)], writes=[P_ + "M0"])
                    yield
                    if need_out:
                        a = npf()
                        for h in range(NH):
                            p.op('pe', lambda e, h=h: e.matmul(PF[a][:, h, :], pw["kc"][:, h, :], pw["qc"][:, h, :], start=True, stop=True),
                                 reads=[P_ + "kc", P_ + "qc"], writes=[('PF', a)])
                        p.op('dve', lambda e: e.tensor_tensor(out=sw["attnT"][:, :, :], in0=PF[a][:, :, :], in1=pw["E"][:, :, :], op=ALU.mult),
                             reads=[('PF', a), P_ + "E"], writes=[S_ + "attnT"])
                        p.op('pool', lambda e: e.tensor_tensor(out=sw["qdT"][:, :, :], in0=pw["qc"][:, :, :], in1=pw["eGr"][:, :, :], op=ALU.mult),
                             reads=[P_ + "qc", P_ + "rhsG"], writes=[S_ + "qdT"])
                        yield
                    def mmg(dst_ps, lk, rk, lkey, rkey):
                        for h in range(NH):
                            p.op('pe', lambda e, h=h: e.matmul(PF[dst_ps][:, h, :], lk[:, h, :], rk[:, h, :], start=True, stop=True),
                                 reads=[lkey, rkey], writes=[('PF', dst_ps)])
                    N_, M_, T_, W_ = pw["N0"], pw["M0"], pw["Q0"], pw["Q1"]
                    kN, kM, kT, kW = P_ + "N0", P_ + "M0", P_ + "Q0", P_ + "Q1"
                    No1, Mo1, No2 = pw["N1"], pw["M1"], pw["ktok"]
                    kNo1, kMo1, kNo2 = P_ + "N1", P_ + "M1", P_ + "ktok"
                    p.op('dve', lambda e: e.tensor_tensor(out=No1[:, :, :], in0=N_[:, :, :], in1=mo64h[:, :, :], op=ALU.mult), reads=[kN, 'mskb'], writes=[kNo1])
                    p.op('dve', lambda e: e.tensor_tensor(out=Mo1[:, :, :], in0=M_[:, :, :], in1=mo64h[:, :, :], op=ALU.mult), reads=[kM, 'mskb'], writes=[kMo1])
                    p.op('dve', lambda e: e.tensor_tensor(out=No2[:, :, :], in0=N_[:, :, :], in1=mo128h[:, :, :], op=ALU.mult), reads=[kN, 'mskb'], writes=[kNo2])
                    p.op('dve', lambda e: e.tensor_tensor(out=N_[:, :, :], in0=N_[:, :, :], in1=m32h[:, :, :], op=ALU.mult), reads=[kN, 'mskb'], writes=[kN])
                    p.op('dve', lambda e: e.tensor_tensor(out=M_[:, :, :], in0=M_[:, :, :], in1=m32h[:, :, :], op=ALU.mult), reads=[kM, 'mskb'], writes=[kM])
                    p.op('dve', lambda e: e.tensor_tensor(out=T_[:, :, :], in0=N_[:, :, :], in1=identh[:, :, :], op=ALU.add), reads=[kN, 'mskb'], writes=[kT])
                    p.op('dve', lambda e: e.tensor_tensor(out=W_[:, :, :], in0=M_[:, :, :], in1=identh[:, :, :], op=ALU.add), reads=[kM, 'mskb'], writes=[kW])
                    yield
                    for lvl in range(1, 5):
                        a1, a2 = npf(), npf()
                        mmg(a1, M_, N_, kM, kN)
                        mmg(a2, N_, M_, kN, kM)
                        p.op('act', lambda e: e.activation(out=N_[:, :, :], in_=PF[a1][:, :, :], func=AF.Copy), reads=[('PF', a1)], writes=[kN])
                        p.op('act', lambda e: e.activation(out=M_[:, :, :], in_=PF[a2][:, :, :], func=AF.Copy), reads=[('PF', a2)], writes=[kM])
                        yield
                        a1, a2 = npf(), npf()
                        mmg(a1, M_, T_, kM, kT)
                        mmg(a2, N_, W_, kN, kW)
                        p.op('dve', lambda e: e.tensor_tensor(out=T_[:, :, :], in0=PF[a1][:, :, :], in1=T_[:, :, :], op=ALU.add), reads=[('PF', a1), kT], writes=[kT])
                        p.op('dve', lambda e: e.tensor_tensor(out=W_[:, :, :], in0=PF[a2][:, :, :], in1=W_[:, :, :], op=ALU.add), reads=[('PF', a2), kW], writes=[kW])
                        yield
                    a1, a2 = npf(), npf()
                    mmg(a1, Mo1, T_, kMo1, kT)
                    mmg(a2, No1, W_, kNo1, kW)
                    p.op('act', lambda e: e.activation(out=N_[:, :, :], in_=PF[a1][:, :, :], func=AF.Copy), reads=[('PF', a1)], writes=[kN])
                    p.op('dve', lambda e: e.tensor_copy(M_[:, :, :], PF[a2][:, :, :]), reads=[('PF', a2)], writes=[kM])
                    yield
                    a1, a2 = npf(), npf()
                    mmg(a1, W_, N_, kW, kN)
                    mmg(a2, T_, M_, kT, kM)
                    p.op('dve', lambda e: e.tensor_tensor(out=T_[:, :, :], in0=PF[a1][:, :, :], in1=T_[:, :, :], op=ALU.add), reads=[('PF', a1), kT], writes=[kT])
                    p.op('dve', lambda e: e.tensor_tensor(out=W_[:, :, :], in0=PF[a2][:, :, :], in1=W_[:, :, :], op=ALU.add), reads=[('PF', a2), kW], writes=[kW])
                    yield
                    a1 = npf()
                    mmg(a1, No2, W_, kNo2, kW)
                    p.op('act', lambda e: e.activation(out=M_[:, :, :], in_=PF[a1][:, :, :], func=AF.Copy), reads=[('PF', a1)], writes=[kM])
                    yield
                    a1 = npf()
                    mmg(a1, T_, M_, kT, kM)
                    p.op('dve', lambda e: e.tensor_tensor(out=sw["Q"][:, :, :], in0=PF[a1][:, :, :], in1=W_[:, :, :], op=ALU.add), reads=[('PF', a1), kW], writes=[S_ + "Q"])
                    yield

                    b = npb()
                    for h in range(NH):
                        p.op('pe', lambda e, h=h: e.transpose(PB[b][:, h, :], pw["kc"][:, h, :], ident_b[:, :]),
                             reads=[P_ + "kc", 'ident_b'], writes=[('PB', b)])
                    p.op('act', lambda e: e.activation(out=pw["ktok"][:, :, :], in_=PB[b][:, :, :], func=AF.Copy), reads=[('PB', b)], writes=[P_ + "ktok"])
                    p.op('pool', lambda e: e.tensor_tensor(out=pw["kbg"][:, :, :], in0=pw["ktok"][:, :, :], in1=bc_j(beg[d][:, n, :]), op=ALU.mult),
                         reads=[P_ + "ktok", ('beg', d)], writes=[P_ + "N1"])
                    p.op('pool', lambda e: e.tensor_tensor(out=sw["kdec"][:, :, :], in0=pw["ktok"][:, :, :], in1=bc_j(kd[d][:, n, :]), op=ALU.mult),
                         reads=[P_ + "ktok", ('kd', d)], writes=[S_ + "kdec"])
                    yield
                    b = npb()
                    for h in range(NH):
                        p.op('pe', lambda e, h=h: e.transpose(PB[b][:, h, :], pw["kbg"][:, h, :], ident_b[:, :]),
                             reads=[P_ + "N1", 'ident_b'], writes=[('PB', b)])
                    p.op('act', lambda e: e.activation(out=sw["kbgT"][:, :, :], in_=PB[b][:, :, :], func=AF.Copy), reads=[('PB', b)], writes=[S_ + "kbgT"])
                    yield
                    b = npb()
                    for h in range(NH):
                        p.op('pe', lambda e, h=h: e.transpose(PB[b][:, h, :], pw["vc"][:, h, :], ident_b[:, :]),
                             reads=[P_ + "vc", 'ident_b'], writes=[('PB', b)])
                    p.op('dve', lambda e: e.tensor_tensor(out=sw["vb"][:, :, :], in0=PB[b][:, :, :], in1=bc_j(tt[:, n, 8 * d:8 * d + 8]), op=ALU.mult),
                         reads=[('PB', b), 'tt'], writes=[S_ + "vb"])
                    yield

                def seq_gen(u, n, d, own, need_out, final_dir):
                    sw = SW[u % 3]
                    S_ = f"sw{u % 3}"
                    r0 = n * 128
                    if need_out and final_dir:
                        p.dma('sp', oBt[:, :, :].rearrange("p h e -> p (h e)"), OBS[r0:r0 + 128, :], reads=[('OBS', n)], writes=['oBt'])
                        p.dma('sp', zsc[:, :, :], ZS[:, r0:r0 + 128].rearrange("(h d) t -> d h t", d=128),
                              reads=[('ZS', h, r0 // W) for h in range(NH)], writes=['zsc'])
                    a = npf()
                    for h in range(NH):
                        p.op('pe', lambda e, h=h: e.matmul(PF[a][:, h, :], sw["kbgT"][:, h, :], Sb[:, h, :], start=True, stop=True),
                             reads=[S_ + "kbgT", 'Sb'], writes=[('PF', a)])
                    p.op('dve', lambda e: e.tensor_tensor(out=rb[:, :, :], in0=sw["vb"][:, :, :], in1=PF[a][:, :, :], op=ALU.subtract),
                         reads=[('PF', a), S_ + "vb"], writes=['rb'])
                    yield
                    a = npf()
                    for h in range(NH):
                        p.op('pe', lambda e, h=h: e.matmul(PF[a][:, h, :], sw["Q"][:, h, :], rb[:, h, :], start=True, stop=True),
                             reads=[S_ + "Q", 'rb'], writes=[('PF', a)])
                    p.op('act', lambda e: e.activation(out=vnb[:, :, :], in_=PF[a][:, :, :], func=AF.Copy), reads=[('PF', a)], writes=['vnb'])
                    yield
                    if need_out:
                        ao = npf()
                        for h in range(NH):
                            p.op('pe', lambda e, h=h: e.matmul(PF[ao][:, h, :], sw["qdT"][:, h, :], Sb[:, h, :], start=True, stop=False),
                                 reads=[S_ + "qdT", 'Sb'], writes=[('PF', ao)])
                            p.op('pe', lambda e, h=h: e.matmul(PF[ao][:, h, :], sw["attnT"][:, h, :], vnb[:, h, :], start=False, stop=True),
                                 reads=[S_ + "attnT", 'vnb'], writes=[('PF', ao)])
                    a = npf()
                    for h in range(NH):
                        p.op('pe', lambda e, h=h: e.matmul(PF[a][:, h, :], sw["kdec"][:, h, :], vnb[:, h, :], start=True, stop=True),
                             reads=[S_ + "kdec", 'vnb'], writes=[('PF', a)])
                    p.op('pool', lambda e: e.tensor_tensor(out=S[:, :, :], in0=S[:, :, :], in1=bc_j(gam[:, n, 8 * d:8 * d + 8]), op=ALU.mult),
                         reads=['S', 'gam'], writes=['S'])
                    p.op('dve', lambda e: e.tensor_tensor(out=S[:, :, :], in0=S[:, :, :], in1=PF[a][:, :, :], op=ALU.add),
                         reads=['S', ('PF', a)], writes=['S'])
                    p.op('act', lambda e: e.activation(out=Sb[:, :, :], in_=S[:, :, :], func=AF.Copy), reads=['S'], writes=['Sb'])
                    if need_out:
                        if not final_dir:
                            p.op('act', lambda e: e.activation(out=osum[:, :, :], in_=PF[ao][:, :, :], func=AF.Copy), reads=[('PF', ao)], writes=['osum'])
                        else:
                            p.op('dve', lambda e: e.tensor_tensor(out=osum[:, :, :], in0=PF[ao][:, :, :], in1=oBt[:, :, :], op=ALU.add),
                                 reads=[('PF', ao), 'oBt'], writes=['osum'])
                    yield
                    if need_out:
                        if not final_dir:
                            p.dma('act', OBS[r0:r0 + 128, :], osum[:, :, :].rearrange("p h e -> p (h e)"), reads=['osum'], writes=[('OBS', n)], key='osum')
                        else:
                            p.op('pool', lambda e: e.tensor_tensor(out=oBt[:, :, :], in0=osum[:, :, :], in1=osum[:, :, :], op=ALU.mult),
                                 reads=['osum'], writes=['oBt'])
                            p.op('dve', lambda e: e.tensor_reduce(out=oss[:, :], in_=oBt[:, :, :], axis=AX.X, op=ALU.add), reads=['oBt'], writes=['oss'])
                            p.op('act', lambda e: e.activation(out=oss[:, :], in_=oss[:, :], func=AF.Ln, scale=1.0 / 128, bias=EPS), reads=['oss'], writes=['oss'])
                            p.op('act', lambda e: e.activation(out=oss[:, :], in_=oss[:, :], func=AF.Exp, scale=-0.5), reads=['oss'], writes=['oss'])
                            p.op('pool', lambda e: e.tensor_tensor(out=onb[:, :, :], in0=osum[:, :, :], in1=bc_j(oss[:, :]), op=ALU.mult),
                                 reads=['osum', 'oss'], writes=['onb'])
                            b = npb()
                            for h in range(NH):
                                p.op('pe', lambda e, h=h: e.transpose(PB[b][:, h, :], onb[:, h, :], ident_b[:, :]),
                                     reads=['onb', 'ident_b'], writes=[('PB', b)])
                            p.op('dve', lambda e: e.tensor_tensor(out=ogc[:, :, :], in0=PB[b][:, :, :], in1=zsc[:, :, :], op=ALU.mult),
                                 reads=[('PB', b), 'zsc'], writes=['ogc'])
                            p.dma('act', OG[:, r0:r0 + 128].rearrange("(h d) t -> d h t", d=128), ogc[:, :, :], reads=['ogc'], writes=['OGall'], key='ogc')
                        yield

                def run_units(units):
                    preps = {}
                    done_prep = set()
                    nxt_prep = 0
                    cur_seq = None
                    cur_u = 0
                    nun = len(units)
                    while cur_u < nun:
                        while nxt_prep < nun and nxt_prep <= cur_u + 2 and len(preps) < cfg.get('dn_par', 2) and (nxt_prep - 2) not in preps:
                            n, d, own, no, fd = units[nxt_prep]
                            preps[nxt_prep] = prep_gen(nxt_prep, n, d, own, no)
                            nxt_prep += 1
                        if cur_seq is None and cur_u in done_prep:
                            n, d, own, no, fd = units[cur_u]
                            cur_seq = seq_gen(cur_u, n, d, own, no, fd)
                        progressed = False
                        if cur_seq is not None:
                            try:
                                next(cur_seq)
                            except StopIteration:
                                cur_seq = None
                                cur_u += 1
                            progressed = True
                        for uu in sorted(list(preps.keys())):
                            try:
                                next(preps[uu])
                            except StopIteration:
                                del preps[uu]
                                done_prep.add(uu)
                            progressed = True
                        assert progressed or cur_u >= nun

                p.op('pool', lambda e: e.memset(S[:, :, :], 0.0), writes=['S'])
                p.op('pool', lambda e: e.memset(Sb[:, :, :], 0.0), writes=['Sb'])
                nck = cfg.get("nchunks", 32)
                if cfg.get("dn_test") == "A":
                    prep_half(True)
                    if "prep_steps" in cfg:
                        g = prep_gen(0, 0, 0, True, True)
                        for _ in range(cfg["prep_steps"]):
                            next(g)
                        p.barrier()
                        return
                    run_units([(n, 0, True, True, False) for n in range(nck)])
                    p.barrier()
                    return
                prep_half(False)
                run_units([(n, 1, False, False, False) for n in range(nck - 1, -1, -1)])
                p.barrier()
                prep_half(True)
                run_units([(n, 1, True, True, False) for n in range(nck - 1, -1, -1)])
                p.barrier()
                p.op('pool', lambda e: e.memset(S[:, :, :], 0.0), writes=['S'])
                p.op('pool', lambda e: e.memset(Sb[:, :, :], 0.0), writes=['Sb'])
                run_units([(n, 0, True, True, True) for n in range(nck)])
                p.barrier()

        if "dn" in cfg["phases"]:
            phase_B()

        N1, N2, NF = 97, 128, 97 * 128
        NEXT = 24608
        NPAD = 12800

        def phase_C1():
            with ExitStack() as sC:
                hd2 = sb(sC, "hd2", [64, NPAD], F32)
                w1t = sb(sC, "w1t", [33, 64], F32)
                w2t = sb(sC, "w2t", [64, 64], F32)
                w3t = sb(sC, "w3t", [64, 3, D], F32)
                frt = sb(sC, "frt", [64, 1], F32)
                fb1 = sb(sC, "fb1", [64, 1], F32)
                fb2 = sb(sC, "fb2", [64, 1], F32)
                ldt = sb(sC, "ldt", [128, 3, 8], F32)
                rate = sb(sC, "rate", [128, 3, 8], F32)
                nrate = sb(sC, "nrate", [128, 3, 8], F32)
                dl = sb(sC, "dl", [128, 512], F32)
                tp0 = sb(sC, "tp0", [128, 25], F32)
                bq = sb(sC, "bq", [128, 25], F32)
                zp = [sb(sC, f"zp{i}", [33, 512], F32) for i in range(2)]
                arg = [sb(sC, f"arg{i}", [64, 512], F32) for i in range(2)]
                kint = [sb(sC, f"kint{i}", [64, 512], mybir.dt.int32) for i in range(2)]
                kf = [sb(sC, f"kf{i}", [64, 512], F32) for i in range(2)]
                h1 = [sb(sC, f"h1_{i}", [64, 512], F32) for i in range(2)]
                win = [sb(sC, f"win{i}", [128, 512], F32) for i in range(2)]
                kl = [sb(sC, f"kl{i}", [128, NPAD], BF16) for i in range(2)]
                pm = [ps(sC, f"pm{i}", [128, 512], F32) for i in range(4)]
                PI = float(np.pi)
                p.dma('sp', w1t[:, :], hy_w1[:, :], writes=['w1t'])
                p.dma('sp', w2t[:, :], hy_w2[:, :], writes=['w2t'])
                p.dma('sp', w3t[:, :, :], hy_w3[:, :, :], writes=['w3t'])
                p.dma('sp', frt[:, :], hy_freq.rearrange("(p o) -> p o", o=1), writes=['frt'], allow_slow_non_contiguous=True)
                p.dma('sp', fb1[:, :], hy_b1.rearrange("(p o) -> p o", o=1), writes=['fb1'], allow_slow_non_contiguous=True)
                p.dma('sp', fb2[:, :], hy_b2.rearrange("(p o) -> p o", o=1), writes=['fb2'], allow_slow_non_contiguous=True)
                for s_ in range(3):
                    p.dma('sp', ldt[:, s_, :], hy_log_decay[s_, :].rearrange("(b p) -> p b", p=128), writes=[('ldt', s_)], allow_slow_non_contiguous=True)
                p.dma('sp', dl[:, :], c_dl[0, :].partition_broadcast(128), writes=['dl'])
                p.dma('sp', tp0[:, :], c_tp0.partition_broadcast(128), writes=['tp0'])
                p.op('dve', lambda e: e.tensor_tensor(out=fb1[:, :], in0=fb1[:, :], in1=frt[:, :], op=ALU.mult), reads=['fb1', 'frt'], writes=['fb1'])
                p.op('dve', lambda e: e.tensor_tensor(out=fb2[:, :], in0=fb2[:, :], in1=frt[:, :], op=ALU.mult), reads=['fb2', 'frt'], writes=['fb2'])
                p.op('act', lambda e: e.activation(out=rate[:, :, :], in_=ldt[:, :, :], func=AF.Exp), reads=[('ldt', i) for i in range(3)], writes=['rate'])
                p.op('dve', lambda e: e.tensor_scalar(out=nrate[:, :, :], in0=rate[:, :, :], scalar1=-1.0, scalar2=None, op0=ALU.mult), reads=['rate'], writes=['nrate'])
                pmc = [0]

                def npm():
                    v = pmc[0] % 4
                    pmc[0] += 1
                    return v

                def sin_layer(src_ps, src_key, fbias, fkey, dst, dst_keys, i):
                    p.op('dve', lambda e: e.tensor_scalar(out=arg[i][:, :], in0=src_ps, scalar1=frt[:, 0:1], scalar2=fbias[:, 0:1], op0=ALU.mult, op1=ALU.add),
                         reads=['frt', fkey, src_key], writes=[('arg', i)])
                    p.op('dve', lambda e: e.tensor_scalar(out=kint[i][:, :], in0=arg[i][:, :], scalar1=1.0 / (2 * PI), scalar2=64.0, op0=ALU.mult, op1=ALU.add),
                         reads=[('arg', i)], writes=[('kint', i)])
                    p.op('dve', lambda e: e.tensor_scalar(out=kf[i][:, :], in0=kint[i][:, :], scalar1=-64.0, scalar2=None, op0=ALU.add),
                         reads=[('kint', i)], writes=[('kf', i)])
                    p.op('dve', lambda e: e.scalar_tensor_tensor(out=arg[i][:, :], in0=kf[i][:, :], scalar=-2 * PI, in1=arg[i][:, :], op0=ALU.mult, op1=ALU.add),
                         reads=[('kf', i), ('arg', i)], writes=[('arg', i)])
                    p.op('act', lambda e: e.activation(out=dst, in_=arg[i][:, :], func=AF.Sin), reads=[('arg', i)], writes=dst_keys)

                for q in range(25):
                    i = q % 2
                    p.dma('sp', zp[i][:, :], c_zpos[:, q * 512:(q + 1) * 512], writes=[('zp', i)])
                    a = npm()
                    p.op('pe', lambda e: e.matmul(pm[a][0:64, :], w1t[:, :], zp[i][:, :], start=True, stop=True), reads=['w1t', ('zp', i)], writes=[('pm', a)])
                    sin_layer(pm[a][0:64, :], ('pm', a), fb1, 'fb1', h1[i][:, :], [('h1', i)], i)
                    a = npm()
                    p.op('pe', lambda e: e.matmul(pm[a][0:64, :], w2t[:, :], h1[i][:, :], start=True, stop=True), reads=['w2t', ('h1', i)], writes=[('pm', a)])
                    sin_layer(pm[a][0:64, :], ('pm', a), fb2, 'fb2', hd2[:, q * 512:(q + 1) * 512], [('hd2', q)], i)
                for cb in range(8):
                    kb = cb % 2
                    p.op('dve', lambda e: e.tensor_scalar(out=bq[:, 0:8], in0=tp0[:, 0:8], scalar1=nrate[:, 0, cb:cb + 1], scalar2=None, op0=ALU.mult),
                         reads=['tp0', 'nrate'], writes=['bq'])
                    p.op('dve', lambda e: e.tensor_scalar(out=bq[:, 8:25], in0=tp0[:, 8:25], scalar1=nrate[:, 1, cb:cb + 1], scalar2=None, op0=ALU.mult),
                         reads=['tp0', 'nrate'], writes=['bq'])
                    for q in range(25):
                        st = 0 if q < 8 else 1
                        a = npm()
                        wi = q % 2
                        p.op('pe', lambda e: e.matmul(pm[a][:, :], w3t[:, st, cb * 128:(cb + 1) * 128], hd2[:, q * 512:(q + 1) * 512], start=True, stop=True),
                             reads=['w3t', ('hd2', q)], writes=[('pm', a)])
                        sc = nrate[:, 0, cb:cb + 1] if q < 8 else rate[:, 1, cb:cb + 1]
                        p.op('act', lambda e: e.activation(out=win[wi][:, :], in_=dl[:, :], func=AF.Exp, scale=sc, bias=bq[:, q:q + 1]),
                             reads=['dl', 'rate', 'nrate', 'bq'], writes=[('win', wi)])
                        p.op('dve', lambda e: e.tensor_tensor(out=kl[kb][:, q * 512:(q + 1) * 512], in0=pm[a][:, :], in1=win[wi][:, :], op=ALU.mult),
                             reads=[('pm', a), ('win', wi)], writes=[('kl', kb, q)])
                    a = npm()
                    p.op('pe', lambda e: e.matmul(pm[a][:, 0:1], w3t[:, 2, cb * 128:(cb + 1) * 128], hd2[:, 0:1], start=True, stop=True),
                         reads=['w3t', ('hd2', 0)], writes=[('pm', a)])
                    p.op('dve', lambda e: e.tensor_copy(kl[kb][:, 0:1], pm[a][:, 0:1]), reads=[('pm', a)], writes=[('kl', kb, 0)])
                    p.op('pool', lambda e: e.memset(kl[kb][:, HALF:HALF + 129], 0.0), writes=[('kl', kb, 8)])
                    p.dma('sp', KLS[cb * 128:(cb + 1) * 128, :], kl[kb][:, 0:NF], reads=[('kl', kb, q) for q in range(25)], writes=[('KLS', cb)], key=('kl', kb))
                p.barrier()

        def phase_C2():
            KP = 4
            with ExitStack() as sC:
                EXT = sb(sC, "EXT", [128, NEXT], BF16)
                Xt = sb(sC, "Xt", [128, N2, 128], BF16)
                Kr = sb(sC, "Kr", [128, 128, 65], BF16)
                Ki = sb(sC, "Ki", [128, 128, 65], BF16)
                nKi = sb(sC, "nKi", [128, 128, 65], BF16)
                YE = sb(sC, "YE", [128, HALF], BF16)
                G0 = sb(sC, "G0", [128, HALF], BF16)
                ub = sb(sC, "ub", [128, HALF], BF16)
                yo = sb(sC, "yo", [128, HALF], F32)
                yhb = sb(sC, "yhb", [128, HALF], BF16)
                hbt = sb(sC, "hbt", [128, 8], F32)
                p.dma('sp', hbt[:, :], hy_bias.rearrange("(b p) -> p b", p=128), writes=['hbt'], allow_slow_non_contiguous=True)
                mats = {}
                stg = sb(sC, "mstg", [128, 194], F32)
                for nm, src, r, c in (("e1", c_e1, 97, 194), ("s2a", c_s2a, 128, 130), ("s2b", c_s2b, 128, 130),
                                      ("i1c", c_i1c, 97, 194), ("i1d", c_i1d, 97, 194), ("cw", c_cw, 65, 128), ("sw", c_sw, 65, 128)):
                    t_ = sb(sC, "m_" + nm, [128, c], BF16)
                    p.dma('sp', stg[0:r, 0:c], src[:, :], writes=['mstg'])
                    p.op('dve', lambda e: e.tensor_copy(t_[0:r, :], stg[0:r, 0:c]), reads=['mstg'], writes=['m_' + nm])
                    mats[nm] = t_
                Y1 = [sb(sC, f"Y1_{i}", [128, 2, 194], BF16) for i in range(KP)]
                Zs = [sb(sC, f"Zs{i}", [128, 2, 130], BF16) for i in range(KP)]
                Zt = [sb(sC, f"Zt{i}", [128, 2, 130], F32) for i in range(KP)]
                Zu = [sb(sC, f"Zu{i}", [128, 2, 130], F32) for i in range(KP)]
                Vs = [sb(sC, f"Vs{i}", [128, 2, 194], BF16) for i in range(KP)]
                Yc = [sb(sC, f"Yc{i}", [128, 8, N1], BF16) for i in range(2)]
                KP = 4
                PA_ = [ps(sC, f"PA_{i}", [128, 512], F32) for i in range(KP)]
                PB_ = [ps(sC, f"PB_{i}", [128, 512], F32) for i in range(KP)]
                P1 = PA_
                cnt = {'pr': 0, 'tp': 0}

                def to_Xt():
                    for g in range(N2 // 8):
                        b = cnt['tp'] % 4
                        cnt['tp'] += 1
                        pt = P1[b][:, :].bitcast(BF16).rearrange("p (a c) -> p a c", c=128)
                        for a in range(8):
                            t2 = g * 8 + a
                            p.op('pe', lambda e, a=a, t2=t2: e.transpose(pt[0:N1, a, :], EXT[:, 97 * t2:97 * t2 + 128 * (N1 - 1) + 1:128], ident_b[:, :]),
                                 reads=['EXT', 'ident_b'], writes=[('P1', b)])
                        eng = 'act' if g % 2 == 0 else 'dve'
                        if eng == 'act':
                            p.op('act', lambda e: e.activation(out=Xt[0:N1, g * 8:(g + 1) * 8, :], in_=pt[0:N1, 0:8, :], func=AF.Copy),
                                 reads=[('P1', b)], writes=[('Xt', g)])
                        else:
                            p.op('dve', lambda e: e.tensor_copy(Xt[0:N1, g * 8:(g + 1) * 8, :], pt[0:N1, 0:8, :]), reads=[('P1', b)], writes=[('Xt', g)])

                XtK = [('Xt', g) for g in range(N2 // 8)]
                XtC = [('Xtc', c0) for c0 in range(0, 128, 2)]

                def pair_gen(i, spec):
                    c0, is_filter = spec
                    kA, kB = ('P1', i), ('P2', i)
                    p1 = PA_[i][:, 0:388].rearrange("p (a f) -> p a f", f=194)
                    p2 = PB_[i][:, 0:260].rearrange("p (a f) -> p a f", f=130)
                    for a in range(2):
                        p.op('pe', lambda e, a=a: e.matmul(p1[:, a, :], Xt[0:N1, :, c0 + a], mats["e1"][0:N1, :], start=True, stop=True),
                             reads=XtK + [('Xtc', c0), 'm_e1'], writes=[kA])
                    p.op('act', lambda e: e.activation(out=Y1[i][:, :, :], in_=p1, func=AF.Copy), reads=[kA], writes=[('Y1', i)])
                    yield
                    for a in range(2):
                        p.op('pe', lambda e, a=a: e.matmul(p2[0:N1, a, :], Y1[i][:, a, 0:97], mats["s2a"][:, :], start=True, stop=False),
                             reads=[('Y1', i), 'm_s2a'], writes=[kB])
                        p.op('pe', lambda e, a=a: e.matmul(p2[0:N1, a, :], Y1[i][:, a, 97:194], mats["s2b"][:, :], start=False, stop=True),
                             reads=[('Y1', i), 'm_s2b'], writes=[kB])
                    if is_filter:
                        p.op('act', lambda e: e.activation(out=Kr[0:N1, c0:c0 + 2, :], in_=p2[0:N1, :, 0:65], func=AF.Copy), reads=[kB], writes=[('K', c0)])
                        p.op('dve', lambda e: e.tensor_copy(Ki[0:N1, c0:c0 + 2, :], p2[0:N1, :, 65:130]), reads=[kB], writes=[('K', c0)])
                        p.op('dve', lambda e: e.tensor_scalar(out=nKi[0:N1, c0:c0 + 2, :], in0=p2[0:N1, :, 65:130], scalar1=-1.0, scalar2=None, op0=ALU.mult),
                             reads=[kB], writes=[('K', c0)])
                        return
                    krb = Kr[0:N1, c0:c0 + 2, :].unsqueeze(2).broadcast_to([N1, 2, 2, 65])
                    p2v = p2[0:N1, :, :].rearrange("p a (r f) -> p a r f", r=2)
                    p.op('dve', lambda e: e.tensor_tensor(out=Zt[i][0:N1, :, :].rearrange("p a (r f) -> p a r f", r=2), in0=p2v, in1=krb, op=ALU.mult),
                         reads=[kB, ('K', c0)], writes=[('Zt', i)])
                    p.op('dve', lambda e: e.tensor_tensor(out=Zu[i][0:N1, :, 0:65], in0=p2[0:N1, :, 65:130], in1=nKi[0:N1, c0:c0 + 2, :], op=ALU.mult),
                         reads=[kB, ('K', c0)], writes=[('Zu', i)])
                    p.op('dve', lambda e: e.tensor_tensor(out=Zu[i][0:N1, :, 65:130], in0=p2[0:N1, :, 0:65], in1=Ki[0:N1, c0:c0 + 2, :], op=ALU.mult),
                         reads=[kB, ('K', c0)], writes=[('Zu', i)])
                    p.op('pool', lambda e: e.tensor_tensor(out=Zs[i][0:N1, :, :], in0=Zt[i][0:N1, :, :], in1=Zu[i][0:N1, :, :], op=ALU.add),
                         reads=[('Zt', i), ('Zu', i)], writes=[('Zs', i)])
                    yield
                    p3 = PA_[i][:, 0:388].rearrange("p (a f) -> p a f", f=194)
                    for a in range(2):
                        p.op('pe', lambda e, a=a: e.matmul(p3[0:65, a, :], Zs[i][0:N1, a, 0:65], mats["i1c"][0:N1, :], start=True, stop=False),
                             reads=[('Zs', i), 'm_i1c'], writes=[kA])
                        p.op('pe', lambda e, a=a: e.matmul(p3[0:65, a, :], Zs[i][0:N1, a, 65:130], mats["i1d"][0:N1, :], start=False, stop=True),
                             reads=[('Zs', i), 'm_i1d'], writes=[kA])
                    p.op('act', lambda e: e.activation(out=Vs[i][0:65, :, :], in_=p3[0:65, :, :], func=AF.Copy), reads=[kA], writes=[('Vs', i)])
                    yield
                    p4 = PB_[i][:, 0:256].rearrange("p (a f) -> p a f", f=128)
                    for a in range(2):
                        p.op('pe', lambda e, a=a: e.matmul(p4[0:N1, a, :], Vs[i][0:65, a, 0:97], mats["cw"][0:65, :], start=True, stop=False),
                             reads=[('Vs', i), 'm_cw'], writes=[kB])
                        p.op('pe', lambda e, a=a: e.matmul(p4[0:N1, a, :], Vs[i][0:65, a, 97:194], mats["sw"][0:65, :], start=False, stop=True),
                             reads=[('Vs', i), 'm_sw'], writes=[kB])
                    p.op('act', lambda e: e.activation(out=Xt[0:N1, :, c0:c0 + 2].rearrange("p t a -> p a t"), in_=p4[0:N1, :, :], func=AF.Copy),
                         reads=[kB], writes=[('Xtc', c0)])

                def from_Xt():
                    for g in range(N2 // 8):
                        b = cnt['tp'] % 4
                        cnt['tp'] += 1
                        pt = P1[b][:, :].bitcast(BF16)[:, 0:8 * 98].rearrange("p (a t) -> p a t", t=98)[:, :, 0:N1]
                        for a in range(8):
                            t2 = g * 8 + a
                            p.op('pe', lambda e, a=a, t2=t2: e.transpose(pt[:, a, :], Xt[0:N1, t2, :], ident_b[0:N1, 0:N1]),
                                 reads=XtK + XtC + ['ident_b'], writes=[('P1', b)])
                        yb = g % 2
                        p.op('act', lambda e: e.activation(out=Yc[yb][:, :, :], in_=pt, func=AF.Copy), reads=[('P1', b)], writes=[('Yc', yb)])
                        for a in range(8):
                            t2 = g * 8 + a
                            lo0 = 0
                            hi0 = min(N1, max(0, -(-(HALF - 97 * t2) // 128)))
                            if hi0 > lo0:
                                p.op('pool', lambda e, a=a, t2=t2, hi0=hi0: e.tensor_copy(YE[:, 97 * t2:97 * t2 + 128 * (hi0 - 1) + 1:128], Yc[yb][:, a, 0:hi0]),
                                     reads=[('Yc', yb)], writes=['YE'])
                            lo1 = max(0, -(-(NF - 97 * t2) // 128))
                            hi1 = min(N1, -(-(NF + HALF - 97 * t2) // 128))
                            if hi1 > lo1:
                                s0 = 97 * t2 + 128 * lo1 - NF
                                n_ = hi1 - lo1
                                p.op('pool', lambda e, a=a, s0=s0, n_=n_, lo1=lo1, hi1=hi1: e.tensor_copy(YE[:, s0:s0 + 128 * (n_ - 1) + 1:128], Yc[yb][:, a, lo1:hi1]),
                                     reads=[('Yc', yb)], writes=['YE'])

                nblk = cfg.get("hy_blocks", 8)
                for cb in range(nblk):
                    rows = slice(cb * 128, (cb + 1) * 128)
                    p.dma('sp', EXT[:, 0:NF], KLS[rows, :], reads=[('KLS', cb)], writes=['EXT'])
                    p.dma('sp', EXT[:, NF:NEXT], KLS[rows, 0:NEXT - NF], reads=[('KLS', cb)], writes=['EXT'], key='EXTb')
                    to_Xt()
                    run_rr([(c0, True) for c0 in range(0, 128, 2)], pair_gen, KP, stagger=cfg.get('stagC', 0))
                    p.dma('sp', EXT[:, 0:L], US[rows, :], reads=[('US', cb, o_, q_) for o_ in (True, False) for q_ in range(2)], writes=['EXT'])
                    p.dma('sp', EXT[:, NF:NF + L], US[rows, :], reads=[('US', cb, o_, q_) for o_ in (True, False) for q_ in range(2)], writes=['EXT'], key='EXTb')
                    p.op('pool', lambda e: e.memset(EXT[:, L:NF], 0.0), writes=['EXT'])
                    p.op('pool', lambda e: e.memset(EXT[:, NF + L:NEXT], 0.0), writes=['EXT'])
                    p.dma('sp', G0[:, :], G0S[rows, :], reads=[('G0S', cb, 0), ('G0S', cb, 1)], writes=['G0'])
                    p.dma('sp', ub[:, :], US[rows, 0:HALF], reads=[('US', cb, True, q_) for q_ in range(2)], writes=['ub'])
                    to_Xt()
                    run_rr([(c0, False) for c0 in range(0, 128, 2)], pair_gen, KP, stagger=cfg.get('stagC', 0))
                    from_Xt()
                    p.op('dve', lambda e: e.scalar_tensor_tensor(out=yo[:, :], in0=ub[:, :], scalar=hbt[:, cb:cb + 1], in1=YE[:, :], op0=ALU.mult, op1=ALU.add),
                         reads=['ub', 'hbt', 'YE'], writes=['yo'])
                    p.op('pool', lambda e: e.tensor_tensor(out=yhb[:, :], in0=yo[:, :], in1=G0[:, :], op=ALU.mult), reads=['yo', 'G0'], writes=['yhb'])
                    p.dma('pool', YH[rows, :], yhb[:, :], reads=['yhb'], writes=['YHall'], key='yhb')
                    if "YE" in cfg.get("dbg", ()):
                        p.dma('sp', dbg_out["YE"][rows, :], YE[:, :], reads=['YE'], writes=[('dbgYE', cb)], key='dbgYE')
                p.barrier()

        if "hyena" in cfg["phases"]:
            if "YE" in cfg.get("dbg", ()):
                ddbg("YE", [D, HALF], BF16)
            if "skipC1" not in cfg.get("dbg", ()):
                phase_C1()
            phase_C2()

        if "out" in cfg["phases"]:
            with ExitStack() as s4:
                wst4 = [sb(s4, f"w4st{i}", [128, 8, 512], F32) for i in range(2)]
                now_t = sb(s4, "now_t", [128, D], F32)
                p.dma('sp', now_t[:], norm_out_w.partition_broadcast(128), writes=['now_t'])
                wdn = sb(s4, "wdn", [128, 8, D], BF16)
                why = sb(s4, "why", [128, 8, D], BF16)
                wo = sb(s4, "wo", [128, 8, D], BF16)
                ci = 0
                for wsrc, wdst, nm in ((w_dn_out, wdn, 'wdn'), (w_hy_out, why, 'why'), (w_out, wo, 'wo')):
                    for hh in range(2):
                        b = ci % 2
                        ci += 1
                        p.dma('sp', wst4[b][:, :, :], wsrc[:, hh * 512:(hh + 1) * 512].rearrange("(k p) c -> p k c", p=128),
                              writes=[('w4st', b)])
                        p.op('pool', lambda e: e.tensor_copy(wdst[:, :, hh * 512:(hh + 1) * 512], wst4[b][:, :, :]),
                             reads=[('w4st', b)], writes=[(nm, hh)])
                ogb = [sb(s4, f"ogb{i}", [128, 8, 512], BF16) for i in range(2)]
                yhb = [sb(s4, f"yhb{i}", [128, 8, 512], BF16) for i in range(2)]
                gtb = [sb(s4, f"gtb{i}", [128, 16, 512], BF16) for i in range(2)]
                m1 = [sb(s4, f"m1_{i}", [128, 512], F32) for i in range(2)]
                m2 = [sb(s4, f"m2_{i}", [128, 512], F32) for i in range(2)]
                mb = [sb(s4, f"mb{i}", [128, 8, 512], BF16) for i in range(2)]
                xr = [sb(s4, f"xr{i}", [128, D], F32) for i in range(2)]
                res = [sb(s4, f"res{i}", [128, D], F32) for i in range(2)]
                junk4 = sb(s4, "junk4", [128, D], BF16)
                ss4 = [sb(s4, f"ss4_{i}", [128, 1], F32) for i in range(2)]
                ot = [sb(s4, f"ot{i}", [128, D], F32) for i in range(2)]
                pa = [ps(s4, f"pa{i}", [128, 512], F32) for i in range(2)]
                pb = [ps(s4, f"pb{i}", [128, 512], F32) for i in range(2)]
                pf = [ps(s4, f"pf{i}", [128, 512], F32) for i in range(4)]
                out_toks = []
                ti = 0
                for bk in range(8):
                    b = bk % 2
                    tsl = slice(bk * 512, (bk + 1) * 512)
                    p.dma('sp', ogb[b][:, :, :], OG[:, tsl].rearrange("(k p) t -> p k t", p=128),
                          reads=[('OG', k) for k in range(8)] + ['OGall'], writes=[('ogb', b)])
                    p.dma('sp', yhb[b][:, :, :], YH[:, tsl].rearrange("(k p) t -> p k t", p=128),
                          reads=[('YH', k) for k in range(8)] + ['YHall'], writes=[('yhb', b)])
                    p.dma('sp', gtb[b][:, :, :], GS[:, tsl].rearrange("(k p) t -> p k t", p=128),
                          reads=[('GS', k) for k in range(16)], writes=[('gtb', b)])
                    for dg in range(8):
                        q2 = dg % 2
                        for k in range(8):
                            p.op('pe', lambda e, k=k: e.matmul(pa[q2][:, :], wdn[:, k, dg * 128:(dg + 1) * 128], ogb[b][:, k, :],
                                                               start=(k == 0), stop=(k == 7)),
                                 reads=[('wdn', dg // 4), ('ogb', b)], writes=[('pa', q2)])
                        for k in range(8):
                            p.op('pe', lambda e, k=k: e.matmul(pb[q2][:, :], why[:, k, dg * 128:(dg + 1) * 128], yhb[b][:, k, :],
                                                               start=(k == 0), stop=(k == 7)),
                                 reads=[('why', dg // 4), ('yhb', b)], writes=[('pb', q2)])
                        p.op('dve', lambda e: e.tensor_tensor(out=m1[q2][:, :], in0=pa[q2][:, :], in1=gtb[b][:, dg, :], op=ALU.mult),
                             reads=[('pa', q2), ('gtb', b)], writes=[('m1', q2)])
                        p.op('dve', lambda e: e.tensor_tensor(out=m2[q2][:, :], in0=pb[q2][:, :], in1=gtb[b][:, 8 + dg, :], op=ALU.mult),
                             reads=[('pb', q2), ('gtb', b)], writes=[('m2', q2)])
                        p.op('pool', lambda e: e.tensor_tensor(out=mb[b][:, dg, :], in0=m1[q2][:, :], in1=m2[q2][:, :], op=ALU.add),
                             reads=[('m1', q2), ('m2', q2)], writes=[('mb', b, dg)])
                    for tt in range(4):
                        t0 = bk * 512 + tt * 128
                        r = ti % 2
                        ti += 1
                        p.dma('sp', xr[r][:, :], x[t0:t0 + 128, :], writes=[('xr', r)])
                        for nh in range(2):
                            fi = (2 * ti + nh) % 4
                            for k in range(8):
                                p.op('pe', lambda e, k=k: e.matmul(pf[fi][:, :], mb[b][:, k, tt * 128:(tt + 1) * 128],
                                                                   wo[:, k, nh * 512:(nh + 1) * 512], start=(k == 0), stop=(k == 7)),
                                     reads=[('mb', b, k), ('wo', nh)], writes=[('pf', fi)])
                            p.op('dve', lambda e: e.tensor_tensor(out=res[r][:, nh * 512:(nh + 1) * 512], in0=pf[fi][:, :],
                                                                  in1=xr[r][:, nh * 512:(nh + 1) * 512], op=ALU.add),
                                 reads=[('pf', fi), ('xr', r)], writes=[('res', r, nh)])
                        p.op('act', lambda e: e.activation(out=junk4[:, :], in_=res[r][:, :], func=AF.Square, accum_out=ss4[r][:, :]),
                             reads=[('res', r, 0), ('res', r, 1)], writes=['junk4', ('ss4', r)])
                        p.op('act', lambda e: e.activation(out=ss4[r][:, :], in_=ss4[r][:, :], func=AF.Ln, scale=1.0 / D, bias=EPS),
                             reads=[('ss4', r)], writes=[('ss4', r)])
                        p.op('act', lambda e: e.activation(out=ss4[r][:, :], in_=ss4[r][:, :], func=AF.Exp, scale=-0.5),
                             reads=[('ss4', r)], writes=[('ss4', r)])
                        p.op('dve', lambda e: e.scalar_tensor_tensor(out=ot[r][:, :], in0=res[r][:, :], scalar=ss4[r][:, :],
                                                                      in1=now_t[:, :], op0=ALU.mult, op1=ALU.mult),
                             reads=[('res', r, 0), ('res', r, 1), ('ss4', r), 'now_t'], writes=[('ot', r)])
                        out_toks.append(p.dma('act', y[t0:t0 + 128, :], ot[r][:, :], reads=[('ot', r)], writes=[('y', t0)], key=('ot', r)))
                p.barrier()
        for nm in cfg.get("dump", ()):
            src = {"OG": OG, "YH": YH, "GS": GS, "QS": QS, "KS": KS, "VS": VS, "ZS": ZS, "BGS": BGS, "OBS": OBS, "US": US, "G0S": G0S, "KLS": KLS}[nm]
            dst = ddbg(nm, src.shape, src.dtype)
            nr = src.shape[0]
            step = 128 if nr >= 128 else nr
            for r0 in range(0, nr, step):
                p.dma('sp', dst[r0:r0 + step, :], src[r0:r0 + step, :], reads=[], writes=[('dump', nm, r0)], key=('dump', (r0 // step) % 4))
        p.barrier()
        print("instr counts", p.ninstr, "nsem", p.nsem)
    return nc


def _core_inputs(inputs, b, hf):
    xs = inputs["x"][b]
    w_in = inputs["w_in"][0]
    if hf == 1:
        xs = xs[::-1]
        perm = np.arange(INW)
        for base in (C_BETA, C_A):
            perm[base:base + 8] = np.arange(base + 8, base + 16)
            perm[base + 8:base + 16] = np.arange(base, base + 8)
        w_in = w_in[:, perm]
    dcw = inputs["dn_conv_w"][0]
    hcw = inputs["hy_conv_w"][0]
    alog = inputs["dn_a_log"][0]
    dtb = inputs["dn_dt_bias"][0]
    if hf == 1:
        dcw, hcw, alog, dtb = dcw[::-1], hcw[::-1], alog[::-1], dtb[::-1]
    w3 = inputs["hy_w3"][0]
    ld = inputs["hy_log_decay"][0]
    w3f, w3b, ldf, ldb = w3[:, :D], w3[:, D:], ld[:D], ld[D:]
    if hf == 0:
        w3s, lds = np.stack([w3f, w3b, w3f], axis=1), np.stack([ldf, ldb, ldf], axis=0)
    else:
        w3s, lds = np.stack([w3b, w3f, w3f], axis=1), np.stack([ldb, ldf, ldf], axis=0)
    m = {
        "hy_w1": np.ascontiguousarray(inputs["hy_w1"][0]), "hy_b1": np.ascontiguousarray(inputs["hy_b1"][0]),
        "hy_w2": np.ascontiguousarray(inputs["hy_w2"][0]), "hy_b2": np.ascontiguousarray(inputs["hy_b2"][0]),
        "hy_freq": np.ascontiguousarray(inputs["hy_freq"][0]), "hy_w3": np.ascontiguousarray(w3s),
        "hy_log_decay": np.ascontiguousarray(lds), "hy_bias": np.ascontiguousarray(inputs["hy_bias"][0]),
        "dn_conv_w": np.ascontiguousarray(dcw), "hy_conv_w": np.ascontiguousarray(hcw),
        "dn_a_log": np.ascontiguousarray(alog).reshape(16), "dn_dt_bias": np.ascontiguousarray(dtb).reshape(16),
        "dn_norm_w": np.ascontiguousarray(inputs["dn_norm_w"][0]),
        "x": np.ascontiguousarray(xs, dtype=np.float32),
        "norm_in_w": np.ascontiguousarray(inputs["norm_in_w"][0]),
        "w_in": np.ascontiguousarray(w_in),
        "w_dn_out": np.ascontiguousarray(inputs["w_dn_out"][0]),
        "w_hy_out": np.ascontiguousarray(inputs["w_hy_out"][0]),
        "w_out": np.ascontiguousarray(inputs["w_out"][0]),
        "norm_out_w": np.ascontiguousarray(inputs["norm_out_w"]),
    }
    m.update(_consts())
    return m


FULL_CFG = {"phases": ("projA", "hyproj", "gates", "dn", "hyena", "out")}


def kernel(**inputs):
    nc = build(FULL_CFG)
    in_maps = [_core_inputs(inputs, c // 2, c % 2) for c in range(8)]
    res = run_bass_kernel_spmd(nc, in_maps, core_ids=list(range(8)))
    out = np.empty((4, L, D), np.float32)
    for c in range(8):
        b, hf = c // 2, c % 2
        yc = res.results[c]["y"]
        if hf == 0:
            out[b, :HALF] = yc
        else:
            out[b, HALF:] = yc[::-1]
    return out
```

```python
import numpy as np
import concourse.bass as bass
import concourse.mybir as mybir
from concourse.bass_utils import run_bass_kernel_spmd
from contextlib import ExitStack

F32 = mybir.dt.float32
BF16 = mybir.dt.bfloat16
AF = mybir.ActivationFunctionType
ALU = mybir.AluOpType
AX = mybir.AxisListType

D = 1024
L = 8192
HALF = 4096
NH = 8
INW = 10272
EPS = 1e-6
C_QKV, C_DNZ, C_BETA, C_A, C_HXV, C_HZ, C_GATE = 0, 3072, 4096, 4112, 4128, 7200, 8224


class Prog:
    SEM_EPOCH = 20000

    def __init__(self, nc, es, same_engine_sync=True):
        self.nc = nc
        self.es = es
        self.engs = {'pe': nc.tensor, 'act': nc.scalar, 'dve': nc.vector, 'pool': nc.gpsimd, 'sp': nc.sync}
        self.sem = {}
        self.cnt = {}
        self.nsem = 0
        for e in self.engs:
            self._new_eng_sem(e)
        self.waited = {e: {} for e in self.engs}
        self.last_w = {}
        self.readers = {}
        self.dma_sem = {}
        self.same = same_engine_sync
        self.ninstr = {e: 0 for e in self.engs}
        self.last_tok = {}
        self.dma_toks = []

    def _mksem(self, name):
        self.nsem += 1
        return self.es.enter_context(self.nc.semaphore(f"{name}_{self.nsem}"))

    def _new_eng_sem(self, e):
        self.sem[e] = self._mksem("s" + e)
        self.cnt[e] = 0

    def _wait(self, e, tok):
        if tok is None:
            return
        sem, val, src = tok
        if src == e and (not self.same or e == 'pe'):
            return
        w = self.waited[e]
        k = id(sem)
        if k in w and w[k] >= val:
            return
        w[k] = val
        self.engs[e].wait_ge(sem, val)
        self.ninstr[e] += 1

    def _deps(self, e, reads, writes):
        for k in reads:
            self._wait(e, self.last_w.get(k))
        for k in writes:
            t = self.last_w.get(k)
            if t is not None and (t[2] != e or k in reads):
                self._wait(e, t)
            for t in self.readers.get(k, ()):
                if t[2] != e:
                    self._wait(e, t)

    def _commit(self, tok, reads, writes):
        for k in reads:
            self.readers.setdefault(k, []).append(tok)
        for k in writes:
            self.last_w[k] = tok
            self.readers[k] = []

    PSUM_NAMES = ('PF', 'PB', 'PFx', 'pp', 'pn', 'ph', 'pa', 'pb', 'pf', 'pTo', 'pTx', 'pm', 'P1', 'P2', 'P3', 'P4')

    def _excl(self, reads, writes):
        r2, w2 = [], list(writes)
        for k in reads:
            nm = k[0] if isinstance(k, tuple) else k
            if nm in self.PSUM_NAMES:
                if k not in w2:
                    w2.append(k)
            else:
                r2.append(k)
        return r2, w2

    def op(self, e, fn, reads=(), writes=()):
        reads, writes = self._excl(reads, writes)
        self._deps(e, reads, writes)
        if self.cnt[e] >= self.SEM_EPOCH:
            self._new_eng_sem(e)
        ins = fn(self.engs[e])
        self.cnt[e] += 1
        ins.then_inc(self.sem[e], 1)
        self.ninstr[e] += 1
        tok = (self.sem[e], self.cnt[e], e)
        self.last_tok[e] = tok
        self._commit(tok, reads, writes)
        return tok

    def dma(self, e, out, in_, reads=(), writes=(), key=None, **kw):
        self._deps(e, reads, writes)
        if key is None:
            key = (writes[0] if writes else reads[0])
        ds = self.dma_sem.get(key)
        if ds is None or ds[1] + 16 > self.SEM_EPOCH:
            ds = [self._mksem("d"), 0]
            self.dma_sem[key] = ds
        ds[1] += 16
        self.engs[e].dma_start(out=out, in_=in_, **kw).then_inc(ds[0], 16)
        self.ninstr[e] += 1
        tok = (ds[0], ds[1], 'dma')
        self.dma_toks.append(tok)
        self._commit(tok, reads, writes)
        return tok

    def barrier(self):
        toks = list(self.last_tok.values())
        latest = {}
        for t in self.dma_toks:
            k = id(t[0])
            if k not in latest or latest[k][1] < t[1]:
                latest[k] = t
        toks += list(latest.values())
        self.dma_toks = list(latest.values())
        for e in self.engs:
            for t in toks:
                if t[2] == e:
                    continue
                self._wait(e, t)
        self.last_w = {}
        self.readers = {}


def _consts():
    ident = np.eye(128, dtype=np.float32)
    pi, fi = np.meshgrid(np.arange(128), np.arange(128), indexing="ij")
    NEG = -30000.0
    tri = np.stack([
        (pi <= fi).astype(np.float32),
        (pi >= fi).astype(np.float32),
        np.where(fi < pi, 0.0, NEG),
        np.where(fi >= pi, 0.0, NEG),
        np.where(fi > pi, 0.0, NEG),
        np.where(fi <= pi, 0.0, NEG),
    ]).astype(np.float32)
    sel = np.zeros((32, 2), np.float32)
    sel[:16, 0] = 1.0
    sel[16:, 1] = 1.0
    msk = np.stack([(pi // 32 == fi // 32), (pi // 64 == fi // 64) & (pi // 32 != fi // 32), (pi // 64 != fi // 64)]).astype(np.float32)
    out = {"c_ident": ident, "c_tri": tri, "c_sel": sel, "c_msk": msk}
    NF, NP = 97 * 128, 12800
    j = np.arange(NP)
    pos = np.where(j < HALF, j, NF - j).astype(np.float64)
    pos = np.clip(pos, 0, None)
    tl = pos / (L - 1)
    bands = 16
    fb = np.linspace(1e-4, bands - 1, bands)
    ang = (2.0 * np.pi / L) * pos[None, :] * fb[:, None]
    out["c_zpos"] = np.concatenate([tl[None, :], np.cos(ang), -np.sin(ang)], axis=0).astype(np.float32)
    out["c_dl"] = (np.arange(512) / (L - 1)).astype(np.float32)[None, :]
    q = np.arange(25)
    out["c_tp0"] = np.where(q < 8, 512 * q / (L - 1), (NF - 512 * q) / (L - 1)).astype(np.float32)
    a1 = 2 * np.pi * np.outer(np.arange(97), np.arange(97)) / 97
    c1, s1 = np.cos(a1), np.sin(a1)
    a2 = 2 * np.pi * np.outer(np.arange(128), np.arange(65)) / 128
    c2, s2 = np.cos(a2), np.sin(a2)
    out["c_e1"] = np.concatenate([c1, -s1], axis=1).astype(np.float32)
    out["c_s2a"] = np.concatenate([c2, -s2], axis=1).astype(np.float32)
    out["c_s2b"] = np.concatenate([s2, c2], axis=1).astype(np.float32)
    out["c_i1c"] = np.concatenate([c1, s1], axis=1).astype(np.float32)
    out["c_i1d"] = np.concatenate([-s1, c1], axis=1).astype(np.float32)
    wgt = np.full(65, 2.0)
    wgt[0] = wgt[64] = 1.0
    out["c_cw"] = (wgt[:, None] * c2.T / NF).astype(np.float32)
    out["c_sw"] = (-wgt[:, None] * s2.T / NF).astype(np.float32)
    return out


def build(cfg):
    nc = bass.Bass("TRN2", target_bir_lowering=False)
    dbg = cfg.get("dbg", ())

    def din(name, shape, dt=F32):
        return nc.dram_tensor(name, list(shape), dt, kind="ExternalInput").ap()

    def dscr(name, shape, dt, ext=False):
        kind = "ExternalInput" if ext else "Internal"
        return nc.dram_tensor(name, list(shape), dt, kind=kind).ap()

    x = din("x", [L, D])
    norm_in_w = din("norm_in_w", [D])
    w_in = din("w_in", [D, INW])
    w_dn_out = din("w_dn_out", [D, D])
    w_hy_out = din("w_hy_out", [D, D])
    w_out = din("w_out", [D, D])
    norm_out_w = din("norm_out_w", [D])
    c_ident = din("c_ident", [128, 128])
    y = nc.dram_tensor("y", [HALF, D], F32, kind="ExternalOutput").ap()

    ext = cfg.get("ext_scratch", ())
    OG = dscr("OG", [D, HALF], BF16, "OG" in ext)
    YH = dscr("YH", [D, HALF], BF16, "YH" in ext)
    GS = dscr("GS", [2 * D, HALF], BF16, "GS" in ext)
    QS = dscr("QS", [D, HALF], BF16, "QS" in ext)
    KS = dscr("KS", [D, L], BF16, "KS" in ext)
    VS = dscr("VS", [D, L], BF16, "VS" in ext)
    ZS = dscr("ZS", [D, HALF], BF16, "ZS" in ext)
    BGS = dscr("BGS", [32, L], F32, "BGS" in ext)
    OBS = dscr("OBS", [HALF, D], F32, "OBS" in ext)
    US = dscr("US", [D, L], BF16, "US" in ext)
    G0S = dscr("G0S", [D, HALF], BF16, "G0S" in ext)
    KLS = dscr("KLS", [D, 97 * 128], BF16, "KLS" in ext)
    hy_w1 = din("hy_w1", [33, 64])
    hy_b1 = din("hy_b1", [64])
    hy_w2 = din("hy_w2", [64, 64])
    hy_b2 = din("hy_b2", [64])
    hy_freq = din("hy_freq", [64])
    hy_w3 = din("hy_w3", [64, 3, D])
    hy_log_decay = din("hy_log_decay", [3, D])
    hy_bias = din("hy_bias", [D])
    c_zpos = din("c_zpos", [33, 12800])
    c_dl = din("c_dl", [1, 512])
    c_tp0 = din("c_tp0", [25])
    c_e1 = din("c_e1", [97, 194])
    c_s2a = din("c_s2a", [128, 130])
    c_s2b = din("c_s2b", [128, 130])
    c_i1c = din("c_i1c", [97, 194])
    c_i1d = din("c_i1d", [97, 194])
    c_cw = din("c_cw", [65, 128])
    c_sw = din("c_sw", [65, 128])
    dn_conv_w = din("dn_conv_w", [3, 3 * D])
    hy_conv_w = din("hy_conv_w", [3, 3 * D])
    dn_a_log = din("dn_a_log", [16])
    dn_dt_bias = din("dn_dt_bias", [16])
    dn_norm_w = din("dn_norm_w", [128])
    c_sel = din("c_sel", [32, 2])
    c_tri = din("c_tri", [6, 128, 128])
    c_msk = din("c_msk", [3, 128, 128])

    dbg_out = {}

    def ddbg(name, shape, dt=F32):
        dbg_out[name] = nc.dram_tensor("dbg_" + name, list(shape), dt, kind="ExternalOutput").ap()
        return dbg_out[name]

    with ExitStack() as es:
        p = Prog(nc, es)
        cs = ExitStack()
        es.enter_context(cs)

        uniq = [0]

        def sb(stack, name, shape, dt):
            uniq[0] += 1
            return stack.enter_context(nc.sbuf_tensor(f"{name}_{uniq[0]}", list(shape), dt))

        def ps(stack, name, shape, dt=F32):
            uniq[0] += 1
            return stack.enter_context(nc.psum_tensor(f"{name}_{uniq[0]}", list(shape), dt))

        ident_f = sb(cs, "ident_f", [128, 128], F32)
        ident_b = sb(cs, "ident_b", [128, 128], BF16)
        nw_t = sb(cs, "nw_t", [128, 8], F32)
        p.dma('sp', ident_f[:], c_ident[:, :], writes=['ident_f'])
        p.op('dve', lambda e: e.tensor_copy(ident_b[:], ident_f[:]), reads=['ident_f'], writes=['ident_b'])
        p.dma('sp', nw_t[:], norm_in_w.rearrange("(k p) -> p k", p=128), writes=['nw_t'],
              allow_slow_non_contiguous=True)
        cwd = sb(cs, "cwd", [128, 24, 3], F32)
        cwh = sb(cs, "cwh", [128, 24, 3], F32)
        nwd = sb(cs, "nwd", [128, 1], F32)
        dtb = sb(cs, "dtb", [32, 1], F32)
        negA = sb(cs, "negA", [32, 1], F32)
        selt = sb(cs, "selt", [32, 2], F32)
        selb, selg = selt[:, 0:1], selt[:, 1:2]
        for j in range(3):
            p.dma('sp', cwd[:, :, j], dn_conv_w[j, :].rearrange("(g p) -> p g", p=128), writes=[('cwd', j)], allow_slow_non_contiguous=True)
            p.dma('sp', cwh[:, :, j], hy_conv_w[j, :].rearrange("(g p) -> p g", p=128), writes=[('cwh', j)], allow_slow_non_contiguous=True)
        p.dma('sp', nwd[:, :], dn_norm_w.rearrange("(p o) -> p o", o=1), writes=['nwd'], allow_slow_non_contiguous=True)
        p.op('pool', lambda e: e.memset(dtb[:], 0.0), writes=['dtb'])
        p.op('pool', lambda e: e.memset(negA[:], 0.0), writes=['negA'])
        p.dma('sp', dtb[16:32, :], dn_dt_bias.rearrange("(p o) -> p o", o=1), reads=[], writes=['dtb'], allow_slow_non_contiguous=True)
        p.dma('sp', negA[16:32, :], dn_a_log.rearrange("(p o) -> p o", o=1), writes=['negA'], allow_slow_non_contiguous=True)
        p.dma('sp', selt[:, :], c_sel[:, :], writes=['selb'])
        p.op('act', lambda e: e.activation(out=negA[:], in_=negA[:], func=AF.Exp), reads=['negA'], writes=['negA'])
        p.op('dve', lambda e: e.tensor_scalar(out=negA[:], in0=negA[:], scalar1=-1.0, scalar2=None, op0=ALU.mult), reads=['negA'], writes=['negA'])
        p.barrier()

        def build_hT(stk, hT, tok_base, pre_tok, post_tok, tag):
            with ExitStack() as ls:
                xt = [sb(ls, f"xt{tag}{i}", [128, D], F32) for i in range(2)]
                junk = sb(ls, f"junk{tag}", [128, D], BF16)
                xn = [sb(ls, f"xn{tag}{i}", [128, D], BF16) for i in range(2)]
                ssq = [sb(ls, f"ssq{tag}{i}", [128, 1], F32) for i in range(2)]
                pT = [ps(ls, f"pT{tag}{i}", [128, 8, 128], BF16) for i in range(2)]
                jobs = [(tok_base + 128 * i, 128, 1 + 128 * i) for i in range(HALF // 128)]
                for hc, tk in ((0, pre_tok), (HALF + 1, post_tok)):
                    if tk is None:
                        p.op('pool', lambda e, hc=hc: e.memset(hT[:, :, hc:hc + 1], 0.0), writes=[('hT', 'halo', hc)])
                    else:
                        jobs.append((tk, 1, hc))
                for ji, (t0, n, c0) in enumerate(jobs):
                    b = ji % 2
                    kx, kn, ks, kp = (f'xt{tag}', b), (f'xn{tag}', b), (f'ssq{tag}', b), (f'pT{tag}', b)
                    p.dma('sp', xt[b][0:n, :], x[t0:t0 + n, :], writes=[kx])
                    p.op('act', lambda e: e.activation(out=junk[0:n, :], in_=xt[b][0:n, :], func=AF.Square,
                                                       accum_out=ssq[b][0:n, :]), reads=[kx], writes=['junk' + tag, ks])
                    p.op('act', lambda e: e.activation(out=ssq[b][0:n, :], in_=ssq[b][0:n, :], func=AF.Ln, scale=1.0 / D, bias=EPS),
                         reads=[ks], writes=[ks])
                    p.op('act', lambda e: e.activation(out=ssq[b][0:n, :], in_=ssq[b][0:n, :], func=AF.Exp, scale=-0.5),
                         reads=[ks], writes=[ks])
                    p.op('dve', lambda e: e.tensor_scalar(out=xn[b][0:n, :], in0=xt[b][0:n, :], scalar1=ssq[b][0:n, :],
                                                          scalar2=None, op0=ALU.mult), reads=[kx, ks], writes=[kn])
                    for k in range(8):
                        p.op('pe', lambda e, k=k: e.transpose(pT[b][:, k, 0:n], xn[b][0:n, k * 128:(k + 1) * 128],
                                                              ident_b[0:n, 0:n]),
                             reads=[kn, 'ident_b'], writes=[kp])
                    key = ('hT', (c0 - 1) // 512) if n == 128 else ('hT', 'halo', c0)
                    p.op('dve', lambda e: e.tensor_tensor(out=hT[:, :, c0:c0 + n], in0=pT[b][:, :, 0:n],
                                                          in1=nw_t[:, :].unsqueeze(2).broadcast_to([128, 8, n]), op=ALU.mult),
                         reads=[kp, 'nw_t'], writes=[key])
                p.barrier()

        def hT_keys(nblk=8):
            return [('hT', i) for i in range(nblk)]

        wctr = [0]

        def load_w_group(wst, wbf, src_ap):
            b = wctr[0] % len(wst)
            wctr[0] += 1
            ncol = src_ap.shape[1]
            p.dma('sp', wst[b][:, :, 0:ncol], src_ap.rearrange("(k p) c -> p k c", p=128), writes=[('wst', b)])
            p.op('pool', lambda e: e.tensor_copy(wbf[b][:, :, 0:ncol], wst[b][:, :, 0:ncol]), reads=[('wst', b)],
                 writes=[('wbf', b, k) for k in range(8)])
            return b

        W = 2048
        NB = W // 512

        def run_rr(job_specs, make_gen, K, stagger=1):
            active = {}
            nxt_job = 0
            free = list(range(K))
            since = stagger
            while nxt_job < len(job_specs) or active:
                since += 1
                while free and nxt_job < len(job_specs) and since > stagger:
                    sl = free.pop(0)
                    active[sl] = make_gen(sl, job_specs[nxt_job])
                    nxt_job += 1
                    since = 0 if stagger > 0 else since
                for sl in sorted(active.keys()):
                    try:
                        next(active[sl])
                    except StopIteration:
                        del active[sl]
                        free.append(sl)

        def phase_A(own):
            tok_base = 0 if own else HALF
            KJ = 3
            with ExitStack() as s2:
                hT = sb(s2, "hT", [128, 8, HALF + 2], BF16)
                if own:
                    build_hT(s2, hT, 0, None, HALF, "o")
                else:
                    build_hT(s2, hT, HALF, HALF - 1, None, "x")
                wst = [sb(s2, f"wst{i}", [128, 8, 128], F32) for i in range(3)]
                wbf = [sb(s2, f"wbf{i}", [128, 8, 128], BF16) for i in range(3)]
                pp = [ps(s2, f"pp{i}", [128, 512], F32) for i in range(5)]
                pn = [ps(s2, f"pn{i}", [128, 512], F32) for i in range(3)]
                R = [sb(s2, f"R{i}", [128, W + 2], F32) for i in range(KJ)]
                acc = [sb(s2, f"acc{i}", [128, W], F32) for i in range(KJ)]
                sil = acc
                sqb = [sb(s2, f"sqb{i}", [128, W], BF16) for i in range(KJ)]
                rst = [sb(s2, f"rst{i}", [128, W], F32) for i in range(KJ)]
                ob = [sb(s2, f"ob{i}", [128, W], BF16) for i in range(KJ)]
                keep2 = [sb(s2, f"keep{i}", [128, W], F32) for i in range(2)]
                ones_b = sb(s2, "ones_b", [128, 128], BF16)
                p.op('pool', lambda e: e.memset(ones_b[:], 1.0), writes=['ones_b'])
                ctr = {'pp': 0, 'pn': 0}

                def nxt(nm, n):
                    v = ctr[nm] % n
                    ctr[nm] += 1
                    return v

                def Rkeys(ri):
                    return [('R', ri, bk) for bk in range(5)]

                BW = (W + 2) // 5

                def project(sl, wb, ncol, ps_i):
                    c0 = ps_i * W
                    for bk in range(5):
                        pi = nxt('pp', 5)
                        cs = c0 + bk * BW
                        for k in range(8):
                            p.op('pe', lambda e, k=k: e.matmul(pp[pi][0:ncol, 0:BW], wbf[wb][:, k, 0:ncol], hT[:, k, cs:cs + BW],
                                                               start=(k == 0), stop=(k == 7)),
                                 reads=[('wbf', wb, k)], writes=[('pp', pi)])
                        dst = R[sl][0:ncol, bk * BW:(bk + 1) * BW]
                        if bk % 2 == 0:
                            p.op('act', lambda e: e.activation(out=dst, in_=pp[pi][0:ncol, 0:BW], func=AF.Copy), reads=[('pp', pi)], writes=[('R', sl, bk)])
                        else:
                            p.op('dve', lambda e: e.tensor_copy(dst, pp[pi][0:ncol, 0:BW]), reads=[('pp', pi)], writes=[('R', sl, bk)])

                def conv(sl, cw, g):
                    p.op('act', lambda e: e.activation(out=acc[sl][:, :], in_=R[sl][:, 0:W], func=AF.Copy, scale=cw[:, g, 0:1]),
                         reads=Rkeys(sl), writes=[('acc', sl)])
                    for j in (1, 2):
                        p.op('dve', lambda e, j=j: e.scalar_tensor_tensor(out=acc[sl][:, :], in0=R[sl][:, j:j + W], scalar=cw[:, g, j:j + 1],
                                                                           in1=acc[sl][:, :], op0=ALU.mult, op1=ALU.add),
                             reads=Rkeys(sl) + [('acc', sl)], writes=[('acc', sl)])

                def store(dst_rows, sl, ps_i, key, eng):
                    c0 = tok_base + ps_i * W if dst_rows.shape[1] == L else ps_i * W
                    p.dma(eng, dst_rows[:, c0:c0 + W], ob[sl][:, :], reads=[('ob', sl)], writes=[key], key=('ob', sl))

                RST = lambda sl: [('rst', sl, bk) for bk in range(NB)]

                def job(sl, spec):
                    kind, wb, ps_i, idx = spec
                    if kind == 'gate':
                        for bk in range(NB):
                            pi = nxt('pp', 5)
                            cs = 1 + ps_i * W + bk * 512
                            for k in range(8):
                                p.op('pe', lambda e, k=k: e.matmul(pp[pi][:, :], wbf[wb][:, k, :], hT[:, k, cs:cs + 512], start=(k == 0), stop=(k == 7)),
                                     reads=[('wbf', wb, k)], writes=[('pp', pi)])
                            p.op('act', lambda e: e.activation(out=ob[sl][:, bk * 512:(bk + 1) * 512], in_=pp[pi][:, :], func=AF.Sigmoid),
                                 reads=[('pp', pi)], writes=[('ob', sl)])
                        store(GS[idx * 128:(idx + 1) * 128, :], sl, ps_i, ('GS', idx, ps_i), 'act')
                        return
                    ncol = 32 if kind == 'bg' else 128
                    project(sl, wb, ncol, ps_i)
                    yield
                    if kind == 'bg':
                        xin = R[sl][0:32, 1:W + 1]
                        rk = Rkeys(sl)
                        t0, t1, t2 = acc[sl][0:32, :], xin, rst[sl][0:32, :]
                        K0, K1, K2 = ('acc', sl), ('R', sl, 0), RST(sl)
                        p.op('act', lambda e: e.activation(out=t0, in_=xin, func=AF.Exp, scale=-1.0), reads=rk, writes=[K0])
                        p.op('dve', lambda e: e.tensor_scalar(out=t0, in0=t0, scalar1=1.0, scalar2=None, op0=ALU.add), reads=[K0], writes=[K0])
                        p.op('dve', lambda e: e.reciprocal(out=t0, in_=t0), reads=[K0], writes=[K0])
                        p.op('dve', lambda e: e.tensor_scalar(out=t1, in0=xin, scalar1=dtb[:, 0:1], scalar2=None, op0=ALU.add),
                             reads=rk + ['dtb'], writes=rk)
                        p.op('act', lambda e: e.activation(out=t2, in_=t1, func=AF.Abs), reads=[K1], writes=K2)
                        p.op('act', lambda e: e.activation(out=t2, in_=t2, func=AF.Exp, scale=-1.0), reads=K2, writes=K2)
                        p.op('act', lambda e: e.activation(out=t2, in_=t2, func=AF.Ln, bias=1.0), reads=K2, writes=K2)
                        p.op('dve', lambda e: e.scalar_tensor_tensor(out=t1, in0=t1, scalar=0.0, in1=t2, op0=ALU.max, op1=ALU.add),
                             reads=[K1] + K2, writes=[K1])
                        p.op('dve', lambda e: e.tensor_scalar(out=t1, in0=t1, scalar1=negA[:, 0:1], scalar2=selg[:, 0:1], op0=ALU.mult, op1=ALU.mult),
                             reads=[K1, 'negA', 'selb'], writes=[K1])
                        p.op('dve', lambda e: e.scalar_tensor_tensor(out=t2, in0=t0, scalar=selb[:, 0:1], in1=t1, op0=ALU.mult, op1=ALU.add),
                             reads=[K0, K1, 'selb'], writes=K2)
                        c0 = tok_base + ps_i * W
                        p.dma('pool', BGS[:, c0:c0 + W], t2, reads=K2, writes=[('BGS', own, ps_i)], key=('rst', sl))
                        return
                    if kind in ('z', 'hz'):
                        p.op('act', lambda e: e.activation(out=sil[sl][:, :], in_=R[sl][:, 1:W + 1], func=AF.Silu), reads=Rkeys(sl), writes=[('acc', sl)])
                        yield
                        if kind == 'z':
                            p.op('dve', lambda e: e.tensor_scalar(out=ob[sl][:, :], in0=sil[sl][:, :], scalar1=nwd[:, 0:1], scalar2=None, op0=ALU.mult),
                                 reads=[('acc', sl), 'nwd'], writes=[('ob', sl)])
                            store(ZS[idx * 128:(idx + 1) * 128, :], sl, ps_i, ('ZS', idx, ps_i), 'pool')
                        else:
                            p.op('pool', lambda e: e.tensor_tensor(out=ob[sl][:, :], in0=sil[sl][:, :], in1=keep2[ps_i][:, :], op=ALU.mult),
                                 reads=[('acc', sl), ('keep', ps_i)], writes=[('ob', sl)])
                            store(G0S[idx * 128:(idx + 1) * 128, :], sl, ps_i, ('G0S', idx, ps_i), 'pool')
                        return
                    if kind in ('q', 'k', 'v'):
                        gi = {'q': idx, 'k': NH + idx, 'v': 2 * NH + idx}[kind]
                        conv(sl, cwd, gi)
                        yield
                        if kind == 'v':
                            p.op('act', lambda e: e.activation(out=ob[sl][:, :], in_=acc[sl][:, :], func=AF.Silu), reads=[('acc', sl)], writes=[('ob', sl)])
                            store(VS[idx * 128:(idx + 1) * 128, :], sl, ps_i, ('VS', idx, own, ps_i), 'act')
                            return
                        p.op('act', lambda e: e.activation(out=sil[sl][:, :], in_=acc[sl][:, :], func=AF.Silu), reads=[('acc', sl)], writes=[('acc', sl)])
                        yield
                        p.op('act', lambda e: e.activation(out=sqb[sl][:, :], in_=sil[sl][:, :], func=AF.Square),
                             reads=[('acc', sl)], writes=[('sqb', sl)])
                        yield
                        for bk in range(NB):
                            ni = nxt('pn', 3)
                            p.op('pe', lambda e: e.matmul(pn[ni][:, :], ones_b[:, :], sqb[sl][:, bk * 512:(bk + 1) * 512], start=True, stop=True),
                                 reads=['ones_b', ('sqb', sl)], writes=[('pn', ni)])
                            p.op('act', lambda e: e.activation(out=rst[sl][:, bk * 512:(bk + 1) * 512], in_=pn[ni][:, :], func=AF.Ln, bias=EPS),
                                 reads=[('pn', ni)], writes=[('rst', sl, bk)])
                        scale = 128.0 ** -0.5 if kind == 'q' else 1.0
                        p.op('act', lambda e: e.activation(out=rst[sl][:, :], in_=rst[sl][:, :], func=AF.Exp, scale=-0.5, bias=float(np.log(scale))),
                             reads=RST(sl), writes=RST(sl))
                        yield
                        p.op('pool', lambda e: e.tensor_tensor(out=ob[sl][:, :], in0=sil[sl][:, :], in1=rst[sl][:, :], op=ALU.mult),
                             reads=[('acc', sl)] + RST(sl), writes=[('ob', sl)])
                        dstT = QS if kind == 'q' else KS
                        key = ('QS', idx, ps_i) if kind == 'q' else ('KS', idx, own, ps_i)
                        store(dstT[idx * 128:(idx + 1) * 128, :], sl, ps_i, key, 'pool')
                        return
                    gi = {'x0': idx, 'x1': 8 + idx, 'hv': 16 + idx}[kind]
                    conv(sl, cwh, gi)
                    yield
                    if kind in ('x0', 'x1'):
                        p.op('pool', lambda e: e.tensor_copy(keep2[ps_i][:, :], acc[sl][:, :]), reads=[('acc', sl)], writes=[('keep', ps_i)])
                    else:
                        p.op('pool', lambda e: e.tensor_tensor(out=ob[sl][:, :], in0=acc[sl][:, :], in1=keep2[ps_i][:, :], op=ALU.mult),
                             reads=[('acc', sl), ('keep', ps_i)], writes=[('ob', sl)])
                        store(US[idx * 128:(idx + 1) * 128, :], sl, ps_i, ('US', idx, own, ps_i), 'pool')

                groups = []
                for h in range(NH):
                    for kind in (('q', 'k', 'v', 'z') if own else ('k', 'v')):
                        gi = {'q': h, 'k': NH + h, 'v': 2 * NH + h}.get(kind)
                        groups.append((kind, C_QKV + gi * 128 if kind != 'z' else C_DNZ + h * 128, 128, h))
                groups.append(('bg', C_BETA, 32, 0))
                if "hyproj" in cfg["phases"]:
                    for cb in range(8):
                        for kind in (('x0', 'hz', 'x1', 'hv') if own else ('x1', 'hv')):
                            gi = {'x0': cb, 'x1': 8 + cb, 'hv': 16 + cb}.get(kind)
                            groups.append((kind, C_HXV + gi * 128 if kind != 'hz' else C_HZ + cb * 128, 128, cb))
                if own and "gates" in cfg["phases"]:
                    for gi in range(16):
                        groups.append(('gate', C_GATE + gi * 128, 128, gi))
                specs = []
                for (kind, col0, ncol, idx) in groups:
                    for ps_i in range(2):
                        specs.append((kind, col0, ncol, ps_i, idx))
                wb_of = {}

                gorder = [(g[0], g[3]) for g in groups]
                ginfo = {(g[0], g[3]): g for g in groups}

                def ensure_w(gk):
                    if gk not in wb_of:
                        kind, col0, ncol, idx = ginfo[gk]
                        wb_of[gk] = load_w_group(wst, wbf, w_in[:, col0:col0 + ncol])

                def make(sl, sp):
                    kind, col0, ncol, ps_i, idx = sp
                    ensure_w((kind, idx))
                    gi_ = gorder.index((kind, idx))
                    if ps_i == 0 and gi_ + 1 < len(gorder):
                        ensure_w(gorder[gi_ + 1])
                    return job(sl, (kind, wb_of[(kind, idx)], ps_i, idx))

                run_rr(specs, make, KJ, stagger=cfg.get('stagA', 0))
                p.barrier()

        if "projA" in cfg["phases"]:
            phase_A(False)
            phase_A(True)

        def phase_B():
            with ExitStack() as sB:
                NEG = -30000.0
                cst = sb(sB, "tricst", [128, 6, 128], F32)
                p.dma('sp', cst[:, :, :], c_tri.rearrange("c p f -> p c f"), writes=['tricst'])
                ones_f = sb(sB, "ones_f", [128, 128], F32)
                p.op('pool', lambda e: e.memset(ones_f[:], 1.0), writes=['ones_f'])
                mskf = sb(sB, "mskf", [128, 3, 128], F32)
                p.dma('sp', mskf[:, :, :], c_msk.rearrange("c p f -> p c f"), writes=['mskf'])
                m32h = sb(sB, "m32h", [128, 8, 128], BF16)
                mo64h = sb(sB, "mo64h", [128, 8, 128], BF16)
                mo128h = sb(sB, "mo128h", [128, 8, 128], BF16)
                identh = sb(sB, "identh", [128, 8, 128], BF16)
                for i_, t_ in enumerate((m32h, mo64h, mo128h)):
                    for h in range(NH):
                        p.op('dve', lambda e, i_=i_, t_=t_, h=h: e.tensor_copy(t_[:, h, :], mskf[:, i_, :]), reads=['mskf'], writes=['mskb'])
                for h in range(NH):
                    p.op('dve', lambda e, h=h: e.tensor_copy(identh[:, h, :], ident_f[:, :]), reads=['ident_f'], writes=['mskb'])
                tri = {0: cst[:, 0, :], 1: cst[:, 1, :]}
                nmd = {0: cst[:, 2, :], 1: cst[:, 4, :]}
                nme = {0: cst[:, 3, :], 1: cst[:, 5, :]}
                tt = sb(sB, "tt", [128, 32, 32], F32)
                Gp = [sb(sB, f"Gp{d}", [128, 32, 8], F32) for d in range(2)]
                Glb = sb(sB, "Glb", [128, 32, 16], F32)
                eG = [sb(sB, f"eG{d}", [128, 32, 8], F32) for d in range(2)]
                kd = [sb(sB, f"kd{d}", [128, 32, 8], F32) for d in range(2)]
                gam = sb(sB, "gam", [128, 32, 16], F32)
                bneg = [sb(sB, f"bneg{d}", [128, 32, 8], F32) for d in range(2)]
                beg = [sb(sB, f"beg{d}", [128, 32, 8], F32) for d in range(2)]
                PF = [ps(sB, f"PF{i}", [128, 8, 128], F32) for i in range(3)]
                PB = [ps(sB, f"PB{i}", [128, 8, 128], BF16) for i in range(2)]
                pfc = [0]
                pbc = [0]

                def npf():
                    v = pfc[0] % 3
                    pfc[0] += 1
                    return v

                def npb():
                    v = pbc[0] % 2
                    pbc[0] += 1
                    return v

                HB = [128, 8, 128]
                PW = []
                for i in range(2):
                    d_ = {}
                    for nm, dt in (("rhsG", F32), ("tmp", F32), ("E", F32), ("N0", BF16), ("N1", BF16),
                                   ("M0", BF16), ("M1", BF16), ("Q0", BF16), ("Q1", BF16), ("ktok", BF16),
                                   ("kc", BF16), ("vc", BF16), ("qc", BF16)):
                        d_[nm] = sb(sB, f"pw{i}_{nm}", HB, dt)
                    d_["eGr"] = d_["rhsG"]
                    d_["kbg"] = d_["N1"]
                    PW.append(d_)
                SW = []
                for i in range(3):
                    d_ = {}
                    for nm, dt in (("vb", F32), ("kbgT", BF16), ("kdec", BF16), ("attnT", BF16), ("qdT", BF16), ("Q", BF16)):
                        d_[nm] = sb(sB, f"sw{i}_{nm}", HB, dt)
                    SW.append(d_)
                S = sb(sB, "S", HB, F32)
                rb = sb(sB, "rb", HB, BF16)
                Sb = sb(sB, "Sb", HB, BF16)
                vnb = sb(sB, "vnb", HB, BF16)
                osum = sb(sB, "osum", HB, F32)
                oBt = sb(sB, "oBt", HB, F32)
                oss = sb(sB, "oss", [128, 8], F32)
                onb = sb(sB, "onb", HB, BF16)
                zsc = sb(sB, "zsc", HB, BF16)
                ogc = sb(sB, "ogc", HB, BF16)
                identb3 = ident_b[:, :].unsqueeze(1).broadcast_to(HB)

                def bc_j(ap2):
                    return ap2.unsqueeze(2).broadcast_to(HB)

                def bc_h(ap2):
                    return ap2.unsqueeze(1).broadcast_to(HB)

                def prep_half(own):
                    with ExitStack() as sh:
                        bgsb = sb(sh, "bgsb", [32, HALF], F32)
                        prep_half_(own, bgsb)

                def prep_half_(own, bgsb):
                    base = 0 if own else HALF
                    p.dma('sp', bgsb[:, :], BGS[:, base:base + HALF], reads=[('BGS', own, 0), ('BGS', own, 1)], writes=['bgsb'])
                    pf = PF[npf()]
                    pfv = pf[:, :, :].rearrange("p h j -> p (h j)")
                    for n in range(32):
                        p.op('pe', lambda e, n=n: e.transpose(pfv[:, n * 32:(n + 1) * 32], bgsb[0:32, n * 128:(n + 1) * 128], ident_f[0:32, 0:32]),
                             reads=['bgsb', 'ident_f'], writes=['PFx'])
                    p.op('dve', lambda e: e.tensor_copy(tt[:, :, :].rearrange("p c r -> p (c r)"), pfv), reads=['PFx'], writes=['tt'])
                    for d in range(2):
                        p.op('pe', lambda e, d=d: e.matmul(pfv[:, 0:256], tri[d], tt[:, :, 16 + 8 * d:24 + 8 * d], start=True, stop=True),
                             reads=['tt', 'tricst'], writes=['PFx'])
                        p.op('dve', lambda e, d=d: e.tensor_copy(Gp[d][:, :, :].rearrange("p c h -> p (c h)"), pfv[:, 0:256]),
                             reads=['PFx'], writes=[('Gp', d)])
                    p.op('pe', lambda e: e.matmul(pfv[:, 0:512], ones_f[:, :], tt[:, :, 16:32], start=True, stop=True),
                         reads=['tt', 'ones_f'], writes=['PFx'])
                    p.op('dve', lambda e: e.tensor_copy(Glb[:, :, :].rearrange("p c h -> p (c h)"), pfv[:, 0:512]), reads=['PFx'], writes=['Glb'])
                    p.op('act', lambda e: e.activation(out=gam[:, :, :], in_=Glb[:, :, :], func=AF.Exp), reads=['Glb'], writes=['gam'])
                    for d in range(2):
                        p.op('act', lambda e, d=d: e.activation(out=eG[d][:, :, :], in_=Gp[d][:, :, :], func=AF.Exp), reads=[('Gp', d)], writes=[('eG', d)])
                        p.op('dve', lambda e, d=d: e.tensor_tensor(out=kd[d][:, :, :], in0=Glb[:, :, 8 * d:8 * d + 8], in1=Gp[d][:, :, :], op=ALU.subtract),
                             reads=['Glb', ('Gp', d)], writes=[('kd', d)])
                        p.op('act', lambda e, d=d: e.activation(out=kd[d][:, :, :], in_=kd[d][:, :, :], func=AF.Exp), reads=[('kd', d)], writes=[('kd', d)])
                        p.op('dve', lambda e, d=d: e.tensor_scalar(out=bneg[d][:, :, :], in0=tt[:, :, 8 * d:8 * d + 8], scalar1=-1.0, scalar2=None, op0=ALU.mult),
                             reads=['tt'], writes=[('bneg', d)])
                        p.op('dve', lambda e, d=d: e.tensor_tensor(out=beg[d][:, :, :], in0=tt[:, :, 8 * d:8 * d + 8], in1=eG[d][:, :, :], op=ALU.mult),
                             reads=['tt', ('eG', d)], writes=[('beg', d)])
                    p.barrier()

                def prep_gen(u, n, d, own, need_out):
                    pw = PW[u % 2]
                    sw = SW[u % 3]
                    P_ = f"pw{u % 2}"
                    S_ = f"sw{u % 3}"
                    t0 = (0 if own else HALF) + n * 128
                    kq = [('KS', h, own, (n * 128) // W) for h in range(NH)]
                    kv = [('VS', h, own, (n * 128) // W) for h in range(NH)]
                    p.dma('sp', pw["kc"][:, :, :], KS[:, t0:t0 + 128].rearrange("(h d) t -> d h t", d=128), reads=kq, writes=[P_ + "kc"])
                    p.dma('sp', pw["vc"][:, :, :], VS[:, t0:t0 + 128].rearrange("(h d) t -> d h t", d=128), reads=kv, writes=[P_ + "vc"])
                    if need_out:
                        p.dma('sp', pw["qc"][:, :, :], QS[:, t0:t0 + 128].rearrange("(h d) t -> d h t", d=128),
                              reads=[('QS', h, (n * 128) // W) for h in range(NH)], writes=[P_ + "qc"])
                    p.op('dve', lambda e: e.tensor_tensor(out=pw["rhsG"][:, :, :], in0=bc_h(tri[d]), in1=bc_j(tt[:, n, 16 + 8 * d:24 + 8 * d]), op=ALU.mult),
                         reads=['tt', 'tricst'], writes=[P_ + "rhsG"])
                    yield
                    a = npf()
                    for hh in range(2):
                        p.op('pe', lambda e, hh=hh: e.matmul(PF[a][:, 4 * hh:4 * hh + 4, :], ones_f[:, :], pw["rhsG"][:, 4 * hh:4 * hh + 4, :], start=True, stop=True),
                             reads=[P_ + "rhsG", 'ones_f'], writes=[('PF', a)])
                    p.op('dve', lambda e: e.tensor_tensor(out=pw["tmp"][:, :, :], in0=PF[a][:, :, :], in1=bc_j(Gp[d][:, n, :]), op=ALU.subtract),
                         reads=[('PF', a), ('Gp', d)], writes=[P_ + "tmp"])
                    if need_out:
                        p.op('act', lambda e: e.activation(out=pw["eGr"][:, :, :], in_=PF[a][:, :, :], func=AF.Exp), reads=[('PF', a)], writes=[P_ + "rhsG"])
                    yield
                    if need_out:
                        p.op('pool', lambda e: e.tensor_tensor(out=pw["E"][:, :, :], in0=pw["tmp"][:, :, :], in1=bc_h(nme[d]), op=ALU.add),
                             reads=[P_ + "tmp", 'tricst'], writes=[P_ + "E"])
                        p.op('act', lambda e: e.activation(out=pw["E"][:, :, :], in_=pw["E"][:, :, :], func=AF.Exp), reads=[P_ + "E"], writes=[P_ + "E"])
                    p.op('pool', lambda e: e.tensor_tensor(out=pw["tmp"][:, :, :], in0=bc_h(nmd[d]), in1=pw["tmp"][:, :, :], op=ALU.subtract),
                         reads=[P_ + "tmp", 'tricst'], writes=[P_ + "tmp"])
                    p.op('act', lambda e: e.activation(out=pw["tmp"][:, :, :], in_=pw["tmp"][:, :, :], func=AF.Exp), reads=[P_ + "tmp"], writes=[P_ + "tmp"])
                    yield
                    a = npf()
                    for h in range(NH):
                        p.op('pe', lambda e, h=h: e.matmul(PF[a][:, h, :], pw["kc"][:, h, :], pw["kc"][:, h, :], start=True, stop=True),
                             reads=[P_ + "kc"], writes=[('PF', a)])
                    p.op('pool', lambda e: e.tensor_tensor(out=pw["tmp"][:, :, :], in0=pw["tmp"][:, :, :], in1=bc_j(bneg[d][:, n, :]), op=ALU.mult),
                         reads=[P_ + "tmp", ('bneg', d)], writes=[P_ + "tmp"])
                    p.op('dve', lambda e: e.tensor_tensor(out=pw["N0"][:, :, :], in0=PF[a][:, :, :], in1=pw["tmp"][:, :, :], op=ALU.mult),
                         reads=[('PF', a), P_ + "tmp"], writes=[P_ + "N0"])
                    yield
                    b = npb()
                    for h in range(NH):
                        p.op('pe', lambda e, h=h: e.transpose(PB[b][:, h, :], pw["N0"][:, h, :], ident_b[:, :]),
                             reads=[P_ + "N0", 'ident_b'], writes=[('PB', b)])
                    p.op('act', lambda e: e.activation(out=pw["M0"][:, :, :], in_=PB[b][:, :, :], func=AF.Copy), reads=[('PB', b)], writes=[P_ + "M0"])
                    yield
                    if need_out:
                        a = npf()
                        for h in range(NH):
                            p.op('pe', lambda e, h=h: e.matmul(PF[a][:, h, :], pw["kc"][:, h, :], pw["qc"][:, h, :], start=True, stop=True),
                                 reads=[P_ + "kc", P_ + "qc"], writes=[('PF', a)])
                        p.op('dve', lambda e: e.tensor_tensor(out=sw["attnT"][:, :, :], in0=PF[a][:, :, :], in1=pw["E"][:, :, :], op=ALU.mult),
                             reads=[('PF', a), P_ + "E"], writes=[S_ + "attnT"])
                        p.op('pool', lambda e: e.tensor_tensor(out=sw["qdT"][:, :, :], in0=pw["qc"][:, :, :], in1=pw["eGr"][:, :, :], op=ALU.mult),
                             reads=[P_ + "qc", P_ + "rhsG"], writes=[S_ + "qdT"])
                        yield
                    def mmg(dst_ps, lk, rk, lkey, rkey):
                        for h in range(NH):
                            p.op('pe', lambda e, h=h: e.matmul(PF[dst_ps][:, h, :], lk[:, h, :], rk[:, h, :], start=True, stop=True),
                                 reads=[lkey, rkey], writes=[('PF', dst_ps)])
                    N_, M_, T_, W_ = pw["N0"], pw["M0"], pw["Q0"], pw["Q1"]
                    kN, kM, kT, kW = P_ + "N0", P_ + "M0", P_ + "Q0", P_ + "Q1"
                    No1, Mo1, No2 = pw["N1"], pw["M1"], pw["ktok"]
                    kNo1, kMo1, kNo2 = P_ + "N1", P_ + "M1", P_ + "ktok"
                    p.op('dve', lambda e: e.tensor_tensor(out=No1[:, :, :], in0=N_[:, :, :], in1=mo64h[:, :, :], op=ALU.mult), reads=[kN, 'mskb'], writes=[kNo1])
                    p.op('dve', lambda e: e.tensor_tensor(out=Mo1[:, :, :], in0=M_[:, :, :], in1=mo64h[:, :, :], op=ALU.mult), reads=[kM, 'mskb'], writes=[kMo1])
                    p.op('dve', lambda e: e.tensor_tensor(out=No2[:, :, :], in0=N_[:, :, :], in1=mo128h[:, :, :], op=ALU.mult), reads=[kN, 'mskb'], writes=[kNo2])
                    p.op('dve', lambda e: e.tensor_tensor(out=N_[:, :, :], in0=N_[:, :, :], in1=m32h[:, :, :], op=ALU.mult), reads=[kN, 'mskb'], writes=[kN])
                    p.op('dve', lambda e: e.tensor_tensor(out=M_[:, :, :], in0=M_[:, :, :], in1=m32h[:, :, :], op=ALU.mult), reads=[kM, 'mskb'], writes=[kM])
                    p.op('dve', lambda e: e.tensor_tensor(out=T_[:, :, :], in0=N_[:, :, :], in1=identh[:, :, :], op=ALU.add), reads=[kN, 'mskb'], writes=[kT])
                    p.op('dve', lambda e: e.tensor_tensor(out=W_[:, :, :], in0=M_[:, :, :], in1=identh[:, :, :], op=ALU.add), reads=[kM, 'mskb'], writes=[kW])
                    yield
                    for lvl in range(1, 5):
                        a1, a2 = npf(), npf()
                        mmg(a1, M_, N_, kM, kN)
                        mmg(a2, N_, M_, kN, kM)
                        p.op('act', lambda e: e.activation(out=N_[:, :, :], in_=PF[a1][:, :, :], func=AF.Copy), reads=[('PF', a1)], writes=[kN])
                        p.op('act', lambda e: e.activation(out=M_[:, :, :], in_=PF[a2][:, :, :], func=AF.Copy), reads=[('PF', a2)], writes=[kM])
                        yield
                        a1, a2 = npf(), npf()
                        mmg(a1, M_, T_, kM, kT)
                        mmg(a2, N_, W_, kN, kW)
                        p.op('dve', lambda e: e.tensor_tensor(out=T_[:, :, :], in0=PF[a1][:, :, :], in1=T_[:, :, :], op=ALU.add), reads=[('PF', a1), kT], writes=[kT])
                        p.op('dve', lambda e: e.tensor_tensor(out=W_[:, :, :], in0=PF[a2][:, :, :], in1=W_[:, :, :], op=ALU.add), reads=[('PF', a2), kW], writes=[kW])
                        yield
                    a1, a2 = npf(), npf()
                    mmg(a1, Mo1, T_, kMo1, kT)
                    mmg(a2, No1, W_, kNo1, kW)
                    p.op('act', lambda e: e.activation(out=N_[:, :, :], in_=PF[a1][:, :, :], func=AF.Copy), reads=[('PF', a1)], writes=[kN])
                    p.op('dve', lambda e: e.tensor_copy(M_[:, :, :], PF[a2][:, :, :]), reads=[('PF', a2)], writes=[kM])
                    yield
                    a1, a2 = npf(), npf()
                    mmg(a1, W_, N_, kW, kN)
                    mmg(a2, T_, M_, kT, kM)
                    p.op('dve', lambda e: e.tensor_tensor(out=T_[:, :, :], in0=PF[a1][:, :, :], in1=T_[:, :, :], op=ALU.add), reads=[('PF', a1), kT], writes=[kT])
                    p.op('dve', lambda e: e.tensor_tensor(out=W_[:, :, :], in0=PF[a2][:, :, :], in1=W_[:, :, :], op=ALU.add), reads=[('PF', a2), kW], writes=[kW])
                    yield
                    a1 = npf()
                    mmg(a1, No2, W_, kNo2, kW)
                    p.op('act', lambda e: e.activation(out=M_[:, :, :], in_=PF[a1][:, :, :], func=AF.Copy), reads=[('PF', a1)], writes=[kM])
                    yield
                    a1 = npf()
                    mmg(a1, T_, M_, kT, kM)
                    p.op('dve', lambda e: e.tensor_tensor(out=sw["Q"][:, :, :], in0=PF[a1][:, :, :], in1=W_[:, :, :], op=ALU.add), reads=[('PF', a1), kW], writes=[S_ + "Q"])
                    yield

                    b = npb()
                    for h in range(NH):
                        p.op('pe', lambda e, h=h: e.transpose(PB[b][:, h, :], pw["kc"][:, h, :], ident_b[:, :]),
                             reads=[P_ + "kc", 'ident_b'], writes=[('PB', b)])
                    p.op('act', lambda e: e.activation(out=pw["ktok"][:, :, :], in_=PB[b][:, :, :], func=AF.Copy), reads=[('PB', b)], writes=[P_ + "ktok"])
                    p.op('pool', lambda e: e.tensor_tensor(out=pw["kbg"][:, :, :], in0=pw["ktok"][:, :, :], in1=bc_j(beg[d][:, n, :]), op=ALU.mult),
                         reads=[P_ + "ktok", ('beg', d)], writes=[P_ + "N1"])
                    p.op('pool', lambda e: e.tensor_tensor(out=sw["kdec"][:, :, :], in0=pw["ktok"][:, :, :], in1=bc_j(kd[d][:, n, :]), op=ALU.mult),
                         reads=[P_ + "ktok", ('kd', d)], writes=[S_ + "kdec"])
                    yield
                    b = npb()
                    for h in range(NH):
                        p.op('pe', lambda e, h=h: e.transpose(PB[b][:, h, :], pw["kbg"][:, h, :], ident_b[:, :]),
                             reads=[P_ + "N1", 'ident_b'], writes=[('PB', b)])
                    p.op('act', lambda e: e.activation(out=sw["kbgT"][:, :, :], in_=PB[b][:, :, :], func=AF.Copy), reads=[('PB', b)], writes=[S_ + "kbgT"])
                    yield
                    b = npb()
                    for h in range(NH):
                        p.op('pe', lambda e, h=h: e.transpose(PB[b][:, h, :], pw["vc"][:, h, :], ident_b[:, :]),
                             reads=[P_ + "vc", 'ident_b'], writes=[('PB', b)])
                    p.op('dve', lambda e: e.tensor_tensor(out=sw["vb"][:, :, :], in0=PB[b][:, :, :], in1=bc_j(tt[:, n, 8 * d:8 * d + 8]), op=ALU.mult),
                         reads=[('PB', b), 'tt'], writes=[S_ + "vb"])
                    yield

                def seq_gen(u, n, d, own, need_out, final_dir):
                    sw = SW[u % 3]
                    S_ = f"sw{u % 3}"
                    r0 = n * 128
                    if need_out and final_dir:
                        p.dma('sp', oBt[:, :, :].rearrange("p h e -> p (h e)"), OBS[r0:r0 + 128, :], reads=[('OBS', n)], writes=['oBt'])
                        p.dma('sp', zsc[:, :, :], ZS[:, r0:r0 + 128].rearrange("(h d) t -> d h t", d=128),
                              reads=[('ZS', h, r0 // W) for h in range(NH)], writes=['zsc'])
                    a = npf()
                    for h in range(NH):
                        p.op('pe', lambda e, h=h: e.matmul(PF[a][:, h, :], sw["kbgT"][:, h, :], Sb[:, h, :], start=True, stop=True),
                             reads=[S_ + "kbgT", 'Sb'], writes=[('PF', a)])
                    p.op('dve', lambda e: e.tensor_tensor(out=rb[:, :, :], in0=sw["vb"][:, :, :], in1=PF[a][:, :, :], op=ALU.subtract),
                         reads=[('PF', a), S_ + "vb"], writes=['rb'])
                    yield
                    a = npf()
                    for h in range(NH):
                        p.op('pe', lambda e, h=h: e.matmul(PF[a][:, h, :], sw["Q"][:, h, :], rb[:, h, :], start=True, stop=True),
                             reads=[S_ + "Q", 'rb'], writes=[('PF', a)])
                    p.op('act', lambda e: e.activation(out=vnb[:, :, :], in_=PF[a][:, :, :], func=AF.Copy), reads=[('PF', a)], writes=['vnb'])
                    yield
                    if need_out:
                        ao = npf()
                        for h in range(NH):
                            p.op('pe', lambda e, h=h: e.matmul(PF[ao][:, h, :], sw["qdT"][:, h, :], Sb[:, h, :], start=True, stop=False),
                                 reads=[S_ + "qdT", 'Sb'], writes=[('PF', ao)])
                            p.op('pe', lambda e, h=h: e.matmul(PF[ao][:, h, :], sw["attnT"][:, h, :], vnb[:, h, :], start=False, stop=True),
                                 reads=[S_ + "attnT", 'vnb'], writes=[('PF', ao)])
                    a = npf()
                    for h in range(NH):
                        p.op('pe', lambda e, h=h: e.matmul(PF[a][:, h, :], sw["kdec"][:, h, :], vnb[:, h, :], start=True, stop=True),
                             reads=[S_ + "kdec", 'vnb'], writes=[('PF', a)])
                    p.op('pool', lambda e: e.tensor_tensor(out=S[:, :, :], in0=S[:, :, :], in1=bc_j(gam[:, n, 8 * d:8 * d + 8]), op=ALU.mult),
                         reads=['S', 'gam'], writes=['S'])
                    p.op('dve', lambda e: e.tensor_tensor(out=S[:, :, :], in0=S[:, :, :], in1=PF[a][:, :, :], op=ALU.add),
                         reads=['S', ('PF', a)], writes=['S'])
                    p.op('act', lambda e: e.activation(out=Sb[:, :, :], in_=S[:, :, :], func=AF.Copy), reads=['S'], writes=['Sb'])
                    if need_out:
                        if not final_dir:
                            p.op('act', lambda e: e.activation(out=osum[:, :, :], in_=PF[ao][:, :, :], func=AF.Copy), reads=[('PF', ao)], writes=['osum'])
                        else:
                            p.op('dve', lambda e: e.tensor_tensor(out=osum[:, :, :], in0=PF[ao][:, :, :], in1=oBt[:, :, :], op=ALU.add),
                                 reads=[('PF', ao), 'oBt'], writes=['osum'])
                    yield
                    if need_out:
                        if not final_dir:
                            p.dma('act', OBS[r0:r0 + 128, :], osum[:, :, :].rearrange("p h e -> p (h e)"), reads=['osum'], writes=[('OBS', n)], key='osum')
                        else:
                            p.op('pool', lambda e: e.tensor_tensor(out=oBt[:, :, :], in0=osum[:, :, :], in1=osum[:, :, :], op=ALU.mult),
                                 reads=['osum'], writes=['oBt'])
                            p.op('dve', lambda e: e.tensor_reduce(out=oss[:, :], in_=oBt[:, :, :], axis=AX.X, op=ALU.add), reads=['oBt'], writes=['oss'])
                            p.op('act', lambda e: e.activation(out=oss[:, :], in_=oss[:, :], func=AF.Ln, scale=1.0 / 128, bias=EPS), reads=['oss'], writes=['oss'])
                            p.op('act', lambda e: e.activation(out=oss[:, :], in_=oss[:, :], func=AF.Exp, scale=-0.5), reads=['oss'], writes=['oss'])
                            p.op('pool', lambda e: e.tensor_tensor(out=onb[:, :, :], in0=osum[:, :, :], in1=bc_j(oss[:, :]), op=ALU.mult),
                                 reads=['osum', 'oss'], writes=['onb'])
                            b = npb()
                            for h in range(NH):
                                p.op('pe', lambda e, h=h: e.transpose(PB[b][:, h, :], onb[:, h, :], ident_b[:, :]),
                                     reads=['onb', 'ident_b'], writes=[('PB', b)])
                            p.op('dve', lambda e: e.tensor_tensor(out=ogc[:, :, :], in0=PB[b][:, :, :], in1=zsc[:, :, :], op=ALU.mult),
                                 reads=[('PB', b), 'zsc'], writes=['ogc'])
                            p.dma('act', OG[:, r0:r0 + 128].rearrange("(h d) t -> d h t", d=128), ogc[:, :, :], reads=['ogc'], writes=['OGall'], key='ogc')
                        yield

                def run_units(units):
                    preps = {}
                    done_prep = set()
                    nxt_prep = 0
                    cur_seq = None
                    cur_u = 0
                    nun = len(units)
                    while cur_u < nun:
                        while nxt_prep < nun and nxt_prep <= cur_u + 2 and len(preps) < cfg.get('dn_par', 2) and (nxt_prep - 2) not in preps:
                            n, d, own, no, fd = units[nxt_prep]
                            preps[nxt_prep] = prep_gen(nxt_prep, n, d, own, no)
                            nxt_prep += 1
                        if cur_seq is None and cur_u in done_prep:
                            n, d, own, no, fd = units[cur_u]
                            cur_seq = seq_gen(cur_u, n, d, own, no, fd)
                        progressed = False
                        if cur_seq is not None:
                            try:
                                next(cur_seq)
                            except StopIteration:
                                cur_seq = None
                                cur_u += 1
                            progressed = True
                        for uu in sorted(list(preps.keys())):
                            try:
                                next(preps[uu])
                            except StopIteration:
                                del preps[uu]
                                done_prep.add(uu)
                            progressed = True
                        assert progressed or cur_u >= nun

                p.op('pool', lambda e: e.memset(S[:, :, :], 0.0), writes=['S'])
                p.op('pool', lambda e: e.memset(Sb[:, :, :], 0.0), writes=['Sb'])
                nck = cfg.get("nchunks", 32)
                if cfg.get("dn_test") == "A":
                    prep_half(True)
                    if "prep_steps" in cfg:
                        g = prep_gen(0, 0, 0, True, True)
                        for _ in range(cfg["prep_steps"]):
                            next(g)
                        p.barrier()
                        return
                    run_units([(n, 0, True, True, False) for n in range(nck)])
                    p.barrier()
                    return
                prep_half(False)
                run_units([(n, 1, False, False, False) for n in range(nck - 1, -1, -1)])
                p.barrier()
                prep_half(True)
                run_units([(n, 1, True, True, False) for n in range(nck - 1, -1, -1)])
                p.barrier()
                p.op('pool', lambda e: e.memset(S[:, :, :], 0.0), writes=['S'])
                p.op('pool', lambda e: e.memset(Sb[:, :, :], 0.0), writes=['Sb'])
                run_units([(n, 0, True, True, True) for n in range(nck)])
                p.barrier()

        if "dn" in cfg["phases"]:
            phase_B()

        N1, N2, NF = 97, 128, 97 * 128
        NEXT = 24608
        NPAD = 12800

        def phase_C1():
            with ExitStack() as sC:
                hd2 = sb(sC, "hd2", [64, NPAD], F32)
                w1t = sb(sC, "w1t", [33, 64], F32)
                w2t = sb(sC, "w2t", [64, 64], F32)
                w3t = sb(sC, "w3t", [64, 3, D], F32)
                frt = sb(sC, "frt", [64, 1], F32)
                fb1 = sb(sC, "fb1", [64, 1], F32)
                fb2 = sb(sC, "fb2", [64, 1], F32)
                ldt = sb(sC, "ldt", [128, 3, 8], F32)
                rate = sb(sC, "rate", [128, 3, 8], F32)
                nrate = sb(sC, "nrate", [128, 3, 8], F32)
                dl = sb(sC, "dl", [128, 512], F32)
                tp0 = sb(sC, "tp0", [128, 25], F32)
                bq = sb(sC, "bq", [128, 25], F32)
                zp = [sb(sC, f"zp{i}", [33, 512], F32) for i in range(2)]
                arg = [sb(sC, f"arg{i}", [64, 512], F32) for i in range(2)]
                kint = [sb(sC, f"kint{i}", [64, 512], mybir.dt.int32) for i in range(2)]
                kf = [sb(sC, f"kf{i}", [64, 512], F32) for i in range(2)]
                h1 = [sb(sC, f"h1_{i}", [64, 512], F32) for i in range(2)]
                win = [sb(sC, f"win{i}", [128, 512], F32) for i in range(2)]
                kl = [sb(sC, f"kl{i}", [128, NPAD], BF16) for i in range(2)]
                pm = [ps(sC, f"pm{i}", [128, 512], F32) for i in range(4)]
                PI = float(np.pi)
                p.dma('sp', w1t[:, :], hy_w1[:, :], writes=['w1t'])
                p.dma('sp', w2t[:, :], hy_w2[:, :], writes=['w2t'])
                p.dma('sp', w3t[:, :, :], hy_w3[:, :, :], writes=['w3t'])
                p.dma('sp', frt[:, :], hy_freq.rearrange("(p o) -> p o", o=1), writes=['frt'], allow_slow_non_contiguous=True)
                p.dma('sp', fb1[:, :], hy_b1.rearrange("(p o) -> p o", o=1), writes=['fb1'], allow_slow_non_contiguous=True)
                p.dma('sp', fb2[:, :], hy_b2.rearrange("(p o) -> p o", o=1), writes=['fb2'], allow_slow_non_contiguous=True)
                for s_ in range(3):
                    p.dma('sp', ldt[:, s_, :], hy_log_decay[s_, :].rearrange("(b p) -> p b", p=128), writes=[('ldt', s_)], allow_slow_non_contiguous=True)
                p.dma('sp', dl[:, :], c_dl[0, :].partition_broadcast(128), writes=['dl'])
                p.dma('sp', tp0[:, :], c_tp0.partition_broadcast(128), writes=['tp0'])
                p.op('dve', lambda e: e.tensor_tensor(out=fb1[:, :], in0=fb1[:, :], in1=frt[:, :], op=ALU.mult), reads=['fb1', 'frt'], writes=['fb1'])
                p.op('dve', lambda e: e.tensor_tensor(out=fb2[:, :], in0=fb2[:, :], in1=frt[:, :], op=ALU.mult), reads=['fb2', 'frt'], writes=['fb2'])
                p.op('act', lambda e: e.activation(out=rate[:, :, :], in_=ldt[:, :, :], func=AF.Exp), reads=[('ldt', i) for i in range(3)], writes=['rate'])
                p.op('dve', lambda e: e.tensor_scalar(out=nrate[:, :, :], in0=rate[:, :, :], scalar1=-1.0, scalar2=None, op0=ALU.mult), reads=['rate'], writes=['nrate'])
                pmc = [0]

                def npm():
                    v = pmc[0] % 4
                    pmc[0] += 1
                    return v

                def sin_layer(src_ps, src_key, fbias, fkey, dst, dst_keys, i):
                    p.op('dve', lambda e: e.tensor_scalar(out=arg[i][:, :], in0=src_ps, scalar1=frt[:, 0:1], scalar2=fbias[:, 0:1], op0=ALU.mult, op1=ALU.add),
                         reads=['frt', fkey, src_key], writes=[('arg', i)])
                    p.op('dve', lambda e: e.tensor_scalar(out=kint[i][:, :], in0=arg[i][:, :], scalar1=1.0 / (2 * PI), scalar2=64.0, op0=ALU.mult, op1=ALU.add),
                         reads=[('arg', i)], writes=[('kint', i)])
                    p.op('dve', lambda e: e.tensor_scalar(out=kf[i][:, :], in0=kint[i][:, :], scalar1=-64.0, scalar2=None, op0=ALU.add),
                         reads=[('kint', i)], writes=[('kf', i)])
                    p.op('dve', lambda e: e.scalar_tensor_tensor(out=arg[i][:, :], in0=kf[i][:, :], scalar=-2 * PI, in1=arg[i][:, :], op0=ALU.mult, op1=ALU.add),
                         reads=[('kf', i), ('arg', i)], writes=[('arg', i)])
                    p.op('act', lambda e: e.activation(out=dst, in_=arg[i][:, :], func=AF.Sin), reads=[('arg', i)], writes=dst_keys)

                for q in range(25):
                    i = q % 2
                    p.dma('sp', zp[i][:, :], c_zpos[:, q * 512:(q + 1) * 512], writes=[('zp', i)])
                    a = npm()
                    p.op('pe', lambda e: e.matmul(pm[a][0:64, :], w1t[:, :], zp[i][:, :], start=True, stop=True), reads=['w1t', ('zp', i)], writes=[('pm', a)])
                    sin_layer(pm[a][0:64, :], ('pm', a), fb1, 'fb1', h1[i][:, :], [('h1', i)], i)
                    a = npm()
                    p.op('pe', lambda e: e.matmul(pm[a][0:64, :], w2t[:, :], h1[i][:, :], start=True, stop=True), reads=['w2t', ('h1', i)], writes=[('pm', a)])
                    sin_layer(pm[a][0:64, :], ('pm', a), fb2, 'fb2', hd2[:, q * 512:(q + 1) * 512], [('hd2', q)], i)
                for cb in range(8):
                    kb = cb % 2
                    p.op('dve', lambda e: e.tensor_scalar(out=bq[:, 0:8], in0=tp0[:, 0:8], scalar1=nrate[:, 0, cb:cb + 1], scalar2=None, op0=ALU.mult),
                         reads=['tp0', 'nrate'], writes=['bq'])
                    p.op('dve', lambda e: e.tensor_scalar(out=bq[:, 8:25], in0=tp0[:, 8:25], scalar1=nrate[:, 1, cb:cb + 1], scalar2=None, op0=ALU.mult),
                         reads=['tp0', 'nrate'], writes=['bq'])
                    for q in range(25):
                        st = 0 if q < 8 else 1
                        a = npm()
                        wi = q % 2
                        p.op('pe', lambda e: e.matmul(pm[a][:, :], w3t[:, st, cb * 128:(cb + 1) * 128], hd2[:, q * 512:(q + 1) * 512], start=True, stop=True),
                             reads=['w3t', ('hd2', q)], writes=[('pm', a)])
                        sc = nrate[:, 0, cb:cb + 1] if q < 8 else rate[:, 1, cb:cb + 1]
                        p.op('act', lambda e: e.activation(out=win[wi][:, :], in_=dl[:, :], func=AF.Exp, scale=sc, bias=bq[:, q:q + 1]),
                             reads=['dl', 'rate', 'nrate', 'bq'], writes=[('win', wi)])
                        p.op('dve', lambda e: e.tensor_tensor(out=kl[kb][:, q * 512:(q + 1) * 512], in0=pm[a][:, :], in1=win[wi][:, :], op=ALU.mult),
                             reads=[('pm', a), ('win', wi)], writes=[('kl', kb, q)])
                    a = npm()
                    p.op('pe', lambda e: e.matmul(pm[a][:, 0:1], w3t[:, 2, cb * 128:(cb + 1) * 128], hd2[:, 0:1], start=True, stop=True),
                         reads=['w3t', ('hd2', 0)], writes=[('pm', a)])
                    p.op('dve', lambda e: e.tensor_copy(kl[kb][:, 0:1], pm[a][:, 0:1]), reads=[('pm', a)], writes=[('kl', kb, 0)])
                    p.op('pool', lambda e: e.memset(kl[kb][:, HALF:HALF + 129], 0.0), writes=[('kl', kb, 8)])
                    p.dma('sp', KLS[cb * 128:(cb + 1) * 128, :], kl[kb][:, 0:NF], reads=[('kl', kb, q) for q in range(25)], writes=[('KLS', cb)], key=('kl', kb))
                p.barrier()

        def phase_C2():
            KP = 4
            with ExitStack() as sC:
                EXT = sb(sC, "EXT", [128, NEXT], BF16)
                Xt = sb(sC, "Xt", [128, N2, 128], BF16)
                Kr = sb(sC, "Kr", [128, 128, 65], BF16)
                Ki = sb(sC, "Ki", [128, 128, 65], BF16)
                nKi = sb(sC, "nKi", [128, 128, 65], BF16)
                YE = sb(sC, "YE", [128, HALF], BF16)
                G0 = sb(sC, "G0", [128, HALF], BF16)
                ub = sb(sC, "ub", [128, HALF], BF16)
                yo = sb(sC, "yo", [128, HALF], F32)
                yhb = sb(sC, "yhb", [128, HALF], BF16)
                hbt = sb(sC, "hbt", [128, 8], F32)
                p.dma('sp', hbt[:, :], hy_bias.rearrange("(b p) -> p b", p=128), writes=['hbt'], allow_slow_non_contiguous=True)
                mats = {}
                stg = sb(sC, "mstg", [128, 194], F32)
                for nm, src, r, c in (("e1", c_e1, 97, 194), ("s2a", c_s2a, 128, 130), ("s2b", c_s2b, 128, 130),
                                      ("i1c", c_i1c, 97, 194), ("i1d", c_i1d, 97, 194), ("cw", c_cw, 65, 128), ("sw", c_sw, 65, 128)):
                    t_ = sb(sC, "m_" + nm, [128, c], BF16)
                    p.dma('sp', stg[0:r, 0:c], src[:, :], writes=['mstg'])
                    p.op('dve', lambda e: e.tensor_copy(t_[0:r, :], stg[0:r, 0:c]), reads=['mstg'], writes=['m_' + nm])
                    mats[nm] = t_
                Y1 = [sb(sC, f"Y1_{i}", [128, 2, 194], BF16) for i in range(KP)]
                Zs = [sb(sC, f"Zs{i}", [128, 2, 130], BF16) for i in range(KP)]
                Zt = [sb(sC, f"Zt{i}", [128, 2, 130], F32) for i in range(KP)]
                Zu = [sb(sC, f"Zu{i}", [128, 2, 130], F32) for i in range(KP)]
                Vs = [sb(sC, f"Vs{i}", [128, 2, 194], BF16) for i in range(KP)]
                Yc = [sb(sC, f"Yc{i}", [128, 8, N1], BF16) for i in range(2)]
                KP = 4
                PA_ = [ps(sC, f"PA_{i}", [128, 512], F32) for i in range(KP)]
                PB_ = [ps(sC, f"PB_{i}", [128, 512], F32) for i in range(KP)]
                P1 = PA_
                cnt = {'pr': 0, 'tp': 0}

                def to_Xt():
                    for g in range(N2 // 8):
                        b = cnt['tp'] % 4
                        cnt['tp'] += 1
                        pt = P1[b][:, :].bitcast(BF16).rearrange("p (a c) -> p a c", c=128)
                        for a in range(8):
                            t2 = g * 8 + a
                            p.op('pe', lambda e, a=a, t2=t2: e.transpose(pt[0:N1, a, :], EXT[:, 97 * t2:97 * t2 + 128 * (N1 - 1) + 1:128], ident_b[:, :]),
                                 reads=['EXT', 'EXTb', 'EXTp', 'EXTq', 'ident_b'], writes=[('P1', b)])
                        eng = 'act' if g % 2 == 0 else 'dve'
                        if eng == 'act':
                            p.op('act', lambda e: e.activation(out=Xt[0:N1, g * 8:(g + 1) * 8, :], in_=pt[0:N1, 0:8, :], func=AF.Copy),
                                 reads=[('P1', b)], writes=[('Xt', g)])
                        else:
                            p.op('dve', lambda e: e.tensor_copy(Xt[0:N1, g * 8:(g + 1) * 8, :], pt[0:N1, 0:8, :]), reads=[('P1', b)], writes=[('Xt', g)])

                XtK = [('Xt', g) for g in range(N2 // 8)]
                XtC = [('Xtc', c0) for c0 in range(0, 128, 2)]

                def pair_gen(i, spec):
                    c0, is_filter = spec
                    kA, kB = ('P1', i), ('P2', i)
                    p1 = PA_[i][:, 0:388].rearrange("p (a f) -> p a f", f=194)
                    p2 = PB_[i][:, 0:260].rearrange("p (a f) -> p a f", f=130)
                    for a in range(2):
                        p.op('pe', lambda e, a=a: e.matmul(p1[:, a, :], Xt[0:N1, :, c0 + a], mats["e1"][0:N1, :], start=True, stop=True),
                             reads=XtK + [('Xtc', c0), 'm_e1'], writes=[kA])
                    p.op('act', lambda e: e.activation(out=Y1[i][:, :, :], in_=p1, func=AF.Copy), reads=[kA], writes=[('Y1', i)])
                    yield
                    for a in range(2):
                        p.op('pe', lambda e, a=a: e.matmul(p2[0:N1, a, :], Y1[i][:, a, 0:97], mats["s2a"][:, :], start=True, stop=False),
                             reads=[('Y1', i), 'm_s2a'], writes=[kB])
                        p.op('pe', lambda e, a=a: e.matmul(p2[0:N1, a, :], Y1[i][:, a, 97:194], mats["s2b"][:, :], start=False, stop=True),
                             reads=[('Y1', i), 'm_s2b'], writes=[kB])
                    if is_filter:
                        p.op('act', lambda e: e.activation(out=Kr[0:N1, c0:c0 + 2, :], in_=p2[0:N1, :, 0:65], func=AF.Copy), reads=[kB], writes=[('K', c0)])
                        p.op('dve', lambda e: e.tensor_copy(Ki[0:N1, c0:c0 + 2, :], p2[0:N1, :, 65:130]), reads=[kB], writes=[('K', c0)])
                        p.op('dve', lambda e: e.tensor_scalar(out=nKi[0:N1, c0:c0 + 2, :], in0=p2[0:N1, :, 65:130], scalar1=-1.0, scalar2=None, op0=ALU.mult),
                             reads=[kB], writes=[('K', c0)])
                        return
                    krb = Kr[0:N1, c0:c0 + 2, :].unsqueeze(2).broadcast_to([N1, 2, 2, 65])
                    p2v = p2[0:N1, :, :].rearrange("p a (r f) -> p a r f", r=2)
                    p.op('dve', lambda e: e.tensor_tensor(out=Zt[i][0:N1, :, :].rearrange("p a (r f) -> p a r f", r=2), in0=p2v, in1=krb, op=ALU.mult),
                         reads=[kB, ('K', c0)], writes=[('Zt', i)])
                    p.op('dve', lambda e: e.tensor_tensor(out=Zu[i][0:N1, :, 0:65], in0=p2[0:N1, :, 65:130], in1=nKi[0:N1, c0:c0 + 2, :], op=ALU.mult),
                         reads=[kB, ('K', c0)], writes=[('Zu', i)])
                    p.op('dve', lambda e: e.tensor_tensor(out=Zu[i][0:N1, :, 65:130], in0=p2[0:N1, :, 0:65], in1=Ki[0:N1, c0:c0 + 2, :], op=ALU.mult),
                         reads=[kB, ('K', c0)], writes=[('Zu', i)])
                    p.op('pool', lambda e: e.tensor_tensor(out=Zs[i][0:N1, :, :], in0=Zt[i][0:N1, :, :], in1=Zu[i][0:N1, :, :], op=ALU.add),
                         reads=[('Zt', i), ('Zu', i)], writes=[('Zs', i)])
                    yield
                    p3 = PA_[i][:, 0:388].rearrange("p (a f) -> p a f", f=194)
                    for a in range(2):
                        p.op('pe', lambda e, a=a: e.matmul(p3[0:65, a, :], Zs[i][0:N1, a, 0:65], mats["i1c"][0:N1, :], start=True, stop=False),
                             reads=[('Zs', i), 'm_i1c'], writes=[kA])
                        p.op('pe', lambda e, a=a: e.matmul(p3[0:65, a, :], Zs[i][0:N1, a, 65:130], mats["i1d"][0:N1, :], start=False, stop=True),
                             reads=[('Zs', i), 'm_i1d'], writes=[kA])
                    p.op('act', lambda e: e.activation(out=Vs[i][0:65, :, :], in_=p3[0:65, :, :], func=AF.Copy), reads=[kA], writes=[('Vs', i)])
                    yield
                    p4 = PB_[i][:, 0:256].rearrange("p (a f) -> p a f", f=128)
                    for a in range(2):
                        p.op('pe', lambda e, a=a: e.matmul(p4[0:N1, a, :], Vs[i][0:65, a, 0:97], mats["cw"][0:65, :], start=True, stop=False),
                             reads=[('Vs', i), 'm_cw'], writes=[kB])
                        p.op('pe', lambda e, a=a: e.matmul(p4[0:N1, a, :], Vs[i][0:65, a, 97:194], mats["sw"][0:65, :], start=False, stop=True),
                             reads=[('Vs', i), 'm_sw'], writes=[kB])
                    p.op('act', lambda e: e.activation(out=Xt[0:N1, :, c0:c0 + 2].rearrange("p t a -> p a t"), in_=p4[0:N1, :, :], func=AF.Copy),
                         reads=[kB], writes=[('Xtc', c0)])

                def from_Xt():
                    for g in range(N2 // 8):
                        b = cnt['tp'] % 4
                        cnt['tp'] += 1
                        pt = P1[b][:, :].bitcast(BF16)[:, 0:8 * 98].rearrange("p (a t) -> p a t", t=98)[:, :, 0:N1]
                        for a in range(8):
                            t2 = g * 8 + a
                            p.op('pe', lambda e, a=a, t2=t2: e.transpose(pt[:, a, :], Xt[0:N1, t2, :], ident_b[0:N1, 0:N1]),
                                 reads=XtK + XtC + ['ident_b'], writes=[('P1', b)])
                        yb = g % 2
                        p.op('act', lambda e: e.activation(out=Yc[yb][:, :, :], in_=pt, func=AF.Copy), reads=[('P1', b)], writes=[('Yc', yb)])
                        for a in range(8):
                            t2 = g * 8 + a
                            lo0 = 0
                            hi0 = min(N1, max(0, -(-(HALF - 97 * t2) // 128)))
                            if hi0 > lo0:
                                p.op('pool', lambda e, a=a, t2=t2, hi0=hi0: e.tensor_copy(YE[:, 97 * t2:97 * t2 + 128 * (hi0 - 1) + 1:128], Yc[yb][:, a, 0:hi0]),
                                     reads=[('Yc', yb)], writes=['YE'])
                            lo1 = max(0, -(-(NF - 97 * t2) // 128))
                            hi1 = min(N1, -(-(NF + HALF - 97 * t2) // 128))
                            if hi1 > lo1:
                                s0 = 97 * t2 + 128 * lo1 - NF
                                n_ = hi1 - lo1
                                p.op('pool', lambda e, a=a, s0=s0, n_=n_, lo1=lo1, hi1=hi1: e.tensor_copy(YE[:, s0:s0 + 128 * (n_ - 1) + 1:128], Yc[yb][:, a, lo1:hi1]),
                                     reads=[('Yc', yb)], writes=['YE'])

                nblk = cfg.get("hy_blocks", 8)
                for cb in range(nblk):
                    rows = slice(cb * 128, (cb + 1) * 128)
                    p.dma('sp', EXT[:, 0:NF], KLS[rows, :], reads=[('KLS', cb)], writes=['EXT'])
                    p.dma('sp', EXT[:, NF:NEXT], KLS[rows, 0:NEXT - NF], reads=[('KLS', cb)], writes=['EXTb'], key='EXTb')
                    to_Xt()
                    run_rr([(c0, True) for c0 in range(0, 128, 2)], pair_gen, KP, stagger=cfg.get('stagC', 0))
                    p.dma('sp', EXT[:, 0:L], US[rows, :], reads=[('US', cb, o_, q_) for o_ in (True, False) for q_ in range(2)], writes=['EXT'])
                    p.dma('sp', EXT[:, NF:NF + L], US[rows, :], reads=[('US', cb, o_, q_) for o_ in (True, False) for q_ in range(2)], writes=['EXTb'], key='EXTb')
                    p.op('pool', lambda e: e.memset(EXT[:, L:NF], 0.0), writes=['EXTp'])
                    p.op('pool', lambda e: e.memset(EXT[:, NF + L:NEXT], 0.0), writes=['EXTq'])
                    p.dma('sp', G0[:, :], G0S[rows, :], reads=[('G0S', cb, 0), ('G0S', cb, 1)], writes=['G0'])
                    p.dma('sp', ub[:, :], US[rows, 0:HALF], reads=[('US', cb, True, q_) for q_ in range(2)], writes=['ub'])
                    to_Xt()
                    run_rr([(c0, False) for c0 in range(0, 128, 2)], pair_gen, KP, stagger=cfg.get('stagC', 0))
                    from_Xt()
                    p.op('dve', lambda e: e.scalar_tensor_tensor(out=yo[:, :], in0=ub[:, :], scalar=hbt[:, cb:cb + 1], in1=YE[:, :], op0=ALU.mult, op1=ALU.add),
                         reads=['ub', 'hbt', 'YE'], writes=['yo'])
                    p.op('pool', lambda e: e.tensor_tensor(out=yhb[:, :], in0=yo[:, :], in1=G0[:, :], op=ALU.mult), reads=['yo', 'G0'], writes=['yhb'])
                    p.dma('pool', YH[rows, :], yhb[:, :], reads=['yhb'], writes=['YHall'], key='yhb')
                    if "YE" in cfg.get("dbg", ()):
                        p.dma('sp', dbg_out["YE"][rows, :], YE[:, :], reads=['YE'], writes=[('dbgYE', cb)], key='dbgYE')
                p.barrier()

        if "hyena" in cfg["phases"]:
            if "YE" in cfg.get("dbg", ()):
                ddbg("YE", [D, HALF], BF16)
            if "skipC1" not in cfg.get("dbg", ()):
                phase_C1()
            phase_C2()

        if "out" in cfg["phases"]:
            with ExitStack() as s4:
                wst4 = [sb(s4, f"w4st{i}", [128, 8, 512], F32) for i in range(2)]
                now_t = sb(s4, "now_t", [128, D], F32)
                p.dma('sp', now_t[:], norm_out_w.partition_broadcast(128), writes=['now_t'])
                wdn = sb(s4, "wdn", [128, 8, D], BF16)
                why = sb(s4, "why", [128, 8, D], BF16)
                wo = sb(s4, "wo", [128, 8, D], BF16)
                ci = 0
                for wsrc, wdst, nm in ((w_dn_out, wdn, 'wdn'), (w_hy_out, why, 'why'), (w_out, wo, 'wo')):
                    for hh in range(2):
                        b = ci % 2
                        ci += 1
                        p.dma('sp', wst4[b][:, :, :], wsrc[:, hh * 512:(hh + 1) * 512].rearrange("(k p) c -> p k c", p=128),
                              writes=[('w4st', b)])
                        p.op('pool', lambda e: e.tensor_copy(wdst[:, :, hh * 512:(hh + 1) * 512], wst4[b][:, :, :]),
                             reads=[('w4st', b)], writes=[(nm, hh)])
                ogb = [sb(s4, f"ogb{i}", [128, 8, 512], BF16) for i in range(2)]
                yhb = [sb(s4, f"yhb{i}", [128, 8, 512], BF16) for i in range(2)]
                gtb = [sb(s4, f"gtb{i}", [128, 16, 512], BF16) for i in range(2)]
                m1 = [sb(s4, f"m1_{i}", [128, 512], F32) for i in range(2)]
                m2 = [sb(s4, f"m2_{i}", [128, 512], F32) for i in range(2)]
                mb = [sb(s4, f"mb{i}", [128, 8, 512], BF16) for i in range(2)]
                xr = [sb(s4, f"xr{i}", [128, D], F32) for i in range(2)]
                res = [sb(s4, f"res{i}", [128, D], F32) for i in range(2)]
                junk4 = sb(s4, "junk4", [128, D], BF16)
                ss4 = [sb(s4, f"ss4_{i}", [128, 1], F32) for i in range(2)]
                ot = [sb(s4, f"ot{i}", [128, D], F32) for i in range(2)]
                pa = [ps(s4, f"pa{i}", [128, 512], F32) for i in range(2)]
                pb = [ps(s4, f"pb{i}", [128, 512], F32) for i in range(2)]
                pf = [ps(s4, f"pf{i}", [128, 512], F32) for i in range(4)]
                out_toks = []
                ti = 0
                for bk in range(8):
                    b = bk % 2
                    tsl = slice(bk * 512, (bk + 1) * 512)
                    p.dma('sp', ogb[b][:, :, :], OG[:, tsl].rearrange("(k p) t -> p k t", p=128),
                          reads=[('OG', k) for k in range(8)] + ['OGall'], writes=[('ogb', b)])
                    p.dma('sp', yhb[b][:, :, :], YH[:, tsl].rearrange("(k p) t -> p k t", p=128),
                          reads=[('YH', k) for k in range(8)] + ['YHall'], writes=[('yhb', b)])
                    p.dma('sp', gtb[b][:, :, :], GS[:, tsl].rearrange("(k p) t -> p k t", p=128),
                          reads=[('GS', k) for k in range(16)], writes=[('gtb', b)])
                    for dg in range(8):
                        q2 = dg % 2
                        for k in range(8):
                            p.op('pe', lambda e, k=k: e.matmul(pa[q2][:, :], wdn[:, k, dg * 128:(dg + 1) * 128], ogb[b][:, k, :],
                                                               start=(k == 0), stop=(k == 7)),
                                 reads=[('wdn', dg // 4), ('ogb', b)], writes=[('pa', q2)])
                        for k in range(8):
                            p.op('pe', lambda e, k=k: e.matmul(pb[q2][:, :], why[:, k, dg * 128:(dg + 1) * 128], yhb[b][:, k, :],
                                                               start=(k == 0), stop=(k == 7)),
                                 reads=[('why', dg // 4), ('yhb', b)], writes=[('pb', q2)])
                        p.op('dve', lambda e: e.tensor_tensor(out=m1[q2][:, :], in0=pa[q2][:, :], in1=gtb[b][:, dg, :], op=ALU.mult),
                             reads=[('pa', q2), ('gtb', b)], writes=[('m1', q2)])
                        p.op('dve', lambda e: e.tensor_tensor(out=m2[q2][:, :], in0=pb[q2][:, :], in1=gtb[b][:, 8 + dg, :], op=ALU.mult),
                             reads=[('pb', q2), ('gtb', b)], writes=[('m2', q2)])
                        p.op('pool', lambda e: e.tensor_tensor(out=mb[b][:, dg, :], in0=m1[q2][:, :], in1=m2[q2][:, :], op=ALU.add),
                             reads=[('m1', q2), ('m2', q2)], writes=[('mb', b, dg)])
                    for tt in range(4):
                        t0 = bk * 512 + tt * 128
                        r = ti % 2
                        ti += 1
                        p.dma('sp', xr[r][:, :], x[t0:t0 + 128, :], writes=[('xr', r)])
                        for nh in range(2):
                            fi = (2 * ti + nh) % 4
                            for k in range(8):
                                p.op('pe', lambda e, k=k: e.matmul(pf[fi][:, :], mb[b][:, k, tt * 128:(tt + 1) * 128],
                                                                   wo[:, k, nh * 512:(nh + 1) * 512], start=(k == 0), stop=(k == 7)),
                                     reads=[('mb', b, k), ('wo', nh)], writes=[('pf', fi)])
                            p.op('dve', lambda e: e.tensor_tensor(out=res[r][:, nh * 512:(nh + 1) * 512], in0=pf[fi][:, :],
                                                                  in1=xr[r][:, nh * 512:(nh + 1) * 512], op=ALU.add),
                                 reads=[('pf', fi), ('xr', r)], writes=[('res', r, nh)])
                        p.op('act', lambda e: e.activation(out=junk4[:, :], in_=res[r][:, :], func=AF.Square, accum_out=ss4[r][:, :]),
                             reads=[('res', r, 0), ('res', r, 1)], writes=['junk4', ('ss4', r)])
                        p.op('act', lambda e: e.activation(out=ss4[r][:, :], in_=ss4[r][:, :], func=AF.Ln, scale=1.0 / D, bias=EPS),
                             reads=[('ss4', r)], writes=[('ss4', r)])
                        p.op('act', lambda e: e.activation(out=ss4[r][:, :], in_=ss4[r][:, :], func=AF.Exp, scale=-0.5),
                             reads=[('ss4', r)], writes=[('ss4', r)])
                        p.op('dve', lambda e: e.scalar_tensor_tensor(out=ot[r][:, :], in0=res[r][:, :], scalar=ss4[r][:, :],
                                                                      in1=now_t[:, :], op0=ALU.mult, op1=ALU.mult),
                             reads=[('res', r, 0), ('res', r, 1), ('ss4', r), 'now_t'], writes=[('ot', r)])
                        out_toks.append(p.dma('act', y[t0:t0 + 128, :], ot[r][:, :], reads=[('ot', r)], writes=[('y', t0)], key=('ot', r)))
                p.barrier()
        for nm in cfg.get("dump", ()):
            src = {"OG": OG, "YH": YH, "GS": GS, "QS": QS, "KS": KS, "VS": VS, "ZS": ZS, "BGS": BGS, "OBS": OBS, "US": US, "G0S": G0S, "KLS": KLS}[nm]
            dst = ddbg(nm, src.shape, src.dtype)
            nr = src.shape[0]
            step = 128 if nr >= 128 else nr
            for r0 in range(0, nr, step):
                p.dma('sp', dst[r0:r0 + step, :], src[r0:r0 + step, :], reads=[], writes=[('dump', nm, r0)], key=('dump', (r0 // step) % 4))
        p.barrier()
        print("instr counts", p.ninstr, "nsem", p.nsem)
    return nc


def _core_inputs(inputs, b, hf):
    xs = inputs["x"][b]
    w_in = inputs["w_in"][0]
    if hf == 1:
        xs = xs[::-1]
        perm = np.arange(INW)
        for base in (C_BETA, C_A):
            perm[base:base + 8] = np.arange(base + 8, base + 16)
            perm[base + 8:base + 16] = np.arange(base, base + 8)
        w_in = w_in[:, perm]
    dcw = inputs["dn_conv_w"][0]
    hcw = inputs["hy_conv_w"][0]
    alog = inputs["dn_a_log"][0]
    dtb = inputs["dn_dt_bias"][0]
    if hf == 1:
        dcw, hcw, alog, dtb = dcw[::-1], hcw[::-1], alog[::-1], dtb[::-1]
    w3 = inputs["hy_w3"][0]
    ld = inputs["hy_log_decay"][0]
    w3f, w3b, ldf, ldb = w3[:, :D], w3[:, D:], ld[:D], ld[D:]
    if hf == 0:
        w3s, lds = np.stack([w3f, w3b, w3f], axis=1), np.stack([ldf, ldb, ldf], axis=0)
    else:
        w3s, lds = np.stack([w3b, w3f, w3f], axis=1), np.stack([ldb, ldf, ldf], axis=0)
    m = {
        "hy_w1": np.ascontiguousarray(inputs["hy_w1"][0]), "hy_b1": np.ascontiguousarray(inputs["hy_b1"][0]),
        "hy_w2": np.ascontiguousarray(inputs["hy_w2"][0]), "hy_b2": np.ascontiguousarray(inputs["hy_b2"][0]),
        "hy_freq": np.ascontiguousarray(inputs["hy_freq"][0]), "hy_w3": np.ascontiguousarray(w3s),
        "hy_log_decay": np.ascontiguousarray(lds), "hy_bias": np.ascontiguousarray(inputs["hy_bias"][0]),
        "dn_conv_w": np.ascontiguousarray(dcw), "hy_conv_w": np.ascontiguousarray(hcw),
        "dn_a_log": np.ascontiguousarray(alog).reshape(16), "dn_dt_bias": np.ascontiguousarray(dtb).reshape(16),
        "dn_norm_w": np.ascontiguousarray(inputs["dn_norm_w"][0]),
        "x": np.ascontiguousarray(xs, dtype=np.float32),
        "norm_in_w": np.ascontiguousarray(inputs["norm_in_w"][0]),
        "w_in": np.ascontiguousarray(w_in),
        "w_dn_out": np.ascontiguousarray(inputs["w_dn_out"][0]),
        "w_hy_out": np.ascontiguousarray(inputs["w_hy_out"][0]),
        "w_out": np.ascontiguousarray(inputs["w_out"][0]),
        "norm_out_w": np.ascontiguousarray(inputs["norm_out_w"]),
    }
    m.update(_consts())
    return m


FULL_CFG = {"phases": ("projA", "hyproj", "gates", "dn", "hyena", "out")}


def kernel(**inputs):
    nc = build(FULL_CFG)
    in_maps = [_core_inputs(inputs, c // 2, c % 2) for c in range(8)]
    res = run_bass_kernel_spmd(nc, in_maps, core_ids=list(range(8)))
    out = np.empty((4, L, D), np.float32)
    for c in range(8):
        b, hf = c // 2, c % 2
        yc = res.results[c]["y"]
        if hf == 0:
            out[b, :HALF] = yc
        else:
            out[b, HALF:] = yc[::-1]
    return out
```
